# Optimizing a Trainium2 kernel written in Bass

```python
import jax, jax.numpy as jnp
from jax import lax
import numpy as np

D_MODEL = 1024
BATCH = 8
SEQ = 4096
DEPTH = 4

CTX_LEN = 256
GRID_W = 64

POOL_WIDTH = D_MODEL // 4
POOL_WINDOWS = (2, 4, 8, 16)
POOL_GROUP = POOL_WIDTH // len(POOL_WINDOWS)
SGU_WIDTH = D_MODEL // 4
SGU_HEADS = 4
SGU_HEAD_DIM = SGU_WIDTH // SGU_HEADS
SGU_CHUNK = 128
ATTN_WIDTH = D_MODEL - POOL_WIDTH - SGU_WIDTH
HEAD_DIM = 64
N_HEADS = ATTN_WIDTH // HEAD_DIM
N_KV_HEADS = 2
KV_GROUP = N_HEADS // N_KV_HEADS
KV_WIDTH = N_KV_HEADS * HEAD_DIM
Q_BLOCK = 128
ROPE_THETA = 10000.0
ROPE_AXIS_FREQS = HEAD_DIM // 4
MIX_WIDTH = POOL_WIDTH + SGU_WIDTH + ATTN_WIDTH

OFF_POOL = 0
OFF_U = OFF_POOL + POOL_WIDTH
OFF_V = OFF_U + SGU_WIDTH
OFF_Q = OFF_V + SGU_WIDTH
OFF_K = OFF_Q + ATTN_WIDTH
OFF_VAL = OFF_K + KV_WIDTH
IN_WIDTH = OFF_VAL + KV_WIDTH

N_EXPERTS = 16
EXPERT_FF = D_MODEL
EC_CAPACITY_FACTOR = 2

DN_ALPHA = (2 * DEPTH) ** 0.25
DN_BETA = (8 * DEPTH) ** -0.25
LN_EPS = 1e-6

kernel_name = "hybrid_pool_sgu_gqa_ec_moe_deepnorm"


def layer_norm(x, g=None, b=None):
    xf = x.astype(jnp.float32)
    mu = jnp.mean(xf, axis=-1, keepdims=True)
    var = jnp.mean(jnp.square(xf - mu), axis=-1, keepdims=True)
    y = (xf - mu) * lax.rsqrt(var + LN_EPS)
    if g is not None:
        y = y * g.astype(jnp.float32) + b.astype(jnp.float32)
    return y.astype(x.dtype)


def rms_norm(x, g):
    xf = x.astype(jnp.float32)
    y = xf * lax.rsqrt(jnp.mean(jnp.square(xf), axis=-1, keepdims=True) + LN_EPS)
    return (y * g.astype(jnp.float32)).astype(x.dtype)


def modulation(cond, w_mod, b_mod):
    m = jax.nn.silu(cond) @ w_mod + b_mod
    return jnp.split(m[:, None, :], 6, axis=-1)


def modulate(x, shift, scale):
    return layer_norm(x) * (1 + scale) + shift


def axial_rope(n):
    rows = n // GRID_W
    r = jnp.repeat(jnp.arange(rows, dtype=jnp.float32), GRID_W)
    col = jnp.tile(jnp.arange(GRID_W, dtype=jnp.float32), rows)
    inv = ROPE_THETA ** (-jnp.arange(ROPE_AXIS_FREQS, dtype=jnp.float32) / ROPE_AXIS_FREQS)
    ang = jnp.concatenate([r[:, None] * inv, col[:, None] * inv], axis=-1)
    return jnp.cos(ang), jnp.sin(ang)


def apply_rope(x, cos, sin):
    half = HEAD_DIM // 2
    xf = x.astype(jnp.float32)
    x1, x2 = xf[..., :half], xf[..., half:]
    cs, sn = cos[None, :, None, :], sin[None, :, None, :]
    return jnp.concatenate([x1 * cs - x2 * sn, x2 * cs + x1 * sn], axis=-1).astype(x.dtype)


def heads(z, off, n):
    return z[..., off:off + n * HEAD_DIM].reshape(z.shape[0], z.shape[1], n, HEAD_DIM)


def multiscale_pool(p, w, scale):
    L = p.shape[1]
    t = jnp.arange(L)
    outs = []
    for g, win in enumerate(POOL_WINDOWS):
        pg = p[..., g * POOL_GROUP:(g + 1) * POOL_GROUP].astype(jnp.float32)
        cs = jnp.concatenate([jnp.zeros_like(pg[:, :1]), jnp.cumsum(pg, axis=1)], axis=1)
        lo = jnp.clip(t - win // 2, 0, L)
        hi = jnp.clip(t + win // 2, 0, L)
        mean = (cs[:, hi] - cs[:, lo]) / (hi - lo).astype(jnp.float32)[None, :, None]
        outs.append((mean - pg).astype(p.dtype) @ w[g])
    return jnp.concatenate(outs, axis=-1) * scale


def spatial_gating(u, v, g, w_s, b_s):
    B, L, _ = u.shape
    shp = (B, L // SGU_CHUNK, SGU_CHUNK, SGU_HEADS, SGU_HEAD_DIM)
    uh = jax.nn.gelu(u).reshape(shp)
    vh = layer_norm(jax.nn.gelu(v).reshape(shp)) * g
    mixed = jnp.einsum('hpq,bcqhd->bcphd', w_s, vh) + b_s.T[:, :, None]
    return (uh * mixed).reshape(B, L, SGU_WIDTH)


def block_attention(q, k, v):
    B, Lq = q.shape[0], q.shape[1]
    nb = Lq // Q_BLOCK
    qb = q.reshape(B, nb, Q_BLOCK, N_KV_HEADS, KV_GROUP, HEAD_DIM).transpose(1, 0, 2, 3, 4, 5)
    scale = HEAD_DIM ** -0.5

    def one_block(qi):
        s = jnp.einsum('bqkgd,bskd->bkgqs', qi, k).astype(jnp.float32) * scale
        p = jax.nn.softmax(s, axis=-1).astype(v.dtype)
        return jnp.einsum('bkgqs,bskd->bqkgd', p, v)

    o = lax.map(one_block, qb)
    return o.transpose(1, 0, 2, 3, 4, 5).reshape(B, Lq, N_HEADS * HEAD_DIM)


def mixer_output(z, attn, pool_w, pool_scale, sgu_g, sgu_w, sgu_b, w_out):
    pool = multiscale_pool(z[..., OFF_POOL:OFF_U], pool_w, pool_scale)
    sgu = spatial_gating(z[..., OFF_U:OFF_V], z[..., OFF_V:OFF_Q], sgu_g, sgu_w, sgu_b)
    return jnp.concatenate([pool, sgu, attn], axis=-1) @ w_out


def expert_choice_moe(h, w_router, w1, w3, w2):
    B, L, D = h.shape
    cap = EC_CAPACITY_FACTOR * L // N_EXPERTS
    aff = jax.nn.softmax((h @ w_router).astype(jnp.float32), axis=-1)
    gate, idx = lax.top_k(jnp.swapaxes(aff, 1, 2), cap)
    xg = jax.vmap(lambda hb, ib: hb[ib])(h, idx)
    hid = jax.nn.silu(jnp.einsum('becd,edf->becf', xg, w1)) * jnp.einsum('becd,edf->becf', xg, w3)
    y = jnp.einsum('becf,efd->becd', hid, w2) * gate[..., None].astype(h.dtype)
    return jax.vmap(lambda ib, yb: jnp.zeros((L, D), yb.dtype).at[ib.reshape(-1)].add(yb.reshape(-1, D)))(idx, y)


def setup_inputs(seed: int = 0) -> dict:
    key = jax.random.key(seed)
    ks = jax.random.split(key, 24)
    f32 = jnp.float32

    def nrm(k, shape, std):
        return jax.random.normal(k, shape, f32) * std

    w_in = nrm(ks[6], (DEPTH, D_MODEL, IN_WIDTH), D_MODEL ** -0.5)
    w_in = w_in.at[:, :, OFF_VAL:].multiply(DN_BETA)
    return {
        "x": nrm(ks[0], (BATCH, SEQ, D_MODEL), 1.0),
        "c": nrm(ks[1], (BATCH, D_MODEL), 1.0),
        "ctx": nrm(ks[2], (BATCH, CTX_LEN, D_MODEL), 1.0),
        "c_ctx": nrm(ks[3], (D_MODEL,), 1.0),
        "w_mod": nrm(ks[4], (DEPTH, D_MODEL, 6 * D_MODEL), 0.5 * D_MODEL ** -0.5),
        "b_mod": nrm(ks[5], (DEPTH, 6 * D_MODEL), 0.02),
        "w_in": w_in,
        "pool_w": nrm(ks[7], (DEPTH, len(POOL_WINDOWS), POOL_GROUP, POOL_GROUP), POOL_GROUP ** -0.5),
        "pool_scale": 1.0 + nrm(ks[8], (DEPTH, POOL_WIDTH), 0.02),
        "sgu_g": 1.0 + nrm(ks[9], (DEPTH, SGU_HEADS, SGU_HEAD_DIM), 0.02),
        "sgu_w": nrm(ks[10], (DEPTH, SGU_HEADS, SGU_CHUNK, SGU_CHUNK), 0.5 * SGU_CHUNK ** -0.5),
        "sgu_b": 1.0 + nrm(ks[11], (DEPTH, SGU_HEADS, SGU_CHUNK), 0.02),
        "q_g": 1.0 + nrm(ks[12], (DEPTH, HEAD_DIM), 0.02),
        "k_g": 1.0 + nrm(ks[13], (DEPTH, HEAD_DIM), 0.02),
        "w_out": nrm(ks[14], (DEPTH, MIX_WIDTH, D_MODEL), DN_BETA * MIX_WIDTH ** -0.5),
        "ln1_g": 1.0 + nrm(ks[15], (DEPTH, D_MODEL), 0.02),
        "ln1_b": nrm(ks[16], (DEPTH, D_MODEL), 0.02),
        "w_router": nrm(ks[17], (DEPTH, D_MODEL, N_EXPERTS), D_MODEL ** -0.5),
        "w1": nrm(ks[18], (DEPTH, N_EXPERTS, D_MODEL, EXPERT_FF), D_MODEL ** -0.5),
        "w3": nrm(ks[19], (DEPTH, N_EXPERTS, D_MODEL, EXPERT_FF), D_MODEL ** -0.5),
        "w2": nrm(ks[20], (DEPTH, N_EXPERTS, EXPERT_FF, D_MODEL), DN_BETA * EXPERT_FF ** -0.5),
        "ln2_g": 1.0 + nrm(ks[21], (DEPTH, D_MODEL), 0.02),
        "ln2_b": nrm(ks[22], (DEPTH, D_MODEL), 0.02),
    }


def reference(x, c, ctx, c_ctx, w_mod, b_mod, w_in, pool_w, pool_scale, sgu_g, sgu_w, sgu_b,
              q_g, k_g, w_out, ln1_g, ln1_b, w_router, w1, w3, w2, ln2_g, ln2_b):
    B, S, _ = x.shape
    cos, sin = axial_rope(S)
    xc = ctx
    for l in range(DEPTH):
        last = l == DEPTH - 1
        sh1, sc1, g1, sh2, sc2, g2 = modulation(c, w_mod[l], b_mod[l])
        csh1, csc1, cg1, csh2, csc2, cg2 = modulation(c_ctx[None, :], w_mod[l], b_mod[l])

        hl = modulate(x, sh1, sc1)
        hc = modulate(xc, csh1, csc1)
        zl = hl @ w_in[l]
        zc_kv = hc @ w_in[l][:, OFF_K:]
        kc = rms_norm(zc_kv[..., :KV_WIDTH].reshape(B, -1, N_KV_HEADS, HEAD_DIM), k_g[l])
        vc = zc_kv[..., KV_WIDTH:].reshape(B, -1, N_KV_HEADS, HEAD_DIM)

        ql = apply_rope(rms_norm(heads(zl, OFF_Q, N_HEADS), q_g[l]), cos, sin)
        kl = apply_rope(rms_norm(heads(zl, OFF_K, N_KV_HEADS), k_g[l]), cos, sin)
        vl = heads(zl, OFF_VAL, N_KV_HEADS)
        attn_l = block_attention(ql, jnp.concatenate([kl, kc], axis=1), jnp.concatenate([vl, vc], axis=1))
        yl = mixer_output(zl, attn_l, pool_w[l], pool_scale[l], sgu_g[l], sgu_w[l], sgu_b[l], w_out[l])

        if not last:
            zc = hc @ w_in[l][:, :OFF_K]
            qc = rms_norm(heads(zc, OFF_Q, N_HEADS), q_g[l])
            attn_c = block_attention(qc, kc, vc)
            yc = mixer_output(zc, attn_c, pool_w[l], pool_scale[l], sgu_g[l], sgu_w[l], sgu_b[l], w_out[l])
            xc = layer_norm(DN_ALPHA * xc + cg1 * yc, ln1_g[l], ln1_b[l])
            mc = expert_choice_moe(modulate(xc, csh2, csc2), w_router[l], w1[l], w3[l], w2[l])
            xc = layer_norm(DN_ALPHA * xc + cg2 * mc, ln2_g[l], ln2_b[l])

        x = layer_norm(DN_ALPHA * x + g1 * yl, ln1_g[l], ln1_b[l])
        ml = expert_choice_moe(modulate(x, sh2, sc2), w_router[l], w1[l], w3[l], w2[l])
        x = layer_norm(DN_ALPHA * x + g2 * ml, ln2_g[l], ln2_b[l])
    return x
```

```python
import numpy as np
from contextlib import ExitStack
import concourse.bass as bass
import concourse.mybir as mybir
from concourse.bass_utils import run_bass_kernel_spmd

F32 = mybir.dt.float32
BF16 = mybir.dt.bfloat16
I32 = mybir.dt.int32
AF = mybir.ActivationFunctionType
ALU = mybir.AluOpType
AX = mybir.AxisListType

DEPTH = 4
D = 1024
SEQ = 4096
CTX = 256
NT = 34
NTL = 32
NTOK = SEQ + CTX
NE = 16
CAP_L = 512
CAP_C = 32
ALPHA = float((2 * DEPTH) ** 0.25)
EPS = 1e-6
BIG = 100000.0


class Sched:
    def __init__(self, nc, es, n_dma_sems=48):
        self.nc = nc
        self.eng = {"pe": nc.tensor, "act": nc.scalar, "dve": nc.vector, "pool": nc.gpsimd, "sp": nc.sync}
        self.sem = {k: es.enter_context(nc.semaphore("prog_" + k)) for k in self.eng}
        self.cnt = {k: 0 for k in self.eng}
        self.dsem = [es.enter_context(nc.semaphore("dma%d" % i)) for i in range(n_dma_sems)]
        self.dtot = [0] * n_dma_sems
        self.dnext = 0
        self.waited = {k: {} for k in self.eng}
        self.res = {}
        self.n_inst = 0
        self.n_wait = 0

    def _semobj(self, key):
        return self.sem[key] if isinstance(key, str) else self.dsem[key]

    def _collect(self, reads, writes):
        deps = {}

        def add(d):
            if d is None:
                return
            k, v = d
            if deps.get(k, 0) < v:
                deps[k] = v
        for r in reads:
            e = self.res.get(r)
            if e is not None:
                add(e["w"])
        for w in writes:
            e = self.res.get(w)
            if e is not None:
                add(e["w"])
                for k, v in e["r"].items():
                    add((k, v))
        return deps

    def _wait(self, F, deps, skip_self=False):
        for k, v in deps.items():
            if skip_self and k == F:
                continue
            if self.waited[F].get(k, 0) < v:
                self.eng[F].wait_ge(self._semobj(k), v)
                self.waited[F][k] = v
                self.n_wait += 1

    def _update(self, dep, reads, writes):
        k, v = dep
        for r in reads:
            e = self.res.setdefault(r, {"w": None, "r": {}})
            if e["r"].get(k, 0) < v:
                e["r"][k] = v
        for w in writes:
            self.res[w] = {"w": dep, "r": {}}

    def op(self, F, fn, reads=(), writes=(), skip_self=None):
        if skip_self is None:
            skip_self = (F == "pe")
        deps = self._collect(reads, writes)
        self._wait(F, deps, skip_self=skip_self)
        inst = fn(self.eng[F])
        self.cnt[F] += 1
        inst.then_inc(self.sem[F], 1)
        self._update((F, self.cnt[F]), reads, writes)
        self.n_inst += 1
        return inst

    def dma(self, Q, fn, reads=(), writes=()):
        deps = self._collect(reads, writes)
        self._wait(Q, deps)
        i = self.dnext
        self.dnext = (self.dnext + 1) % len(self.dsem)
        if self.dtot[i] > 0 and self.waited[Q].get(i, 0) < self.dtot[i]:
            self.eng[Q].wait_ge(self.dsem[i], self.dtot[i])
            self.waited[Q][i] = self.dtot[i]
            self.n_wait += 1
        inst = fn(self.eng[Q])
        self.dtot[i] += 16
        inst.then_inc(self.dsem[i], 16)
        self._update((i, self.dtot[i]), reads, writes)
        self.n_inst += 1
        return inst

    def barrier(self):
        for F in self.eng:
            for i, t in enumerate(self.dtot):
                if t > 0 and self.waited[F].get(i, 0) < t:
                    self.eng[F].wait_ge(self.dsem[i], t)
                    self.waited[F][i] = t
            for k in self.eng:
                if k != F and self.cnt[k] > 0 and self.waited[F].get(k, 0) < self.cnt[k]:
                    self.eng[F].wait_ge(self.sem[k], self.cnt[k])
                    self.waited[F][k] = self.cnt[k]
        self.res = {}


def build(L, debug=False):
    nc = bass.Bass("TRN2", target_bir_lowering=False)

    def DT(name, shape, dt=F32, kind="ExternalInput"):
        return nc.dram_tensor(name, shape, dt, kind=kind).ap()

    x_in = DT("x", [SEQ, D])
    ctx_in = DT("ctx", [CTX, D])
    cond_in = DT("cond", [128, 8, 2])
    w_mod = DT("w_mod", [L, D, 6 * D])
    b_modT = DT("b_modT", [L, 128, 48])
    w_in = DT("w_in", [L, D, 1536])
    wbd_in = DT("wbd", [L, 2, 128, 128])
    pscT_in = DT("pscT", [L, 128, 2])
    band_in = DT("band", [128, 4, 5, 128])
    sgug_in = DT("sgu_g", [L, 256])
    sguwT_in = DT("sgu_wT", [L, 128, 4, 128])
    sgubT_in = DT("sgu_bT", [L, 128, 4])
    qkg_in = DT("qkg", [L, 640])
    w_out = DT("w_out", [L, D, D])
    ln_in = DT("ln", [L, 4, D])
    wr_in = DT("w_router", [L, 128, 8, NE])
    w1 = DT("w1", [L, NE, D, D])
    w3 = DT("w3", [L, NE, D, D])
    w2 = DT("w2", [L, NE, D, D])
    cs_in = DT("cs", [128, NTL, 64])
    out = DT("out", [SEQ, D], kind="ExternalOutput")
    ctx_out = DT("ctx_out", [CTX, D], kind="ExternalOutput")
    if debug:
        dbg_x1 = DT("dbg_x1", [NTOK, D], kind="ExternalOutput")
        dbg_aff = DT("dbg_aff", [128, NT, NE], kind="ExternalOutput")
        dbg_mix = DT("dbg_mix", [128, 4, NTOK], BF16, kind="ExternalOutput")
        dbg_idx = DT("dbg_idx", [128, NE, 5], I32, kind="ExternalOutput")
        dbg_gate = DT("dbg_gate", [128, NE, 5], kind="ExternalOutput")
        dbg_macc = DT("dbg_macc", [NTOK, D], kind="ExternalOutput")
        dbg_pos = DT("dbg_pos", [128, NT, NE], kind="ExternalOutput")
    xs = DT("xs", [NTOK, D], kind="Internal")
    macc = DT("macc", [NTOK, D], kind="Internal")
    xh2 = DT("xh2", [NTOK, D], BF16, kind="Internal")

    def src_tile(ll, j):
        if ll == 0:
            return x_in[j * 128:(j + 1) * 128, :] if j < NTL else ctx_in[(j - NTL) * 128:(j - NTL + 1) * 128, :]
        return xs[j * 128:(j + 1) * 128, :]

    def dst_tile(ll, j):
        if ll == L - 1:
            return out[j * 128:(j + 1) * 128, :] if j < NTL else ctx_out[(j - NTL) * 128:(j - NTL + 1) * 128, :]
        return xs[j * 128:(j + 1) * 128, :]

    def src_key(ll, j):
        return ("xin", j) if ll == 0 else ("xs", j)

    def dst_key(ll, j):
        return ("xout", j) if ll == L - 1 else ("xs", j)

    es0 = ExitStack()
    with es0:
        S = Sched(nc, es0)
        bcreg = es0.enter_context(nc.gpsimd.register("bcreg"))
        nc.gpsimd.reg_mov(bcreg, NTOK - 1)

        uid = [0]

        def SB(es, name, shape, dt):
            uid[0] += 1
            return es.enter_context(nc.sbuf_tensor("s%d_%s" % (uid[0], name), shape, dt))

        def PS(es, name, shape, dt):
            uid[0] += 1
            return es.enter_context(nc.psum_tensor("p%d_%s" % (uid[0], name), shape, dt))

        identf = SB(es0, "identf", [128, 128], F32)
        identb = SB(es0, "identb", [128, 128], BF16)
        ones_f = SB(es0, "ones_f", [128, 128], F32)
        lmat = SB(es0, "lmat", [128, 128], F32)
        iota_row = SB(es0, "iota_row", [128, 512], mybir.dt.float16)
        tokid = SB(es0, "tokid", [128, NT], F32)
        jmp = SB(es0, "jmp", [128, 128], F32)
        modT = SB(es0, "modT", [128, 48, 2], F32)
        aff_all = SB(es0, "aff_all", [128, NT, NE], F32)
        epsc = SB(es0, "epsc", [128, 1], F32)
        S.op("pool", lambda e: e.memset(epsc[:], EPS), writes=["epsc"])
        S.op("pool", lambda e: e.iota(jmp[:], [[1, 128]], base=0, channel_multiplier=-1,
                                      allow_small_or_imprecise_dtypes=True), writes=["jmp"])
        S.op("pool", lambda e: e.tensor_single_scalar(out=identf[:], in_=jmp[:], scalar=0.0, op=ALU.is_equal),
             reads=["jmp"], writes=["identf"])
        S.op("pool", lambda e: e.tensor_single_scalar(out=identb[:], in_=jmp[:], scalar=0.0, op=ALU.is_equal),
             reads=["jmp"], writes=["identb"])
        S.op("pool", lambda e: e.tensor_single_scalar(out=lmat[:], in_=jmp[:], scalar=0.0, op=ALU.is_gt),
             reads=["jmp"], writes=["lmat"])
        S.op("pool", lambda e: e.memset(ones_f[:], 1.0), writes=["ones_f"])
        S.op("pool", lambda e: e.iota(iota_row[:], [[1, 512]], base=0, channel_multiplier=0,
                                      allow_small_or_imprecise_dtypes=True), writes=["iota_row"])
        S.op("pool", lambda e: e.iota(tokid[:], [[128, NT]], base=0, channel_multiplier=1,
                                      allow_small_or_imprecise_dtypes=True), writes=["tokid"])
        tokA = SB(es0, "tokA", [128, NT], F32)
        tokB = SB(es0, "tokB", [128, 1], F32)
        S.op("pool", lambda e: e.iota(tokA[:], [[128, NT]], base=0, channel_multiplier=0,
                                      allow_small_or_imprecise_dtypes=True), writes=["tokA"])
        S.op("pool", lambda e: e.iota(tokB[:], [[0, 1]], base=0, channel_multiplier=1,
                                      allow_small_or_imprecise_dtypes=True), writes=["tokB"])

        def ln_stats(es_tiles, src, key_src, tag):
            st, mv, rstd = es_tiles
            S.op("dve", lambda e: e.bn_stats(st[:, 0, :], src[:, 0:512]), reads=[key_src], writes=[tag + "st0"])
            S.op("dve", lambda e: e.bn_stats(st[:, 1, :], src[:, 512:1024]), reads=[key_src], writes=[tag + "st1"])
            S.op("dve", lambda e: e.bn_aggr(mv[:], st[:]), reads=[tag + "st0", tag + "st1"], writes=[tag + "mv"])
            S.op("act", lambda e: e.activation(out=rstd[:], in_=mv[:, 1:2], func=AF.Sqrt, bias=epsc[:], scale=1.0),
                 reads=[tag + "mv", "epsc"], writes=[tag + "rstd"])
            S.op("dve", lambda e: e.reciprocal(rstd[:], rstd[:]), reads=[tag + "rstd"], writes=[tag + "rstd"])

        for ll in range(L):
            esA = ExitStack()
            with esA:
                condT = SB(esA, "condT", [128, 8, 2], F32)
                bmT = SB(esA, "bmT", [128, 48], F32)
                wblk = [SB(esA, "wblk%d" % i, [128, 8, 512], F32) for i in range(2)]
                pmod = PS(esA, "pmod", [128, 48, 2], F32)
                S.dma("sp", lambda e: e.dma_start(out=condT[:], in_=cond_in), writes=["condT"])
                S.dma("sp", lambda e: e.dma_start(out=bmT[:], in_=b_modT[ll]), writes=["bmT"])
                S.op("act", lambda e: e.activation(out=condT[:], in_=condT[:], func=AF.Silu),
                     reads=["condT"], writes=["condT"])
                wm = w_mod[ll].rearrange("(k p) c -> p k c", p=128)
                for cb in range(12):
                    wb_ = wblk[cb % 2]
                    wk = "wblk%d" % (cb % 2)
                    S.dma("sp", lambda e: e.dma_start(out=wb_[:], in_=wm[:, :, cb * 512:(cb + 1) * 512]), writes=[wk])
                    for sub in range(4):
                        j = cb * 4 + sub
                        for k in range(8):
                            S.op("pe", lambda e: e.matmul(pmod[:, j, :], lhsT=wb_[:, k, sub * 128:(sub + 1) * 128],
                                                          rhs=condT[:, k, :], start=(k == 0), stop=(k == 7)),
                                 reads=[wk, "condT"], writes=["pmod"])
                S.op("dve", lambda e: e.tensor_tensor(out=modT[:], in0=pmod[:],
                                                      in1=bmT[:].unsqueeze(2).broadcast_to([128, 48, 2]), op=ALU.add),
                     reads=["pmod", "bmT"], writes=["modT"])
                for base in (8, 32):
                    S.op("dve", lambda e: e.tensor_scalar_add(modT[:, base:base + 8, :], modT[:, base:base + 8, :], 1.0),
                         reads=["modT"], writes=["modT"])
            S.barrier()

            def build_gate_rows(es, vbase, tagname):
                rows = [SB(es, "%s_%d" % (tagname, s), [128, D], F32) for s in range(2)]
                est = ExitStack()
                with est:
                  diag = [SB(est, "%s_dg%d" % (tagname, i), [128, 128], F32) for i in range(2)]
                  pg = PS(est, tagname + "_pg", [128, D], F32)
                  n = 0
                  for s in range(2):
                    for dc in range(8):
                        dg = diag[n % 2]
                        dk = "%s_dg%d" % (tagname, n % 2)
                        n += 1
                        S.op("dve", lambda e: e.tensor_scalar(out=dg[:], in0=identf[:], scalar1=modT[:, vbase + dc, s:s + 1],
                                                              scalar2=None, op0=ALU.mult),
                             reads=["identf", "modT"], writes=[dk])
                        S.op("pe", lambda e: e.matmul(pg[:, dc * 128:(dc + 1) * 128], lhsT=ones_f[:], rhs=dg[:],
                                                      start=True, stop=True),
                             reads=[dk, "ones_f"], writes=[tagname + "_pg"])
                    S.op("act", lambda e: e.activation(out=rows[s][:], in_=pg[:], func=AF.Copy),
                         reads=[tagname + "_pg"], writes=["%s_%d" % (tagname, s)])
                  S.barrier()
                return rows

            esBE = ExitStack()
            with esBE:
                mixps = SB(esBE, "mixps", [128, 4, NTOK], BF16)
                qT_all = SB(esBE, "qT_all", [128, 4, NTOK], BF16)
                kT_all = SB(esBE, "kT_all", [128, 2, NTOK], BF16)
                S.op("pool", lambda e: e.memset(kT_all[:], 0.0), writes=["kT_zero"])
                v_all = SB(esBE, "v_all", [128, NT, 2, 65], BF16)
                S.op("pool", lambda e: e.memset(v_all[:, :, :, 64:65], 1.0), writes=["v_ones"])
                esBC = ExitStack()
                with esBC:
                    p_all = SB(esBC, "p_all", [128, NT, 256], BF16)
                    esB = ExitStack()
                    with esB:
                        winb = SB(esB, "winb", [128, 8, 1536], BF16)
                        cs = SB(esB, "cs", [128, NTL, 64], F32)
                        sgug = SB(esB, "sgug", [128, 256], F32)
                        wsT = SB(esB, "wsT", [128, 4, 128], BF16)
                        sbT = SB(esB, "sbT", [128, 4], F32)
                        qkg = SB(esB, "qkg", [128, 640], F32)
                        xt = [SB(esB, "xt%d" % i, [128, D], F32) for i in range(2)]
                        st = SB(esB, "st", [128, 2, 6], F32)
                        mv = SB(esB, "mv", [128, 2], F32)
                        rstd = SB(esB, "rstd", [128, 1], F32)
                        xb = SB(esB, "xb", [128, D], BF16)
                        hT = SB(esB, "hT", [128, 8, 128], BF16)
                        gu = SB(esB, "gu", [128, 256], BF16)
                        gv = SB(esB, "gv", [128, 256], F32)
                        stv = SB(esB, "stv", [128, 4, 6], F32)
                        mvv = SB(esB, "mvv", [128, 4, 2], F32)
                        rsv = SB(esB, "rsv", [128, 4], F32)
                        vn = SB(esB, "vn", [128, 256], F32)
                        vh = SB(esB, "vh", [128, 256], BF16)
                        sgo = SB(esB, "sgo", [128, 256], BF16)
                        sq = SB(esB, "sq", [128, 640], F32)
                        ss = SB(esB, "ss", [128, 10], F32)
                        qn = SB(esB, "qn", [128, 10, 64], F32)
                        ta = SB(esB, "ta", [128, 10, 32], F32)
                        tb = SB(esB, "tb", [128, 10, 32], F32)
                        qr = SB(esB, "qr", [128, 640], BF16)
                        pT = PS(esB, "pT", [128, 8, 128], BF16)
                        pz = PS(esB, "pz", [128, 1536], F32)
                        psg = PS(esB, "psg", [128, 256], F32)
                        ptr = PS(esB, "ptr", [128, 7, 128], BF16)
                        S.dma("pool", lambda e: e.dma_start(out=winb[:], in_=w_in[ll].rearrange("(k p) c -> p k c", p=128)),
                              writes=["winb"])
                        S.dma("pool", lambda e: e.dma_start(out=wsT[:], in_=sguwT_in[ll]), writes=["wsT"])
                        S.dma("sp", lambda e: e.dma_start(out=cs[:], in_=cs_in), writes=["cs"])
                        S.dma("sp", lambda e: e.dma_start(out=sgug[:], in_=sgug_in[ll:ll + 1, :].broadcast_to([128, 256])),
                              writes=["sgug"])
                        S.dma("sp", lambda e: e.dma_start(out=qkg[:], in_=qkg_in[ll:ll + 1, :].broadcast_to([128, 640])),
                              writes=["qkg"])
                        S.dma("sp", lambda e: e.dma_start(out=sbT[:], in_=sgubT_in[ll]), writes=["sbT"])
                        def front_B(j):
                            s = 0 if j < NTL else 1
                            tok = slice(j * 128, (j + 1) * 128)
                            xtj = xt[j % 2]
                            xk = "xt%d" % (j % 2)
                            S.dma("sp", lambda e: e.dma_start(out=xtj[:], in_=src_tile(ll, j)),
                                  reads=[src_key(ll, j)], writes=[xk])
                            ln_stats((st, mv, rstd), xtj, xk, "B")
                            S.op("dve", lambda e: e.tensor_scalar(out=xb[:], in0=xtj[:], scalar1=mv[:, 0:1], scalar2=rstd[:],
                                                                  op0=ALU.subtract, op1=ALU.mult),
                                 reads=[xk, "Bmv", "Brstd"], writes=["xb"])
                            for k in range(8):
                                S.op("pe", lambda e: e.transpose(pT[:, k, :], xb[:, k * 128:(k + 1) * 128], identb[:]),
                                     reads=["xb", "identb"], writes=["pT"])
                            for k in range(8):
                                S.op("act", lambda e: e.activation(out=hT[:, k, :], in_=pT[:, k, :], func=AF.Identity,
                                                                   bias=modT[:, 0 + k, s:s + 1], scale=modT[:, 8 + k, s:s + 1]),
                                     reads=["pT", "modT"], writes=[("hT", k)])
                        front_B(0)
                        for j in range(NT):
                            s = 0 if j < NTL else 1
                            tok = slice(j * 128, (j + 1) * 128)
                            for cblk in range(3):
                                for k in range(8):
                                    S.op("pe", lambda e: e.matmul(pz[:, cblk * 512:(cblk + 1) * 512], lhsT=hT[:, k, :],
                                                                  rhs=winb[:, k, cblk * 512:(cblk + 1) * 512],
                                                                  start=(k == 0), stop=(k == 7)),
                                         reads=[("hT", k), "winb"], writes=[("pz", cblk)])
                            if j + 1 < NT:
                                front_B(j + 1)
                            S.op("act", lambda e: e.activation(out=p_all[:, j, :], in_=pz[:, 0:256], func=AF.Copy),
                                 reads=[("pz", 0)], writes=[("p_all", j)])
                            S.op("act", lambda e: e.activation(out=gu[:], in_=pz[:, 256:512], func=AF.Gelu_apprx_tanh),
                                 reads=[("pz", 0)], writes=["gu"])
                            S.op("act", lambda e: e.activation(out=gv[:], in_=pz[:, 512:768], func=AF.Gelu_apprx_tanh),
                                 reads=[("pz", 1)], writes=["gv"])
                            for h in range(4):
                                S.op("dve", lambda e: e.bn_stats(stv[:, h, :], gv[:, h * 64:(h + 1) * 64]),
                                     reads=["gv"], writes=[("stv", h)])
                                S.op("dve", lambda e: e.bn_aggr(mvv[:, h, :], stv[:, h, :]),
                                     reads=[("stv", h)], writes=["mvv"])
                            S.op("act", lambda e: e.activation(out=rsv[:], in_=mvv[:, :, 1], func=AF.Sqrt, bias=epsc[:], scale=1.0),
                                 reads=["mvv", "epsc"], writes=["rsv"])
                            S.op("dve", lambda e: e.reciprocal(rsv[:], rsv[:]), reads=["rsv"], writes=["rsv"])
                            for h in range(4):
                                S.op("dve", lambda e: e.tensor_scalar(out=vn[:, h * 64:(h + 1) * 64], in0=gv[:, h * 64:(h + 1) * 64],
                                                                      scalar1=mvv[:, h, 0:1], scalar2=rsv[:, h:h + 1],
                                                                      op0=ALU.subtract, op1=ALU.mult),
                                     reads=["gv", "mvv", "rsv"], writes=["vn"])
                            S.op("dve", lambda e: e.tensor_tensor(out=vh[:], in0=vn[:], in1=sgug[:], op=ALU.mult),
                                 reads=["vn", "sgug"], writes=["vh"])
                            for h in range(4):
                                S.op("pe", lambda e: e.matmul(psg[:, h * 64:(h + 1) * 64], lhsT=wsT[:, h, :],
                                                              rhs=vh[:, h * 64:(h + 1) * 64], start=True, stop=True),
                                     reads=["wsT", "vh"], writes=["psg"])
                            for h in range(4):
                                S.op("dve", lambda e: e.scalar_tensor_tensor(out=sgo[:, h * 64:(h + 1) * 64],
                                                                             in0=psg[:, h * 64:(h + 1) * 64],
                                                                             scalar=sbT[:, h:h + 1],
                                                                             in1=gu[:, h * 64:(h + 1) * 64],
                                                                             op0=ALU.add, op1=ALU.mult),
                                     reads=["psg", "sbT", "gu"], writes=["sgo"])
                            for c in range(2):
                                S.op("pe", lambda e: e.transpose(ptr[:, c, :], sgo[:, c * 128:(c + 1) * 128], identb[:]),
                                     reads=["sgo", "identb"], writes=[("ptr", 0)])
                            S.op("act", lambda e: e.activation(out=mixps[:, 2:4, tok], in_=ptr[:, 0:2, :], func=AF.Copy),
                                 reads=[("ptr", 0)], writes=[("mixps_s", j)])
                            S.op("act", lambda e: e.activation(out=sq[:], in_=pz[:, 768:1408], func=AF.Square),
                                 reads=[("pz", 1), ("pz", 2)], writes=["sq"])
                            S.op("dve", lambda e: e.tensor_reduce(out=ss[:], in_=sq[:].rearrange("p (h d) -> p h d", d=64),
                                                                  axis=AX.X, op=ALU.add),
                                 reads=["sq"], writes=["ss"])
                            S.op("act", lambda e: e.activation(out=ss[:], in_=ss[:], func=AF.Sqrt, bias=epsc[:], scale=1.0 / 64),
                                 reads=["ss", "epsc"], writes=["ss"])
                            S.op("dve", lambda e: e.reciprocal(ss[:], ss[:]), reads=["ss"], writes=["ss"])
                            S.op("dve", lambda e: e.tensor_tensor(out=qn[:], in0=pz[:, 768:1408].rearrange("p (h d) -> p h d", d=64),
                                                                  in1=ss[:].unsqueeze(2).broadcast_to([128, 10, 64]), op=ALU.mult),
                                 reads=[("pz", 1), ("pz", 2), "ss"], writes=["qn"])
                            S.op("dve", lambda e: e.tensor_tensor(out=qn[:], in0=qn[:],
                                                                  in1=qkg[:].rearrange("p (h d) -> p h d", d=64), op=ALU.mult),
                                 reads=["qn", "qkg"], writes=["qn"])
                            qdst = qr[:, 0:512].rearrange("p (g k d) -> p k g d", g=4, k=2, d=64)
                            kdst = qr[:, 512:640].rearrange("p (k d) -> p k d", d=64)
                            qsrc = qn[:, 0:8, :].rearrange("p (k g) d -> p k g d", k=2)
                            ksrc = qn[:, 8:10, :]
                            if j < NTL:
                                cosb = cs[:, j:j + 1, 0:32].broadcast_to([128, 10, 32])
                                sinb = cs[:, j:j + 1, 32:64].broadcast_to([128, 10, 32])
                                x1 = qn[:, :, 0:32]
                                x2 = qn[:, :, 32:64]
                                S.op("dve", lambda e: e.tensor_tensor(out=ta[:], in0=x1, in1=cosb, op=ALU.mult),
                                     reads=["qn", "cs"], writes=["ta"])
                                S.op("dve", lambda e: e.tensor_tensor(out=tb[:], in0=x2, in1=sinb, op=ALU.mult),
                                     reads=["qn", "cs"], writes=["tb"])
                                S.op("dve", lambda e: e.tensor_tensor(out=qdst[:, :, :, 0:32],
                                                                      in0=ta[:, 0:8, :].rearrange("p (k g) d -> p k g d", k=2),
                                                                      in1=tb[:, 0:8, :].rearrange("p (k g) d -> p k g d", k=2),
                                                                      op=ALU.subtract),
                                     reads=["ta", "tb"], writes=["qr_a"])
                                S.op("dve", lambda e: e.tensor_tensor(out=kdst[:, :, 0:32], in0=ta[:, 8:10, :], in1=tb[:, 8:10, :],
                                                                      op=ALU.subtract),
                                     reads=["ta", "tb"], writes=["qr_b"])
                                S.op("dve", lambda e: e.tensor_tensor(out=ta[:], in0=x2, in1=cosb, op=ALU.mult),
                                     reads=["qn", "cs"], writes=["ta"])
                                S.op("dve", lambda e: e.tensor_tensor(out=tb[:], in0=x1, in1=sinb, op=ALU.mult),
                                     reads=["qn", "cs"], writes=["tb"])
                                S.op("dve", lambda e: e.tensor_tensor(out=qdst[:, :, :, 32:64],
                                                                      in0=ta[:, 0:8, :].rearrange("p (k g) d -> p k g d", k=2),
                                                                      in1=tb[:, 0:8, :].rearrange("p (k g) d -> p k g d", k=2),
                                                                      op=ALU.add),
                                     reads=["ta", "tb"], writes=["qr_c"])
                                S.op("dve", lambda e: e.tensor_tensor(out=kdst[:, :, 32:64], in0=ta[:, 8:10, :], in1=tb[:, 8:10, :],
                                                                      op=ALU.add),
                                     reads=["ta", "tb"], writes=["qr_d"])
                            else:
                                S.op("dve", lambda e: e.tensor_copy(out=qdst, in_=qsrc), reads=["qn"], writes=["qr_a", "qr_c"])
                                S.op("dve", lambda e: e.tensor_copy(out=kdst, in_=ksrc), reads=["qn"], writes=["qr_b", "qr_d"])
                            for c in range(5):
                                S.op("pe", lambda e: e.transpose(ptr[:, 2 + c, :], qr[:, c * 128:(c + 1) * 128], identb[:]),
                                     reads=["qr_a", "qr_b", "qr_c", "qr_d", "identb"], writes=[("ptr", 1)])
                            S.op("act", lambda e: e.activation(out=qT_all[:, :, tok], in_=ptr[:, 2:6, :], func=AF.Copy),
                                 reads=[("ptr", 1)], writes=[("qT", j)])
                            for kv_ in range(2):
                                S.op("act", lambda e: e.activation(out=kT_all[kv_ * 64:(kv_ + 1) * 64, kv_, tok],
                                                                   in_=ptr[kv_ * 64:(kv_ + 1) * 64, 6, :], func=AF.Copy),
                                     reads=[("ptr", 1), "kT_zero"], writes=[("kT", j, kv_)])
                            S.op("dve", lambda e: e.tensor_copy(out=v_all[:, j, :, 0:64],
                                                                in_=pz[:, 1408:1536].rearrange("p (k d) -> p k d", d=64)),
                                 reads=[("pz", 2)], writes=[("v", j)])
                    S.barrier()
                    esC = ExitStack()
                    with esC:
                        bandf = SB(esC, "bandf", [128, 4, 5, 128], F32)
                        band = SB(esC, "band", [128, 4, 5, 128], BF16)
                        wbd = SB(esC, "wbd", [128, 2, 128], BF16)
                        pscT = SB(esC, "pscT", [128, 2], F32)
                        pooledT = SB(esC, "pooledT", [128, 2, 128], BF16)
                        ppool = PS(esC, "ppool", [128, 4, 128], F32)
                        pp2 = PS(esC, "pp2", [128, 2, 128], F32)
                        S.dma("sp", lambda e: e.dma_start(out=bandf[:], in_=band_in), writes=["bandf"])
                        S.op("dve", lambda e: e.tensor_copy(out=band[:], in_=bandf[:]), reads=["bandf"], writes=["band"])
                        S.dma("pool", lambda e: e.dma_start(out=wbd[:], in_=wbd_in[ll].rearrange("c p q -> p c q")), writes=["wbd"])
                        S.dma("sp", lambda e: e.dma_start(out=pscT[:], in_=pscT_in[ll]), writes=["pscT"])
                        for j in range(NT):
                            tok = slice(j * 128, (j + 1) * 128)
                            lo_t, hi_t = (0, NTL - 1) if j < NTL else (NTL, NT - 1)
                            rels = [r for r in (-1, 0, 1) if lo_t <= j + r <= hi_t]
                            for c in range(2):
                                for bi in range(2):
                                    g = 2 * c + bi
                                    for ri, r in enumerate(rels):
                                        if r == -1:
                                            v = 0
                                        elif r == 1:
                                            v = 4
                                        else:
                                            v = 2 if j == lo_t else (3 if j == hi_t else 1)
                                        S.op("pe", lambda e: e.matmul(ppool[:, c * 2 + bi, :], lhsT=p_all[:, j + r, c * 128:(c + 1) * 128],
                                                                      rhs=band[:, g, v, :], start=(ri == 0), stop=(ri == len(rels) - 1)),
                                             reads=["band"], writes=["ppool"])
                                S.op("act", lambda e: e.activation(out=pooledT[0:64, c, :], in_=ppool[0:64, c * 2, :], func=AF.Copy),
                                     reads=["ppool"], writes=[("pooledT", c, 0)])
                                S.op("act", lambda e: e.activation(out=pooledT[64:128, c, :], in_=ppool[64:128, c * 2 + 1, :], func=AF.Copy),
                                     reads=["ppool"], writes=[("pooledT", c, 1)])
                            for c in range(2):
                                S.op("pe", lambda e: e.matmul(pp2[:, c, :], lhsT=wbd[:, c, :], rhs=pooledT[:, c, :], start=True, stop=True),
                                     reads=["wbd", ("pooledT", c, 0), ("pooledT", c, 1)], writes=["pp2"])
                            for c in range(2):
                                S.op("act", lambda e: e.activation(out=mixps[:, c, tok], in_=pp2[:, c, :], func=AF.Copy,
                                                                   scale=pscT[:, c:c + 1]),
                                     reads=["pp2", "pscT"], writes=[("mixps_p", j)])
                    S.barrier()
                if debug and ll == 0:
                    S.dma("sp", lambda e: e.dma_start(out=dbg_mix, in_=mixps[:]), writes=["dbgmix"])
                esD = ExitStack()
                with esD:
                    woutb = SB(esD, "woutb", [128, 8, D], BF16)
                    lnr = SB(esD, "lnr", [128, 2, D], F32)
                    wr = SB(esD, "wr", [128, 8, NE], F32)
                    PT = [SB(esD, "PT%d" % i, [128, 512], BF16) for i in range(3)]
                    attn_tok = [SB(esD, "attn_tok%d" % i, [128, 4, 512], BF16) for i in range(2)]
                    rden = SB(esD, "rden", [128, 4], F32)
                    mixA = SB(esD, "mixA", [128, 4, 128], BF16)
                    bufA = SB(esD, "bufA", [128, D], F32)
                    bufB = SB(esD, "bufB", [128, D], F32)
                    xhb = SB(esD, "xhb", [128, D], BF16)
                    h2T = SB(esD, "h2T", [128, 8, 128], F32)
                    st = SB(esD, "stE", [128, 2, 6], F32)
                    mv = SB(esD, "mvE", [128, 2], F32)
                    rstd = SB(esD, "rstdE", [128, 1], F32)
                    mx = SB(esD, "mx", [128, 1], F32)
                    sm = SB(esD, "sm", [128, 1], F32)
                    ex = SB(esD, "ex", [128, NE], F32)
                    g1rows = build_gate_rows(esD, 16, "g1r")
                    pS = [PS(esD, "pS%d" % i, [128, 512], F32) for i in range(3)]
                    pO = [PS(esD, "pO%d" % i, [128, 4, 128], F32) for i in range(1)]
                    ptr = PS(esD, "ptrD", [128, 4, 128], BF16)
                    pbig = PS(esD, "pbig", [128, D], F32)
                    pr = PS(esD, "pr", [128, NE], F32)
                    S.dma("pool", lambda e: e.dma_start(out=woutb[:], in_=w_out[ll].rearrange("(k p) c -> p k c", p=128)),
                          writes=["woutb"])
                    S.dma("sp", lambda e: e.dma_start(out=lnr[:], in_=ln_in[ll:ll + 1, 0:2, :].broadcast_to([128, 2, D])),
                          writes=["lnr"])
                    S.dma("sp", lambda e: e.dma_start(out=wr[:], in_=wr_in[ll]), writes=["wr"])

                    def rstd_act(tag):
                        S.op("act", lambda e: e.activation(out=rstd[:], in_=mv[:, 1:2], func=AF.Ln, bias=epsc[:], scale=1.0),
                             reads=[tag + "mv", "epsc"], writes=[tag + "rstd"])
                        S.op("act", lambda e: e.activation(out=rstd[:], in_=rstd[:], func=AF.Exp, scale=-0.5),
                             reads=[tag + "rstd"], writes=[tag + "rstd"])

                    def stats_dve(src, key_src, tag):
                        S.op("dve", lambda e: e.bn_stats(st[:, 0, :], src[:, 0:512]), reads=[key_src], writes=[tag + "st0"])
                        S.op("dve", lambda e: e.bn_stats(st[:, 1, :], src[:, 512:1024]), reads=[key_src], writes=[tag + "st1"])
                        S.op("dve", lambda e: e.bn_aggr(mv[:], st[:]), reads=[tag + "st0", tag + "st1"], writes=[tag + "mv"])

                    def make_ef_stages(j, qs, par):
                        s = 0 if j < NTL else 1
                        tok = slice(j * 128, (j + 1) * 128)
                        at = attn_tok[par]

                        def st0():
                            for cc in range(4):
                                S.op("pe", lambda e: e.transpose(ptr[:, cc, :], at[:, qs, cc * 128:(cc + 1) * 128], identb[:]),
                                     reads=[("attn_tok", par, qs), "identb"], writes=["ptrD"])
                            S.dma("sp", lambda e: e.dma_start(out=bufA[:], in_=src_tile(ll, j)), reads=[src_key(ll, j)], writes=["bufA"])

                        def st1():
                            S.op("act", lambda e: e.activation(out=mixA[:], in_=ptr[:], func=AF.Copy), reads=["ptrD"], writes=["mixA"])

                        def st2():
                            for half in range(2):
                                for c8 in range(8):
                                    lhs = mixps[:, c8, tok] if c8 < 4 else mixA[:, c8 - 4, :]
                                    S.op("pe", lambda e: e.matmul(pbig[:, half * 512:(half + 1) * 512], lhsT=lhs,
                                                                  rhs=woutb[:, c8, half * 512:(half + 1) * 512],
                                                                  start=(c8 == 0), stop=(c8 == 7)),
                                         reads=["mixA", "woutb"], writes=["pbig"])

                        def st3():
                            S.op("dve", lambda e: e.tensor_tensor(out=bufB[:], in0=pbig[:], in1=g1rows[s][:], op=ALU.mult),
                                 reads=["pbig", "g1r_%d" % s], writes=["bufB"])
                            S.op("dve", lambda e: e.scalar_tensor_tensor(out=bufA[:], in0=bufA[:], scalar=ALPHA, in1=bufB[:],
                                                                         op0=ALU.mult, op1=ALU.add),
                                 reads=["bufA", "bufB"], writes=["bufA"])
                            stats_dve(bufA, "bufA", "E")

                        def st4():
                            rstd_act("E")

                        def st5():
                            S.op("dve", lambda e: e.tensor_scalar(out=bufB[:], in0=bufA[:], scalar1=mv[:, 0:1], scalar2=rstd[:],
                                                                  op0=ALU.subtract, op1=ALU.mult),
                                 reads=["bufA", "Emv", "Erstd"], writes=["bufB"])
                            S.op("dve", lambda e: e.tensor_tensor(out=bufB[:], in0=bufB[:], in1=lnr[:, 0, :], op=ALU.mult),
                                 reads=["bufB", "lnr"], writes=["bufB"])
                            S.op("dve", lambda e: e.tensor_tensor(out=bufB[:], in0=bufB[:], in1=lnr[:, 1, :], op=ALU.add),
                                 reads=["bufB", "lnr"], writes=["bufB"])
                            S.op("dve", lambda e: e.tensor_scalar(out=bufA[:], in0=bufB[:], scalar1=ALPHA, scalar2=None, op0=ALU.mult),
                                 reads=["bufB"], writes=["bufA"])
                            S.dma("sp", lambda e: e.dma_start(out=macc[tok, :], in_=bufA[:]), reads=["bufA"], writes=[("macc", j)])
                            if debug and ll == 0:
                                S.dma("sp", lambda e: e.dma_start(out=dbg_x1[tok, :], in_=bufA[:]), reads=["bufA"], writes=[("dbgx1", j)])
                            stats_dve(bufB, "bufB", "E")

                        def st6():
                            rstd_act("E")

                        def st7():
                            S.op("dve", lambda e: e.tensor_scalar(out=bufB[:], in0=bufB[:], scalar1=mv[:, 0:1], scalar2=rstd[:],
                                                                  op0=ALU.subtract, op1=ALU.mult),
                                 reads=["bufB", "Emv", "Erstd"], writes=["bufB"])
                            S.op("dve", lambda e: e.tensor_copy(out=xhb[:], in_=bufB[:]), reads=["bufB"], writes=["xhb"])
                            S.dma("sp", lambda e: e.dma_start(out=xh2[tok, :], in_=xhb[:]), reads=["xhb"], writes=[("xh2", j)])

                        def st8():
                            for k in range(8):
                                S.op("pe", lambda e: e.transpose(pbig[:, k * 128:(k + 1) * 128], bufB[:, k * 128:(k + 1) * 128], identf[:]),
                                     reads=["bufB", "identf"], writes=["pbig"])

                        def st9():
                            for k in range(8):
                                S.op("dve", lambda e: e.tensor_scalar(out=h2T[:, k, :], in0=pbig[:, k * 128:(k + 1) * 128],
                                                                      scalar1=modT[:, 32 + k, s:s + 1], scalar2=modT[:, 24 + k, s:s + 1],
                                                                      op0=ALU.mult, op1=ALU.add),
                                     reads=["pbig", "modT"], writes=[("h2T", k)])

                        def st10():
                            for k in range(8):
                                S.op("pe", lambda e: e.matmul(pr[:], lhsT=h2T[:, k, :], rhs=wr[:, k, :], start=(k == 0), stop=(k == 7)),
                                     reads=[("h2T", k), "wr"], writes=["pr"])

                        def st11():
                            S.op("dve", lambda e: e.reduce_max(out=mx[:], in_=pr[:], axis=AX.X), reads=["pr"], writes=["mx"])
                            S.op("dve", lambda e: e.tensor_scalar(out=mx[:], in0=mx[:], scalar1=-1.0, scalar2=None, op0=ALU.mult),
                                 reads=["mx"], writes=["mx"])

                        def st12():
                            S.op("act", lambda e: e.activation(out=ex[:], in_=pr[:], func=AF.Exp, bias=mx[:], scale=1.0, accum_out=sm[:]),
                                 reads=["pr", "mx"], writes=["ex", "sm"])

                        def st13():
                            S.op("dve", lambda e: e.reciprocal(sm[:], sm[:]), reads=["sm"], writes=["sm"])
                            S.op("dve", lambda e: e.tensor_scalar(out=aff_all[:, j, :], in0=ex[:], scalar1=sm[:], scalar2=None, op0=ALU.mult),
                                 reads=["ex", "sm"], writes=[("aff", j)])

                        return [st0, st1, st2, st3, st4, st5, st6, st7, st8, st9, st10, st11, st12, st13]

                    blocks = [(NTL, 2)] + [(qb * 4, 4) for qb in range(8)]
                    nS = 0
                    pending = []
                    for bi, (t0, ntile) in enumerate(blocks):
                        par = bi % 2
                        N = ntile * 128
                        qtok = slice(t0 * 128, t0 * 128 + N)
                        kts = list(range(NT)) if t0 < NTL else [NTL, NTL + 1]
                        steps = [(c, kv, ki, kt) for c in range(4) for kv in range(2) for ki, kt in enumerate(kts)]
                        spacing = max(1, (len(steps) - 4) // (len(pending) + 1))

                        def emit_S(i):
                            c, kv, ki, kt = steps[i]
                            pSb = pS[(nS + i) % 3]
                            pSk = "pS%d" % ((nS + i) % 3)
                            PTb = PT[(nS + i) % 3]
                            PTk = "PT%d" % ((nS + i) % 3)
                            S.op("pe", lambda e: e.matmul(pSb[:, 0:N], lhsT=kT_all[:, kv, kt * 128:(kt + 1) * 128],
                                                          rhs=qT_all[:, c, qtok], start=True, stop=True),
                                 writes=[pSk])
                            S.op("act", lambda e: e.activation(out=PTb[:, 0:N], in_=pSb[:, 0:N], func=AF.Exp, scale=0.125),
                                 reads=[pSk], writes=[PTk])

                        emit_S(0)
                        emit_S(1)
                        for i, (c, kv, ki, kt) in enumerate(steps):
                            h = kv * 4 + c
                            if i + 2 < len(steps):
                                emit_S(i + 2)
                            if pending and i % spacing == spacing - 1:
                                pending.pop(0)()
                            if ki == 0:
                                S.op("dve", lambda e: e.memset(pO[0][:], 0.0), writes=["pO0"])
                            pOb = pO[0]
                            pOk = "pO0"
                            PTb = PT[(nS + i) % 3]
                            PTk = "PT%d" % ((nS + i) % 3)
                            for qs in range(ntile):
                                S.op("pe", lambda e: e.matmul(pOb[:, qs, 0:65], lhsT=PTb[:, qs * 128:(qs + 1) * 128],
                                                              rhs=v_all[:, kt, kv, :], start=False, stop=(ki == len(kts) - 1)),
                                     reads=[PTk], writes=[pOk])
                            if ki == len(kts) - 1:
                                S.op("dve", lambda e: e.reciprocal(rden[:, 0:ntile], pOb[:, 0:ntile, 64]),
                                     reads=[pOk], writes=["rden"])
                                for qs in range(ntile):
                                    S.op("dve", lambda e: e.tensor_scalar(out=attn_tok[par][:, qs, h * 64:(h + 1) * 64], in0=pOb[:, qs, 0:64],
                                                                          scalar1=rden[:, qs:qs + 1], scalar2=None, op0=ALU.mult),
                                         reads=[pOk, "rden"], writes=[("attn_tok", par, qs)])
                        nS += len(steps)
                        while pending:
                            pending.pop(0)()
                        for qs in range(ntile):
                            pending.extend(make_ef_stages(t0 + qs, qs, par))
                    while pending:
                        pending.pop(0)()
                S.barrier()
            if debug and ll == 0:
                S.dma("sp", lambda e: e.dma_start(out=dbg_aff, in_=aff_all[:]), writes=["dbgaff"])
            esGH = ExitStack()
            with esGH:
                idx_i = SB(esGH, "idx_i", [128, NE, 5], I32)
                gate_s = SB(esGH, "gate_s", [128, NE, 5], F32)
                esG = ExitStack()
                with esG:
                    lo = SB(esG, "lo", [128, 2, NE], F32)
                    hi = SB(esG, "hi", [128, 2, NE], F32)
                    mid = SB(esG, "mid", [128, 2, NE], F32)
                    capv = SB(esG, "capv", [128, 2, NE], F32)
                    cmp_ = SB(esG, "cmp", [128, NT, NE], F32)
                    cnt = SB(esG, "cnt", [128, 2, NE], F32)
                    ge = SB(esG, "ge", [128, 2, NE], F32)
                    d1 = SB(esG, "d1", [128, 2, NE], F32)
                    mask = SB(esG, "mask", [128, NT, NE], F32)
                    tg = SB(esG, "tg", [128, NT, NE, 5], BF16)
                    gr1 = SB(esG, "gr1", [128, NT, NE], F32)
                    gr2 = SB(esG, "gr2", [128, NT, NE], F32)
                    lst = SB(esG, "lst", [128, 5, 5], F32)
                    pos = SB(esG, "pos", [128, NT, NE], F32)
                    off = SB(esG, "off", [128, NT, NE], F32)
                    tot = SB(esG, "tot", [128, NT, NE], F32)
                    oh = [SB(esG, "oh%d" % i, [128, 512], BF16) for i in range(4)]
                    idx_f = SB(esG, "idx_f", [128, NE, 5], F32)
                    ptot = PS(esG, "ptot", [128, 2, NE], F32)
                    pwl = PS(esG, "pwl", [128, 512], F32)
                    pwc = PS(esG, "pwc", [128, 32], F32)
                    ptl = PS(esG, "ptl", [128, 512], F32)
                    ptc = PS(esG, "ptc", [128, 32], F32)
                    plist = [PS(esG, "plist0", [128, 5, 128], F32)]
                    S.op("pool", lambda e: e.memset(lo[:], 0.0), writes=["lo"])
                    S.op("pool", lambda e: e.memset(hi[:], 1.0), writes=["hi"])
                    S.op("pool", lambda e: e.memset(capv[:, 0, :], float(CAP_L)), writes=["capv0"])
                    S.op("pool", lambda e: e.memset(capv[:, 1, :], float(CAP_C)), writes=["capv1"])
                    aff_l = aff_all[:, 0:NTL, :]
                    aff_c = aff_all[:, NTL:NT, :]
                    for it in range(30):
                        S.op("dve", lambda e: e.tensor_tensor(out=mid[:], in0=lo[:], in1=hi[:], op=ALU.add),
                             reads=["lo", "hi"], writes=["mid"])
                        S.op("dve", lambda e: e.tensor_scalar(out=mid[:], in0=mid[:], scalar1=0.5, scalar2=None, op0=ALU.mult),
                             reads=["mid"], writes=["mid"])
                        S.op("dve", lambda e: e.tensor_tensor(out=cmp_[:, 0:NTL, :], in0=aff_l,
                                                              in1=mid[:, 0:1, :].broadcast_to([128, NTL, NE]), op=ALU.is_ge),
                             reads=["mid"], writes=["cmp_l"])
                        S.op("dve", lambda e: e.tensor_tensor(out=cmp_[:, NTL:NT, :], in0=aff_c,
                                                              in1=mid[:, 1:2, :].broadcast_to([128, 2, NE]), op=ALU.is_ge),
                             reads=["mid"], writes=["cmp_c"])
                        S.op("dve", lambda e: e.tensor_reduce(out=cnt[:, 0, :], in_=cmp_[:, 0:NTL, :].rearrange("p j e -> p e j"),
                                                              axis=AX.X, op=ALU.add),
                             reads=["cmp_l"], writes=["cnt0"])
                        S.op("dve", lambda e: e.tensor_reduce(out=cnt[:, 1, :], in_=cmp_[:, NTL:NT, :].rearrange("p j e -> p e j"),
                                                              axis=AX.X, op=ALU.add),
                             reads=["cmp_c"], writes=["cnt1"])
                        S.op("pe", lambda e: e.matmul(ptot[:], lhsT=ones_f[:], rhs=cnt[:], start=True, stop=True),
                             reads=["cnt0", "cnt1", "ones_f"], writes=["ptot"])
                        S.op("dve", lambda e: e.tensor_tensor(out=ge[:], in0=ptot[:], in1=capv[:], op=ALU.is_ge),
                             reads=["ptot", "capv0", "capv1"], writes=["ge"])
                        S.op("dve", lambda e: e.tensor_tensor(out=d1[:], in0=mid[:], in1=lo[:], op=ALU.subtract),
                             reads=["mid", "lo"], writes=["d1"])
                        S.op("dve", lambda e: e.tensor_tensor(out=d1[:], in0=d1[:], in1=ge[:], op=ALU.mult),
                             reads=["d1", "ge"], writes=["d1"])
                        S.op("dve", lambda e: e.tensor_tensor(out=lo[:], in0=lo[:], in1=d1[:], op=ALU.add),
                             reads=["d1", "lo"], writes=["lo"])
                        S.op("dve", lambda e: e.tensor_tensor(out=d1[:], in0=hi[:], in1=mid[:], op=ALU.subtract),
                             reads=["mid", "hi"], writes=["d1"])
                        S.op("dve", lambda e: e.tensor_tensor(out=d1[:], in0=d1[:], in1=ge[:], op=ALU.mult),
                             reads=["d1", "ge"], writes=["d1"])
                        S.op("dve", lambda e: e.tensor_tensor(out=hi[:], in0=mid[:], in1=d1[:], op=ALU.add),
                             reads=["d1", "mid"], writes=["hi"])
                    S.op("dve", lambda e: e.tensor_tensor(out=mask[:, 0:NTL, :], in0=aff_l,
                                                          in1=lo[:, 0:1, :].broadcast_to([128, NTL, NE]), op=ALU.is_ge),
                         reads=["lo"], writes=["mask_l"])
                    S.op("dve", lambda e: e.tensor_tensor(out=mask[:, NTL:NT, :], in0=aff_c,
                                                          in1=lo[:, 1:2, :].broadcast_to([128, 2, NE]), op=ALU.is_ge),
                         reads=["lo"], writes=["mask_c"])
                    S.op("dve", lambda e: e.tensor_copy(out=tg[:, :, :, 0], in_=tokA[:].unsqueeze(2).broadcast_to([128, NT, NE])),
                         reads=["tokA"], writes=["tg0"])
                    S.op("dve", lambda e: e.tensor_copy(out=tg[:, :, :, 1], in_=tokB[:].unsqueeze(2).broadcast_to([128, NT, NE])),
                         reads=["tokB"], writes=["tg0"])
                    S.op("dve", lambda e: e.tensor_copy(out=tg[:, :, :, 2], in_=aff_all[:]), writes=["tg1"])
                    S.op("dve", lambda e: e.tensor_tensor(out=gr1[:], in0=aff_all[:], in1=tg[:, :, :, 2], op=ALU.subtract),
                         reads=["tg1"], writes=["gr1"])
                    S.op("dve", lambda e: e.tensor_copy(out=tg[:, :, :, 3], in_=gr1[:]), reads=["gr1"], writes=["tg1"])
                    S.op("dve", lambda e: e.tensor_tensor(out=gr2[:], in0=gr1[:], in1=tg[:, :, :, 3], op=ALU.subtract),
                         reads=["gr1", "tg1"], writes=["gr2"])
                    S.op("dve", lambda e: e.tensor_copy(out=tg[:, :, :, 4], in_=gr2[:]), reads=["gr2"], writes=["tg1"])
                    mk2 = mask[:].rearrange("p j e -> p (j e)")
                    S.op("pe", lambda e: e.matmul(pwl[:], lhsT=lmat[:], rhs=mk2[:, 0:512], start=True, stop=True),
                         reads=["mask_l", "lmat"], writes=["pwl"])
                    S.op("pe", lambda e: e.matmul(pwc[:], lhsT=lmat[:], rhs=mk2[:, 512:544], start=True, stop=True),
                         reads=["mask_c", "lmat"], writes=["pwc"])
                    S.op("pe", lambda e: e.matmul(ptl[:], lhsT=ones_f[:], rhs=mk2[:, 0:512], start=True, stop=True),
                         reads=["mask_l", "ones_f"], writes=["ptl"])
                    S.op("pe", lambda e: e.matmul(ptc[:], lhsT=ones_f[:], rhs=mk2[:, 512:544], start=True, stop=True),
                         reads=["mask_c", "ones_f"], writes=["ptc"])
                    pos2 = pos[:].rearrange("p j e -> p (j e)")
                    tot2 = tot[:].rearrange("p j e -> p (j e)")
                    S.op("act", lambda e: e.activation(out=pos2[:, 0:512], in_=pwl[:], func=AF.Copy), reads=["pwl"], writes=["pos_l"])
                    S.op("act", lambda e: e.activation(out=pos2[:, 512:544], in_=pwc[:], func=AF.Copy), reads=["pwc"], writes=["pos_c"])
                    S.op("act", lambda e: e.activation(out=tot2[:, 0:512], in_=ptl[:], func=AF.Copy), reads=["ptl"], writes=["tot"])
                    S.op("act", lambda e: e.activation(out=tot2[:, 512:544], in_=ptc[:], func=AF.Copy), reads=["ptc"], writes=["tot"])
                    S.op("pool", lambda e: e.memset(off[:], 0.0), writes=["off"])
                    for j in range(1, NTL):
                        S.op("dve", lambda e: e.tensor_tensor(out=off[:, j, :], in0=off[:, j - 1, :], in1=tot[:, j - 1, :], op=ALU.add),
                             reads=["off", "tot"], writes=["off"])
                    S.op("dve", lambda e: e.tensor_copy(out=off[:, NTL + 1, :], in_=tot[:, NTL, :]), reads=["off", "tot"], writes=["off"])
                    S.op("dve", lambda e: e.tensor_tensor(out=pos[:], in0=pos[:], in1=off[:], op=ALU.add),
                         reads=["pos_l", "pos_c", "off"], writes=["pos_l", "pos_c"])
                    S.op("dve", lambda e: e.scalar_tensor_tensor(out=pos[:], in0=pos[:], scalar=-BIG, in1=mask[:], op0=ALU.add, op1=ALU.mult),
                         reads=["pos_l", "pos_c", "mask_l", "mask_c"], writes=["pos_l", "pos_c"])
                    S.op("dve", lambda e: e.tensor_scalar(out=pos[:], in0=pos[:], scalar1=BIG, scalar2=None, op0=ALU.add),
                         reads=["pos_l", "pos_c"], writes=["pos"])
                    n_oh = 0
                    for ex_ in range(NE):
                        pl = plist[0]
                        plk = "plist0"
                        S.op("dve", lambda e: e.memset(pl[:], 0.0), writes=[plk])
                        for j in range(NT):
                            ohb = oh[n_oh % 4]
                            ohk = "oh%d" % (n_oh % 4)
                            eng_ = "dve"
                            n_oh += 1
                            if j < NTL:
                                S.op(eng_, lambda e: e.tensor_scalar(out=ohb[:], in0=iota_row[:], scalar1=pos[:, j, ex_:ex_ + 1],
                                                                     scalar2=None, op0=ALU.is_equal),
                                     reads=["iota_row", "pos"], writes=[ohk])
                                for st_ in range(4):
                                    S.op("pe", lambda e: e.matmul(pl[:, st_, 0:5], lhsT=ohb[:, st_ * 128:(st_ + 1) * 128],
                                                                  rhs=tg[:, j, ex_, :], start=False, stop=(j == NTL - 1)),
                                         reads=[ohk, "tg0", "tg1"], writes=[plk])
                            else:
                                S.op(eng_, lambda e: e.tensor_scalar(out=ohb[:, 0:128], in0=iota_row[:, 0:128], scalar1=pos[:, j, ex_:ex_ + 1],
                                                                     scalar2=None, op0=ALU.is_equal),
                                     reads=["iota_row", "pos"], writes=[ohk])
                                S.op("pe", lambda e: e.matmul(pl[:, 4, 0:5], lhsT=ohb[:, 0:128], rhs=tg[:, j, ex_, :],
                                                              start=False, stop=(j == NT - 1)),
                                     reads=[ohk, "tg0", "tg1"], writes=[plk])
                        S.op("act", lambda e: e.activation(out=lst[:], in_=pl[:, :, 0:5], func=AF.Copy),
                             reads=[plk], writes=["lst"])
                        S.op("dve", lambda e: e.tensor_tensor(out=idx_f[:, ex_, :], in0=lst[:, :, 0], in1=lst[:, :, 1], op=ALU.add),
                             reads=["lst"], writes=["idx_f"])
                        S.op("dve", lambda e: e.tensor_tensor(out=gate_s[:, ex_, :], in0=lst[:, :, 2], in1=lst[:, :, 3], op=ALU.add),
                             reads=["lst"], writes=["gate_s"])
                        S.op("dve", lambda e: e.tensor_tensor(out=gate_s[:, ex_, :], in0=gate_s[:, ex_, :], in1=lst[:, :, 4], op=ALU.add),
                             reads=["lst", "gate_s"], writes=["gate_s"])
                    S.op("dve", lambda e: e.tensor_copy(out=idx_i[:], in_=idx_f[:]), reads=["idx_f"], writes=["idx_i"])
                    if debug and ll == 0:
                        S.dma("sp", lambda e: e.dma_start(out=dbg_idx, in_=idx_i[:]), reads=["idx_i"], writes=["dbgidx"])
                        S.dma("sp", lambda e: e.dma_start(out=dbg_gate, in_=gate_s[:]), reads=["gate_s"], writes=["dbggate"])
                        S.dma("sp", lambda e: e.dma_start(out=dbg_pos, in_=pos[:]), reads=["pos"], writes=["dbgpos"])
                S.barrier()
                esH = ExitStack()
                with esH:
                    g2rows = build_gate_rows(esH, 40, "g2r")
                    wb = [[SB(esH, "wb%d_%d" % (m, i), [128, 8, D], BF16) for m in range(3)] for i in range(2)]
                    xg = [[SB(esH, "xg%d_%d" % (st_, i), [128, D], BF16) for st_ in range(5)] for i in range(2)]
                    xgT = [SB(esH, "xgT%d" % i, [128, 8, 544], BF16) for i in range(2)]
                    hidT = SB(esH, "hidT", [128, 8, 544], BF16)
                    s1 = SB(esH, "s1", [128, 544], F32)
                    ysc = [SB(esH, "ysc%d" % i, [128, D], F32) for i in range(4)]
                    pxT = [PS(esH, "pxT%d" % i, [128, 8, 128], BF16) for i in range(2)]
                    ph1 = PS(esH, "ph1", [128, 512], F32)
                    ph3 = PS(esH, "ph3", [128, 512], F32)
                    phc = PS(esH, "phc", [128, 2, 32], F32)
                    py = [PS(esH, "pyH%d" % i, [128, 512], F32) for i in range(3)]
                    wsrc = (w1, w3, w2)

                    def issue_loads(ex_):
                        i = ex_ % 2
                        for st_ in range(5):
                            npart = 128 if st_ < 4 else CAP_C
                            S.dma("pool", lambda e: e.indirect_dma_start(
                                out=xg[i][st_][0:npart, :], out_offset=None, in_=xh2[:, :],
                                in_offset=bass.IndirectOffsetOnAxis(ap=idx_i[0:npart, ex_, st_:st_ + 1], axis=0),
                                bounds_check=bcreg, oob_is_err=False),
                                reads=["idx_i"] + [("xh2", j) for j in range(NT)], writes=["xg%d_%d" % (st_, i)])
                        for m in range(3):
                            S.dma("pool", lambda e: e.dma_start(out=wb[i][m][:], in_=wsrc[m][ll, ex_].rearrange("(k p) c -> p k c", p=128)),
                                  writes=["wb%d_%d" % (m, i)])

                    def prep_tile(ex_, st_):
                        i = ex_ % 2
                        npart = 128 if st_ < 4 else CAP_C
                        s = 0 if st_ < 4 else 1
                        c0 = st_ * 128
                        npx[0] += 1
                        pb = pxT[npx[0] % 2]
                        pk = "pxT%d" % (npx[0] % 2)
                        for k in range(8):
                            S.op("pe", lambda e: e.transpose(pb[:, k, 0:npart], xg[i][st_][0:npart, k * 128:(k + 1) * 128],
                                                             identb[0:npart, 0:npart]),
                                 reads=["xg%d_%d" % (st_, i), "identb"], writes=[pk])
                        for k in range(8):
                            if k % 2 == 0:
                                S.op("act", lambda e: e.activation(out=xgT[i][:, k, c0:c0 + npart], in_=pb[:, k, 0:npart], func=AF.Identity,
                                                                   bias=modT[:, 24 + k, s:s + 1], scale=modT[:, 32 + k, s:s + 1]),
                                     reads=[pk, "modT"], writes=["xgT%d" % i])
                            else:
                                S.op("dve", lambda e: e.tensor_scalar(out=xgT[i][:, k, c0:c0 + npart], in0=pb[:, k, 0:npart],
                                                                      scalar1=modT[:, 32 + k, s:s + 1], scalar2=modT[:, 24 + k, s:s + 1],
                                                                      op0=ALU.mult, op1=ALU.add),
                                     reads=[pk, "modT"], writes=["xgT%d" % i])

                    npx = [0]
                    issue_loads(0)
                    for st_ in range(5):
                        prep_tile(0, st_)
                    nys = 0
                    npy = 0
                    for ex_ in range(NE):
                        i = ex_ % 2
                        xgTi = xgT[i]
                        xk_ = "xgT%d" % i
                        if ex_ + 1 < NE:
                            issue_loads(ex_ + 1)
                        for fc in range(8):
                            for m, ph in ((0, ph1), (1, ph3)):
                                for k in range(8):
                                    S.op("pe", lambda e: e.matmul(ph[:], lhsT=wb[i][m][:, k, fc * 128:(fc + 1) * 128], rhs=xgTi[:, k, 0:512],
                                                                  start=(k == 0), stop=(k == 7)),
                                         reads=[xk_, "wb%d_%d" % (m, i)], writes=["ph%d" % m])
                                for k in range(8):
                                    S.op("pe", lambda e: e.matmul(phc[:, m, :], lhsT=wb[i][m][:, k, fc * 128:(fc + 1) * 128], rhs=xgTi[:, k, 512:544],
                                                                  start=(k == 0), stop=(k == 7)),
                                         reads=[xk_, "wb%d_%d" % (m, i)], writes=["phc"])
                            if ex_ + 1 < NE and fc < 5:
                                prep_tile(ex_ + 1, fc)
                            S.op("act", lambda e: e.activation(out=s1[:, 0:512], in_=ph1[:], func=AF.Silu), reads=["ph0"], writes=["s1a"])
                            S.op("act", lambda e: e.activation(out=s1[:, 512:544], in_=phc[:, 0, :], func=AF.Silu), reads=["phc"], writes=["s1b"])
                            S.op("dve", lambda e: e.tensor_tensor(out=hidT[:, fc, 0:512], in0=s1[:, 0:512], in1=ph3[:], op=ALU.mult),
                                 reads=["s1a", "ph1"], writes=["hidT"])
                            S.op("dve", lambda e: e.tensor_tensor(out=hidT[:, fc, 512:544], in0=s1[:, 512:544], in1=phc[:, 1, :], op=ALU.mult),
                                 reads=["s1b", "phc"], writes=["hidT"])
                        for st_ in range(5):
                            npart = 128 if st_ < 4 else CAP_C
                            s = 0 if st_ < 4 else 1
                            c0 = st_ * 128
                            yb = ysc[nys % 4]
                            yk = "ysc%d" % (nys % 4)
                            nys += 1
                            for half in range(2):
                                pyb = py[npy % 3]
                                pyk = "pyH%d" % (npy % 3)
                                npy += 1
                                for fc in range(8):
                                    S.op("pe", lambda e: e.matmul(pyb[0:npart, :], lhsT=hidT[:, fc, c0:c0 + npart],
                                                                  rhs=wb[i][2][:, fc, half * 512:(half + 1) * 512],
                                                                  start=(fc == 0), stop=(fc == 7)),
                                         reads=["hidT", "wb2_%d" % i], writes=[pyk])
                                S.op("dve", lambda e: e.scalar_tensor_tensor(out=yb[0:npart, half * 512:(half + 1) * 512], in0=pyb[0:npart, :],
                                                                             scalar=gate_s[0:npart, ex_, st_:st_ + 1],
                                                                             in1=g2rows[s][0:npart, half * 512:(half + 1) * 512],
                                                                             op0=ALU.mult, op1=ALU.mult),
                                     reads=[pyk, "gate_s", "g2r_%d" % s], writes=[(yk, half)])
                            S.dma("pool", lambda e: e.indirect_dma_start(
                                out=macc[:, :], out_offset=bass.IndirectOffsetOnAxis(ap=idx_i[0:npart, ex_, st_:st_ + 1], axis=0),
                                in_=yb[0:npart, :], in_offset=None, bounds_check=bcreg, oob_is_err=False,
                                compute_op=ALU.add),
                                reads=[(yk, 0), (yk, 1), "idx_i"] + [("msc", ex_ - 1, k) for k in range(5)],
                                writes=[("msc", ex_, st_)])
                S.barrier()
            if debug and ll == 0:
                S.dma("sp", lambda e: e.dma_start(out=dbg_macc, in_=macc), writes=["dbgmacc"])
                S.barrier()
            esI = ExitStack()
            with esI:
                lnr2 = SB(esI, "lnr2", [128, 2, D], F32)
                mt = [SB(esI, "mt%d" % i, [128, D], F32) for i in range(2)]
                xo = [SB(esI, "xo%d" % i, [128, D], F32) for i in range(2)]
                st = SB(esI, "stI", [128, 2, 6], F32)
                mv = SB(esI, "mvI", [128, 2], F32)
                rstd = SB(esI, "rstdI", [128, 1], F32)
                nmr = SB(esI, "nmr", [128, 1], F32)
                S.dma("sp", lambda e: e.dma_start(out=lnr2[:], in_=ln_in[ll:ll + 1, 2:4, :].broadcast_to([128, 2, D])),
                      writes=["lnr2"])
                for j in range(NT):
                    tok = slice(j * 128, (j + 1) * 128)
                    m_ = mt[j % 2]
                    mk = "mt%d" % (j % 2)
                    o_ = xo[j % 2]
                    ok = "xo%d" % (j % 2)
                    S.dma("sp", lambda e: e.dma_start(out=m_[:], in_=macc[tok, :]), reads=[("macc", j)], writes=[mk])
                    ln_stats((st, mv, rstd), m_, mk, "I")
                    S.op("dve", lambda e: e.scalar_tensor_tensor(out=nmr[:], in0=mv[:, 0:1], scalar=-1.0, in1=rstd[:],
                                                                 op0=ALU.mult, op1=ALU.mult),
                         reads=["Imv", "Irstd"], writes=["nmr"])
                    S.op("act", lambda e: e.activation(out=o_[:], in_=m_[:], func=AF.Identity, bias=nmr[:], scale=rstd[:]),
                         reads=[mk, "nmr", "Irstd"], writes=[ok])
                    S.op("dve", lambda e: e.tensor_tensor(out=o_[:], in0=o_[:], in1=lnr2[:, 0, :], op=ALU.mult),
                         reads=[ok, "lnr2"], writes=[ok])
                    S.op("dve", lambda e: e.tensor_tensor(out=o_[:], in0=o_[:], in1=lnr2[:, 1, :], op=ALU.add),
                         reads=[ok, "lnr2"], writes=[ok])
                    S.dma("sp", lambda e: e.dma_start(out=dst_tile(ll, j), in_=o_[:]), reads=[ok], writes=[dst_key(ll, j)])
            S.barrier()
        print("instructions", S.n_inst, "waits", S.n_wait, flush=True)
    return nc


def _band_tables():
    wins = (2, 4, 8, 16)
    Ls = 384
    band = np.zeros((128, 4, 5, 128), np.float32)
    for g, w in enumerate(wins):
        A = np.zeros((Ls, Ls), np.float64)
        for t in range(Ls):
            lo = min(max(t - w // 2, 0), Ls)
            hi = min(max(t + w // 2, 0), Ls)
            A[t, lo:hi] = 1.0 / (hi - lo)
            A[t, t] -= 1.0
        AT = A.T
        band[:, g, 0, :] = AT[0:128, 128:256]
        band[:, g, 1, :] = AT[128:256, 128:256]
        band[:, g, 2, :] = AT[0:128, 0:128]
        band[:, g, 3, :] = AT[256:384, 256:384]
        band[:, g, 4, :] = AT[256:384, 128:256]
    return band


def _rope_tables():
    t = np.arange(SEQ)
    r = (t // 64).astype(np.float32)
    col = (t % 64).astype(np.float32)
    inv = (np.float32(10000.0) ** (-np.arange(16, dtype=np.float32) / np.float32(16))).astype(np.float32)
    ang = np.concatenate([r[:, None] * inv, col[:, None] * inv], axis=-1).astype(np.float32)
    cs = np.concatenate([np.cos(ang), np.sin(ang)], axis=-1).astype(np.float32)
    return np.ascontiguousarray(cs.reshape(NTL, 128, 64).transpose(1, 0, 2))


def _prep_common(inp, layers):
    L = len(layers)
    sl = lambda a: np.ascontiguousarray(np.asarray(a)[layers])
    pool_w = sl(inp["pool_w"])
    wbd = np.zeros((L, 2, 128, 128), np.float32)
    for c in range(2):
        for gi in range(2):
            wbd[:, c, gi * 64:(gi + 1) * 64, gi * 64:(gi + 1) * 64] = pool_w[:, 2 * c + gi]
    com = {
        "w_mod": sl(inp["w_mod"]),
        "b_modT": np.ascontiguousarray(sl(inp["b_mod"]).reshape(L, 48, 128).transpose(0, 2, 1)),
        "w_in": sl(inp["w_in"]),
        "wbd": wbd,
        "pscT": np.ascontiguousarray(sl(inp["pool_scale"]).reshape(L, 2, 128).transpose(0, 2, 1)),
        "band": _band_tables(),
        "sgu_g": np.ascontiguousarray(sl(inp["sgu_g"]).reshape(L, 256)),
        "sgu_wT": np.ascontiguousarray(sl(inp["sgu_w"]).transpose(0, 3, 1, 2)),
        "sgu_bT": np.ascontiguousarray(sl(inp["sgu_b"]).transpose(0, 2, 1)),
        "qkg": np.ascontiguousarray(np.concatenate([np.tile(sl(inp["q_g"]), (1, 8)), np.tile(sl(inp["k_g"]), (1, 2))], axis=1)),
        "w_out": sl(inp["w_out"]),
        "ln": np.ascontiguousarray(np.stack([sl(inp["ln1_g"]), sl(inp["ln1_b"]), sl(inp["ln2_g"]), sl(inp["ln2_b"])], axis=1)),
        "w_router": np.ascontiguousarray(sl(inp["w_router"]).reshape(L, 8, 128, NE).transpose(0, 2, 1, 3)),
        "w1": sl(inp["w1"]), "w3": sl(inp["w3"]), "w2": sl(inp["w2"]),
        "cs": _rope_tables(),
    }
    return com


_NC_CACHE = {}


def _run(inp, x, ctx, layers, n_cores=8):
    L = len(layers)
    if L not in _NC_CACHE:
        _NC_CACHE[L] = build(L)
    nc = _NC_CACHE[L]
    com = _prep_common(inp, layers)
    c = np.asarray(inp["c"], np.float32)
    c_ctx = np.asarray(inp["c_ctx"], np.float32)
    in_maps = []
    for b in range(n_cores):
        cond = np.stack([c[b].reshape(8, 128).T, c_ctx.reshape(8, 128).T], axis=-1)
        m = dict(com)
        m["x"] = np.ascontiguousarray(x[b])
        m["ctx"] = np.ascontiguousarray(ctx[b])
        m["cond"] = np.ascontiguousarray(cond.astype(np.float32))
        in_maps.append(m)
    res = run_bass_kernel_spmd(nc, in_maps, core_ids=list(range(n_cores)))
    xo = np.stack([np.asarray(r["out"]) for r in res.results], 0)
    co = np.stack([np.asarray(r["ctx_out"]) for r in res.results], 0)
    return xo, co


LAYERS_PER_LAUNCH = 4


def kernel(**inputs):
    inp = {k: np.asarray(v) for k, v in inputs.items()}
    x = np.asarray(inp["x"], np.float32)
    ctx = np.asarray(inp["ctx"], np.float32)
    for l0 in range(0, DEPTH, LAYERS_PER_LAUNCH):
        x, ctx = _run(inp, x, ctx, list(range(l0, l0 + LAYERS_PER_LAUNCH)))
    return x.astype(np.float32)
```

```python
import numpy as np
from contextlib import ExitStack
import concourse.bass as bass
import concourse.mybir as mybir
from concourse.bass_utils import run_bass_kernel_spmd

F32 = mybir.dt.float32
BF16 = mybir.dt.bfloat16
I32 = mybir.dt.int32
AF = mybir.ActivationFunctionType
ALU = mybir.AluOpType
AX = mybir.AxisListType

DEPTH = 4
D = 1024
SEQ = 4096
CTX = 256
NT = 34
NTL = 32
NTOK = SEQ + CTX
NE = 16
CAP_L = 512
CAP_C = 32
ALPHA = float((2 * DEPTH) ** 0.25)
EPS = 1e-6
BIG = 100000.0


class Sched:
    def __init__(self, nc, es, n_dma_sems=48):
        self.nc = nc
        self.eng = {"pe": nc.tensor, "act": nc.scalar, "dve": nc.vector, "pool": nc.gpsimd, "sp": nc.sync}
        self.sem = {k: es.enter_context(nc.semaphore("prog_" + k)) for k in self.eng}
        self.cnt = {k: 0 for k in self.eng}
        self.dsem = [es.enter_context(nc.semaphore("dma%d" % i)) for i in range(n_dma_sems)]
        self.dtot = [0] * n_dma_sems
        self.dnext = 0
        self.dnext_sw = 0
        self.waited = {k: {} for k in self.eng}
        self.res = {}
        self.n_inst = 0
        self.n_wait = 0

    def _semobj(self, key):
        return self.sem[key] if isinstance(key, str) else self.dsem[key]

    def _collect(self, reads, writes):
        deps = {}

        def add(d):
            if d is None:
                return
            k, v = d
            if deps.get(k, 0) < v:
                deps[k] = v
        for r in reads:
            e = self.res.get(r)
            if e is not None:
                add(e["w"])
        for w in writes:
            e = self.res.get(w)
            if e is not None:
                add(e["w"])
                for k, v in e["r"].items():
                    add((k, v))
        return deps

    def _wait(self, F, deps, skip_self=False):
        for k, v in deps.items():
            if skip_self and k == F:
                continue
            if self.waited[F].get(k, 0) < v:
                self.eng[F].wait_ge(self._semobj(k), v)
                self.waited[F][k] = v
                self.n_wait += 1

    def _update(self, dep, reads, writes):
        k, v = dep
        for r in reads:
            e = self.res.setdefault(r, {"w": None, "r": {}})
            if e["r"].get(k, 0) < v:
                e["r"][k] = v
        for w in writes:
            self.res[w] = {"w": dep, "r": {}}

    def op(self, F, fn, reads=(), writes=(), skip_self=None):
        if skip_self is None:
            skip_self = (F == "pe")
        deps = self._collect(reads, writes)
        self._wait(F, deps, skip_self=skip_self)
        inst = fn(self.eng[F])
        self.cnt[F] += 1
        inst.then_inc(self.sem[F], 1)
        self._update((F, self.cnt[F]), reads, writes)
        self.n_inst += 1
        return inst

    def dma(self, Q, fn, reads=(), writes=()):
        deps = self._collect(reads, writes)
        self._wait(Q, deps)
        half = len(self.dsem) // 2
        if Q == "pool":
            i = half + self.dnext_sw
            self.dnext_sw = (self.dnext_sw + 1) % (len(self.dsem) - half)
        else:
            i = self.dnext
            self.dnext = (self.dnext + 1) % half
        if self.dtot[i] > 0 and self.waited[Q].get(i, 0) < self.dtot[i]:
            self.eng[Q].wait_ge(self.dsem[i], self.dtot[i])
            self.waited[Q][i] = self.dtot[i]
            self.n_wait += 1
        inst = fn(self.eng[Q])
        self.dtot[i] += 16
        inst.then_inc(self.dsem[i], 16)
        self._update((i, self.dtot[i]), reads, writes)
        self.n_inst += 1
        return inst

    def barrier(self):
        for F in self.eng:
            for i, t in enumerate(self.dtot):
                if t > 0 and self.waited[F].get(i, 0) < t:
                    self.eng[F].wait_ge(self.dsem[i], t)
                    self.waited[F][i] = t
            for k in self.eng:
                if k != F and self.cnt[k] > 0 and self.waited[F].get(k, 0) < self.cnt[k]:
                    self.eng[F].wait_ge(self.sem[k], self.cnt[k])
                    self.waited[F][k] = self.cnt[k]
        self.res = {}


def build(L, debug=False):
    nc = bass.Bass("TRN2", target_bir_lowering=False)

    def DT(name, shape, dt=F32, kind="ExternalInput"):
        return nc.dram_tensor(name, shape, dt, kind=kind).ap()

    x_in = DT("x", [SEQ, D])
    ctx_in = DT("ctx", [CTX, D])
    cond_in = DT("cond", [128, 8, 2])
    w_mod = DT("w_mod", [L, D, 6 * D])
    b_modT = DT("b_modT", [L, 128, 48])
    w_in = DT("w_in", [L, D, 1536])
    wbd_in = DT("wbd", [L, 2, 128, 128])
    pscT_in = DT("pscT", [L, 128, 2])
    band_in = DT("band", [128, 4, 5, 128])
    sgug_in = DT("sgu_g", [L, 256])
    sguwT_in = DT("sgu_wT", [L, 128, 4, 128])
    sgubT_in = DT("sgu_bT", [L, 128, 4])
    qkg_in = DT("qkg", [L, 640])
    w_out = DT("w_out", [L, D, D])
    ln_in = DT("ln", [L, 4, D])
    wr_in = DT("w_router", [L, 128, 8, NE])
    w1 = DT("w1", [L, NE, D, D])
    w3 = DT("w3", [L, NE, D, D])
    w2 = DT("w2", [L, NE, D, D])
    cs_in = DT("cs", [128, NTL, 64])
    out = DT("out", [SEQ, D], kind="ExternalOutput")
    ctx_out = DT("ctx_out", [CTX, D], kind="ExternalOutput")
    if debug:
        dbg_x1 = DT("dbg_x1", [NTOK, D], kind="ExternalOutput")
        dbg_aff = DT("dbg_aff", [128, NT, NE], kind="ExternalOutput")
        dbg_mix = DT("dbg_mix", [128, 4, NTOK], BF16, kind="ExternalOutput")
        dbg_idx = DT("dbg_idx", [128, NE, 5], I32, kind="ExternalOutput")
        dbg_gate = DT("dbg_gate", [128, NE, 5], kind="ExternalOutput")
        dbg_macc = DT("dbg_macc", [NTOK, D], kind="ExternalOutput")
        dbg_pos = DT("dbg_pos", [128, NT, NE], kind="ExternalOutput")
    xs = DT("xs", [NTOK, D], kind="Internal")
    macc = DT("macc", [NTOK, D], kind="Internal")
    xh2 = DT("xh2", [NTOK, D], BF16, kind="Internal")

    def src_tile(ll, j):
        if ll == 0:
            return x_in[j * 128:(j + 1) * 128, :] if j < NTL else ctx_in[(j - NTL) * 128:(j - NTL + 1) * 128, :]
        return xs[j * 128:(j + 1) * 128, :]

    def dst_tile(ll, j):
        if ll == L - 1:
            return out[j * 128:(j + 1) * 128, :] if j < NTL else ctx_out[(j - NTL) * 128:(j - NTL + 1) * 128, :]
        return xs[j * 128:(j + 1) * 128, :]

    def src_key(ll, j):
        return ("xin", j) if ll == 0 else ("xs", j)

    def dst_key(ll, j):
        return ("xout", j) if ll == L - 1 else ("xs", j)

    es0 = ExitStack()
    with es0:
        S = Sched(nc, es0)
        bcreg = es0.enter_context(nc.gpsimd.register("bcreg"))
        nc.gpsimd.reg_mov(bcreg, NTOK - 1)

        uid = [0]

        def SB(es, name, shape, dt):
            uid[0] += 1
            return es.enter_context(nc.sbuf_tensor("s%d_%s" % (uid[0], name), shape, dt))

        def PS(es, name, shape, dt):
            uid[0] += 1
            return es.enter_context(nc.psum_tensor("p%d_%s" % (uid[0], name), shape, dt))

        identf = SB(es0, "identf", [128, 128], F32)
        identb = SB(es0, "identb", [128, 128], BF16)
        ones_f = SB(es0, "ones_f", [128, 128], F32)
        lmat = SB(es0, "lmat", [128, 128], F32)
        iota_row = SB(es0, "iota_row", [128, 512], mybir.dt.float16)
        tokid = SB(es0, "tokid", [128, NT], F32)
        jmp = SB(es0, "jmp", [128, 128], F32)
        modT = SB(es0, "modT", [128, 48, 2], F32)
        aff_all = SB(es0, "aff_all", [128, NT, NE], F32)
        epsc = SB(es0, "epsc", [128, 1], F32)
        S.op("pool", lambda e: e.memset(epsc[:], EPS), writes=["epsc"])
        S.op("pool", lambda e: e.iota(jmp[:], [[1, 128]], base=0, channel_multiplier=-1,
                                      allow_small_or_imprecise_dtypes=True), writes=["jmp"])
        S.op("pool", lambda e: e.tensor_single_scalar(out=identf[:], in_=jmp[:], scalar=0.0, op=ALU.is_equal),
             reads=["jmp"], writes=["identf"])
        S.op("pool", lambda e: e.tensor_single_scalar(out=identb[:], in_=jmp[:], scalar=0.0, op=ALU.is_equal),
             reads=["jmp"], writes=["identb"])
        S.op("pool", lambda e: e.tensor_single_scalar(out=lmat[:], in_=jmp[:], scalar=0.0, op=ALU.is_gt),
             reads=["jmp"], writes=["lmat"])
        S.op("pool", lambda e: e.memset(ones_f[:], 1.0), writes=["ones_f"])
        S.op("pool", lambda e: e.iota(iota_row[:], [[1, 512]], base=0, channel_multiplier=0,
                                      allow_small_or_imprecise_dtypes=True), writes=["iota_row"])
        S.op("pool", lambda e: e.iota(tokid[:], [[128, NT]], base=0, channel_multiplier=1,
                                      allow_small_or_imprecise_dtypes=True), writes=["tokid"])
        tokA = SB(es0, "tokA", [128, NT], F32)
        tokB = SB(es0, "tokB", [128, 1], F32)
        S.op("pool", lambda e: e.iota(tokA[:], [[128, NT]], base=0, channel_multiplier=0,
                                      allow_small_or_imprecise_dtypes=True), writes=["tokA"])
        S.op("pool", lambda e: e.iota(tokB[:], [[0, 1]], base=0, channel_multiplier=1,
                                      allow_small_or_imprecise_dtypes=True), writes=["tokB"])

        def ln_stats(es_tiles, src, key_src, tag):
            st, mv, rstd = es_tiles
            S.op("dve", lambda e: e.bn_stats(st[:, 0, :], src[:, 0:512]), reads=[key_src], writes=[tag + "st0"])
            S.op("dve", lambda e: e.bn_stats(st[:, 1, :], src[:, 512:1024]), reads=[key_src], writes=[tag + "st1"])
            S.op("dve", lambda e: e.bn_aggr(mv[:], st[:]), reads=[tag + "st0", tag + "st1"], writes=[tag + "mv"])
            S.op("act", lambda e: e.activation(out=rstd[:], in_=mv[:, 1:2], func=AF.Sqrt, bias=epsc[:], scale=1.0),
                 reads=[tag + "mv", "epsc"], writes=[tag + "rstd"])
            S.op("dve", lambda e: e.reciprocal(rstd[:], rstd[:]), reads=[tag + "rstd"], writes=[tag + "rstd"])

        for ll in range(L):
            esA = ExitStack()
            with esA:
                condT = SB(esA, "condT", [128, 8, 2], F32)
                bmT = SB(esA, "bmT", [128, 48], F32)
                wblk = [SB(esA, "wblk%d" % i, [128, 8, 512], F32) for i in range(2)]
                pmod = PS(esA, "pmod", [128, 48, 2], F32)
                S.dma("sp", lambda e: e.dma_start(out=condT[:], in_=cond_in), writes=["condT"])
                S.dma("sp", lambda e: e.dma_start(out=bmT[:], in_=b_modT[ll]), writes=["bmT"])
                S.op("act", lambda e: e.activation(out=condT[:], in_=condT[:], func=AF.Silu),
                     reads=["condT"], writes=["condT"])
                wm = w_mod[ll].rearrange("(k p) c -> p k c", p=128)
                for cb in range(12):
                    wb_ = wblk[cb % 2]
                    wk = "wblk%d" % (cb % 2)
                    S.dma("sp", lambda e: e.dma_start(out=wb_[:], in_=wm[:, :, cb * 512:(cb + 1) * 512]), writes=[wk])
                    for sub in range(4):
                        j = cb * 4 + sub
                        for k in range(8):
                            S.op("pe", lambda e: e.matmul(pmod[:, j, :], lhsT=wb_[:, k, sub * 128:(sub + 1) * 128],
                                                          rhs=condT[:, k, :], start=(k == 0), stop=(k == 7)),
                                 reads=[wk, "condT"], writes=["pmod"])
                S.op("dve", lambda e: e.tensor_tensor(out=modT[:], in0=pmod[:],
                                                      in1=bmT[:].unsqueeze(2).broadcast_to([128, 48, 2]), op=ALU.add),
                     reads=["pmod", "bmT"], writes=["modT"])
                for base in (8, 32):
                    S.op("dve", lambda e: e.tensor_scalar_add(modT[:, base:base + 8, :], modT[:, base:base + 8, :], 1.0),
                         reads=["modT"], writes=["modT"])
            S.barrier()

            def build_gate_rows(es, vbase, tagname):
                rows = [SB(es, "%s_%d" % (tagname, s), [128, D], F32) for s in range(2)]
                est = ExitStack()
                with est:
                  diag = [SB(est, "%s_dg%d" % (tagname, i), [128, 128], F32) for i in range(2)]
                  pg = PS(est, tagname + "_pg", [128, D], F32)
                  n = 0
                  for s in range(2):
                    for dc in range(8):
                        dg = diag[n % 2]
                        dk = "%s_dg%d" % (tagname, n % 2)
                        n += 1
                        S.op("dve", lambda e: e.tensor_scalar(out=dg[:], in0=identf[:], scalar1=modT[:, vbase + dc, s:s + 1],
                                                              scalar2=None, op0=ALU.mult),
                             reads=["identf", "modT"], writes=[dk])
                        S.op("pe", lambda e: e.matmul(pg[:, dc * 128:(dc + 1) * 128], lhsT=ones_f[:], rhs=dg[:],
                                                      start=True, stop=True),
                             reads=[dk, "ones_f"], writes=[tagname + "_pg"])
                    S.op("act", lambda e: e.activation(out=rows[s][:], in_=pg[:], func=AF.Copy),
                         reads=[tagname + "_pg"], writes=["%s_%d" % (tagname, s)])
                  S.barrier()
                return rows

            esBE = ExitStack()
            with esBE:
                mixps = SB(esBE, "mixps", [128, 4, NTOK], BF16)
                qT_all = SB(esBE, "qT_all", [128, 4, NTOK], BF16)
                kT_all = SB(esBE, "kT_all", [128, 2, NTOK], BF16)
                S.op("pool", lambda e: e.memset(kT_all[:], 0.0), writes=["kT_zero"])
                v_all = SB(esBE, "v_all", [128, NT, 2, 65], BF16)
                S.op("pool", lambda e: e.memset(v_all[:, :, :, 64:65], 1.0), writes=["v_ones"])
                esBC = ExitStack()
                with esBC:
                    p_all = SB(esBC, "p_all", [128, NT, 256], BF16)
                    esB = ExitStack()
                    with esB:
                        winb = SB(esB, "winb", [128, 8, 1536], BF16)
                        cs = SB(esB, "cs", [128, NTL, 64], F32)
                        sgug = SB(esB, "sgug", [128, 256], F32)
                        wsT = SB(esB, "wsT", [128, 4, 128], BF16)
                        sbT = SB(esB, "sbT", [128, 4], F32)
                        qkg = SB(esB, "qkg", [128, 640], F32)
                        xt = [SB(esB, "xt%d" % i, [128, D], F32) for i in range(2)]
                        st = SB(esB, "st", [128, 2, 6], F32)
                        mv = SB(esB, "mv", [128, 2], F32)
                        rstd = SB(esB, "rstd", [128, 1], F32)
                        xb = SB(esB, "xb", [128, D], BF16)
                        hT = SB(esB, "hT", [128, 8, 128], BF16)
                        gu = SB(esB, "gu", [128, 256], BF16)
                        gv = SB(esB, "gv", [128, 256], F32)
                        stv = SB(esB, "stv", [128, 4, 6], F32)
                        mvv = SB(esB, "mvv", [128, 4, 2], F32)
                        rsv = SB(esB, "rsv", [128, 4], F32)
                        vn = SB(esB, "vn", [128, 256], F32)
                        vh = SB(esB, "vh", [128, 256], BF16)
                        sgo = SB(esB, "sgo", [128, 256], BF16)
                        sq = SB(esB, "sq", [128, 640], F32)
                        ss = SB(esB, "ss", [128, 10], F32)
                        qn = SB(esB, "qn", [128, 10, 64], F32)
                        ta = SB(esB, "ta", [128, 10, 32], F32)
                        tb = SB(esB, "tb", [128, 10, 32], F32)
                        qr = SB(esB, "qr", [128, 640], BF16)
                        pT = PS(esB, "pT", [128, 8, 128], BF16)
                        pz = PS(esB, "pz", [128, 1536], F32)
                        psg = PS(esB, "psg", [128, 256], F32)
                        ptr = PS(esB, "ptr", [128, 7, 128], BF16)
                        S.dma("pool", lambda e: e.dma_start(out=winb[:], in_=w_in[ll].rearrange("(k p) c -> p k c", p=128)),
                              writes=["winb"])
                        S.dma("pool", lambda e: e.dma_start(out=wsT[:], in_=sguwT_in[ll]), writes=["wsT"])
                        S.dma("sp", lambda e: e.dma_start(out=cs[:], in_=cs_in), writes=["cs"])
                        S.dma("sp", lambda e: e.dma_start(out=sgug[:], in_=sgug_in[ll:ll + 1, :].broadcast_to([128, 256])),
                              writes=["sgug"])
                        S.dma("sp", lambda e: e.dma_start(out=qkg[:], in_=qkg_in[ll:ll + 1, :].broadcast_to([128, 640])),
                              writes=["qkg"])
                        S.dma("sp", lambda e: e.dma_start(out=sbT[:], in_=sgubT_in[ll]), writes=["sbT"])
                        def front_B(j):
                            s = 0 if j < NTL else 1
                            tok = slice(j * 128, (j + 1) * 128)
                            xtj = xt[j % 2]
                            xk = "xt%d" % (j % 2)
                            S.dma("sp", lambda e: e.dma_start(out=xtj[:], in_=src_tile(ll, j)),
                                  reads=[src_key(ll, j)], writes=[xk])
                            ln_stats((st, mv, rstd), xtj, xk, "B")
                            S.op("dve", lambda e: e.tensor_scalar(out=xb[:], in0=xtj[:], scalar1=mv[:, 0:1], scalar2=rstd[:],
                                                                  op0=ALU.subtract, op1=ALU.mult),
                                 reads=[xk, "Bmv", "Brstd"], writes=["xb"])
                            for k in range(8):
                                S.op("pe", lambda e: e.transpose(pT[:, k, :], xb[:, k * 128:(k + 1) * 128], identb[:]),
                                     reads=["xb", "identb"], writes=["pT"])
                            for k in range(8):
                                S.op("act", lambda e: e.activation(out=hT[:, k, :], in_=pT[:, k, :], func=AF.Identity,
                                                                   bias=modT[:, 0 + k, s:s + 1], scale=modT[:, 8 + k, s:s + 1]),
                                     reads=["pT", "modT"], writes=[("hT", k)])
                        gu2 = [gu, SB(esB, "gu_b", [128, 256], BF16)]
                        gv2 = [gv, SB(esB, "gv_b", [128, 256], F32)]
                        stv2 = [stv, SB(esB, "stv_b", [128, 4, 6], F32)]
                        mvv2 = [mvv, SB(esB, "mvv_b", [128, 4, 2], F32)]
                        rsv2 = [rsv, SB(esB, "rsv_b", [128, 4], F32)]
                        vn2 = [vn, SB(esB, "vn_b", [128, 256], F32)]
                        vh2 = [vh, SB(esB, "vh_b", [128, 256], BF16)]
                        sgo2 = [sgo, SB(esB, "sgo_b", [128, 256], BF16)]
                        sq2 = [sq, SB(esB, "sq_b", [128, 640], F32)]
                        zq2 = [SB(esB, "zq_a", [128, 640], F32), SB(esB, "zq_b", [128, 640], F32)]
                        ss2 = [ss, SB(esB, "ss_b", [128, 10], F32)]
                        qn2 = [qn, SB(esB, "qn_b", [128, 10, 64], F32)]
                        ta2 = [ta, SB(esB, "ta_b", [128, 10, 32], F32)]
                        tb2 = [tb, SB(esB, "tb_b", [128, 10, 32], F32)]
                        qr2 = [qr, SB(esB, "qr_b", [128, 640], BF16)]
                        psg2 = [psg, PS(esB, "psg_b", [128, 256], F32)]
                        ptr2 = [ptr, PS(esB, "ptr_b", [128, 7, 128], BF16)]

                        def mm_and_evac(j, t):
                            for cblk in range(3):
                                for k in range(8):
                                    S.op("pe", lambda e: e.matmul(pz[:, cblk * 512:(cblk + 1) * 512], lhsT=hT[:, k, :],
                                                                  rhs=winb[:, k, cblk * 512:(cblk + 1) * 512],
                                                                  start=(k == 0), stop=(k == 7)),
                                         reads=[("hT", k), "winb"], writes=[("pz", cblk)])
                            if j + 1 < NT:
                                front_B(j + 1)
                            S.op("act", lambda e: e.activation(out=p_all[:, j, :], in_=pz[:, 0:256], func=AF.Copy),
                                 reads=[("pz", 0)], writes=[("p_all", j)])
                            S.op("act", lambda e: e.activation(out=gu2[t][:], in_=pz[:, 256:512], func=AF.Gelu_apprx_tanh),
                                 reads=[("pz", 0)], writes=[("gu", t)])
                            S.op("act", lambda e: e.activation(out=gv2[t][:], in_=pz[:, 512:768], func=AF.Gelu_apprx_tanh),
                                 reads=[("pz", 1)], writes=[("gv", t)])
                            S.op("act", lambda e: e.activation(out=sq2[t][:], in_=pz[:, 768:1408], func=AF.Square),
                                 reads=[("pz", 1), ("pz", 2)], writes=[("sq", t)])
                            S.op("dve", lambda e: e.tensor_copy(out=zq2[t][:], in_=pz[:, 768:1408]),
                                 reads=[("pz", 1), ("pz", 2)], writes=[("zq", t)])
                            S.op("dve", lambda e: e.tensor_copy(out=v_all[:, j, :, 0:64],
                                                                in_=pz[:, 1408:1536].rearrange("p (k d) -> p k d", d=64)),
                                 reads=[("pz", 2)], writes=[("v", j)])

                        front_B(0)
                        for j0 in range(0, NTL, 2):
                            pair = [(j0, 0), (j0 + 1, 1)]
                            for (j, t) in pair:
                                mm_and_evac(j, t)
                            for h in range(4):
                                for (j, t) in pair:
                                    S.op("dve", lambda e: e.bn_stats(stv2[t][:, h, :], gv2[t][:, h * 64:(h + 1) * 64]),
                                         reads=[("gv", t)], writes=[("stv", t, h)])
                                for (j, t) in pair:
                                    S.op("dve", lambda e: e.bn_aggr(mvv2[t][:, h, :], stv2[t][:, h, :]),
                                         reads=[("stv", t, h)], writes=[("mvv", t)])
                            for (j, t) in pair:
                                S.op("act", lambda e: e.activation(out=rsv2[t][:], in_=mvv2[t][:, :, 1], func=AF.Sqrt, bias=epsc[:], scale=1.0),
                                     reads=[("mvv", t), "epsc"], writes=[("rsv", t)])
                            for (j, t) in pair:
                                S.op("dve", lambda e: e.tensor_reduce(out=ss2[t][:], in_=sq2[t][:].rearrange("p (h d) -> p h d", d=64),
                                                                      axis=AX.X, op=ALU.add),
                                     reads=[("sq", t)], writes=[("ss", t)])
                            for (j, t) in pair:
                                S.op("act", lambda e: e.activation(out=ss2[t][:], in_=ss2[t][:], func=AF.Sqrt, bias=epsc[:], scale=1.0 / 64),
                                     reads=[("ss", t), "epsc"], writes=[("ss", t)])
                            for (j, t) in pair:
                                S.op("dve", lambda e: e.reciprocal(rsv2[t][:], rsv2[t][:]), reads=[("rsv", t)], writes=[("rsv", t)])
                            for h in range(4):
                                for (j, t) in pair:
                                    S.op("dve", lambda e: e.tensor_scalar(out=vn2[t][:, h * 64:(h + 1) * 64], in0=gv2[t][:, h * 64:(h + 1) * 64],
                                                                          scalar1=mvv2[t][:, h, 0:1], scalar2=rsv2[t][:, h:h + 1],
                                                                          op0=ALU.subtract, op1=ALU.mult),
                                         reads=[("gv", t), ("mvv", t), ("rsv", t)], writes=[("vn", t)])
                            for (j, t) in pair:
                                S.op("dve", lambda e: e.tensor_tensor(out=vh2[t][:], in0=vn2[t][:], in1=sgug[:], op=ALU.mult),
                                     reads=[("vn", t), "sgug"], writes=[("vh", t)])
                            for (j, t) in pair:
                                for h in range(4):
                                    S.op("pe", lambda e: e.matmul(psg2[t][:, h * 64:(h + 1) * 64], lhsT=wsT[:, h, :],
                                                                  rhs=vh2[t][:, h * 64:(h + 1) * 64], start=True, stop=True),
                                         reads=["wsT", ("vh", t)], writes=[("psg", t)])
                            for (j, t) in pair:
                                S.op("dve", lambda e: e.reciprocal(ss2[t][:], ss2[t][:]), reads=[("ss", t)], writes=[("ss", t)])
                            for (j, t) in pair:
                                S.op("dve", lambda e: e.tensor_tensor(out=qn2[t][:], in0=zq2[t][:].rearrange("p (h d) -> p h d", d=64),
                                                                      in1=ss2[t][:].unsqueeze(2).broadcast_to([128, 10, 64]), op=ALU.mult),
                                     reads=[("zq", t), ("ss", t)], writes=[("qn", t)])
                            for (j, t) in pair:
                                S.op("dve", lambda e: e.tensor_tensor(out=qn2[t][:], in0=qn2[t][:],
                                                                      in1=qkg[:].rearrange("p (h d) -> p h d", d=64), op=ALU.mult),
                                     reads=[("qn", t), "qkg"], writes=[("qn", t)])
                            for h in range(4):
                                for (j, t) in pair:
                                    S.op("dve", lambda e: e.scalar_tensor_tensor(out=sgo2[t][:, h * 64:(h + 1) * 64],
                                                                                 in0=psg2[t][:, h * 64:(h + 1) * 64],
                                                                                 scalar=sbT[:, h:h + 1],
                                                                                 in1=gu2[t][:, h * 64:(h + 1) * 64],
                                                                                 op0=ALU.add, op1=ALU.mult),
                                         reads=[("psg", t), "sbT", ("gu", t)], writes=[("sgo", t)])
                            for (j, t) in pair:
                                for c in range(2):
                                    S.op("pe", lambda e: e.transpose(ptr2[t][:, c, :], sgo2[t][:, c * 128:(c + 1) * 128], identb[:]),
                                         reads=[("sgo", t), "identb"], writes=[("ptr", t)])
                            for (j, t) in pair:
                                tok = slice(j * 128, (j + 1) * 128)
                                S.op("act", lambda e: e.activation(out=mixps[:, 2:4, tok], in_=ptr2[t][:, 0:2, :], func=AF.Copy),
                                     reads=[("ptr", t)], writes=[("mixps_s", j)])
                            def dsts(t):
                                qdst = qr2[t][:, 0:512].rearrange("p (g k d) -> p k g d", g=4, k=2, d=64)
                                kdst = qr2[t][:, 512:640].rearrange("p (k d) -> p k d", d=64)
                                return qdst, kdst
                            if j0 < NTL:
                                for half_, op_ in ((0, ALU.subtract), (1, ALU.add)):
                                    for (j, t) in pair:
                                        cosb = cs[:, j:j + 1, 0:32].broadcast_to([128, 10, 32])
                                        xa_ = qn2[t][:, :, 0:32] if half_ == 0 else qn2[t][:, :, 32:64]
                                        S.op("dve", lambda e: e.tensor_tensor(out=ta2[t][:], in0=xa_, in1=cosb, op=ALU.mult),
                                             reads=[("qn", t), "cs"], writes=[("ta", t)])
                                    for (j, t) in pair:
                                        sinb = cs[:, j:j + 1, 32:64].broadcast_to([128, 10, 32])
                                        xb_ = qn2[t][:, :, 32:64] if half_ == 0 else qn2[t][:, :, 0:32]
                                        S.op("dve", lambda e: e.tensor_tensor(out=tb2[t][:], in0=xb_, in1=sinb, op=ALU.mult),
                                             reads=[("qn", t), "cs"], writes=[("tb", t)])
                                    for (j, t) in pair:
                                        qdst, kdst = dsts(t)
                                        dsl = slice(0, 32) if half_ == 0 else slice(32, 64)
                                        S.op("dve", lambda e: e.tensor_tensor(out=qdst[:, :, :, dsl],
                                                                              in0=ta2[t][:, 0:8, :].rearrange("p (k g) d -> p k g d", k=2),
                                                                              in1=tb2[t][:, 0:8, :].rearrange("p (k g) d -> p k g d", k=2),
                                                                              op=op_),
                                             reads=[("ta", t), ("tb", t)], writes=[("qr", t, half_, 0)])
                                        S.op("dve", lambda e: e.tensor_tensor(out=kdst[:, :, dsl], in0=ta2[t][:, 8:10, :], in1=tb2[t][:, 8:10, :],
                                                                              op=op_),
                                             reads=[("ta", t), ("tb", t)], writes=[("qr", t, half_, 1)])
                            else:
                                for (j, t) in pair:
                                    qdst, kdst = dsts(t)
                                    S.op("dve", lambda e: e.tensor_copy(out=qdst, in_=qn2[t][:, 0:8, :].rearrange("p (k g) d -> p k g d", k=2)),
                                         reads=[("qn", t)], writes=[("qr", t, 0, 0), ("qr", t, 1, 0)])
                                    S.op("dve", lambda e: e.tensor_copy(out=kdst, in_=qn2[t][:, 8:10, :]),
                                         reads=[("qn", t)], writes=[("qr", t, 0, 1), ("qr", t, 1, 1)])
                            for (j, t) in pair:
                                for c in range(5):
                                    S.op("pe", lambda e: e.transpose(ptr2[t][:, 2 + c, :], qr2[t][:, c * 128:(c + 1) * 128], identb[:]),
                                         reads=[("qr", t, 0, 0), ("qr", t, 0, 1), ("qr", t, 1, 0), ("qr", t, 1, 1), "identb"],
                                         writes=[("ptr", t)])
                            for (j, t) in pair:
                                tok = slice(j * 128, (j + 1) * 128)
                                S.op("act", lambda e: e.activation(out=qT_all[:, :, tok], in_=ptr2[t][:, 2:6, :], func=AF.Copy),
                                     reads=[("ptr", t)], writes=[("qT", j)])
                                for kv_ in range(2):
                                    S.op("act", lambda e: e.activation(out=kT_all[kv_ * 64:(kv_ + 1) * 64, kv_, tok],
                                                                       in_=ptr2[t][kv_ * 64:(kv_ + 1) * 64, 6, :], func=AF.Copy),
                                         reads=[("ptr", t), "kT_zero"], writes=[("kT", j, kv_)])
                        S.barrier()
                        for j in range(NTL, NT):
                            s = 0 if j < NTL else 1
                            tok = slice(j * 128, (j + 1) * 128)
                            for cblk in range(3):
                                for k in range(8):
                                    S.op("pe", lambda e: e.matmul(pz[:, cblk * 512:(cblk + 1) * 512], lhsT=hT[:, k, :],
                                                                  rhs=winb[:, k, cblk * 512:(cblk + 1) * 512],
                                                                  start=(k == 0), stop=(k == 7)),
                                         reads=[("hT", k), "winb"], writes=[("pz", cblk)])
                            if j + 1 < NT:
                                front_B(j + 1)
                            S.op("act", lambda e: e.activation(out=p_all[:, j, :], in_=pz[:, 0:256], func=AF.Copy),
                                 reads=[("pz", 0)], writes=[("p_all", j)])
                            S.op("act", lambda e: e.activation(out=gu[:], in_=pz[:, 256:512], func=AF.Gelu_apprx_tanh),
                                 reads=[("pz", 0)], writes=["gu"])
                            S.op("act", lambda e: e.activation(out=gv[:], in_=pz[:, 512:768], func=AF.Gelu_apprx_tanh),
                                 reads=[("pz", 1)], writes=["gv"])
                            for h in range(4):
                                S.op("dve", lambda e: e.bn_stats(stv[:, h, :], gv[:, h * 64:(h + 1) * 64]),
                                     reads=["gv"], writes=[("stv", h)])
                                S.op("dve", lambda e: e.bn_aggr(mvv[:, h, :], stv[:, h, :]),
                                     reads=[("stv", h)], writes=["mvv"])
                            S.op("act", lambda e: e.activation(out=rsv[:], in_=mvv[:, :, 1], func=AF.Sqrt, bias=epsc[:], scale=1.0),
                                 reads=["mvv", "epsc"], writes=["rsv"])
                            S.op("dve", lambda e: e.reciprocal(rsv[:], rsv[:]), reads=["rsv"], writes=["rsv"])
                            for h in range(4):
                                S.op("dve", lambda e: e.tensor_scalar(out=vn[:, h * 64:(h + 1) * 64], in0=gv[:, h * 64:(h + 1) * 64],
                                                                      scalar1=mvv[:, h, 0:1], scalar2=rsv[:, h:h + 1],
                                                                      op0=ALU.subtract, op1=ALU.mult),
                                     reads=["gv", "mvv", "rsv"], writes=["vn"])
                            S.op("dve", lambda e: e.tensor_tensor(out=vh[:], in0=vn[:], in1=sgug[:], op=ALU.mult),
                                 reads=["vn", "sgug"], writes=["vh"])
                            for h in range(4):
                                S.op("pe", lambda e: e.matmul(psg[:, h * 64:(h + 1) * 64], lhsT=wsT[:, h, :],
                                                              rhs=vh[:, h * 64:(h + 1) * 64], start=True, stop=True),
                                     reads=["wsT", "vh"], writes=["psg"])
                            for h in range(4):
                                S.op("dve", lambda e: e.scalar_tensor_tensor(out=sgo[:, h * 64:(h + 1) * 64],
                                                                             in0=psg[:, h * 64:(h + 1) * 64],
                                                                             scalar=sbT[:, h:h + 1],
                                                                             in1=gu[:, h * 64:(h + 1) * 64],
                                                                             op0=ALU.add, op1=ALU.mult),
                                     reads=["psg", "sbT", "gu"], writes=["sgo"])
                            for c in range(2):
                                S.op("pe", lambda e: e.transpose(ptr[:, c, :], sgo[:, c * 128:(c + 1) * 128], identb[:]),
                                     reads=["sgo", "identb"], writes=[("ptr", 0)])
                            S.op("act", lambda e: e.activation(out=mixps[:, 2:4, tok], in_=ptr[:, 0:2, :], func=AF.Copy),
                                 reads=[("ptr", 0)], writes=[("mixps_s", j)])
                            S.op("act", lambda e: e.activation(out=sq[:], in_=pz[:, 768:1408], func=AF.Square),
                                 reads=[("pz", 1), ("pz", 2)], writes=["sq"])
                            S.op("dve", lambda e: e.tensor_reduce(out=ss[:], in_=sq[:].rearrange("p (h d) -> p h d", d=64),
                                                                  axis=AX.X, op=ALU.add),
                                 reads=["sq"], writes=["ss"])
                            S.op("act", lambda e: e.activation(out=ss[:], in_=ss[:], func=AF.Sqrt, bias=epsc[:], scale=1.0 / 64),
                                 reads=["ss", "epsc"], writes=["ss"])
                            S.op("dve", lambda e: e.reciprocal(ss[:], ss[:]), reads=["ss"], writes=["ss"])
                            S.op("dve", lambda e: e.tensor_tensor(out=qn[:], in0=pz[:, 768:1408].rearrange("p (h d) -> p h d", d=64),
                                                                  in1=ss[:].unsqueeze(2).broadcast_to([128, 10, 64]), op=ALU.mult),
                                 reads=[("pz", 1), ("pz", 2), "ss"], writes=["qn"])
                            S.op("dve", lambda e: e.tensor_tensor(out=qn[:], in0=qn[:],
                                                                  in1=qkg[:].rearrange("p (h d) -> p h d", d=64), op=ALU.mult),
                                 reads=["qn", "qkg"], writes=["qn"])
                            qdst = qr[:, 0:512].rearrange("p (g k d) -> p k g d", g=4, k=2, d=64)
                            kdst = qr[:, 512:640].rearrange("p (k d) -> p k d", d=64)
                            qsrc = qn[:, 0:8, :].rearrange("p (k g) d -> p k g d", k=2)
                            ksrc = qn[:, 8:10, :]
                            if j < NTL:
                                cosb = cs[:, j:j + 1, 0:32].broadcast_to([128, 10, 32])
                                sinb = cs[:, j:j + 1, 32:64].broadcast_to([128, 10, 32])
                                x1 = qn[:, :, 0:32]
                                x2 = qn[:, :, 32:64]
                                S.op("dve", lambda e: e.tensor_tensor(out=ta[:], in0=x1, in1=cosb, op=ALU.mult),
                                     reads=["qn", "cs"], writes=["ta"])
                                S.op("dve", lambda e: e.tensor_tensor(out=tb[:], in0=x2, in1=sinb, op=ALU.mult),
                                     reads=["qn", "cs"], writes=["tb"])
                                S.op("dve", lambda e: e.tensor_tensor(out=qdst[:, :, :, 0:32],
                                                                      in0=ta[:, 0:8, :].rearrange("p (k g) d -> p k g d", k=2),
                                                                      in1=tb[:, 0:8, :].rearrange("p (k g) d -> p k g d", k=2),
                                                                      op=ALU.subtract),
                                     reads=["ta", "tb"], writes=["qr_a"])
                                S.op("dve", lambda e: e.tensor_tensor(out=kdst[:, :, 0:32], in0=ta[:, 8:10, :], in1=tb[:, 8:10, :],
                                                                      op=ALU.subtract),
                                     reads=["ta", "tb"], writes=["qr_b"])
                                S.op("dve", lambda e: e.tensor_tensor(out=ta[:], in0=x2, in1=cosb, op=ALU.mult),
                                     reads=["qn", "cs"], writes=["ta"])
                                S.op("dve", lambda e: e.tensor_tensor(out=tb[:], in0=x1, in1=sinb, op=ALU.mult),
                                     reads=["qn", "cs"], writes=["tb"])
                                S.op("dve", lambda e: e.tensor_tensor(out=qdst[:, :, :, 32:64],
                                                                      in0=ta[:, 0:8, :].rearrange("p (k g) d -> p k g d", k=2),
                                                                      in1=tb[:, 0:8, :].rearrange("p (k g) d -> p k g d", k=2),
                                                                      op=ALU.add),
                                     reads=["ta", "tb"], writes=["qr_c"])
                                S.op("dve", lambda e: e.tensor_tensor(out=kdst[:, :, 32:64], in0=ta[:, 8:10, :], in1=tb[:, 8:10, :],
                                                                      op=ALU.add),
                                     reads=["ta", "tb"], writes=["qr_d"])
                            else:
                                S.op("dve", lambda e: e.tensor_copy(out=qdst, in_=qsrc), reads=["qn"], writes=["qr_a", "qr_c"])
                                S.op("dve", lambda e: e.tensor_copy(out=kdst, in_=ksrc), reads=["qn"], writes=["qr_b", "qr_d"])
                            for c in range(5):
                                S.op("pe", lambda e: e.transpose(ptr[:, 2 + c, :], qr[:, c * 128:(c + 1) * 128], identb[:]),
                                     reads=["qr_a", "qr_b", "qr_c", "qr_d", "identb"], writes=[("ptr", 1)])
                            S.op("act", lambda e: e.activation(out=qT_all[:, :, tok], in_=ptr[:, 2:6, :], func=AF.Copy),
                                 reads=[("ptr", 1)], writes=[("qT", j)])
                            for kv_ in range(2):
                                S.op("act", lambda e: e.activation(out=kT_all[kv_ * 64:(kv_ + 1) * 64, kv_, tok],
                                                                   in_=ptr[kv_ * 64:(kv_ + 1) * 64, 6, :], func=AF.Copy),
                                     reads=[("ptr", 1), "kT_zero"], writes=[("kT", j, kv_)])
                            S.op("dve", lambda e: e.tensor_copy(out=v_all[:, j, :, 0:64],
                                                                in_=pz[:, 1408:1536].rearrange("p (k d) -> p k d", d=64)),
                                 reads=[("pz", 2)], writes=[("v", j)])
                    S.barrier()
                    esC = ExitStack()
                    with esC:
                        bandf = SB(esC, "bandf", [128, 4, 5, 128], F32)
                        band = SB(esC, "band", [128, 4, 5, 128], BF16)
                        wbd = SB(esC, "wbd", [128, 2, 128], BF16)
                        pscT = SB(esC, "pscT", [128, 2], F32)
                        pooledT = SB(esC, "pooledT", [128, 2, 128], BF16)
                        ppool = PS(esC, "ppool", [128, 4, 128], F32)
                        pp2 = PS(esC, "pp2", [128, 2, 128], F32)
                        S.dma("sp", lambda e: e.dma_start(out=bandf[:], in_=band_in), writes=["bandf"])
                        S.op("dve", lambda e: e.tensor_copy(out=band[:], in_=bandf[:]), reads=["bandf"], writes=["band"])
                        S.dma("pool", lambda e: e.dma_start(out=wbd[:], in_=wbd_in[ll].rearrange("c p q -> p c q")), writes=["wbd"])
                        S.dma("sp", lambda e: e.dma_start(out=pscT[:], in_=pscT_in[ll]), writes=["pscT"])
                        for j in range(NT):
                            tok = slice(j * 128, (j + 1) * 128)
                            lo_t, hi_t = (0, NTL - 1) if j < NTL else (NTL, NT - 1)
                            rels = [r for r in (-1, 0, 1) if lo_t <= j + r <= hi_t]
                            for c in range(2):
                                for bi in range(2):
                                    g = 2 * c + bi
                                    for ri, r in enumerate(rels):
                                        if r == -1:
                                            v = 0
                                        elif r == 1:
                                            v = 4
                                        else:
                                            v = 2 if j == lo_t else (3 if j == hi_t else 1)
                                        S.op("pe", lambda e: e.matmul(ppool[:, c * 2 + bi, :], lhsT=p_all[:, j + r, c * 128:(c + 1) * 128],
                                                                      rhs=band[:, g, v, :], start=(ri == 0), stop=(ri == len(rels) - 1)),
                                             reads=["band"], writes=["ppool"])
                                S.op("act", lambda e: e.activation(out=pooledT[0:64, c, :], in_=ppool[0:64, c * 2, :], func=AF.Copy),
                                     reads=["ppool"], writes=[("pooledT", c, 0)])
                                S.op("act", lambda e: e.activation(out=pooledT[64:128, c, :], in_=ppool[64:128, c * 2 + 1, :], func=AF.Copy),
                                     reads=["ppool"], writes=[("pooledT", c, 1)])
                            for c in range(2):
                                S.op("pe", lambda e: e.matmul(pp2[:, c, :], lhsT=wbd[:, c, :], rhs=pooledT[:, c, :], start=True, stop=True),
                                     reads=["wbd", ("pooledT", c, 0), ("pooledT", c, 1)], writes=["pp2"])
                            for c in range(2):
                                S.op("act", lambda e: e.activation(out=mixps[:, c, tok], in_=pp2[:, c, :], func=AF.Copy,
                                                                   scale=pscT[:, c:c + 1]),
                                     reads=["pp2", "pscT"], writes=[("mixps_p", j)])
                    S.barrier()
                if debug and ll == 0:
                    S.dma("sp", lambda e: e.dma_start(out=dbg_mix, in_=mixps[:]), writes=["dbgmix"])
                esD = ExitStack()
                with esD:
                    woutb = SB(esD, "woutb", [128, 8, D], BF16)
                    lnr = SB(esD, "lnr", [128, 2, D], F32)
                    wr = SB(esD, "wr", [128, 8, NE], F32)
                    PT = [SB(esD, "PT%d" % i, [128, 512], BF16) for i in range(3)]
                    attn_tok = [SB(esD, "attn_tok%d" % i, [128, 4, 512], BF16) for i in range(2)]
                    rden = SB(esD, "rden", [128, 4], F32)
                    mixA = SB(esD, "mixA", [128, 4, 128], BF16)
                    bufA = SB(esD, "bufA", [128, D], F32)
                    bufB = SB(esD, "bufB", [128, D], F32)
                    xhb = SB(esD, "xhb", [128, D], BF16)
                    h2T = SB(esD, "h2T", [128, 8, 128], F32)
                    st = SB(esD, "stE", [128, 2, 6], F32)
                    mv = SB(esD, "mvE", [128, 2], F32)
                    rstd = SB(esD, "rstdE", [128, 1], F32)
                    mx = SB(esD, "mx", [128, 1], F32)
                    sm = SB(esD, "sm", [128, 1], F32)
                    ex = SB(esD, "ex", [128, NE], F32)
                    g1rows = build_gate_rows(esD, 16, "g1r")
                    pS = [PS(esD, "pS%d" % i, [128, 512], F32) for i in range(3)]
                    pO = [PS(esD, "pO%d" % i, [128, 4, 128], F32) for i in range(1)]
                    ptr = PS(esD, "ptrD", [128, 4, 128], BF16)
                    pbig = PS(esD, "pbig", [128, D], F32)
                    pr = PS(esD, "pr", [128, NE], F32)
                    S.dma("pool", lambda e: e.dma_start(out=woutb[:], in_=w_out[ll].rearrange("(k p) c -> p k c", p=128)),
                          writes=["woutb"])
                    S.dma("sp", lambda e: e.dma_start(out=lnr[:], in_=ln_in[ll:ll + 1, 0:2, :].broadcast_to([128, 2, D])),
                          writes=["lnr"])
                    S.dma("sp", lambda e: e.dma_start(out=wr[:], in_=wr_in[ll]), writes=["wr"])

                    def rstd_act(tag):
                        S.op("act", lambda e: e.activation(out=rstd[:], in_=mv[:, 1:2], func=AF.Ln, bias=epsc[:], scale=1.0),
                             reads=[tag + "mv", "epsc"], writes=[tag + "rstd"])
                        S.op("act", lambda e: e.activation(out=rstd[:], in_=rstd[:], func=AF.Exp, scale=-0.5),
                             reads=[tag + "rstd"], writes=[tag + "rstd"])

                    def stats_dve(src, key_src, tag):
                        S.op("dve", lambda e: e.bn_stats(st[:, 0, :], src[:, 0:512]), reads=[key_src], writes=[tag + "st0"])
                        S.op("dve", lambda e: e.bn_stats(st[:, 1, :], src[:, 512:1024]), reads=[key_src], writes=[tag + "st1"])
                        S.op("dve", lambda e: e.bn_aggr(mv[:], st[:]), reads=[tag + "st0", tag + "st1"], writes=[tag + "mv"])

                    def make_ef_stages(j, qs, par):
                        s = 0 if j < NTL else 1
                        tok = slice(j * 128, (j + 1) * 128)
                        at = attn_tok[par]

                        def st0():
                            for cc in range(4):
                                S.op("pe", lambda e: e.transpose(ptr[:, cc, :], at[:, qs, cc * 128:(cc + 1) * 128], identb[:]),
                                     reads=[("attn_tok", par, qs), "identb"], writes=["ptrD"])
                            S.dma("sp", lambda e: e.dma_start(out=bufA[:], in_=src_tile(ll, j)), reads=[src_key(ll, j)], writes=["bufA"])

                        def st1():
                            S.op("act", lambda e: e.activation(out=mixA[:], in_=ptr[:], func=AF.Copy), reads=["ptrD"], writes=["mixA"])

                        def st2():
                            for half in range(2):
                                for c8 in range(8):
                                    lhs = mixps[:, c8, tok] if c8 < 4 else mixA[:, c8 - 4, :]
                                    S.op("pe", lambda e: e.matmul(pbig[:, half * 512:(half + 1) * 512], lhsT=lhs,
                                                                  rhs=woutb[:, c8, half * 512:(half + 1) * 512],
                                                                  start=(c8 == 0), stop=(c8 == 7)),
                                         reads=["mixA", "woutb"], writes=["pbig"])

                        def st3():
                            S.op("dve", lambda e: e.tensor_tensor(out=bufB[:], in0=pbig[:], in1=g1rows[s][:], op=ALU.mult),
                                 reads=["pbig", "g1r_%d" % s], writes=["bufB"])
                            S.op("dve", lambda e: e.scalar_tensor_tensor(out=bufA[:], in0=bufA[:], scalar=ALPHA, in1=bufB[:],
                                                                         op0=ALU.mult, op1=ALU.add),
                                 reads=["bufA", "bufB"], writes=["bufA"])
                            stats_dve(bufA, "bufA", "E")

                        def st4():
                            rstd_act("E")

                        def st5():
                            S.op("dve", lambda e: e.tensor_scalar(out=bufB[:], in0=bufA[:], scalar1=mv[:, 0:1], scalar2=rstd[:],
                                                                  op0=ALU.subtract, op1=ALU.mult),
                                 reads=["bufA", "Emv", "Erstd"], writes=["bufB"])
                            S.op("dve", lambda e: e.tensor_tensor(out=bufB[:], in0=bufB[:], in1=lnr[:, 0, :], op=ALU.mult),
                                 reads=["bufB", "lnr"], writes=["bufB"])
                            S.op("dve", lambda e: e.tensor_tensor(out=bufB[:], in0=bufB[:], in1=lnr[:, 1, :], op=ALU.add),
                                 reads=["bufB", "lnr"], writes=["bufB"])
                            S.op("dve", lambda e: e.tensor_scalar(out=bufA[:], in0=bufB[:], scalar1=ALPHA, scalar2=None, op0=ALU.mult),
                                 reads=["bufB"], writes=["bufA"])
                            S.dma("sp", lambda e: e.dma_start(out=macc[tok, :], in_=bufA[:]), reads=["bufA"], writes=[("macc", j)])
                            if debug and ll == 0:
                                S.dma("sp", lambda e: e.dma_start(out=dbg_x1[tok, :], in_=bufA[:]), reads=["bufA"], writes=[("dbgx1", j)])
                            stats_dve(bufB, "bufB", "E")

                        def st6():
                            rstd_act("E")

                        def st7():
                            S.op("dve", lambda e: e.tensor_scalar(out=bufB[:], in0=bufB[:], scalar1=mv[:, 0:1], scalar2=rstd[:],
                                                                  op0=ALU.subtract, op1=ALU.mult),
                                 reads=["bufB", "Emv", "Erstd"], writes=["bufB"])
                            S.op("dve", lambda e: e.tensor_copy(out=xhb[:], in_=bufB[:]), reads=["bufB"], writes=["xhb"])
                            S.dma("sp", lambda e: e.dma_start(out=xh2[tok, :], in_=xhb[:]), reads=["xhb"], writes=[("xh2", j)])

                        def st8():
                            for k in range(8):
                                S.op("pe", lambda e: e.transpose(pbig[:, k * 128:(k + 1) * 128], bufB[:, k * 128:(k + 1) * 128], identf[:]),
                                     reads=["bufB", "identf"], writes=["pbig"])

                        def st9():
                            for k in range(8):
                                S.op("dve", lambda e: e.tensor_scalar(out=h2T[:, k, :], in0=pbig[:, k * 128:(k + 1) * 128],
                                                                      scalar1=modT[:, 32 + k, s:s + 1], scalar2=modT[:, 24 + k, s:s + 1],
                                                                      op0=ALU.mult, op1=ALU.add),
                                     reads=["pbig", "modT"], writes=[("h2T", k)])

                        def st10():
                            for k in range(8):
                                S.op("pe", lambda e: e.matmul(pr[:], lhsT=h2T[:, k, :], rhs=wr[:, k, :], start=(k == 0), stop=(k == 7)),
                                     reads=[("h2T", k), "wr"], writes=["pr"])

                        def st11():
                            S.op("dve", lambda e: e.reduce_max(out=mx[:], in_=pr[:], axis=AX.X), reads=["pr"], writes=["mx"])
                            S.op("dve", lambda e: e.tensor_scalar(out=mx[:], in0=mx[:], scalar1=-1.0, scalar2=None, op0=ALU.mult),
                                 reads=["mx"], writes=["mx"])

                        def st12():
                            S.op("act", lambda e: e.activation(out=ex[:], in_=pr[:], func=AF.Exp, bias=mx[:], scale=1.0, accum_out=sm[:]),
                                 reads=["pr", "mx"], writes=["ex", "sm"])

                        def st13():
                            S.op("dve", lambda e: e.reciprocal(sm[:], sm[:]), reads=["sm"], writes=["sm"])
                            S.op("dve", lambda e: e.tensor_scalar(out=aff_all[:, j, :], in0=ex[:], scalar1=sm[:], scalar2=None, op0=ALU.mult),
                                 reads=["ex", "sm"], writes=[("aff", j)])

                        return [st0, st1, st2, st3, st4, st5, st6, st7, st8, st9, st10, st11, st12, st13]

                    blocks = [(NTL, 2)] + [(qb * 4, 4) for qb in range(8)]
                    nS = 0
                    pending = []
                    for bi, (t0, ntile) in enumerate(blocks):
                        par = bi % 2
                        N = ntile * 128
                        qtok = slice(t0 * 128, t0 * 128 + N)
                        kts = list(range(NT)) if t0 < NTL else [NTL, NTL + 1]
                        steps = [(c, kv, ki, kt) for c in range(4) for kv in range(2) for ki, kt in enumerate(kts)]
                        spacing = max(1, (len(steps) - 4) // (len(pending) + 1))

                        def emit_S(i):
                            c, kv, ki, kt = steps[i]
                            pSb = pS[(nS + i) % 3]
                            pSk = "pS%d" % ((nS + i) % 3)
                            PTb = PT[(nS + i) % 3]
                            PTk = "PT%d" % ((nS + i) % 3)
                            S.op("pe", lambda e: e.matmul(pSb[:, 0:N], lhsT=kT_all[:, kv, kt * 128:(kt + 1) * 128],
                                                          rhs=qT_all[:, c, qtok], start=True, stop=True),
                                 writes=[pSk])
                            S.op("act", lambda e: e.activation(out=PTb[:, 0:N], in_=pSb[:, 0:N], func=AF.Exp, scale=0.125),
                                 reads=[pSk], writes=[PTk])

                        emit_S(0)
                        emit_S(1)
                        for i, (c, kv, ki, kt) in enumerate(steps):
                            h = kv * 4 + c
                            if i + 2 < len(steps):
                                emit_S(i + 2)
                            if pending and i % spacing == spacing - 1:
                                pending.pop(0)()
                            if ki == 0:
                                S.op("dve", lambda e: e.memset(pO[0][:], 0.0), writes=["pO0"])
                            pOb = pO[0]
                            pOk = "pO0"
                            PTb = PT[(nS + i) % 3]
                            PTk = "PT%d" % ((nS + i) % 3)
                            for qs in range(ntile):
                                S.op("pe", lambda e: e.matmul(pOb[:, qs, 0:65], lhsT=PTb[:, qs * 128:(qs + 1) * 128],
                                                              rhs=v_all[:, kt, kv, :], start=False, stop=(ki == len(kts) - 1)),
                                     reads=[PTk], writes=[pOk])
                            if ki == len(kts) - 1:
                                S.op("dve", lambda e: e.reciprocal(rden[:, 0:ntile], pOb[:, 0:ntile, 64]),
                                     reads=[pOk], writes=["rden"])
                                for qs in range(ntile):
                                    S.op("dve", lambda e: e.tensor_scalar(out=attn_tok[par][:, qs, h * 64:(h + 1) * 64], in0=pOb[:, qs, 0:64],
                                                                          scalar1=rden[:, qs:qs + 1], scalar2=None, op0=ALU.mult),
                                         reads=[pOk, "rden"], writes=[("attn_tok", par, qs)])
                        nS += len(steps)
                        while pending:
                            pending.pop(0)()
                        for qs in range(ntile):
                            pending.extend(make_ef_stages(t0 + qs, qs, par))
                    while pending:
                        pending.pop(0)()
                S.barrier()
            if debug and ll == 0:
                S.dma("sp", lambda e: e.dma_start(out=dbg_aff, in_=aff_all[:]), writes=["dbgaff"])
            esGH = ExitStack()
            with esGH:
                idx_i = SB(esGH, "idx_i", [128, NE, 5], I32)
                gate_s = SB(esGH, "gate_s", [128, NE, 5], F32)
                esG = ExitStack()
                with esG:
                    lo = SB(esG, "lo", [128, 2, NE], F32)
                    hi = SB(esG, "hi", [128, 2, NE], F32)
                    mid = SB(esG, "mid", [128, 2, NE], F32)
                    capv = SB(esG, "capv", [128, 2, NE], F32)
                    cmp_ = SB(esG, "cmp", [128, NT, NE], F32)
                    cnt = SB(esG, "cnt", [128, 2, NE], F32)
                    ge = SB(esG, "ge", [128, 2, NE], F32)
                    d1 = SB(esG, "d1", [128, 2, NE], F32)
                    mask = SB(esG, "mask", [128, NT, NE], F32)
                    tg = SB(esG, "tg", [128, NT, NE, 5], BF16)
                    gr1 = SB(esG, "gr1", [128, NT, NE], F32)
                    gr2 = SB(esG, "gr2", [128, NT, NE], F32)
                    lst = SB(esG, "lst", [128, 5, 5], F32)
                    pos = SB(esG, "pos", [128, NT, NE], F32)
                    off = SB(esG, "off", [128, NT, NE], F32)
                    tot = SB(esG, "tot", [128, NT, NE], F32)
                    oh = [SB(esG, "oh%d" % i, [128, 512], BF16) for i in range(4)]
                    idx_f = SB(esG, "idx_f", [128, NE, 5], F32)
                    ptot = PS(esG, "ptot", [128, 2, NE], F32)
                    pwl = PS(esG, "pwl", [128, 512], F32)
                    pwc = PS(esG, "pwc", [128, 32], F32)
                    ptl = PS(esG, "ptl", [128, 512], F32)
                    ptc = PS(esG, "ptc", [128, 32], F32)
                    plist = [PS(esG, "plist0", [128, 5, 128], F32)]
                    S.op("pool", lambda e: e.memset(lo[:], 0.0), writes=["lo"])
                    S.op("pool", lambda e: e.memset(hi[:], 1.0), writes=["hi"])
                    S.op("pool", lambda e: e.memset(capv[:, 0, :], float(CAP_L)), writes=["capv0"])
                    S.op("pool", lambda e: e.memset(capv[:, 1, :], float(CAP_C)), writes=["capv1"])
                    aff_l = aff_all[:, 0:NTL, :]
                    aff_c = aff_all[:, NTL:NT, :]
                    for it in range(30):
                        S.op("dve", lambda e: e.tensor_tensor(out=mid[:], in0=lo[:], in1=hi[:], op=ALU.add),
                             reads=["lo", "hi"], writes=["mid"])
                        S.op("dve", lambda e: e.tensor_scalar(out=mid[:], in0=mid[:], scalar1=0.5, scalar2=None, op0=ALU.mult),
                             reads=["mid"], writes=["mid"])
                        S.op("dve", lambda e: e.tensor_tensor(out=cmp_[:, 0:NTL, :], in0=aff_l,
                                                              in1=mid[:, 0:1, :].broadcast_to([128, NTL, NE]), op=ALU.is_ge),
                             reads=["mid"], writes=["cmp_l"])
                        S.op("dve", lambda e: e.tensor_tensor(out=cmp_[:, NTL:NT, :], in0=aff_c,
                                                              in1=mid[:, 1:2, :].broadcast_to([128, 2, NE]), op=ALU.is_ge),
                             reads=["mid"], writes=["cmp_c"])
                        S.op("dve", lambda e: e.tensor_reduce(out=cnt[:, 0, :], in_=cmp_[:, 0:NTL, :].rearrange("p j e -> p e j"),
                                                              axis=AX.X, op=ALU.add),
                             reads=["cmp_l"], writes=["cnt0"])
                        S.op("dve", lambda e: e.tensor_reduce(out=cnt[:, 1, :], in_=cmp_[:, NTL:NT, :].rearrange("p j e -> p e j"),
                                                              axis=AX.X, op=ALU.add),
                             reads=["cmp_c"], writes=["cnt1"])
                        S.op("pe", lambda e: e.matmul(ptot[:], lhsT=ones_f[:], rhs=cnt[:], start=True, stop=True),
                             reads=["cnt0", "cnt1", "ones_f"], writes=["ptot"])
                        S.op("dve", lambda e: e.tensor_tensor(out=ge[:], in0=ptot[:], in1=capv[:], op=ALU.is_ge),
                             reads=["ptot", "capv0", "capv1"], writes=["ge"])
                        S.op("dve", lambda e: e.tensor_tensor(out=d1[:], in0=mid[:], in1=lo[:], op=ALU.subtract),
                             reads=["mid", "lo"], writes=["d1"])
                        S.op("dve", lambda e: e.tensor_tensor(out=d1[:], in0=d1[:], in1=ge[:], op=ALU.mult),
                             reads=["d1", "ge"], writes=["d1"])
                        S.op("dve", lambda e: e.tensor_tensor(out=lo[:], in0=lo[:], in1=d1[:], op=ALU.add),
                             reads=["d1", "lo"], writes=["lo"])
                        S.op("dve", lambda e: e.tensor_tensor(out=d1[:], in0=hi[:], in1=mid[:], op=ALU.subtract),
                             reads=["mid", "hi"], writes=["d1"])
                        S.op("dve", lambda e: e.tensor_tensor(out=d1[:], in0=d1[:], in1=ge[:], op=ALU.mult),
                             reads=["d1", "ge"], writes=["d1"])
                        S.op("dve", lambda e: e.tensor_tensor(out=hi[:], in0=mid[:], in1=d1[:], op=ALU.add),
                             reads=["d1", "mid"], writes=["hi"])
                    S.op("dve", lambda e: e.tensor_tensor(out=mask[:, 0:NTL, :], in0=aff_l,
                                                          in1=lo[:, 0:1, :].broadcast_to([128, NTL, NE]), op=ALU.is_ge),
                         reads=["lo"], writes=["mask_l"])
                    S.op("dve", lambda e: e.tensor_tensor(out=mask[:, NTL:NT, :], in0=aff_c,
                                                          in1=lo[:, 1:2, :].broadcast_to([128, 2, NE]), op=ALU.is_ge),
                         reads=["lo"], writes=["mask_c"])
                    S.op("dve", lambda e: e.tensor_copy(out=tg[:, :, :, 0], in_=tokA[:].unsqueeze(2).broadcast_to([128, NT, NE])),
                         reads=["tokA"], writes=["tg0"])
                    S.op("dve", lambda e: e.tensor_copy(out=tg[:, :, :, 1], in_=tokB[:].unsqueeze(2).broadcast_to([128, NT, NE])),
                         reads=["tokB"], writes=["tg0"])
                    S.op("dve", lambda e: e.tensor_copy(out=tg[:, :, :, 2], in_=aff_all[:]), writes=["tg1"])
                    S.op("dve", lambda e: e.tensor_tensor(out=gr1[:], in0=aff_all[:], in1=tg[:, :, :, 2], op=ALU.subtract),
                         reads=["tg1"], writes=["gr1"])
                    S.op("dve", lambda e: e.tensor_copy(out=tg[:, :, :, 3], in_=gr1[:]), reads=["gr1"], writes=["tg1"])
                    S.op("dve", lambda e: e.tensor_tensor(out=gr2[:], in0=gr1[:], in1=tg[:, :, :, 3], op=ALU.subtract),
                         reads=["gr1", "tg1"], writes=["gr2"])
                    S.op("dve", lambda e: e.tensor_copy(out=tg[:, :, :, 4], in_=gr2[:]), reads=["gr2"], writes=["tg1"])
                    mk2 = mask[:].rearrange("p j e -> p (j e)")
                    S.op("pe", lambda e: e.matmul(pwl[:], lhsT=lmat[:], rhs=mk2[:, 0:512], start=True, stop=True),
                         reads=["mask_l", "lmat"], writes=["pwl"])
                    S.op("pe", lambda e: e.matmul(pwc[:], lhsT=lmat[:], rhs=mk2[:, 512:544], start=True, stop=True),
                         reads=["mask_c", "lmat"], writes=["pwc"])
                    S.op("pe", lambda e: e.matmul(ptl[:], lhsT=ones_f[:], rhs=mk2[:, 0:512], start=True, stop=True),
                         reads=["mask_l", "ones_f"], writes=["ptl"])
                    S.op("pe", lambda e: e.matmul(ptc[:], lhsT=ones_f[:], rhs=mk2[:, 512:544], start=True, stop=True),
                         reads=["mask_c", "ones_f"], writes=["ptc"])
                    pos2 = pos[:].rearrange("p j e -> p (j e)")
                    tot2 = tot[:].rearrange("p j e -> p (j e)")
                    S.op("act", lambda e: e.activation(out=pos2[:, 0:512], in_=pwl[:], func=AF.Copy), reads=["pwl"], writes=["pos_l"])
                    S.op("act", lambda e: e.activation(out=pos2[:, 512:544], in_=pwc[:], func=AF.Copy), reads=["pwc"], writes=["pos_c"])
                    S.op("act", lambda e: e.activation(out=tot2[:, 0:512], in_=ptl[:], func=AF.Copy), reads=["ptl"], writes=["tot"])
                    S.op("act", lambda e: e.activation(out=tot2[:, 512:544], in_=ptc[:], func=AF.Copy), reads=["ptc"], writes=["tot"])
                    S.op("pool", lambda e: e.memset(off[:], 0.0), writes=["off"])
                    for j in range(1, NTL):
                        S.op("dve", lambda e: e.tensor_tensor(out=off[:, j, :], in0=off[:, j - 1, :], in1=tot[:, j - 1, :], op=ALU.add),
                             reads=["off", "tot"], writes=["off"])
                    S.op("dve", lambda e: e.tensor_copy(out=off[:, NTL + 1, :], in_=tot[:, NTL, :]), reads=["off", "tot"], writes=["off"])
                    S.op("dve", lambda e: e.tensor_tensor(out=pos[:], in0=pos[:], in1=off[:], op=ALU.add),
                         reads=["pos_l", "pos_c", "off"], writes=["pos_l", "pos_c"])
                    S.op("dve", lambda e: e.scalar_tensor_tensor(out=pos[:], in0=pos[:], scalar=-BIG, in1=mask[:], op0=ALU.add, op1=ALU.mult),
                         reads=["pos_l", "pos_c", "mask_l", "mask_c"], writes=["pos_l", "pos_c"])
                    S.op("dve", lambda e: e.tensor_scalar(out=pos[:], in0=pos[:], scalar1=BIG, scalar2=None, op0=ALU.add),
                         reads=["pos_l", "pos_c"], writes=["pos"])
                    n_oh = 0
                    for ex_ in range(NE):
                        pl = plist[0]
                        plk = "plist0"
                        S.op("dve", lambda e: e.memset(pl[:], 0.0), writes=[plk])
                        for j in range(NT):
                            ohb = oh[n_oh % 4]
                            ohk = "oh%d" % (n_oh % 4)
                            eng_ = "dve"
                            n_oh += 1
                            if j < NTL:
                                S.op(eng_, lambda e: e.tensor_scalar(out=ohb[:], in0=iota_row[:], scalar1=pos[:, j, ex_:ex_ + 1],
                                                                     scalar2=None, op0=ALU.is_equal),
                                     reads=["iota_row", "pos"], writes=[ohk])
                                for st_ in range(4):
                                    S.op("pe", lambda e: e.matmul(pl[:, st_, 0:5], lhsT=ohb[:, st_ * 128:(st_ + 1) * 128],
                                                                  rhs=tg[:, j, ex_, :], start=False, stop=(j == NTL - 1)),
                                         reads=[ohk, "tg0", "tg1"], writes=[plk])
                            else:
                                S.op(eng_, lambda e: e.tensor_scalar(out=ohb[:, 0:128], in0=iota_row[:, 0:128], scalar1=pos[:, j, ex_:ex_ + 1],
                                                                     scalar2=None, op0=ALU.is_equal),
                                     reads=["iota_row", "pos"], writes=[ohk])
                                S.op("pe", lambda e: e.matmul(pl[:, 4, 0:5], lhsT=ohb[:, 0:128], rhs=tg[:, j, ex_, :],
                                                              start=False, stop=(j == NT - 1)),
                                     reads=[ohk, "tg0", "tg1"], writes=[plk])
                        S.op("act", lambda e: e.activation(out=lst[:], in_=pl[:, :, 0:5], func=AF.Copy),
                             reads=[plk], writes=["lst"])
                        S.op("dve", lambda e: e.tensor_tensor(out=idx_f[:, ex_, :], in0=lst[:, :, 0], in1=lst[:, :, 1], op=ALU.add),
                             reads=["lst"], writes=["idx_f"])
                        S.op("dve", lambda e: e.tensor_tensor(out=gate_s[:, ex_, :], in0=lst[:, :, 2], in1=lst[:, :, 3], op=ALU.add),
                             reads=["lst"], writes=["gate_s"])
                        S.op("dve", lambda e: e.tensor_tensor(out=gate_s[:, ex_, :], in0=gate_s[:, ex_, :], in1=lst[:, :, 4], op=ALU.add),
                             reads=["lst", "gate_s"], writes=["gate_s"])
                    S.op("dve", lambda e: e.tensor_copy(out=idx_i[:], in_=idx_f[:]), reads=["idx_f"], writes=["idx_i"])
                    if debug and ll == 0:
                        S.dma("sp", lambda e: e.dma_start(out=dbg_idx, in_=idx_i[:]), reads=["idx_i"], writes=["dbgidx"])
                        S.dma("sp", lambda e: e.dma_start(out=dbg_gate, in_=gate_s[:]), reads=["gate_s"], writes=["dbggate"])
                        S.dma("sp", lambda e: e.dma_start(out=dbg_pos, in_=pos[:]), reads=["pos"], writes=["dbgpos"])
                S.barrier()
                esH = ExitStack()
                with esH:
                    g2rows = build_gate_rows(esH, 40, "g2r")
                    wb = [[SB(esH, "wb%d_%d" % (m, i), [128, 8, D], BF16) for m in range(3)] for i in range(2)]
                    xg = [[SB(esH, "xg%d_%d" % (st_, i), [128, D], BF16) for st_ in range(5)] for i in range(2)]
                    xgT = [SB(esH, "xgT%d" % i, [128, 8, 544], BF16) for i in range(2)]
                    hidT = SB(esH, "hidT", [128, 8, 544], BF16)
                    s1 = SB(esH, "s1", [128, 544], F32)
                    ysc = [SB(esH, "ysc%d" % i, [128, D], F32) for i in range(4)]
                    pxT = [PS(esH, "pxT%d" % i, [128, 8, 128], BF16) for i in range(2)]
                    ph1 = PS(esH, "ph1", [128, 512], F32)
                    ph3 = PS(esH, "ph3", [128, 512], F32)
                    phc = PS(esH, "phc", [128, 2, 32], F32)
                    py = [PS(esH, "pyH%d" % i, [128, 512], F32) for i in range(3)]
                    wsrc = (w1, w3, w2)

                    def issue_loads(ex_):
                        i = ex_ % 2
                        for st_ in range(5):
                            npart = 128 if st_ < 4 else CAP_C
                            S.dma("pool", lambda e: e.indirect_dma_start(
                                out=xg[i][st_][0:npart, :], out_offset=None, in_=xh2[:, :],
                                in_offset=bass.IndirectOffsetOnAxis(ap=idx_i[0:npart, ex_, st_:st_ + 1], axis=0),
                                bounds_check=bcreg, oob_is_err=False),
                                reads=["idx_i"] + [("xh2", j) for j in range(NT)], writes=["xg%d_%d" % (st_, i)])
                        for m in range(3):
                            S.dma("pool", lambda e: e.dma_start(out=wb[i][m][:], in_=wsrc[m][ll, ex_].rearrange("(k p) c -> p k c", p=128)),
                                  writes=["wb%d_%d" % (m, i)])

                    def prep_tile(ex_, st_):
                        i = ex_ % 2
                        npart = 128 if st_ < 4 else CAP_C
                        s = 0 if st_ < 4 else 1
                        c0 = st_ * 128
                        npx[0] += 1
                        pb = pxT[npx[0] % 2]
                        pk = "pxT%d" % (npx[0] % 2)
                        for k in range(8):
                            S.op("pe", lambda e: e.transpose(pb[:, k, 0:npart], xg[i][st_][0:npart, k * 128:(k + 1) * 128],
                                                             identb[0:npart, 0:npart]),
                                 reads=["xg%d_%d" % (st_, i), "identb"], writes=[pk])
                        for k in range(8):
                            if k % 2 == 0:
                                S.op("act", lambda e: e.activation(out=xgT[i][:, k, c0:c0 + npart], in_=pb[:, k, 0:npart], func=AF.Identity,
                                                                   bias=modT[:, 24 + k, s:s + 1], scale=modT[:, 32 + k, s:s + 1]),
                                     reads=[pk, "modT"], writes=["xgT%d" % i])
                            else:
                                S.op("dve", lambda e: e.tensor_scalar(out=xgT[i][:, k, c0:c0 + npart], in0=pb[:, k, 0:npart],
                                                                      scalar1=modT[:, 32 + k, s:s + 1], scalar2=modT[:, 24 + k, s:s + 1],
                                                                      op0=ALU.mult, op1=ALU.add),
                                     reads=[pk, "modT"], writes=["xgT%d" % i])

                    npx = [0]
                    issue_loads(0)
                    for st_ in range(5):
                        prep_tile(0, st_)
                    nys = 0
                    npy = 0
                    for ex_ in range(NE):
                        i = ex_ % 2
                        xgTi = xgT[i]
                        xk_ = "xgT%d" % i
                        if ex_ + 1 < NE:
                            issue_loads(ex_ + 1)
                        for fc in range(8):
                            for m, ph in ((0, ph1), (1, ph3)):
                                for k in range(8):
                                    S.op("pe", lambda e: e.matmul(ph[:], lhsT=wb[i][m][:, k, fc * 128:(fc + 1) * 128], rhs=xgTi[:, k, 0:512],
                                                                  start=(k == 0), stop=(k == 7)),
                                         reads=[xk_, "wb%d_%d" % (m, i)], writes=["ph%d" % m])
                                for k in range(8):
                                    S.op("pe", lambda e: e.matmul(phc[:, m, :], lhsT=wb[i][m][:, k, fc * 128:(fc + 1) * 128], rhs=xgTi[:, k, 512:544],
                                                                  start=(k == 0), stop=(k == 7)),
                                         reads=[xk_, "wb%d_%d" % (m, i)], writes=["phc"])
                            if ex_ + 1 < NE and fc < 5:
                                prep_tile(ex_ + 1, fc)
                            S.op("act", lambda e: e.activation(out=s1[:, 0:512], in_=ph1[:], func=AF.Silu), reads=["ph0"], writes=["s1a"])
                            S.op("act", lambda e: e.activation(out=s1[:, 512:544], in_=phc[:, 0, :], func=AF.Silu), reads=["phc"], writes=["s1b"])
                            S.op("dve", lambda e: e.tensor_tensor(out=hidT[:, fc, 0:512], in0=s1[:, 0:512], in1=ph3[:], op=ALU.mult),
                                 reads=["s1a", "ph1"], writes=["hidT"])
                            S.op("dve", lambda e: e.tensor_tensor(out=hidT[:, fc, 512:544], in0=s1[:, 512:544], in1=phc[:, 1, :], op=ALU.mult),
                                 reads=["s1b", "phc"], writes=["hidT"])
                        for st_ in range(5):
                            npart = 128 if st_ < 4 else CAP_C
                            s = 0 if st_ < 4 else 1
                            c0 = st_ * 128
                            yb = ysc[nys % 4]
                            yk = "ysc%d" % (nys % 4)
                            nys += 1
                            for half in range(2):
                                pyb = py[npy % 3]
                                pyk = "pyH%d" % (npy % 3)
                                npy += 1
                                for fc in range(8):
                                    S.op("pe", lambda e: e.matmul(pyb[0:npart, :], lhsT=hidT[:, fc, c0:c0 + npart],
                                                                  rhs=wb[i][2][:, fc, half * 512:(half + 1) * 512],
                                                                  start=(fc == 0), stop=(fc == 7)),
                                         reads=["hidT", "wb2_%d" % i], writes=[pyk])
                                S.op("dve", lambda e: e.scalar_tensor_tensor(out=yb[0:npart, half * 512:(half + 1) * 512], in0=pyb[0:npart, :],
                                                                             scalar=gate_s[0:npart, ex_, st_:st_ + 1],
                                                                             in1=g2rows[s][0:npart, half * 512:(half + 1) * 512],
                                                                             op0=ALU.mult, op1=ALU.mult),
                                     reads=[pyk, "gate_s", "g2r_%d" % s], writes=[(yk, half)])
                            S.dma("pool", lambda e: e.indirect_dma_start(
                                out=macc[:, :], out_offset=bass.IndirectOffsetOnAxis(ap=idx_i[0:npart, ex_, st_:st_ + 1], axis=0),
                                in_=yb[0:npart, :], in_offset=None, bounds_check=bcreg, oob_is_err=False,
                                compute_op=ALU.add),
                                reads=[(yk, 0), (yk, 1), "idx_i"] + [("msc", ex_ - 1, k) for k in range(5)],
                                writes=[("msc", ex_, st_)])
                S.barrier()
            if debug and ll == 0:
                S.dma("sp", lambda e: e.dma_start(out=dbg_macc, in_=macc), writes=["dbgmacc"])
                S.barrier()
            esI = ExitStack()
            with esI:
                lnr2 = SB(esI, "lnr2", [128, 2, D], F32)
                mt = [SB(esI, "mt%d" % i, [128, D], F32) for i in range(2)]
                xo = [SB(esI, "xo%d" % i, [128, D], F32) for i in range(2)]
                st = SB(esI, "stI", [128, 2, 6], F32)
                mv = SB(esI, "mvI", [128, 2], F32)
                rstd = SB(esI, "rstdI", [128, 1], F32)
                nmr = SB(esI, "nmr", [128, 1], F32)
                S.dma("sp", lambda e: e.dma_start(out=lnr2[:], in_=ln_in[ll:ll + 1, 2:4, :].broadcast_to([128, 2, D])),
                      writes=["lnr2"])
                for j in range(NT):
                    tok = slice(j * 128, (j + 1) * 128)
                    m_ = mt[j % 2]
                    mk = "mt%d" % (j % 2)
                    o_ = xo[j % 2]
                    ok = "xo%d" % (j % 2)
                    S.dma("sp", lambda e: e.dma_start(out=m_[:], in_=macc[tok, :]), reads=[("macc", j)], writes=[mk])
                    ln_stats((st, mv, rstd), m_, mk, "I")
                    S.op("dve", lambda e: e.scalar_tensor_tensor(out=nmr[:], in0=mv[:, 0:1], scalar=-1.0, in1=rstd[:],
                                                                 op0=ALU.mult, op1=ALU.mult),
                         reads=["Imv", "Irstd"], writes=["nmr"])
                    S.op("act", lambda e: e.activation(out=o_[:], in_=m_[:], func=AF.Identity, bias=nmr[:], scale=rstd[:]),
                         reads=[mk, "nmr", "Irstd"], writes=[ok])
                    S.op("dve", lambda e: e.tensor_tensor(out=o_[:], in0=o_[:], in1=lnr2[:, 0, :], op=ALU.mult),
                         reads=[ok, "lnr2"], writes=[ok])
                    S.op("dve", lambda e: e.tensor_tensor(out=o_[:], in0=o_[:], in1=lnr2[:, 1, :], op=ALU.add),
                         reads=[ok, "lnr2"], writes=[ok])
                    S.dma("sp", lambda e: e.dma_start(out=dst_tile(ll, j), in_=o_[:]), reads=[ok], writes=[dst_key(ll, j)])
            S.barrier()
        print("instructions", S.n_inst, "waits", S.n_wait, flush=True)
    return nc


def _band_tables():
    wins = (2, 4, 8, 16)
    Ls = 384
    band = np.zeros((128, 4, 5, 128), np.float32)
    for g, w in enumerate(wins):
        A = np.zeros((Ls, Ls), np.float64)
        for t in range(Ls):
            lo = min(max(t - w // 2, 0), Ls)
            hi = min(max(t + w // 2, 0), Ls)
            A[t, lo:hi] = 1.0 / (hi - lo)
            A[t, t] -= 1.0
        AT = A.T
        band[:, g, 0, :] = AT[0:128, 128:256]
        band[:, g, 1, :] = AT[128:256, 128:256]
        band[:, g, 2, :] = AT[0:128, 0:128]
        band[:, g, 3, :] = AT[256:384, 256:384]
        band[:, g, 4, :] = AT[256:384, 128:256]
    return band


def _rope_tables():
    t = np.arange(SEQ)
    r = (t // 64).astype(np.float32)
    col = (t % 64).astype(np.float32)
    inv = (np.float32(10000.0) ** (-np.arange(16, dtype=np.float32) / np.float32(16))).astype(np.float32)
    ang = np.concatenate([r[:, None] * inv, col[:, None] * inv], axis=-1).astype(np.float32)
    cs = np.concatenate([np.cos(ang), np.sin(ang)], axis=-1).astype(np.float32)
    return np.ascontiguousarray(cs.reshape(NTL, 128, 64).transpose(1, 0, 2))


def _prep_common(inp, layers):
    L = len(layers)
    sl = lambda a: np.ascontiguousarray(np.asarray(a)[layers])
    pool_w = sl(inp["pool_w"])
    wbd = np.zeros((L, 2, 128, 128), np.float32)
    for c in range(2):
        for gi in range(2):
            wbd[:, c, gi * 64:(gi + 1) * 64, gi * 64:(gi + 1) * 64] = pool_w[:, 2 * c + gi]
    com = {
        "w_mod": sl(inp["w_mod"]),
        "b_modT": np.ascontiguousarray(sl(inp["b_mod"]).reshape(L, 48, 128).transpose(0, 2, 1)),
        "w_in": sl(inp["w_in"]),
        "wbd": wbd,
        "pscT": np.ascontiguousarray(sl(inp["pool_scale"]).reshape(L, 2, 128).transpose(0, 2, 1)),
        "band": _band_tables(),
        "sgu_g": np.ascontiguousarray(sl(inp["sgu_g"]).reshape(L, 256)),
        "sgu_wT": np.ascontiguousarray(sl(inp["sgu_w"]).transpose(0, 3, 1, 2)),
        "sgu_bT": np.ascontiguousarray(sl(inp["sgu_b"]).transpose(0, 2, 1)),
        "qkg": np.ascontiguousarray(np.concatenate([np.tile(sl(inp["q_g"]), (1, 8)), np.tile(sl(inp["k_g"]), (1, 2))], axis=1)),
        "w_out": sl(inp["w_out"]),
        "ln": np.ascontiguousarray(np.stack([sl(inp["ln1_g"]), sl(inp["ln1_b"]), sl(inp["ln2_g"]), sl(inp["ln2_b"])], axis=1)),
        "w_router": np.ascontiguousarray(sl(inp["w_router"]).reshape(L, 8, 128, NE).transpose(0, 2, 1, 3)),
        "w1": sl(inp["w1"]), "w3": sl(inp["w3"]), "w2": sl(inp["w2"]),
        "cs": _rope_tables(),
    }
    return com


_NC_CACHE = {}


def _run(inp, x, ctx, layers, n_cores=8):
    L = len(layers)
    if L not in _NC_CACHE:
        _NC_CACHE[L] = build(L)
    nc = _NC_CACHE[L]
    com = _prep_common(inp, layers)
    c = np.asarray(inp["c"], np.float32)
    c_ctx = np.asarray(inp["c_ctx"], np.float32)
    in_maps = []
    for b in range(n_cores):
        cond = np.stack([c[b].reshape(8, 128).T, c_ctx.reshape(8, 128).T], axis=-1)
        m = dict(com)
        m["x"] = np.ascontiguousarray(x[b])
        m["ctx"] = np.ascontiguousarray(ctx[b])
        m["cond"] = np.ascontiguousarray(cond.astype(np.float32))
        in_maps.append(m)
    res = run_bass_kernel_spmd(nc, in_maps, core_ids=list(range(n_cores)))
    xo = np.stack([np.asarray(r["out"]) for r in res.results], 0)
    co = np.stack([np.asarray(r["ctx_out"]) for r in res.results], 0)
    return xo, co


LAYERS_PER_LAUNCH = 4


def kernel(**inputs):
    inp = {k: np.asarray(v) for k, v in inputs.items()}
    x = np.asarray(inp["x"], np.float32)
    ctx = np.asarray(inp["ctx"], np.float32)
    for l0 in range(0, DEPTH, LAYERS_PER_LAUNCH):
        x, ctx = _run(inp, x, ctx, list(range(l0, l0 + LAYERS_PER_LAUNCH)))
    return x.astype(np.float32)
```

```python
import numpy as np
from contextlib import ExitStack
import concourse.bass as bass
import concourse.mybir as mybir
from concourse.bass_utils import run_bass_kernel_spmd

F32 = mybir.dt.float32
BF16 = mybir.dt.bfloat16
I32 = mybir.dt.int32
AF = mybir.ActivationFunctionType
ALU = mybir.AluOpType
AX = mybir.AxisListType

DEPTH = 4
D = 1024
SEQ = 4096
CTX = 256
NT = 34
NTL = 32
NTOK = SEQ + CTX
NE = 16
CAP_L = 512
CAP_C = 32
ALPHA = float((2 * DEPTH) ** 0.25)
EPS = 1e-6
BIG = 100000.0


class Sched:
    def __init__(self, nc, es, n_dma_sems=48):
        self.nc = nc
        self.eng = {"pe": nc.tensor, "act": nc.scalar, "dve": nc.vector, "pool": nc.gpsimd, "sp": nc.sync}
        self.sem = {k: es.enter_context(nc.semaphore("prog_" + k)) for k in self.eng}
        self.cnt = {k: 0 for k in self.eng}
        self.dsem = [es.enter_context(nc.semaphore("dma%d" % i)) for i in range(n_dma_sems)]
        self.dtot = [0] * n_dma_sems
        self.dnext = 0
        self.dnext_sw = 0
        self.waited = {k: {} for k in self.eng}
        self.res = {}
        self.n_inst = 0
        self.n_wait = 0

    def _semobj(self, key):
        return self.sem[key] if isinstance(key, str) else self.dsem[key]

    def _collect(self, reads, writes):
        deps = {}

        def add(d):
            if d is None:
                return
            k, v = d
            if deps.get(k, 0) < v:
                deps[k] = v
        for r in reads:
            e = self.res.get(r)
            if e is not None:
                add(e["w"])
        for w in writes:
            e = self.res.get(w)
            if e is not None:
                add(e["w"])
                for k, v in e["r"].items():
                    add((k, v))
        return deps

    def _wait(self, F, deps, skip_self=False):
        for k, v in deps.items():
            if skip_self and k == F:
                continue
            if self.waited[F].get(k, 0) < v:
                self.eng[F].wait_ge(self._semobj(k), v)
                self.waited[F][k] = v
                self.n_wait += 1

    def _update(self, dep, reads, writes):
        k, v = dep
        for r in reads:
            e = self.res.setdefault(r, {"w": None, "r": {}})
            if e["r"].get(k, 0) < v:
                e["r"][k] = v
        for w in writes:
            self.res[w] = {"w": dep, "r": {}}

    def op(self, F, fn, reads=(), writes=(), skip_self=None):
        if skip_self is None:
            skip_self = (F == "pe")
        deps = self._collect(reads, writes)
        self._wait(F, deps, skip_self=skip_self)
        inst = fn(self.eng[F])
        self.cnt[F] += 1
        inst.then_inc(self.sem[F], 1)
        self._update((F, self.cnt[F]), reads, writes)
        self.n_inst += 1
        return inst

    def dma(self, Q, fn, reads=(), writes=()):
        deps = self._collect(reads, writes)
        self._wait(Q, deps)
        half = len(self.dsem) // 2
        if Q == "pool":
            i = half + self.dnext_sw
            self.dnext_sw = (self.dnext_sw + 1) % (len(self.dsem) - half)
        else:
            i = self.dnext
            self.dnext = (self.dnext + 1) % half
        if self.dtot[i] > 0 and self.waited[Q].get(i, 0) < self.dtot[i]:
            self.eng[Q].wait_ge(self.dsem[i], self.dtot[i])
            self.waited[Q][i] = self.dtot[i]
            self.n_wait += 1
        inst = fn(self.eng[Q])
        self.dtot[i] += 16
        inst.then_inc(self.dsem[i], 16)
        self._update((i, self.dtot[i]), reads, writes)
        self.n_inst += 1
        return inst

    def barrier(self):
        for F in self.eng:
            for i, t in enumerate(self.dtot):
                if t > 0 and self.waited[F].get(i, 0) < t:
                    self.eng[F].wait_ge(self.dsem[i], t)
                    self.waited[F][i] = t
            for k in self.eng:
                if k != F and self.cnt[k] > 0 and self.waited[F].get(k, 0) < self.cnt[k]:
                    self.eng[F].wait_ge(self.sem[k], self.cnt[k])
                    self.waited[F][k] = self.cnt[k]
        self.res = {}


def build(L, debug=False):
    nc = bass.Bass("TRN2", target_bir_lowering=False)

    def DT(name, shape, dt=F32, kind="ExternalInput"):
        return nc.dram_tensor(name, shape, dt, kind=kind).ap()

    x_in = DT("x", [SEQ, D])
    ctx_in = DT("ctx", [CTX, D])
    cond_in = DT("cond", [128, 8, 2])
    w_mod = DT("w_mod", [L, D, 6 * D])
    b_modT = DT("b_modT", [L, 128, 48])
    w_in = DT("w_in", [L, D, 1536])
    wbd_in = DT("wbd", [L, 2, 128, 128])
    pscT_in = DT("pscT", [L, 128, 2])
    band_in = DT("band", [128, 4, 5, 128])
    sgug_in = DT("sgu_g", [L, 256])
    sguwT_in = DT("sgu_wT", [L, 128, 4, 128])
    sgubT_in = DT("sgu_bT", [L, 128, 4])
    qkg_in = DT("qkg", [L, 640])
    w_out = DT("w_out", [L, D, D])
    ln_in = DT("ln", [L, 4, D])
    wr_in = DT("w_router", [L, 128, 8, NE])
    w1 = DT("w1", [L, NE, D, D])
    w3 = DT("w3", [L, NE, D, D])
    w2 = DT("w2", [L, NE, D, D])
    cs_in = DT("cs", [128, NTL, 64])
    out = DT("out", [SEQ, D], kind="ExternalOutput")
    ctx_out = DT("ctx_out", [CTX, D], kind="ExternalOutput")
    if debug:
        dbg_x1 = DT("dbg_x1", [NTOK, D], kind="ExternalOutput")
        dbg_aff = DT("dbg_aff", [128, NT, NE], kind="ExternalOutput")
        dbg_mix = DT("dbg_mix", [128, 4, NTOK], BF16, kind="ExternalOutput")
        dbg_idx = DT("dbg_idx", [128, NE, 5], I32, kind="ExternalOutput")
        dbg_gate = DT("dbg_gate", [128, NE, 5], kind="ExternalOutput")
        dbg_macc = DT("dbg_macc", [NTOK, D], kind="ExternalOutput")
        dbg_pos = DT("dbg_pos", [128, NT, NE], kind="ExternalOutput")
    xs = DT("xs", [NTOK, D], kind="Internal")
    macc = DT("macc", [NTOK, D], kind="Internal")
    xh2 = DT("xh2", [NTOK, D], BF16, kind="Internal")

    def src_tile(ll, j):
        if ll == 0:
            return x_in[j * 128:(j + 1) * 128, :] if j < NTL else ctx_in[(j - NTL) * 128:(j - NTL + 1) * 128, :]
        return xs[j * 128:(j + 1) * 128, :]

    def dst_tile(ll, j):
        if ll == L - 1:
            return out[j * 128:(j + 1) * 128, :] if j < NTL else ctx_out[(j - NTL) * 128:(j - NTL + 1) * 128, :]
        return xs[j * 128:(j + 1) * 128, :]

    def src_key(ll, j):
        return ("xin", j) if ll == 0 else ("xs", j)

    def dst_key(ll, j):
        return ("xout", j) if ll == L - 1 else ("xs", j)

    es0 = ExitStack()
    with es0:
        S = Sched(nc, es0)
        bcreg = es0.enter_context(nc.gpsimd.register("bcreg"))
        nc.gpsimd.reg_mov(bcreg, NTOK - 1)

        uid = [0]

        def SB(es, name, shape, dt):
            uid[0] += 1
            return es.enter_context(nc.sbuf_tensor("s%d_%s" % (uid[0], name), shape, dt))

        def PS(es, name, shape, dt):
            uid[0] += 1
            return es.enter_context(nc.psum_tensor("p%d_%s" % (uid[0], name), shape, dt))

        identf = SB(es0, "identf", [128, 128], F32)
        identb = SB(es0, "identb", [128, 128], BF16)
        ones_f = SB(es0, "ones_f", [128, 128], F32)
        lmat = SB(es0, "lmat", [128, 128], F32)
        iota_row = SB(es0, "iota_row", [128, 512], mybir.dt.float16)
        tokid = SB(es0, "tokid", [128, NT], F32)
        jmp = SB(es0, "jmp", [128, 128], F32)
        modT = SB(es0, "modT", [128, 48, 2], F32)
        aff_all = SB(es0, "aff_all", [128, NT, NE], F32)
        epsc = SB(es0, "epsc", [128, 1], F32)
        S.op("pool", lambda e: e.memset(epsc[:], EPS), writes=["epsc"])
        S.op("pool", lambda e: e.iota(jmp[:], [[1, 128]], base=0, channel_multiplier=-1,
                                      allow_small_or_imprecise_dtypes=True), writes=["jmp"])
        S.op("pool", lambda e: e.tensor_single_scalar(out=identf[:], in_=jmp[:], scalar=0.0, op=ALU.is_equal),
             reads=["jmp"], writes=["identf"])
        S.op("pool", lambda e: e.tensor_single_scalar(out=identb[:], in_=jmp[:], scalar=0.0, op=ALU.is_equal),
             reads=["jmp"], writes=["identb"])
        S.op("pool", lambda e: e.tensor_single_scalar(out=lmat[:], in_=jmp[:], scalar=0.0, op=ALU.is_gt),
             reads=["jmp"], writes=["lmat"])
        S.op("pool", lambda e: e.memset(ones_f[:], 1.0), writes=["ones_f"])
        S.op("pool", lambda e: e.iota(iota_row[:], [[1, 512]], base=0, channel_multiplier=0,
                                      allow_small_or_imprecise_dtypes=True), writes=["iota_row"])
        S.op("pool", lambda e: e.iota(tokid[:], [[128, NT]], base=0, channel_multiplier=1,
                                      allow_small_or_imprecise_dtypes=True), writes=["tokid"])
        tokA = SB(es0, "tokA", [128, NT], F32)
        tokB = SB(es0, "tokB", [128, 1], F32)
        S.op("pool", lambda e: e.iota(tokA[:], [[128, NT]], base=0, channel_multiplier=0,
                                      allow_small_or_imprecise_dtypes=True), writes=["tokA"])
        S.op("pool", lambda e: e.iota(tokB[:], [[0, 1]], base=0, channel_multiplier=1,
                                      allow_small_or_imprecise_dtypes=True), writes=["tokB"])

        def ln_stats(es_tiles, src, key_src, tag):
            st, mv, rstd = es_tiles
            S.op("dve", lambda e: e.bn_stats(st[:, 0, :], src[:, 0:512]), reads=[key_src], writes=[tag + "st0"])
            S.op("dve", lambda e: e.bn_stats(st[:, 1, :], src[:, 512:1024]), reads=[key_src], writes=[tag + "st1"])
            S.op("dve", lambda e: e.bn_aggr(mv[:], st[:]), reads=[tag + "st0", tag + "st1"], writes=[tag + "mv"])
            S.op("act", lambda e: e.activation(out=rstd[:], in_=mv[:, 1:2], func=AF.Sqrt, bias=epsc[:], scale=1.0),
                 reads=[tag + "mv", "epsc"], writes=[tag + "rstd"])
            S.op("dve", lambda e: e.reciprocal(rstd[:], rstd[:]), reads=[tag + "rstd"], writes=[tag + "rstd"])

        for ll in range(L):
            esA = ExitStack()
            with esA:
                condT = SB(esA, "condT", [128, 8, 2], F32)
                bmT = SB(esA, "bmT", [128, 48], F32)
                wblk = [SB(esA, "wblk%d" % i, [128, 8, 512], F32) for i in range(2)]
                pmod = PS(esA, "pmod", [128, 48, 2], F32)
                S.dma("sp", lambda e: e.dma_start(out=condT[:], in_=cond_in), writes=["condT"])
                S.dma("sp", lambda e: e.dma_start(out=bmT[:], in_=b_modT[ll]), writes=["bmT"])
                S.op("act", lambda e: e.activation(out=condT[:], in_=condT[:], func=AF.Silu),
                     reads=["condT"], writes=["condT"])
                wm = w_mod[ll].rearrange("(k p) c -> p k c", p=128)
                for cb in range(12):
                    wb_ = wblk[cb % 2]
                    wk = "wblk%d" % (cb % 2)
                    S.dma("sp", lambda e: e.dma_start(out=wb_[:], in_=wm[:, :, cb * 512:(cb + 1) * 512]), writes=[wk])
                    for sub in range(4):
                        j = cb * 4 + sub
                        for k in range(8):
                            S.op("pe", lambda e: e.matmul(pmod[:, j, :], lhsT=wb_[:, k, sub * 128:(sub + 1) * 128],
                                                          rhs=condT[:, k, :], start=(k == 0), stop=(k == 7)),
                                 reads=[wk, "condT"], writes=["pmod"])
                S.op("dve", lambda e: e.tensor_tensor(out=modT[:], in0=pmod[:],
                                                      in1=bmT[:].unsqueeze(2).broadcast_to([128, 48, 2]), op=ALU.add),
                     reads=["pmod", "bmT"], writes=["modT"])
                for base in (8, 32):
                    S.op("dve", lambda e: e.tensor_scalar_add(modT[:, base:base + 8, :], modT[:, base:base + 8, :], 1.0),
                         reads=["modT"], writes=["modT"])
            S.barrier()

            def build_gate_rows(es, vbase, tagname):
                rows = [SB(es, "%s_%d" % (tagname, s), [128, D], F32) for s in range(2)]
                est = ExitStack()
                with est:
                  diag = [SB(est, "%s_dg%d" % (tagname, i), [128, 128], F32) for i in range(2)]
                  pg = PS(est, tagname + "_pg", [128, D], F32)
                  n = 0
                  for s in range(2):
                    for dc in range(8):
                        dg = diag[n % 2]
                        dk = "%s_dg%d" % (tagname, n % 2)
                        n += 1
                        S.op("dve", lambda e: e.tensor_scalar(out=dg[:], in0=identf[:], scalar1=modT[:, vbase + dc, s:s + 1],
                                                              scalar2=None, op0=ALU.mult),
                             reads=["identf", "modT"], writes=[dk])
                        S.op("pe", lambda e: e.matmul(pg[:, dc * 128:(dc + 1) * 128], lhsT=ones_f[:], rhs=dg[:],
                                                      start=True, stop=True),
                             reads=[dk, "ones_f"], writes=[tagname + "_pg"])
                    S.op("act", lambda e: e.activation(out=rows[s][:], in_=pg[:], func=AF.Copy),
                         reads=[tagname + "_pg"], writes=["%s_%d" % (tagname, s)])
                  S.barrier()
                return rows

            esBE = ExitStack()
            with esBE:
                mixps = SB(esBE, "mixps", [128, 4, NTOK], BF16)
                qT_all = SB(esBE, "qT_all", [128, 4, NTOK], BF16)
                kT_all = SB(esBE, "kT_all", [128, 2, NTOK], BF16)
                S.op("pool", lambda e: e.memset(kT_all[:], 0.0), writes=["kT_zero"])
                v_all = SB(esBE, "v_all", [128, NT, 2, 65], BF16)
                S.op("pool", lambda e: e.memset(v_all[:, :, :, 64:65], 1.0), writes=["v_ones"])
                esBC = ExitStack()
                with esBC:
                    p_all = SB(esBC, "p_all", [128, NT, 256], BF16)
                    esB = ExitStack()
                    with esB:
                        winb = SB(esB, "winb", [128, 8, 1536], BF16)
                        cs = SB(esB, "cs", [128, NTL, 64], F32)
                        sgug = SB(esB, "sgug", [128, 256], F32)
                        wsT = SB(esB, "wsT", [128, 4, 128], BF16)
                        sbT = SB(esB, "sbT", [128, 4], F32)
                        qkg = SB(esB, "qkg", [128, 640], F32)
                        xt = [SB(esB, "xt%d" % i, [128, D], F32) for i in range(2)]
                        st = SB(esB, "st", [128, 2, 6], F32)
                        mv = SB(esB, "mv", [128, 2], F32)
                        rstd = SB(esB, "rstd", [128, 1], F32)
                        xb = SB(esB, "xb", [128, D], BF16)
                        hT = SB(esB, "hT", [128, 8, 128], BF16)
                        gu = SB(esB, "gu", [128, 256], BF16)
                        gv = SB(esB, "gv", [128, 256], F32)
                        stv = SB(esB, "stv", [128, 4, 6], F32)
                        mvv = SB(esB, "mvv", [128, 4, 2], F32)
                        rsv = SB(esB, "rsv", [128, 4], F32)
                        vn = SB(esB, "vn", [128, 256], F32)
                        vh = SB(esB, "vh", [128, 256], BF16)
                        sgo = SB(esB, "sgo", [128, 256], BF16)
                        sq = SB(esB, "sq", [128, 640], F32)
                        ss = SB(esB, "ss", [128, 10], F32)
                        qn = SB(esB, "qn", [128, 10, 64], F32)
                        ta = SB(esB, "ta", [128, 10, 32], F32)
                        tb = SB(esB, "tb", [128, 10, 32], F32)
                        qr = SB(esB, "qr", [128, 640], BF16)
                        pT = PS(esB, "pT", [128, 8, 128], BF16)
                        pz = PS(esB, "pz", [128, 1536], F32)
                        psg = PS(esB, "psg", [128, 256], F32)
                        ptr = PS(esB, "ptr", [128, 7, 128], BF16)
                        S.dma("pool", lambda e: e.dma_start(out=winb[:], in_=w_in[ll].rearrange("(k p) c -> p k c", p=128)),
                              writes=["winb"])
                        S.dma("pool", lambda e: e.dma_start(out=wsT[:], in_=sguwT_in[ll]), writes=["wsT"])
                        S.dma("sp", lambda e: e.dma_start(out=cs[:], in_=cs_in), writes=["cs"])
                        S.dma("sp", lambda e: e.dma_start(out=sgug[:], in_=sgug_in[ll:ll + 1, :].broadcast_to([128, 256])),
                              writes=["sgug"])
                        S.dma("sp", lambda e: e.dma_start(out=qkg[:], in_=qkg_in[ll:ll + 1, :].broadcast_to([128, 640])),
                              writes=["qkg"])
                        S.dma("sp", lambda e: e.dma_start(out=sbT[:], in_=sgubT_in[ll]), writes=["sbT"])
                        def front_B(j):
                            s = 0 if j < NTL else 1
                            tok = slice(j * 128, (j + 1) * 128)
                            xtj = xt[j % 2]
                            xk = "xt%d" % (j % 2)
                            S.dma("sp", lambda e: e.dma_start(out=xtj[:], in_=src_tile(ll, j)),
                                  reads=[src_key(ll, j)], writes=[xk])
                            ln_stats((st, mv, rstd), xtj, xk, "B")
                            S.op("dve", lambda e: e.tensor_scalar(out=xb[:], in0=xtj[:], scalar1=mv[:, 0:1], scalar2=rstd[:],
                                                                  op0=ALU.subtract, op1=ALU.mult),
                                 reads=[xk, "Bmv", "Brstd"], writes=["xb"])
                            for k in range(8):
                                S.op("pe", lambda e: e.transpose(pT[:, k, :], xb[:, k * 128:(k + 1) * 128], identb[:]),
                                     reads=["xb", "identb"], writes=["pT"])
                            for k in range(8):
                                S.op("act", lambda e: e.activation(out=hT[:, k, :], in_=pT[:, k, :], func=AF.Identity,
                                                                   bias=modT[:, 0 + k, s:s + 1], scale=modT[:, 8 + k, s:s + 1]),
                                     reads=["pT", "modT"], writes=[("hT", k)])
                        gu2 = [gu, SB(esB, "gu_b", [128, 256], BF16)]
                        gv2 = [gv, SB(esB, "gv_b", [128, 256], F32)]
                        stv2 = [stv, SB(esB, "stv_b", [128, 4, 6], F32)]
                        mvv2 = [mvv, SB(esB, "mvv_b", [128, 4, 2], F32)]
                        rsv2 = [rsv, SB(esB, "rsv_b", [128, 4], F32)]
                        vn2 = [vn, SB(esB, "vn_b", [128, 256], F32)]
                        vh2 = [vh, SB(esB, "vh_b", [128, 256], BF16)]
                        sgo2 = [sgo, SB(esB, "sgo_b", [128, 256], BF16)]
                        sq2 = [sq, SB(esB, "sq_b", [128, 640], F32)]
                        zq2 = [SB(esB, "zq_a", [128, 640], F32), SB(esB, "zq_b", [128, 640], F32)]
                        ss2 = [ss, SB(esB, "ss_b", [128, 10], F32)]
                        qn2 = [qn, SB(esB, "qn_b", [128, 10, 64], F32)]
                        ta2 = [ta, SB(esB, "ta_b", [128, 10, 32], F32)]
                        tb2 = [tb, SB(esB, "tb_b", [128, 10, 32], F32)]
                        qr2 = [qr, SB(esB, "qr_b", [128, 640], BF16)]
                        psg2 = [psg, PS(esB, "psg_b", [128, 256], F32)]
                        ptr2 = [ptr, PS(esB, "ptr_b", [128, 7, 128], BF16)]

                        def mm_and_evac(j, t):
                            for cblk in range(3):
                                for k in range(8):
                                    S.op("pe", lambda e: e.matmul(pz[:, cblk * 512:(cblk + 1) * 512], lhsT=hT[:, k, :],
                                                                  rhs=winb[:, k, cblk * 512:(cblk + 1) * 512],
                                                                  start=(k == 0), stop=(k == 7)),
                                         reads=[("hT", k), "winb"], writes=[("pz", cblk)])
                            if j + 1 < NT:
                                front_B(j + 1)
                            S.op("act", lambda e: e.activation(out=p_all[:, j, :], in_=pz[:, 0:256], func=AF.Copy),
                                 reads=[("pz", 0)], writes=[("p_all", j)])
                            S.op("act", lambda e: e.activation(out=gu2[t][:], in_=pz[:, 256:512], func=AF.Gelu_apprx_tanh),
                                 reads=[("pz", 0)], writes=[("gu", t)])
                            S.op("act", lambda e: e.activation(out=gv2[t][:], in_=pz[:, 512:768], func=AF.Gelu_apprx_tanh),
                                 reads=[("pz", 1)], writes=[("gv", t)])
                            S.op("act", lambda e: e.activation(out=sq2[t][:], in_=pz[:, 768:1408], func=AF.Square),
                                 reads=[("pz", 1), ("pz", 2)], writes=[("sq", t)])
                            S.op("dve", lambda e: e.tensor_copy(out=zq2[t][:], in_=pz[:, 768:1408]),
                                 reads=[("pz", 1), ("pz", 2)], writes=[("zq", t)])
                            S.op("dve", lambda e: e.tensor_copy(out=v_all[:, j, :, 0:64],
                                                                in_=pz[:, 1408:1536].rearrange("p (k d) -> p k d", d=64)),
                                 reads=[("pz", 2)], writes=[("v", j)])

                        front_B(0)
                        for j0 in range(0, NTL, 2):
                            pair = [(j0, 0), (j0 + 1, 1)]
                            for (j, t) in pair:
                                mm_and_evac(j, t)
                            for h in range(4):
                                for (j, t) in pair:
                                    S.op("dve", lambda e: e.bn_stats(stv2[t][:, h, :], gv2[t][:, h * 64:(h + 1) * 64]),
                                         reads=[("gv", t)], writes=[("stv", t, h)])
                                for (j, t) in pair:
                                    S.op("dve", lambda e: e.bn_aggr(mvv2[t][:, h, :], stv2[t][:, h, :]),
                                         reads=[("stv", t, h)], writes=[("mvv", t)])
                            for (j, t) in pair:
                                S.op("act", lambda e: e.activation(out=rsv2[t][:], in_=mvv2[t][:, :, 1], func=AF.Sqrt, bias=epsc[:], scale=1.0),
                                     reads=[("mvv", t), "epsc"], writes=[("rsv", t)])
                            for (j, t) in pair:
                                S.op("dve", lambda e: e.tensor_reduce(out=ss2[t][:], in_=sq2[t][:].rearrange("p (h d) -> p h d", d=64),
                                                                      axis=AX.X, op=ALU.add),
                                     reads=[("sq", t)], writes=[("ss", t)])
                            for (j, t) in pair:
                                S.op("act", lambda e: e.activation(out=ss2[t][:], in_=ss2[t][:], func=AF.Sqrt, bias=epsc[:], scale=1.0 / 64),
                                     reads=[("ss", t), "epsc"], writes=[("ss", t)])
                            for (j, t) in pair:
                                S.op("dve", lambda e: e.reciprocal(rsv2[t][:], rsv2[t][:]), reads=[("rsv", t)], writes=[("rsv", t)])
                            for h in range(4):
                                for (j, t) in pair:
                                    S.op("dve", lambda e: e.tensor_scalar(out=vn2[t][:, h * 64:(h + 1) * 64], in0=gv2[t][:, h * 64:(h + 1) * 64],
                                                                          scalar1=mvv2[t][:, h, 0:1], scalar2=rsv2[t][:, h:h + 1],
                                                                          op0=ALU.subtract, op1=ALU.mult),
                                         reads=[("gv", t), ("mvv", t), ("rsv", t)], writes=[("vn", t)])
                            for (j, t) in pair:
                                S.op("dve", lambda e: e.tensor_tensor(out=vh2[t][:], in0=vn2[t][:], in1=sgug[:], op=ALU.mult),
                                     reads=[("vn", t), "sgug"], writes=[("vh", t)])
                            for (j, t) in pair:
                                for h in range(4):
                                    S.op("pe", lambda e: e.matmul(psg2[t][:, h * 64:(h + 1) * 64], lhsT=wsT[:, h, :],
                                                                  rhs=vh2[t][:, h * 64:(h + 1) * 64], start=True, stop=True),
                                         reads=["wsT", ("vh", t)], writes=[("psg", t)])
                            for (j, t) in pair:
                                S.op("dve", lambda e: e.reciprocal(ss2[t][:], ss2[t][:]), reads=[("ss", t)], writes=[("ss", t)])
                            for (j, t) in pair:
                                S.op("dve", lambda e: e.tensor_tensor(out=qn2[t][:], in0=zq2[t][:].rearrange("p (h d) -> p h d", d=64),
                                                                      in1=ss2[t][:].unsqueeze(2).broadcast_to([128, 10, 64]), op=ALU.mult),
                                     reads=[("zq", t), ("ss", t)], writes=[("qn", t)])
                            for (j, t) in pair:
                                S.op("dve", lambda e: e.tensor_tensor(out=qn2[t][:], in0=qn2[t][:],
                                                                      in1=qkg[:].rearrange("p (h d) -> p h d", d=64), op=ALU.mult),
                                     reads=[("qn", t), "qkg"], writes=[("qn", t)])
                            for h in range(4):
                                for (j, t) in pair:
                                    S.op("dve", lambda e: e.scalar_tensor_tensor(out=sgo2[t][:, h * 64:(h + 1) * 64],
                                                                                 in0=psg2[t][:, h * 64:(h + 1) * 64],
                                                                                 scalar=sbT[:, h:h + 1],
                                                                                 in1=gu2[t][:, h * 64:(h + 1) * 64],
                                                                                 op0=ALU.add, op1=ALU.mult),
                                         reads=[("psg", t), "sbT", ("gu", t)], writes=[("sgo", t)])
                            for (j, t) in pair:
                                for c in range(2):
                                    S.op("pe", lambda e: e.transpose(ptr2[t][:, c, :], sgo2[t][:, c * 128:(c + 1) * 128], identb[:]),
                                         reads=[("sgo", t), "identb"], writes=[("ptr", t)])
                            for (j, t) in pair:
                                tok = slice(j * 128, (j + 1) * 128)
                                S.op("act", lambda e: e.activation(out=mixps[:, 2:4, tok], in_=ptr2[t][:, 0:2, :], func=AF.Copy),
                                     reads=[("ptr", t)], writes=[("mixps_s", j)])
                            def dsts(t):
                                qdst = qr2[t][:, 0:512].rearrange("p (g k d) -> p k g d", g=4, k=2, d=64)
                                kdst = qr2[t][:, 512:640].rearrange("p (k d) -> p k d", d=64)
                                return qdst, kdst
                            if j0 < NTL:
                                for half_, op_ in ((0, ALU.subtract), (1, ALU.add)):
                                    for (j, t) in pair:
                                        cosb = cs[:, j:j + 1, 0:32].broadcast_to([128, 10, 32])
                                        xa_ = qn2[t][:, :, 0:32] if half_ == 0 else qn2[t][:, :, 32:64]
                                        S.op("dve", lambda e: e.tensor_tensor(out=ta2[t][:], in0=xa_, in1=cosb, op=ALU.mult),
                                             reads=[("qn", t), "cs"], writes=[("ta", t)])
                                    for (j, t) in pair:
                                        sinb = cs[:, j:j + 1, 32:64].broadcast_to([128, 10, 32])
                                        xb_ = qn2[t][:, :, 32:64] if half_ == 0 else qn2[t][:, :, 0:32]
                                        S.op("dve", lambda e: e.tensor_tensor(out=tb2[t][:], in0=xb_, in1=sinb, op=ALU.mult),
                                             reads=[("qn", t), "cs"], writes=[("tb", t)])
                                    for (j, t) in pair:
                                        qdst, kdst = dsts(t)
                                        dsl = slice(0, 32) if half_ == 0 else slice(32, 64)
                                        S.op("dve", lambda e: e.tensor_tensor(out=qdst[:, :, :, dsl],
                                                                              in0=ta2[t][:, 0:8, :].rearrange("p (k g) d -> p k g d", k=2),
                                                                              in1=tb2[t][:, 0:8, :].rearrange("p (k g) d -> p k g d", k=2),
                                                                              op=op_),
                                             reads=[("ta", t), ("tb", t)], writes=[("qr", t, half_, 0)])
                                        S.op("dve", lambda e: e.tensor_tensor(out=kdst[:, :, dsl], in0=ta2[t][:, 8:10, :], in1=tb2[t][:, 8:10, :],
                                                                              op=op_),
                                             reads=[("ta", t), ("tb", t)], writes=[("qr", t, half_, 1)])
                            else:
                                for (j, t) in pair:
                                    qdst, kdst = dsts(t)
                                    S.op("dve", lambda e: e.tensor_copy(out=qdst, in_=qn2[t][:, 0:8, :].rearrange("p (k g) d -> p k g d", k=2)),
                                         reads=[("qn", t)], writes=[("qr", t, 0, 0), ("qr", t, 1, 0)])
                                    S.op("dve", lambda e: e.tensor_copy(out=kdst, in_=qn2[t][:, 8:10, :]),
                                         reads=[("qn", t)], writes=[("qr", t, 0, 1), ("qr", t, 1, 1)])
                            for (j, t) in pair:
                                for c in range(5):
                                    S.op("pe", lambda e: e.transpose(ptr2[t][:, 2 + c, :], qr2[t][:, c * 128:(c + 1) * 128], identb[:]),
                                         reads=[("qr", t, 0, 0), ("qr", t, 0, 1), ("qr", t, 1, 0), ("qr", t, 1, 1), "identb"],
                                         writes=[("ptr", t)])
                            for (j, t) in pair:
                                tok = slice(j * 128, (j + 1) * 128)
                                S.op("act", lambda e: e.activation(out=qT_all[:, :, tok], in_=ptr2[t][:, 2:6, :], func=AF.Copy),
                                     reads=[("ptr", t)], writes=[("qT", j)])
                                for kv_ in range(2):
                                    S.op("act", lambda e: e.activation(out=kT_all[kv_ * 64:(kv_ + 1) * 64, kv_, tok],
                                                                       in_=ptr2[t][kv_ * 64:(kv_ + 1) * 64, 6, :], func=AF.Copy),
                                         reads=[("ptr", t), "kT_zero"], writes=[("kT", j, kv_)])
                        S.barrier()
                        for j in range(NTL, NT):
                            s = 0 if j < NTL else 1
                            tok = slice(j * 128, (j + 1) * 128)
                            for cblk in range(3):
                                for k in range(8):
                                    S.op("pe", lambda e: e.matmul(pz[:, cblk * 512:(cblk + 1) * 512], lhsT=hT[:, k, :],
                                                                  rhs=winb[:, k, cblk * 512:(cblk + 1) * 512],
                                                                  start=(k == 0), stop=(k == 7)),
                                         reads=[("hT", k), "winb"], writes=[("pz", cblk)])
                            if j + 1 < NT:
                                front_B(j + 1)
                            S.op("act", lambda e: e.activation(out=p_all[:, j, :], in_=pz[:, 0:256], func=AF.Copy),
                                 reads=[("pz", 0)], writes=[("p_all", j)])
                            S.op("act", lambda e: e.activation(out=gu[:], in_=pz[:, 256:512], func=AF.Gelu_apprx_tanh),
                                 reads=[("pz", 0)], writes=["gu"])
                            S.op("act", lambda e: e.activation(out=gv[:], in_=pz[:, 512:768], func=AF.Gelu_apprx_tanh),
                                 reads=[("pz", 1)], writes=["gv"])
                            for h in range(4):
                                S.op("dve", lambda e: e.bn_stats(stv[:, h, :], gv[:, h * 64:(h + 1) * 64]),
                                     reads=["gv"], writes=[("stv", h)])
                                S.op("dve", lambda e: e.bn_aggr(mvv[:, h, :], stv[:, h, :]),
                                     reads=[("stv", h)], writes=["mvv"])
                            S.op("act", lambda e: e.activation(out=rsv[:], in_=mvv[:, :, 1], func=AF.Sqrt, bias=epsc[:], scale=1.0),
                                 reads=["mvv", "epsc"], writes=["rsv"])
                            S.op("dve", lambda e: e.reciprocal(rsv[:], rsv[:]), reads=["rsv"], writes=["rsv"])
                            for h in range(4):
                                S.op("dve", lambda e: e.tensor_scalar(out=vn[:, h * 64:(h + 1) * 64], in0=gv[:, h * 64:(h + 1) * 64],
                                                                      scalar1=mvv[:, h, 0:1], scalar2=rsv[:, h:h + 1],
                                                                      op0=ALU.subtract, op1=ALU.mult),
                                     reads=["gv", "mvv", "rsv"], writes=["vn"])
                            S.op("dve", lambda e: e.tensor_tensor(out=vh[:], in0=vn[:], in1=sgug[:], op=ALU.mult),
                                 reads=["vn", "sgug"], writes=["vh"])
                            for h in range(4):
                                S.op("pe", lambda e: e.matmul(psg[:, h * 64:(h + 1) * 64], lhsT=wsT[:, h, :],
                                                              rhs=vh[:, h * 64:(h + 1) * 64], start=True, stop=True),
                                     reads=["wsT", "vh"], writes=["psg"])
                            for h in range(4):
                                S.op("dve", lambda e: e.scalar_tensor_tensor(out=sgo[:, h * 64:(h + 1) * 64],
                                                                             in0=psg[:, h * 64:(h + 1) * 64],
                                                                             scalar=sbT[:, h:h + 1],
                                                                             in1=gu[:, h * 64:(h + 1) * 64],
                                                                             op0=ALU.add, op1=ALU.mult),
                                     reads=["psg", "sbT", "gu"], writes=["sgo"])
                            for c in range(2):
                                S.op("pe", lambda e: e.transpose(ptr[:, c, :], sgo[:, c * 128:(c + 1) * 128], identb[:]),
                                     reads=["sgo", "identb"], writes=[("ptr", 0)])
                            S.op("act", lambda e: e.activation(out=mixps[:, 2:4, tok], in_=ptr[:, 0:2, :], func=AF.Copy),
                                 reads=[("ptr", 0)], writes=[("mixps_s", j)])
                            S.op("act", lambda e: e.activation(out=sq[:], in_=pz[:, 768:1408], func=AF.Square),
                                 reads=[("pz", 1), ("pz", 2)], writes=["sq"])
                            S.op("dve", lambda e: e.tensor_reduce(out=ss[:], in_=sq[:].rearrange("p (h d) -> p h d", d=64),
                                                                  axis=AX.X, op=ALU.add),
                                 reads=["sq"], writes=["ss"])
                            S.op("act", lambda e: e.activation(out=ss[:], in_=ss[:], func=AF.Sqrt, bias=epsc[:], scale=1.0 / 64),
                                 reads=["ss", "epsc"], writes=["ss"])
                            S.op("dve", lambda e: e.reciprocal(ss[:], ss[:]), reads=["ss"], writes=["ss"])
                            S.op("dve", lambda e: e.tensor_tensor(out=qn[:], in0=pz[:, 768:1408].rearrange("p (h d) -> p h d", d=64),
                                                                  in1=ss[:].unsqueeze(2).broadcast_to([128, 10, 64]), op=ALU.mult),
                                 reads=[("pz", 1), ("pz", 2), "ss"], writes=["qn"])
                            S.op("dve", lambda e: e.tensor_tensor(out=qn[:], in0=qn[:],
                                                                  in1=qkg[:].rearrange("p (h d) -> p h d", d=64), op=ALU.mult),
                                 reads=["qn", "qkg"], writes=["qn"])
                            qdst = qr[:, 0:512].rearrange("p (g k d) -> p k g d", g=4, k=2, d=64)
                            kdst = qr[:, 512:640].rearrange("p (k d) -> p k d", d=64)
                            qsrc = qn[:, 0:8, :].rearrange("p (k g) d -> p k g d", k=2)
                            ksrc = qn[:, 8:10, :]
                            if j < NTL:
                                cosb = cs[:, j:j + 1, 0:32].broadcast_to([128, 10, 32])
                                sinb = cs[:, j:j + 1, 32:64].broadcast_to([128, 10, 32])
                                x1 = qn[:, :, 0:32]
                                x2 = qn[:, :, 32:64]
                                S.op("dve", lambda e: e.tensor_tensor(out=ta[:], in0=x1, in1=cosb, op=ALU.mult),
                                     reads=["qn", "cs"], writes=["ta"])
                                S.op("dve", lambda e: e.tensor_tensor(out=tb[:], in0=x2, in1=sinb, op=ALU.mult),
                                     reads=["qn", "cs"], writes=["tb"])
                                S.op("dve", lambda e: e.tensor_tensor(out=qdst[:, :, :, 0:32],
                                                                      in0=ta[:, 0:8, :].rearrange("p (k g) d -> p k g d", k=2),
                                                                      in1=tb[:, 0:8, :].rearrange("p (k g) d -> p k g d", k=2),
                                                                      op=ALU.subtract),
                                     reads=["ta", "tb"], writes=["qr_a"])
                                S.op("dve", lambda e: e.tensor_tensor(out=kdst[:, :, 0:32], in0=ta[:, 8:10, :], in1=tb[:, 8:10, :],
                                                                      op=ALU.subtract),
                                     reads=["ta", "tb"], writes=["qr_b"])
                                S.op("dve", lambda e: e.tensor_tensor(out=ta[:], in0=x2, in1=cosb, op=ALU.mult),
                                     reads=["qn", "cs"], writes=["ta"])
                                S.op("dve", lambda e: e.tensor_tensor(out=tb[:], in0=x1, in1=sinb, op=ALU.mult),
                                     reads=["qn", "cs"], writes=["tb"])
                                S.op("dve", lambda e: e.tensor_tensor(out=qdst[:, :, :, 32:64],
                                                                      in0=ta[:, 0:8, :].rearrange("p (k g) d -> p k g d", k=2),
                                                                      in1=tb[:, 0:8, :].rearrange("p (k g) d -> p k g d", k=2),
                                                                      op=ALU.add),
                                     reads=["ta", "tb"], writes=["qr_c"])
                                S.op("dve", lambda e: e.tensor_tensor(out=kdst[:, :, 32:64], in0=ta[:, 8:10, :], in1=tb[:, 8:10, :],
                                                                      op=ALU.add),
                                     reads=["ta", "tb"], writes=["qr_d"])
                            else:
                                S.op("dve", lambda e: e.tensor_copy(out=qdst, in_=qsrc), reads=["qn"], writes=["qr_a", "qr_c"])
                                S.op("dve", lambda e: e.tensor_copy(out=kdst, in_=ksrc), reads=["qn"], writes=["qr_b", "qr_d"])
                            for c in range(5):
                                S.op("pe", lambda e: e.transpose(ptr[:, 2 + c, :], qr[:, c * 128:(c + 1) * 128], identb[:]),
                                     reads=["qr_a", "qr_b", "qr_c", "qr_d", "identb"], writes=[("ptr", 1)])
                            S.op("act", lambda e: e.activation(out=qT_all[:, :, tok], in_=ptr[:, 2:6, :], func=AF.Copy),
                                 reads=[("ptr", 1)], writes=[("qT", j)])
                            for kv_ in range(2):
                                S.op("act", lambda e: e.activation(out=kT_all[kv_ * 64:(kv_ + 1) * 64, kv_, tok],
                                                                   in_=ptr[kv_ * 64:(kv_ + 1) * 64, 6, :], func=AF.Copy),
                                     reads=[("ptr", 1), "kT_zero"], writes=[("kT", j, kv_)])
                            S.op("dve", lambda e: e.tensor_copy(out=v_all[:, j, :, 0:64],
                                                                in_=pz[:, 1408:1536].rearrange("p (k d) -> p k d", d=64)),
                                 reads=[("pz", 2)], writes=[("v", j)])
                    S.barrier()
                    esC = ExitStack()
                    with esC:
                        bandf = SB(esC, "bandf", [128, 4, 5, 128], F32)
                        band = SB(esC, "band", [128, 4, 5, 128], BF16)
                        wbd = SB(esC, "wbd", [128, 2, 128], BF16)
                        pscT = SB(esC, "pscT", [128, 2], F32)
                        pooledT = SB(esC, "pooledT", [128, 2, 128], BF16)
                        ppool = PS(esC, "ppool", [128, 4, 128], F32)
                        pp2 = PS(esC, "pp2", [128, 2, 128], F32)
                        S.dma("sp", lambda e: e.dma_start(out=bandf[:], in_=band_in), writes=["bandf"])
                        S.op("dve", lambda e: e.tensor_copy(out=band[:], in_=bandf[:]), reads=["bandf"], writes=["band"])
                        S.dma("pool", lambda e: e.dma_start(out=wbd[:], in_=wbd_in[ll].rearrange("c p q -> p c q")), writes=["wbd"])
                        S.dma("sp", lambda e: e.dma_start(out=pscT[:], in_=pscT_in[ll]), writes=["pscT"])
                        for j in range(NT):
                            tok = slice(j * 128, (j + 1) * 128)
                            lo_t, hi_t = (0, NTL - 1) if j < NTL else (NTL, NT - 1)
                            rels = [r for r in (-1, 0, 1) if lo_t <= j + r <= hi_t]
                            for c in range(2):
                                for bi in range(2):
                                    g = 2 * c + bi
                                    for ri, r in enumerate(rels):
                                        if r == -1:
                                            v = 0
                                        elif r == 1:
                                            v = 4
                                        else:
                                            v = 2 if j == lo_t else (3 if j == hi_t else 1)
                                        S.op("pe", lambda e: e.matmul(ppool[:, c * 2 + bi, :], lhsT=p_all[:, j + r, c * 128:(c + 1) * 128],
                                                                      rhs=band[:, g, v, :], start=(ri == 0), stop=(ri == len(rels) - 1)),
                                             reads=["band"], writes=["ppool"])
                                S.op("act", lambda e: e.activation(out=pooledT[0:64, c, :], in_=ppool[0:64, c * 2, :], func=AF.Copy),
                                     reads=["ppool"], writes=[("pooledT", c, 0)])
                                S.op("act", lambda e: e.activation(out=pooledT[64:128, c, :], in_=ppool[64:128, c * 2 + 1, :], func=AF.Copy),
                                     reads=["ppool"], writes=[("pooledT", c, 1)])
                            for c in range(2):
                                S.op("pe", lambda e: e.matmul(pp2[:, c, :], lhsT=wbd[:, c, :], rhs=pooledT[:, c, :], start=True, stop=True),
                                     reads=["wbd", ("pooledT", c, 0), ("pooledT", c, 1)], writes=["pp2"])
                            for c in range(2):
                                S.op("act", lambda e: e.activation(out=mixps[:, c, tok], in_=pp2[:, c, :], func=AF.Copy,
                                                                   scale=pscT[:, c:c + 1]),
                                     reads=["pp2", "pscT"], writes=[("mixps_p", j)])
                    S.barrier()
                if debug and ll == 0:
                    S.dma("sp", lambda e: e.dma_start(out=dbg_mix, in_=mixps[:]), writes=["dbgmix"])
                esD = ExitStack()
                with esD:
                    woutb = SB(esD, "woutb", [128, 8, D], BF16)
                    lnr = SB(esD, "lnr", [128, 2, D], F32)
                    wr = SB(esD, "wr", [128, 8, NE], F32)
                    PT = [SB(esD, "PT%d" % i, [128, 512], BF16) for i in range(3)]
                    attn_tok = [SB(esD, "attn_tok%d" % i, [128, 4, 512], BF16) for i in range(2)]
                    rden = SB(esD, "rden", [128, 4], F32)
                    mixA = SB(esD, "mixA", [128, 4, 128], BF16)
                    bufA = SB(esD, "bufA", [128, D], F32)
                    bufB = SB(esD, "bufB", [128, D], F32)
                    xhb = SB(esD, "xhb", [128, D], BF16)
                    h2T = SB(esD, "h2T", [128, 8, 128], F32)
                    st = SB(esD, "stE", [128, 2, 6], F32)
                    mv = SB(esD, "mvE", [128, 2], F32)
                    rstd = SB(esD, "rstdE", [128, 1], F32)
                    mx = SB(esD, "mx", [128, 1], F32)
                    sm = SB(esD, "sm", [128, 1], F32)
                    ex = SB(esD, "ex", [128, NE], F32)
                    g1rows = build_gate_rows(esD, 16, "g1r")
                    pS = [PS(esD, "pS%d" % i, [128, 512], F32) for i in range(3)]
                    pO = [PS(esD, "pO%d" % i, [128, 4, 128], F32) for i in range(1)]
                    ptr = PS(esD, "ptrD", [128, 4, 128], BF16)
                    pbig = PS(esD, "pbig", [128, D], F32)
                    pr = PS(esD, "pr", [128, NE], F32)
                    S.dma("pool", lambda e: e.dma_start(out=woutb[:], in_=w_out[ll].rearrange("(k p) c -> p k c", p=128)),
                          writes=["woutb"])
                    S.dma("sp", lambda e: e.dma_start(out=lnr[:], in_=ln_in[ll:ll + 1, 0:2, :].broadcast_to([128, 2, D])),
                          writes=["lnr"])
                    S.dma("sp", lambda e: e.dma_start(out=wr[:], in_=wr_in[ll]), writes=["wr"])

                    def rstd_act(tag):
                        S.op("act", lambda e: e.activation(out=rstd[:], in_=mv[:, 1:2], func=AF.Ln, bias=epsc[:], scale=1.0),
                             reads=[tag + "mv", "epsc"], writes=[tag + "rstd"])
                        S.op("act", lambda e: e.activation(out=rstd[:], in_=rstd[:], func=AF.Exp, scale=-0.5),
                             reads=[tag + "rstd"], writes=[tag + "rstd"])

                    def stats_dve(src, key_src, tag):
                        S.op("dve", lambda e: e.bn_stats(st[:, 0, :], src[:, 0:512]), reads=[key_src], writes=[tag + "st0"])
                        S.op("dve", lambda e: e.bn_stats(st[:, 1, :], src[:, 512:1024]), reads=[key_src], writes=[tag + "st1"])
                        S.op("dve", lambda e: e.bn_aggr(mv[:], st[:]), reads=[tag + "st0", tag + "st1"], writes=[tag + "mv"])

                    def make_ef_stages(j, qs, par):
                        s = 0 if j < NTL else 1
                        tok = slice(j * 128, (j + 1) * 128)
                        at = attn_tok[par]

                        def st0():
                            for cc in range(4):
                                S.op("pe", lambda e: e.transpose(ptr[:, cc, :], at[:, qs, cc * 128:(cc + 1) * 128], identb[:]),
                                     reads=[("attn_tok", par, qs), "identb"], writes=["ptrD"])
                            S.dma("sp", lambda e: e.dma_start(out=bufA[:], in_=src_tile(ll, j)), reads=[src_key(ll, j)], writes=["bufA"])

                        def st1():
                            S.op("act", lambda e: e.activation(out=mixA[:], in_=ptr[:], func=AF.Copy), reads=["ptrD"], writes=["mixA"])

                        def st2():
                            for half in range(2):
                                for c8 in range(8):
                                    lhs = mixps[:, c8, tok] if c8 < 4 else mixA[:, c8 - 4, :]
                                    S.op("pe", lambda e: e.matmul(pbig[:, half * 512:(half + 1) * 512], lhsT=lhs,
                                                                  rhs=woutb[:, c8, half * 512:(half + 1) * 512],
                                                                  start=(c8 == 0), stop=(c8 == 7)),
                                         reads=["mixA", "woutb"], writes=["pbig"])

                        def st3():
                            S.op("dve", lambda e: e.tensor_tensor(out=bufB[:], in0=pbig[:], in1=g1rows[s][:], op=ALU.mult),
                                 reads=["pbig", "g1r_%d" % s], writes=["bufB"])
                            S.op("dve", lambda e: e.scalar_tensor_tensor(out=bufA[:], in0=bufA[:], scalar=ALPHA, in1=bufB[:],
                                                                         op0=ALU.mult, op1=ALU.add),
                                 reads=["bufA", "bufB"], writes=["bufA"])
                            stats_dve(bufA, "bufA", "E")

                        def st4():
                            rstd_act("E")

                        def st5():
                            S.op("dve", lambda e: e.tensor_scalar(out=bufB[:], in0=bufA[:], scalar1=mv[:, 0:1], scalar2=rstd[:],
                                                                  op0=ALU.subtract, op1=ALU.mult),
                                 reads=["bufA", "Emv", "Erstd"], writes=["bufB"])
                            S.op("dve", lambda e: e.tensor_tensor(out=bufB[:], in0=bufB[:], in1=lnr[:, 0, :], op=ALU.mult),
                                 reads=["bufB", "lnr"], writes=["bufB"])
                            S.op("dve", lambda e: e.tensor_tensor(out=bufB[:], in0=bufB[:], in1=lnr[:, 1, :], op=ALU.add),
                                 reads=["bufB", "lnr"], writes=["bufB"])
                            S.op("dve", lambda e: e.tensor_scalar(out=bufA[:], in0=bufB[:], scalar1=ALPHA, scalar2=None, op0=ALU.mult),
                                 reads=["bufB"], writes=["bufA"])
                            S.dma("sp", lambda e: e.dma_start(out=macc[tok, :], in_=bufA[:]), reads=["bufA"], writes=[("macc", j)])
                            if debug and ll == 0:
                                S.dma("sp", lambda e: e.dma_start(out=dbg_x1[tok, :], in_=bufA[:]), reads=["bufA"], writes=[("dbgx1", j)])
                            stats_dve(bufB, "bufB", "E")

                        def st6():
                            rstd_act("E")

                        def st7():
                            S.op("dve", lambda e: e.tensor_scalar(out=bufB[:], in0=bufB[:], scalar1=mv[:, 0:1], scalar2=rstd[:],
                                                                  op0=ALU.subtract, op1=ALU.mult),
                                 reads=["bufB", "Emv", "Erstd"], writes=["bufB"])
                            S.op("dve", lambda e: e.tensor_copy(out=xhb[:], in_=bufB[:]), reads=["bufB"], writes=["xhb"])
                            S.dma("sp", lambda e: e.dma_start(out=xh2[tok, :], in_=xhb[:]), reads=["xhb"], writes=[("xh2", j)])

                        def st8():
                            for k in range(8):
                                S.op("pe", lambda e: e.transpose(pbig[:, k * 128:(k + 1) * 128], bufB[:, k * 128:(k + 1) * 128], identf[:]),
                                     reads=["bufB", "identf"], writes=["pbig"])

                        def st9():
                            for k in range(8):
                                S.op("dve", lambda e: e.tensor_scalar(out=h2T[:, k, :], in0=pbig[:, k * 128:(k + 1) * 128],
                                                                      scalar1=modT[:, 32 + k, s:s + 1], scalar2=modT[:, 24 + k, s:s + 1],
                                                                      op0=ALU.mult, op1=ALU.add),
                                     reads=["pbig", "modT"], writes=[("h2T", k)])

                        def st10():
                            for k in range(8):
                                S.op("pe", lambda e: e.matmul(pr[:], lhsT=h2T[:, k, :], rhs=wr[:, k, :], start=(k == 0), stop=(k == 7)),
                                     reads=[("h2T", k), "wr"], writes=["pr"])

                        def st11():
                            S.op("dve", lambda e: e.reduce_max(out=mx[:], in_=pr[:], axis=AX.X), reads=["pr"], writes=["mx"])
                            S.op("dve", lambda e: e.tensor_scalar(out=mx[:], in0=mx[:], scalar1=-1.0, scalar2=None, op0=ALU.mult),
                                 reads=["mx"], writes=["mx"])

                        def st12():
                            S.op("act", lambda e: e.activation(out=ex[:], in_=pr[:], func=AF.Exp, bias=mx[:], scale=1.0, accum_out=sm[:]),
                                 reads=["pr", "mx"], writes=["ex", "sm"])

                        def st13():
                            S.op("dve", lambda e: e.reciprocal(sm[:], sm[:]), reads=["sm"], writes=["sm"])
                            S.op("dve", lambda e: e.tensor_scalar(out=aff_all[:, j, :], in0=ex[:], scalar1=sm[:], scalar2=None, op0=ALU.mult),
                                 reads=["ex", "sm"], writes=[("aff", j)])

                        return [st0, st1, st2, st3, st4, st5, st6, st7, st8, st9, st10, st11, st12, st13]

                    blocks = [(NTL, 2)] + [(qb * 4, 4) for qb in range(8)]
                    nS = 0
                    pending = []
                    for bi, (t0, ntile) in enumerate(blocks):
                        par = bi % 2
                        N = ntile * 128
                        qtok = slice(t0 * 128, t0 * 128 + N)
                        kts = list(range(NT)) if t0 < NTL else [NTL, NTL + 1]
                        steps = [(c, kv, ki, kt) for c in range(4) for kv in range(2) for ki, kt in enumerate(kts)]
                        spacing = max(1, (len(steps) - 4) // (len(pending) + 1))

                        def emit_S(i):
                            c, kv, ki, kt = steps[i]
                            pSb = pS[(nS + i) % 3]
                            pSk = "pS%d" % ((nS + i) % 3)
                            PTb = PT[(nS + i) % 3]
                            PTk = "PT%d" % ((nS + i) % 3)
                            S.op("pe", lambda e: e.matmul(pSb[:, 0:N], lhsT=kT_all[:, kv, kt * 128:(kt + 1) * 128],
                                                          rhs=qT_all[:, c, qtok], start=True, stop=True),
                                 writes=[pSk])
                            S.op("act", lambda e: e.activation(out=PTb[:, 0:N], in_=pSb[:, 0:N], func=AF.Exp, scale=0.125),
                                 reads=[pSk], writes=[PTk])

                        emit_S(0)
                        emit_S(1)
                        for i, (c, kv, ki, kt) in enumerate(steps):
                            h = kv * 4 + c
                            if i + 2 < len(steps):
                                emit_S(i + 2)
                            if pending and i % spacing == spacing - 1:
                                pending.pop(0)()
                            if ki == 0:
                                S.op("dve", lambda e: e.memset(pO[0][:], 0.0), writes=["pO0"])
                            pOb = pO[0]
                            pOk = "pO0"
                            PTb = PT[(nS + i) % 3]
                            PTk = "PT%d" % ((nS + i) % 3)
                            for qs in range(ntile):
                                S.op("pe", lambda e: e.matmul(pOb[:, qs, 0:65], lhsT=PTb[:, qs * 128:(qs + 1) * 128],
                                                              rhs=v_all[:, kt, kv, :], start=False, stop=(ki == len(kts) - 1)),
                                     reads=[PTk], writes=[pOk])
                            if ki == len(kts) - 1:
                                S.op("dve", lambda e: e.reciprocal(rden[:, 0:ntile], pOb[:, 0:ntile, 64]),
                                     reads=[pOk], writes=["rden"])
                                for qs in range(ntile):
                                    S.op("dve", lambda e: e.tensor_scalar(out=attn_tok[par][:, qs, h * 64:(h + 1) * 64], in0=pOb[:, qs, 0:64],
                                                                          scalar1=rden[:, qs:qs + 1], scalar2=None, op0=ALU.mult),
                                         reads=[pOk, "rden"], writes=[("attn_tok", par, qs)])
                        nS += len(steps)
                        while pending:
                            pending.pop(0)()
                        for qs in range(ntile):
                            pending.extend(make_ef_stages(t0 + qs, qs, par))
                    while pending:
                        pending.pop(0)()
                S.barrier()
            if debug and ll == 0:
                S.dma("sp", lambda e: e.dma_start(out=dbg_aff, in_=aff_all[:]), writes=["dbgaff"])
            esGH = ExitStack()
            with esGH:
                idx_i = SB(esGH, "idx_i", [128, NE, 5], I32)
                gate_s = SB(esGH, "gate_s", [128, NE, 5], F32)
                esG = ExitStack()
                with esG:
                    lo = SB(esG, "lo", [128, 2, NE], F32)
                    hi = SB(esG, "hi", [128, 2, NE], F32)
                    mid = SB(esG, "mid", [128, 2, NE], F32)
                    capv = SB(esG, "capv", [128, 2, NE], F32)
                    cmp_ = SB(esG, "cmp", [128, NT, NE], F32)
                    cnt = SB(esG, "cnt", [128, 2, NE], F32)
                    ge = SB(esG, "ge", [128, 2, NE], F32)
                    d1 = SB(esG, "d1", [128, 2, NE], F32)
                    mask = SB(esG, "mask", [128, NT, NE], F32)
                    tg = SB(esG, "tg", [128, NT, NE, 5], BF16)
                    gr1 = SB(esG, "gr1", [128, NT, NE], F32)
                    gr2 = SB(esG, "gr2", [128, NT, NE], F32)
                    lst = SB(esG, "lst", [128, 5, 5], F32)
                    pos = SB(esG, "pos", [128, NT, NE], F32)
                    off = SB(esG, "off", [128, NT, NE], F32)
                    tot = SB(esG, "tot", [128, NT, NE], F32)
                    oh = [SB(esG, "oh%d" % i, [128, 512], BF16) for i in range(4)]
                    idx_f = SB(esG, "idx_f", [128, NE, 5], F32)
                    ptot = PS(esG, "ptot", [128, 2, NE], F32)
                    pwl = PS(esG, "pwl", [128, 512], F32)
                    pwc = PS(esG, "pwc", [128, 32], F32)
                    ptl = PS(esG, "ptl", [128, 512], F32)
                    ptc = PS(esG, "ptc", [128, 32], F32)
                    plist = [PS(esG, "plist0", [128, 5, 128], F32)]
                    S.op("pool", lambda e: e.memset(lo[:], 0.0), writes=["lo"])
                    S.op("pool", lambda e: e.memset(hi[:], 1.0), writes=["hi"])
                    S.op("pool", lambda e: e.memset(capv[:, 0, :], float(CAP_L)), writes=["capv0"])
                    S.op("pool", lambda e: e.memset(capv[:, 1, :], float(CAP_C)), writes=["capv1"])
                    aff_l = aff_all[:, 0:NTL, :]
                    aff_c = aff_all[:, NTL:NT, :]
                    for it in range(30):
                        S.op("dve", lambda e: e.tensor_tensor(out=mid[:], in0=lo[:], in1=hi[:], op=ALU.add),
                             reads=["lo", "hi"], writes=["mid"])
                        S.op("dve", lambda e: e.tensor_scalar(out=mid[:], in0=mid[:], scalar1=0.5, scalar2=None, op0=ALU.mult),
                             reads=["mid"], writes=["mid"])
                        S.op("dve", lambda e: e.tensor_tensor(out=cmp_[:, 0:NTL, :], in0=aff_l,
                                                              in1=mid[:, 0:1, :].broadcast_to([128, NTL, NE]), op=ALU.is_ge),
                             reads=["mid"], writes=["cmp_l"])
                        S.op("dve", lambda e: e.tensor_tensor(out=cmp_[:, NTL:NT, :], in0=aff_c,
                                                              in1=mid[:, 1:2, :].broadcast_to([128, 2, NE]), op=ALU.is_ge),
                             reads=["mid"], writes=["cmp_c"])
                        S.op("dve", lambda e: e.tensor_reduce(out=cnt[:, 0, :], in_=cmp_[:, 0:NTL, :].rearrange("p j e -> p e j"),
                                                              axis=AX.X, op=ALU.add),
                             reads=["cmp_l"], writes=["cnt0"])
                        S.op("dve", lambda e: e.tensor_reduce(out=cnt[:, 1, :], in_=cmp_[:, NTL:NT, :].rearrange("p j e -> p e j"),
                                                              axis=AX.X, op=ALU.add),
                             reads=["cmp_c"], writes=["cnt1"])
                        S.op("pe", lambda e: e.matmul(ptot[:], lhsT=ones_f[:], rhs=cnt[:], start=True, stop=True),
                             reads=["cnt0", "cnt1", "ones_f"], writes=["ptot"])
                        S.op("dve", lambda e: e.tensor_tensor(out=ge[:], in0=ptot[:], in1=capv[:], op=ALU.is_ge),
                             reads=["ptot", "capv0", "capv1"], writes=["ge"])
                        S.op("dve", lambda e: e.tensor_tensor(out=d1[:], in0=mid[:], in1=lo[:], op=ALU.subtract),
                             reads=["mid", "lo"], writes=["d1"])
                        S.op("dve", lambda e: e.tensor_tensor(out=d1[:], in0=d1[:], in1=ge[:], op=ALU.mult),
                             reads=["d1", "ge"], writes=["d1"])
                        S.op("dve", lambda e: e.tensor_tensor(out=lo[:], in0=lo[:], in1=d1[:], op=ALU.add),
                             reads=["d1", "lo"], writes=["lo"])
                        S.op("dve", lambda e: e.tensor_tensor(out=d1[:], in0=hi[:], in1=mid[:], op=ALU.subtract),
                             reads=["mid", "hi"], writes=["d1"])
                        S.op("dve", lambda e: e.tensor_tensor(out=d1[:], in0=d1[:], in1=ge[:], op=ALU.mult),
                             reads=["d1", "ge"], writes=["d1"])
                        S.op("dve", lambda e: e.tensor_tensor(out=hi[:], in0=mid[:], in1=d1[:], op=ALU.add),
                             reads=["d1", "mid"], writes=["hi"])
                    S.op("dve", lambda e: e.tensor_tensor(out=mask[:, 0:NTL, :], in0=aff_l,
                                                          in1=lo[:, 0:1, :].broadcast_to([128, NTL, NE]), op=ALU.is_ge),
                         reads=["lo"], writes=["mask_l"])
                    S.op("dve", lambda e: e.tensor_tensor(out=mask[:, NTL:NT, :], in0=aff_c,
                                                          in1=lo[:, 1:2, :].broadcast_to([128, 2, NE]), op=ALU.is_ge),
                         reads=["lo"], writes=["mask_c"])
                    S.op("dve", lambda e: e.tensor_copy(out=tg[:, :, :, 0], in_=tokA[:].unsqueeze(2).broadcast_to([128, NT, NE])),
                         reads=["tokA"], writes=["tg0"])
                    S.op("dve", lambda e: e.tensor_copy(out=tg[:, :, :, 1], in_=tokB[:].unsqueeze(2).broadcast_to([128, NT, NE])),
                         reads=["tokB"], writes=["tg0"])
                    S.op("dve", lambda e: e.tensor_copy(out=tg[:, :, :, 2], in_=aff_all[:]), writes=["tg1"])
                    S.op("dve", lambda e: e.tensor_tensor(out=gr1[:], in0=aff_all[:], in1=tg[:, :, :, 2], op=ALU.subtract),
                         reads=["tg1"], writes=["gr1"])
                    S.op("dve", lambda e: e.tensor_copy(out=tg[:, :, :, 3], in_=gr1[:]), reads=["gr1"], writes=["tg1"])
                    S.op("dve", lambda e: e.tensor_tensor(out=gr2[:], in0=gr1[:], in1=tg[:, :, :, 3], op=ALU.subtract),
                         reads=["gr1", "tg1"], writes=["gr2"])
                    S.op("dve", lambda e: e.tensor_copy(out=tg[:, :, :, 4], in_=gr2[:]), reads=["gr2"], writes=["tg1"])
                    mk2 = mask[:].rearrange("p j e -> p (j e)")
                    S.op("pe", lambda e: e.matmul(pwl[:], lhsT=lmat[:], rhs=mk2[:, 0:512], start=True, stop=True),
                         reads=["mask_l", "lmat"], writes=["pwl"])
                    S.op("pe", lambda e: e.matmul(pwc[:], lhsT=lmat[:], rhs=mk2[:, 512:544], start=True, stop=True),
                         reads=["mask_c", "lmat"], writes=["pwc"])
                    S.op("pe", lambda e: e.matmul(ptl[:], lhsT=ones_f[:], rhs=mk2[:, 0:512], start=True, stop=True),
                         reads=["mask_l", "ones_f"], writes=["ptl"])
                    S.op("pe", lambda e: e.matmul(ptc[:], lhsT=ones_f[:], rhs=mk2[:, 512:544], start=True, stop=True),
                         reads=["mask_c", "ones_f"], writes=["ptc"])
                    pos2 = pos[:].rearrange("p j e -> p (j e)")
                    tot2 = tot[:].rearrange("p j e -> p (j e)")
                    S.op("act", lambda e: e.activation(out=pos2[:, 0:512], in_=pwl[:], func=AF.Copy), reads=["pwl"], writes=["pos_l"])
                    S.op("act", lambda e: e.activation(out=pos2[:, 512:544], in_=pwc[:], func=AF.Copy), reads=["pwc"], writes=["pos_c"])
                    S.op("act", lambda e: e.activation(out=tot2[:, 0:512], in_=ptl[:], func=AF.Copy), reads=["ptl"], writes=["tot"])
                    S.op("act", lambda e: e.activation(out=tot2[:, 512:544], in_=ptc[:], func=AF.Copy), reads=["ptc"], writes=["tot"])
                    S.op("pool", lambda e: e.memset(off[:], 0.0), writes=["off"])
                    for j in range(1, NTL):
                        S.op("dve", lambda e: e.tensor_tensor(out=off[:, j, :], in0=off[:, j - 1, :], in1=tot[:, j - 1, :], op=ALU.add),
                             reads=["off", "tot"], writes=["off"])
                    S.op("dve", lambda e: e.tensor_copy(out=off[:, NTL + 1, :], in_=tot[:, NTL, :]), reads=["off", "tot"], writes=["off"])
                    S.op("dve", lambda e: e.tensor_tensor(out=pos[:], in0=pos[:], in1=off[:], op=ALU.add),
                         reads=["pos_l", "pos_c", "off"], writes=["pos_l", "pos_c"])
                    S.op("dve", lambda e: e.scalar_tensor_tensor(out=pos[:], in0=pos[:], scalar=-BIG, in1=mask[:], op0=ALU.add, op1=ALU.mult),
                         reads=["pos_l", "pos_c", "mask_l", "mask_c"], writes=["pos_l", "pos_c"])
                    S.op("dve", lambda e: e.tensor_scalar(out=pos[:], in0=pos[:], scalar1=BIG, scalar2=None, op0=ALU.add),
                         reads=["pos_l", "pos_c"], writes=["pos"])
                    n_oh = 0
                    for ex_ in range(NE):
                        pl = plist[0]
                        plk = "plist0"
                        S.op("dve", lambda e: e.memset(pl[:], 0.0), writes=[plk])
                        for j in range(NT):
                            ohb = oh[n_oh % 4]
                            ohk = "oh%d" % (n_oh % 4)
                            eng_ = "dve"
                            n_oh += 1
                            if j < NTL:
                                S.op(eng_, lambda e: e.tensor_scalar(out=ohb[:], in0=iota_row[:], scalar1=pos[:, j, ex_:ex_ + 1],
                                                                     scalar2=None, op0=ALU.is_equal),
                                     reads=["iota_row", "pos"], writes=[ohk])
                                for st_ in range(4):
                                    S.op("pe", lambda e: e.matmul(pl[:, st_, 0:5], lhsT=ohb[:, st_ * 128:(st_ + 1) * 128],
                                                                  rhs=tg[:, j, ex_, :], start=False, stop=(j == NTL - 1)),
                                         reads=[ohk, "tg0", "tg1"], writes=[plk])
                            else:
                                S.op(eng_, lambda e: e.tensor_scalar(out=ohb[:, 0:128], in0=iota_row[:, 0:128], scalar1=pos[:, j, ex_:ex_ + 1],
                                                                     scalar2=None, op0=ALU.is_equal),
                                     reads=["iota_row", "pos"], writes=[ohk])
                                S.op("pe", lambda e: e.matmul(pl[:, 4, 0:5], lhsT=ohb[:, 0:128], rhs=tg[:, j, ex_, :],
                                                              start=False, stop=(j == NT - 1)),
                                     reads=[ohk, "tg0", "tg1"], writes=[plk])
                        S.op("act", lambda e: e.activation(out=lst[:], in_=pl[:, :, 0:5], func=AF.Copy),
                             reads=[plk], writes=["lst"])
                        S.op("dve", lambda e: e.tensor_tensor(out=idx_f[:, ex_, :], in0=lst[:, :, 0], in1=lst[:, :, 1], op=ALU.add),
                             reads=["lst"], writes=["idx_f"])
                        S.op("dve", lambda e: e.tensor_tensor(out=gate_s[:, ex_, :], in0=lst[:, :, 2], in1=lst[:, :, 3], op=ALU.add),
                             reads=["lst"], writes=["gate_s"])
                        S.op("dve", lambda e: e.tensor_tensor(out=gate_s[:, ex_, :], in0=gate_s[:, ex_, :], in1=lst[:, :, 4], op=ALU.add),
                             reads=["lst", "gate_s"], writes=["gate_s"])
                    S.op("dve", lambda e: e.tensor_copy(out=idx_i[:], in_=idx_f[:]), reads=["idx_f"], writes=["idx_i"])
                    if debug and ll == 0:
                        S.dma("sp", lambda e: e.dma_start(out=dbg_idx, in_=idx_i[:]), reads=["idx_i"], writes=["dbgidx"])
                        S.dma("sp", lambda e: e.dma_start(out=dbg_gate, in_=gate_s[:]), reads=["gate_s"], writes=["dbggate"])
                        S.dma("sp", lambda e: e.dma_start(out=dbg_pos, in_=pos[:]), reads=["pos"], writes=["dbgpos"])
                S.barrier()
                esH = ExitStack()
                with esH:
                    g2rows = build_gate_rows(esH, 40, "g2r")
                    wb = [[SB(esH, "wb%d_%d" % (m, i), [128, 8, D], BF16) for m in range(3)] for i in range(2)]
                    xg = [[SB(esH, "xg%d_%d" % (st_, i), [128, D], BF16) for st_ in range(5)] for i in range(2)]
                    xgT = [SB(esH, "xgT%d" % i, [128, 8, 544], BF16) for i in range(2)]
                    hidT = SB(esH, "hidT", [128, 8, 544], BF16)
                    s1 = SB(esH, "s1", [128, 544], F32)
                    ysc = [SB(esH, "ysc%d" % i, [128, D], F32) for i in range(4)]
                    pxT = [PS(esH, "pxT%d" % i, [128, 8, 128], BF16) for i in range(1)]
                    ph1 = [PS(esH, "ph1_%d" % i, [128, 512], F32) for i in range(2)]
                    ph3 = [PS(esH, "ph3_%d" % i, [128, 512], F32) for i in range(2)]
                    phc = PS(esH, "phc", [128, 2, 32], F32)
                    py = [PS(esH, "pyH%d" % i, [128, 512], F32) for i in range(2)]
                    wsrc = (w1, w3, w2)

                    def issue_loads(ex_):
                        i = ex_ % 2
                        for st_ in range(5):
                            npart = 128 if st_ < 4 else CAP_C
                            S.dma("pool", lambda e: e.indirect_dma_start(
                                out=xg[i][st_][0:npart, :], out_offset=None, in_=xh2[:, :],
                                in_offset=bass.IndirectOffsetOnAxis(ap=idx_i[0:npart, ex_, st_:st_ + 1], axis=0),
                                bounds_check=bcreg, oob_is_err=False),
                                reads=["idx_i"] + [("xh2", j) for j in range(NT)], writes=["xg%d_%d" % (st_, i)])
                        for m in range(3):
                            S.dma("pool", lambda e: e.dma_start(out=wb[i][m][:], in_=wsrc[m][ll, ex_].rearrange("(k p) c -> p k c", p=128)),
                                  writes=["wb%d_%d" % (m, i)])

                    def prep_tile(ex_, st_):
                        i = ex_ % 2
                        npart = 128 if st_ < 4 else CAP_C
                        s = 0 if st_ < 4 else 1
                        c0 = st_ * 128
                        npx[0] += 1
                        pb = pxT[0]
                        pk = "pxT0"
                        for k in range(8):
                            S.op("pe", lambda e: e.transpose(pb[:, k, 0:npart], xg[i][st_][0:npart, k * 128:(k + 1) * 128],
                                                             identb[0:npart, 0:npart]),
                                 reads=["xg%d_%d" % (st_, i), "identb"], writes=[pk])
                        for k in range(8):
                            if k % 2 == 0:
                                S.op("act", lambda e: e.activation(out=xgT[i][:, k, c0:c0 + npart], in_=pb[:, k, 0:npart], func=AF.Identity,
                                                                   bias=modT[:, 24 + k, s:s + 1], scale=modT[:, 32 + k, s:s + 1]),
                                     reads=[pk, "modT"], writes=["xgT%d" % i])
                            else:
                                S.op("dve", lambda e: e.tensor_scalar(out=xgT[i][:, k, c0:c0 + npart], in0=pb[:, k, 0:npart],
                                                                      scalar1=modT[:, 32 + k, s:s + 1], scalar2=modT[:, 24 + k, s:s + 1],
                                                                      op0=ALU.mult, op1=ALU.add),
                                     reads=[pk, "modT"], writes=["xgT%d" % i])

                    npx = [0]
                    issue_loads(0)
                    for st_ in range(5):
                        prep_tile(0, st_)
                    nys = 0
                    npy = 0
                    for ex_ in range(NE):
                        i = ex_ % 2
                        xgTi = xgT[i]
                        xk_ = "xgT%d" % i
                        if ex_ + 1 < NE:
                            issue_loads(ex_ + 1)
                        for fc in range(8):
                            fb = fc % 2
                            for m, ph in ((0, ph1[fb]), (1, ph3[fb])):
                                for k in range(8):
                                    S.op("pe", lambda e: e.matmul(ph[:], lhsT=wb[i][m][:, k, fc * 128:(fc + 1) * 128], rhs=xgTi[:, k, 0:512],
                                                                  start=(k == 0), stop=(k == 7)),
                                         reads=[xk_, "wb%d_%d" % (m, i)], writes=["ph%d_%d" % (m, fb)])
                                for k in range(8):
                                    S.op("pe", lambda e: e.matmul(phc[:, m, :], lhsT=wb[i][m][:, k, fc * 128:(fc + 1) * 128], rhs=xgTi[:, k, 512:544],
                                                                  start=(k == 0), stop=(k == 7)),
                                         reads=[xk_, "wb%d_%d" % (m, i)], writes=["phc"])
                            if ex_ + 1 < NE and fc < 5:
                                prep_tile(ex_ + 1, fc)
                            S.op("act", lambda e: e.activation(out=s1[:, 0:512], in_=ph1[fb][:], func=AF.Silu), reads=["ph0_%d" % fb], writes=["s1a"])
                            S.op("act", lambda e: e.activation(out=s1[:, 512:544], in_=phc[:, 0, :], func=AF.Silu), reads=["phc"], writes=["s1b"])
                            S.op("dve", lambda e: e.tensor_tensor(out=hidT[:, fc, 0:512], in0=s1[:, 0:512], in1=ph3[fb][:], op=ALU.mult),
                                 reads=["s1a", "ph1_%d" % fb], writes=["hidT"])
                            S.op("dve", lambda e: e.tensor_tensor(out=hidT[:, fc, 512:544], in0=s1[:, 512:544], in1=phc[:, 1, :], op=ALU.mult),
                                 reads=["s1b", "phc"], writes=["hidT"])
                        for st_ in range(5):
                            npart = 128 if st_ < 4 else CAP_C
                            s = 0 if st_ < 4 else 1
                            c0 = st_ * 128
                            yb = ysc[nys % 4]
                            yk = "ysc%d" % (nys % 4)
                            nys += 1
                            for half in range(2):
                                pyb = py[npy % 2]
                                pyk = "pyH%d" % (npy % 2)
                                npy += 1
                                for fc in range(8):
                                    S.op("pe", lambda e: e.matmul(pyb[0:npart, :], lhsT=hidT[:, fc, c0:c0 + npart],
                                                                  rhs=wb[i][2][:, fc, half * 512:(half + 1) * 512],
                                                                  start=(fc == 0), stop=(fc == 7)),
                                         reads=["hidT", "wb2_%d" % i], writes=[pyk])
                                S.op("dve", lambda e: e.scalar_tensor_tensor(out=yb[0:npart, half * 512:(half + 1) * 512], in0=pyb[0:npart, :],
                                                                             scalar=gate_s[0:npart, ex_, st_:st_ + 1],
                                                                             in1=g2rows[s][0:npart, half * 512:(half + 1) * 512],
                                                                             op0=ALU.mult, op1=ALU.mult),
                                     reads=[pyk, "gate_s", "g2r_%d" % s], writes=[(yk, half)])
                            S.dma("pool", lambda e: e.indirect_dma_start(
                                out=macc[:, :], out_offset=bass.IndirectOffsetOnAxis(ap=idx_i[0:npart, ex_, st_:st_ + 1], axis=0),
                                in_=yb[0:npart, :], in_offset=None, bounds_check=bcreg, oob_is_err=False,
                                compute_op=ALU.add),
                                reads=[(yk, 0), (yk, 1), "idx_i"] + [("msc", ex_ - 1, k) for k in range(5)],
                                writes=[("msc", ex_, st_)])
                S.barrier()
            if debug and ll == 0:
                S.dma("sp", lambda e: e.dma_start(out=dbg_macc, in_=macc), writes=["dbgmacc"])
                S.barrier()
            esI = ExitStack()
            with esI:
                lnr2 = SB(esI, "lnr2", [128, 2, D], F32)
                mt = [SB(esI, "mt%d" % i, [128, D], F32) for i in range(4)]
                xo = [SB(esI, "xo%d" % i, [128, D], F32) for i in range(4)]
                st = SB(esI, "stI", [128, 2, 6], F32)
                mv = SB(esI, "mvI", [128, 2], F32)
                rstd = SB(esI, "rstdI", [128, 1], F32)
                nmr = SB(esI, "nmr", [128, 1], F32)
                S.dma("sp", lambda e: e.dma_start(out=lnr2[:], in_=ln_in[ll:ll + 1, 2:4, :].broadcast_to([128, 2, D])),
                      writes=["lnr2"])
                stI = [st, SB(esI, "stI_b", [128, 2, 6], F32)]
                mvI = [mv, SB(esI, "mvI_b", [128, 2], F32)]
                rsI = [rstd, SB(esI, "rstdI_b", [128, 1], F32)]
                nmI = [nmr, SB(esI, "nmr_b", [128, 1], F32)]
                for j0 in range(0, NT, 2):
                    pair = [(j0, 0), (j0 + 1, 1)]
                    bo = (j0 // 2 % 2) * 2
                    for (j, t) in pair:
                        tok = slice(j * 128, (j + 1) * 128)
                        S.dma("sp", lambda e: e.dma_start(out=mt[bo + t][:], in_=macc[tok, :]), reads=[("macc", j)], writes=[("mt", bo + t)])
                    for c_ in range(2):
                        for (j, t) in pair:
                            S.op("dve", lambda e: e.bn_stats(stI[t][:, c_, :], mt[bo + t][:, c_ * 512:(c_ + 1) * 512]),
                                 reads=[("mt", bo + t)], writes=[("Ist", t, c_)])
                    for (j, t) in pair:
                        S.op("dve", lambda e: e.bn_aggr(mvI[t][:], stI[t][:]), reads=[("Ist", t, 0), ("Ist", t, 1)], writes=[("Imv", t)])
                    for (j, t) in pair:
                        S.op("act", lambda e: e.activation(out=rsI[t][:], in_=mvI[t][:, 1:2], func=AF.Sqrt, bias=epsc[:], scale=1.0),
                             reads=[("Imv", t), "epsc"], writes=[("Irs", t)])
                    for (j, t) in pair:
                        S.op("dve", lambda e: e.reciprocal(rsI[t][:], rsI[t][:]), reads=[("Irs", t)], writes=[("Irs", t)])
                    for (j, t) in pair:
                        S.op("dve", lambda e: e.scalar_tensor_tensor(out=nmI[t][:], in0=mvI[t][:, 0:1], scalar=-1.0, in1=rsI[t][:],
                                                                     op0=ALU.mult, op1=ALU.mult),
                             reads=[("Imv", t), ("Irs", t)], writes=[("Inm", t)])
                    for (j, t) in pair:
                        S.op("act", lambda e: e.activation(out=xo[bo + t][:], in_=mt[bo + t][:], func=AF.Identity, bias=nmI[t][:], scale=rsI[t][:]),
                             reads=[("mt", bo + t), ("Inm", t), ("Irs", t)], writes=[("xo", bo + t)])
                    for (j, t) in pair:
                        S.op("dve", lambda e: e.tensor_tensor(out=xo[bo + t][:], in0=xo[bo + t][:], in1=lnr2[:, 0, :], op=ALU.mult),
                             reads=[("xo", bo + t), "lnr2"], writes=[("xo", bo + t)])
                    for (j, t) in pair:
                        S.op("dve", lambda e: e.tensor_tensor(out=xo[bo + t][:], in0=xo[bo + t][:], in1=lnr2[:, 1, :], op=ALU.add),
                             reads=[("xo", bo + t), "lnr2"], writes=[("xo", bo + t)])
                    for (j, t) in pair:
                        S.dma("sp", lambda e: e.dma_start(out=dst_tile(ll, j), in_=xo[bo + t][:]), reads=[("xo", bo + t)], writes=[dst_key(ll, j)])
            S.barrier()
        print("instructions", S.n_inst, "waits", S.n_wait, flush=True)
    return nc


def _band_tables():
    wins = (2, 4, 8, 16)
    Ls = 384
    band = np.zeros((128, 4, 5, 128), np.float32)
    for g, w in enumerate(wins):
        A = np.zeros((Ls, Ls), np.float64)
        for t in range(Ls):
            lo = min(max(t - w // 2, 0), Ls)
            hi = min(max(t + w // 2, 0), Ls)
            A[t, lo:hi] = 1.0 / (hi - lo)
            A[t, t] -= 1.0
        AT = A.T
        band[:, g, 0, :] = AT[0:128, 128:256]
        band[:, g, 1, :] = AT[128:256, 128:256]
        band[:, g, 2, :] = AT[0:128, 0:128]
        band[:, g, 3, :] = AT[256:384, 256:384]
        band[:, g, 4, :] = AT[256:384, 128:256]
    return band


def _rope_tables():
    t = np.arange(SEQ)
    r = (t // 64).astype(np.float32)
    col = (t % 64).astype(np.float32)
    inv = (np.float32(10000.0) ** (-np.arange(16, dtype=np.float32) / np.float32(16))).astype(np.float32)
    ang = np.concatenate([r[:, None] * inv, col[:, None] * inv], axis=-1).astype(np.float32)
    cs = np.concatenate([np.cos(ang), np.sin(ang)], axis=-1).astype(np.float32)
    return np.ascontiguousarray(cs.reshape(NTL, 128, 64).transpose(1, 0, 2))


def _prep_common(inp, layers):
    L = len(layers)
    sl = lambda a: np.ascontiguousarray(np.asarray(a)[layers])
    pool_w = sl(inp["pool_w"])
    wbd = np.zeros((L, 2, 128, 128), np.float32)
    for c in range(2):
        for gi in range(2):
            wbd[:, c, gi * 64:(gi + 1) * 64, gi * 64:(gi + 1) * 64] = pool_w[:, 2 * c + gi]
    com = {
        "w_mod": sl(inp["w_mod"]),
        "b_modT": np.ascontiguousarray(sl(inp["b_mod"]).reshape(L, 48, 128).transpose(0, 2, 1)),
        "w_in": sl(inp["w_in"]),
        "wbd": wbd,
        "pscT": np.ascontiguousarray(sl(inp["pool_scale"]).reshape(L, 2, 128).transpose(0, 2, 1)),
        "band": _band_tables(),
        "sgu_g": np.ascontiguousarray(sl(inp["sgu_g"]).reshape(L, 256)),
        "sgu_wT": np.ascontiguousarray(sl(inp["sgu_w"]).transpose(0, 3, 1, 2)),
        "sgu_bT": np.ascontiguousarray(sl(inp["sgu_b"]).transpose(0, 2, 1)),
        "qkg": np.ascontiguousarray(np.concatenate([np.tile(sl(inp["q_g"]), (1, 8)), np.tile(sl(inp["k_g"]), (1, 2))], axis=1)),
        "w_out": sl(inp["w_out"]),
        "ln": np.ascontiguousarray(np.stack([sl(inp["ln1_g"]), sl(inp["ln1_b"]), sl(inp["ln2_g"]), sl(inp["ln2_b"])], axis=1)),
        "w_router": np.ascontiguousarray(sl(inp["w_router"]).reshape(L, 8, 128, NE).transpose(0, 2, 1, 3)),
        "w1": sl(inp["w1"]), "w3": sl(inp["w3"]), "w2": sl(inp["w2"]),
        "cs": _rope_tables(),
    }
    return com


_NC_CACHE = {}


def _run(inp, x, ctx, layers, n_cores=8):
    L = len(layers)
    if L not in _NC_CACHE:
        _NC_CACHE[L] = build(L)
    nc = _NC_CACHE[L]
    com = _prep_common(inp, layers)
    c = np.asarray(inp["c"], np.float32)
    c_ctx = np.asarray(inp["c_ctx"], np.float32)
    in_maps = []
    for b in range(n_cores):
        cond = np.stack([c[b].reshape(8, 128).T, c_ctx.reshape(8, 128).T], axis=-1)
        m = dict(com)
        m["x"] = np.ascontiguousarray(x[b])
        m["ctx"] = np.ascontiguousarray(ctx[b])
        m["cond"] = np.ascontiguousarray(cond.astype(np.float32))
        in_maps.append(m)
    res = run_bass_kernel_spmd(nc, in_maps, core_ids=list(range(n_cores)))
    xo = np.stack([np.asarray(r["out"]) for r in res.results], 0)
    co = np.stack([np.asarray(r["ctx_out"]) for r in res.results], 0)
    return xo, co


LAYERS_PER_LAUNCH = 4


def kernel(**inputs):
    inp = {k: np.asarray(v) for k, v in inputs.items()}
    x = np.asarray(inp["x"], np.float32)
    ctx = np.asarray(inp["ctx"], np.float32)
    for l0 in range(0, DEPTH, LAYERS_PER_LAUNCH):
        x, ctx = _run(inp, x, ctx, list(range(l0, l0 + LAYERS_PER_LAUNCH)))
    return x.astype(np.float32)
```

```python
import numpy as np
from contextlib import ExitStack
import concourse.bass as bass
import concourse.mybir as mybir
from concourse.bass_utils import run_bass_kernel_spmd

F32 = mybir.dt.float32
BF16 = mybir.dt.bfloat16
I32 = mybir.dt.int32
AF = mybir.ActivationFunctionType
ALU = mybir.AluOpType
AX = mybir.AxisListType

DEPTH = 4
D = 1024
SEQ = 4096
CTX = 256
NT = 34
NTL = 32
NTOK = SEQ + CTX
NE = 16
CAP_L = 512
CAP_C = 32
ALPHA = float((2 * DEPTH) ** 0.25)
EPS = 1e-6
BIG = 100000.0


class Sched:
    def __init__(self, nc, es, n_dma_sems=48):
        self.nc = nc
        self.eng = {"pe": nc.tensor, "act": nc.scalar, "dve": nc.vector, "pool": nc.gpsimd, "sp": nc.sync}
        self.sem = {k: es.enter_context(nc.semaphore("prog_" + k)) for k in self.eng}
        self.cnt = {k: 0 for k in self.eng}
        self.dsem = [es.enter_context(nc.semaphore("dma%d" % i)) for i in range(n_dma_sems)]
        self.dtot = [0] * n_dma_sems
        self.dnext = 0
        self.dnext_sw = 0
        self.waited = {k: {} for k in self.eng}
        self.res = {}
        self.n_inst = 0
        self.n_wait = 0

    def _semobj(self, key):
        return self.sem[key] if isinstance(key, str) else self.dsem[key]

    def _collect(self, reads, writes):
        deps = {}

        def add(d):
            if d is None:
                return
            k, v = d
            if deps.get(k, 0) < v:
                deps[k] = v
        for r in reads:
            e = self.res.get(r)
            if e is not None:
                add(e["w"])
        for w in writes:
            e = self.res.get(w)
            if e is not None:
                add(e["w"])
                for k, v in e["r"].items():
                    add((k, v))
        return deps

    def _wait(self, F, deps, skip_self=False):
        for k, v in deps.items():
            if skip_self and k == F:
                continue
            if self.waited[F].get(k, 0) < v:
                self.eng[F].wait_ge(self._semobj(k), v)
                self.waited[F][k] = v
                self.n_wait += 1

    def _update(self, dep, reads, writes):
        k, v = dep
        for r in reads:
            e = self.res.setdefault(r, {"w": None, "r": {}})
            if e["r"].get(k, 0) < v:
                e["r"][k] = v
        for w in writes:
            self.res[w] = {"w": dep, "r": {}}

    def op(self, F, fn, reads=(), writes=(), skip_self=None):
        if skip_self is None:
            skip_self = (F == "pe")
        deps = self._collect(reads, writes)
        self._wait(F, deps, skip_self=skip_self)
        inst = fn(self.eng[F])
        self.cnt[F] += 1
        inst.then_inc(self.sem[F], 1)
        self._update((F, self.cnt[F]), reads, writes)
        self.n_inst += 1
        return inst

    def dma(self, Q, fn, reads=(), writes=()):
        deps = self._collect(reads, writes)
        self._wait(Q, deps)
        half = len(self.dsem) // 2
        if Q == "pool":
            i = half + self.dnext_sw
            self.dnext_sw = (self.dnext_sw + 1) % (len(self.dsem) - half)
        else:
            i = self.dnext
            self.dnext = (self.dnext + 1) % half
        if self.dtot[i] > 0 and self.waited[Q].get(i, 0) < self.dtot[i]:
            self.eng[Q].wait_ge(self.dsem[i], self.dtot[i])
            self.waited[Q][i] = self.dtot[i]
            self.n_wait += 1
        inst = fn(self.eng[Q])
        self.dtot[i] += 16
        inst.then_inc(self.dsem[i], 16)
        self._update((i, self.dtot[i]), reads, writes)
        self.n_inst += 1
        return inst

    def barrier(self):
        for F in self.eng:
            for i, t in enumerate(self.dtot):
                if t > 0 and self.waited[F].get(i, 0) < t:
                    self.eng[F].wait_ge(self.dsem[i], t)
                    self.waited[F][i] = t
            for k in self.eng:
                if k != F and self.cnt[k] > 0 and self.waited[F].get(k, 0) < self.cnt[k]:
                    self.eng[F].wait_ge(self.sem[k], self.cnt[k])
                    self.waited[F][k] = self.cnt[k]
        self.res = {}


def build(L, debug=False):
    nc = bass.Bass("TRN2", target_bir_lowering=False)

    def DT(name, shape, dt=F32, kind="ExternalInput"):
        return nc.dram_tensor(name, shape, dt, kind=kind).ap()

    x_in = DT("x", [SEQ, D])
    ctx_in = DT("ctx", [CTX, D])
    cond_in = DT("cond", [128, 8, 2])
    w_mod = DT("w_mod", [L, D, 6 * D])
    b_modT = DT("b_modT", [L, 128, 48])
    w_in = DT("w_in", [L, D, 1536])
    wbd_in = DT("wbd", [L, 2, 128, 128])
    pscT_in = DT("pscT", [L, 128, 2])
    band_in = DT("band", [128, 4, 5, 128])
    sgug_in = DT("sgu_g", [L, 256])
    sguwT_in = DT("sgu_wT", [L, 128, 4, 128])
    sgubT_in = DT("sgu_bT", [L, 128, 4])
    qkg_in = DT("qkg", [L, 640])
    w_out = DT("w_out", [L, D, D])
    ln_in = DT("ln", [L, 4, D])
    wr_in = DT("w_router", [L, 128, 8, NE])
    w1 = DT("w1", [L, NE, D, D])
    w3 = DT("w3", [L, NE, D, D])
    w2 = DT("w2", [L, NE, D, D])
    cs_in = DT("cs", [128, NTL, 64])
    out = DT("out", [SEQ, D], kind="ExternalOutput")
    ctx_out = DT("ctx_out", [CTX, D], kind="ExternalOutput")
    if debug:
        dbg_x1 = DT("dbg_x1", [NTOK, D], kind="ExternalOutput")
        dbg_aff = DT("dbg_aff", [128, NT, NE], kind="ExternalOutput")
        dbg_mix = DT("dbg_mix", [128, 4, NTOK], BF16, kind="ExternalOutput")
        dbg_idx = DT("dbg_idx", [128, NE, 5], I32, kind="ExternalOutput")
        dbg_gate = DT("dbg_gate", [128, NE, 5], kind="ExternalOutput")
        dbg_macc = DT("dbg_macc", [NTOK, D], kind="ExternalOutput")
        dbg_pos = DT("dbg_pos", [128, NT, NE], kind="ExternalOutput")
    xs = DT("xs", [NTOK, D], kind="Internal")
    macc = DT("macc", [NTOK, D], kind="Internal")
    xh2 = DT("xh2", [NTOK, D], BF16, kind="Internal")

    def src_tile(ll, j):
        if ll == 0:
            return x_in[j * 128:(j + 1) * 128, :] if j < NTL else ctx_in[(j - NTL) * 128:(j - NTL + 1) * 128, :]
        return xs[j * 128:(j + 1) * 128, :]

    def dst_tile(ll, j):
        if ll == L - 1:
            return out[j * 128:(j + 1) * 128, :] if j < NTL else ctx_out[(j - NTL) * 128:(j - NTL + 1) * 128, :]
        return xs[j * 128:(j + 1) * 128, :]

    def src_key(ll, j):
        return ("xin", j) if ll == 0 else ("xs", j)

    def dst_key(ll, j):
        return ("xout", j) if ll == L - 1 else ("xs", j)

    es0 = ExitStack()
    with es0:
        S = Sched(nc, es0)
        bcreg = es0.enter_context(nc.gpsimd.register("bcreg"))
        nc.gpsimd.reg_mov(bcreg, NTOK - 1)

        uid = [0]

        def SB(es, name, shape, dt):
            uid[0] += 1
            return es.enter_context(nc.sbuf_tensor("s%d_%s" % (uid[0], name), shape, dt))

        def PS(es, name, shape, dt):
            uid[0] += 1
            return es.enter_context(nc.psum_tensor("p%d_%s" % (uid[0], name), shape, dt))

        identf = SB(es0, "identf", [128, 128], F32)
        identb = SB(es0, "identb", [128, 128], BF16)
        ones_f = SB(es0, "ones_f", [128, 128], F32)
        lmat = SB(es0, "lmat", [128, 128], F32)
        iota_row = SB(es0, "iota_row", [128, 512], mybir.dt.float16)
        tokid = SB(es0, "tokid", [128, NT], F32)
        jmp = SB(es0, "jmp", [128, 128], F32)
        modT = SB(es0, "modT", [128, 48, 2], F32)
        aff_all = SB(es0, "aff_all", [128, NT, NE], F32)
        epsc = SB(es0, "epsc", [128, 1], F32)
        S.op("pool", lambda e: e.memset(epsc[:], EPS), writes=["epsc"])
        S.op("pool", lambda e: e.iota(jmp[:], [[1, 128]], base=0, channel_multiplier=-1,
                                      allow_small_or_imprecise_dtypes=True), writes=["jmp"])
        S.op("pool", lambda e: e.tensor_single_scalar(out=identf[:], in_=jmp[:], scalar=0.0, op=ALU.is_equal),
             reads=["jmp"], writes=["identf"])
        S.op("pool", lambda e: e.tensor_single_scalar(out=identb[:], in_=jmp[:], scalar=0.0, op=ALU.is_equal),
             reads=["jmp"], writes=["identb"])
        S.op("pool", lambda e: e.tensor_single_scalar(out=lmat[:], in_=jmp[:], scalar=0.0, op=ALU.is_gt),
             reads=["jmp"], writes=["lmat"])
        S.op("pool", lambda e: e.memset(ones_f[:], 1.0), writes=["ones_f"])
        S.op("pool", lambda e: e.iota(iota_row[:], [[1, 512]], base=0, channel_multiplier=0,
                                      allow_small_or_imprecise_dtypes=True), writes=["iota_row"])
        S.op("pool", lambda e: e.iota(tokid[:], [[128, NT]], base=0, channel_multiplier=1,
                                      allow_small_or_imprecise_dtypes=True), writes=["tokid"])
        tokA = SB(es0, "tokA", [128, NT], F32)
        tokB = SB(es0, "tokB", [128, 1], F32)
        S.op("pool", lambda e: e.iota(tokA[:], [[128, NT]], base=0, channel_multiplier=0,
                                      allow_small_or_imprecise_dtypes=True), writes=["tokA"])
        S.op("pool", lambda e: e.iota(tokB[:], [[0, 1]], base=0, channel_multiplier=1,
                                      allow_small_or_imprecise_dtypes=True), writes=["tokB"])

        def ln_stats(es_tiles, src, key_src, tag):
            st, mv, rstd = es_tiles
            S.op("dve", lambda e: e.bn_stats(st[:, 0, :], src[:, 0:512]), reads=[key_src], writes=[tag + "st0"])
            S.op("dve", lambda e: e.bn_stats(st[:, 1, :], src[:, 512:1024]), reads=[key_src], writes=[tag + "st1"])
            S.op("dve", lambda e: e.bn_aggr(mv[:], st[:]), reads=[tag + "st0", tag + "st1"], writes=[tag + "mv"])
            S.op("act", lambda e: e.activation(out=rstd[:], in_=mv[:, 1:2], func=AF.Sqrt, bias=epsc[:], scale=1.0),
                 reads=[tag + "mv", "epsc"], writes=[tag + "rstd"])
            S.op("dve", lambda e: e.reciprocal(rstd[:], rstd[:]), reads=[tag + "rstd"], writes=[tag + "rstd"])

        for ll in range(L):
            esA = ExitStack()
            with esA:
                condT = SB(esA, "condT", [128, 8, 2], F32)
                bmT = SB(esA, "bmT", [128, 48], F32)
                wblk = [SB(esA, "wblk%d" % i, [128, 8, 512], F32) for i in range(2)]
                pmod = PS(esA, "pmod", [128, 48, 2], F32)
                S.dma("sp", lambda e: e.dma_start(out=condT[:], in_=cond_in), writes=["condT"])
                S.dma("sp", lambda e: e.dma_start(out=bmT[:], in_=b_modT[ll]), writes=["bmT"])
                S.op("act", lambda e: e.activation(out=condT[:], in_=condT[:], func=AF.Silu),
                     reads=["condT"], writes=["condT"])
                wm = w_mod[ll].rearrange("(k p) c -> p k c", p=128)
                for cb in range(12):
                    wb_ = wblk[cb % 2]
                    wk = "wblk%d" % (cb % 2)
                    S.dma("sp", lambda e: e.dma_start(out=wb_[:], in_=wm[:, :, cb * 512:(cb + 1) * 512]), writes=[wk])
                    for sub in range(4):
                        j = cb * 4 + sub
                        for k in range(8):
                            S.op("pe", lambda e: e.matmul(pmod[:, j, :], lhsT=wb_[:, k, sub * 128:(sub + 1) * 128],
                                                          rhs=condT[:, k, :], start=(k == 0), stop=(k == 7)),
                                 reads=[wk, "condT"], writes=["pmod"])
                S.op("dve", lambda e: e.tensor_tensor(out=modT[:], in0=pmod[:],
                                                      in1=bmT[:].unsqueeze(2).broadcast_to([128, 48, 2]), op=ALU.add),
                     reads=["pmod", "bmT"], writes=["modT"])
                for base in (8, 32):
                    S.op("dve", lambda e: e.tensor_scalar_add(modT[:, base:base + 8, :], modT[:, base:base + 8, :], 1.0),
                         reads=["modT"], writes=["modT"])
            S.barrier()

            def build_gate_rows(es, vbase, tagname):
                rows = [SB(es, "%s_%d" % (tagname, s), [128, D], F32) for s in range(2)]
                est = ExitStack()
                with est:
                  diag = [SB(est, "%s_dg%d" % (tagname, i), [128, 128], F32) for i in range(2)]
                  pg = PS(est, tagname + "_pg", [128, D], F32)
                  n = 0
                  for s in range(2):
                    for dc in range(8):
                        dg = diag[n % 2]
                        dk = "%s_dg%d" % (tagname, n % 2)
                        n += 1
                        S.op("dve", lambda e: e.tensor_scalar(out=dg[:], in0=identf[:], scalar1=modT[:, vbase + dc, s:s + 1],
                                                              scalar2=None, op0=ALU.mult),
                             reads=["identf", "modT"], writes=[dk])
                        S.op("pe", lambda e: e.matmul(pg[:, dc * 128:(dc + 1) * 128], lhsT=ones_f[:], rhs=dg[:],
                                                      start=True, stop=True),
                             reads=[dk, "ones_f"], writes=[tagname + "_pg"])
                    S.op("act", lambda e: e.activation(out=rows[s][:], in_=pg[:], func=AF.Copy),
                         reads=[tagname + "_pg"], writes=["%s_%d" % (tagname, s)])
                  S.barrier()
                return rows

            esBE = ExitStack()
            with esBE:
                mixps = SB(esBE, "mixps", [128, 4, NTOK], BF16)
                qT_all = SB(esBE, "qT_all", [128, 4, NTOK], BF16)
                kT_all = SB(esBE, "kT_all", [128, 2, NTOK], BF16)
                S.op("pool", lambda e: e.memset(kT_all[:], 0.0), writes=["kT_zero"])
                v_all = SB(esBE, "v_all", [128, NT, 2, 65], BF16)
                S.op("pool", lambda e: e.memset(v_all[:, :, :, 64:65], 1.0), writes=["v_ones"])
                esBC = ExitStack()
                with esBC:
                    p_all = SB(esBC, "p_all", [128, NT, 256], BF16)
                    esB = ExitStack()
                    with esB:
                        winb = SB(esB, "winb", [128, 8, 1536], BF16)
                        cs = SB(esB, "cs", [128, NTL, 64], F32)
                        sgug = SB(esB, "sgug", [128, 256], F32)
                        wsT = SB(esB, "wsT", [128, 4, 128], BF16)
                        sbT = SB(esB, "sbT", [128, 4], F32)
                        qkg = SB(esB, "qkg", [128, 640], F32)
                        xt = [SB(esB, "xt%d" % i, [128, D], F32) for i in range(2)]
                        st = SB(esB, "st", [128, 2, 6], F32)
                        mv = SB(esB, "mv", [128, 2], F32)
                        rstd = SB(esB, "rstd", [128, 1], F32)
                        xb = SB(esB, "xb", [128, D], BF16)
                        hT = SB(esB, "hT", [128, 8, 128], BF16)
                        gu = SB(esB, "gu", [128, 256], BF16)
                        gv = SB(esB, "gv", [128, 256], F32)
                        stv = SB(esB, "stv", [128, 4, 6], F32)
                        mvv = SB(esB, "mvv", [128, 4, 2], F32)
                        rsv = SB(esB, "rsv", [128, 4], F32)
                        vn = SB(esB, "vn", [128, 256], F32)
                        vh = SB(esB, "vh", [128, 256], BF16)
                        sgo = SB(esB, "sgo", [128, 256], BF16)
                        sq = SB(esB, "sq", [128, 640], F32)
                        ss = SB(esB, "ss", [128, 10], F32)
                        qn = SB(esB, "qn", [128, 10, 64], F32)
                        ta = SB(esB, "ta", [128, 10, 32], F32)
                        tb = SB(esB, "tb", [128, 10, 32], F32)
                        qr = SB(esB, "qr", [128, 640], BF16)
                        pT = PS(esB, "pT", [128, 8, 128], BF16)
                        pz = PS(esB, "pz", [128, 1536], F32)
                        psg = PS(esB, "psg", [128, 256], F32)
                        ptr = PS(esB, "ptr", [128, 7, 128], BF16)
                        S.dma("pool", lambda e: e.dma_start(out=winb[:], in_=w_in[ll].rearrange("(k p) c -> p k c", p=128)),
                              writes=["winb"])
                        S.dma("pool", lambda e: e.dma_start(out=wsT[:], in_=sguwT_in[ll]), writes=["wsT"])
                        S.dma("sp", lambda e: e.dma_start(out=cs[:], in_=cs_in), writes=["cs"])
                        S.dma("sp", lambda e: e.dma_start(out=sgug[:], in_=sgug_in[ll:ll + 1, :].broadcast_to([128, 256])),
                              writes=["sgug"])
                        S.dma("sp", lambda e: e.dma_start(out=qkg[:], in_=qkg_in[ll:ll + 1, :].broadcast_to([128, 640])),
                              writes=["qkg"])
                        S.dma("sp", lambda e: e.dma_start(out=sbT[:], in_=sgubT_in[ll]), writes=["sbT"])
                        def front_B(j):
                            s = 0 if j < NTL else 1
                            tok = slice(j * 128, (j + 1) * 128)
                            xtj = xt[j % 2]
                            xk = "xt%d" % (j % 2)
                            S.dma("sp", lambda e: e.dma_start(out=xtj[:], in_=src_tile(ll, j)),
                                  reads=[src_key(ll, j)], writes=[xk])
                            ln_stats((st, mv, rstd), xtj, xk, "B")
                            S.op("dve", lambda e: e.tensor_scalar(out=xb[:], in0=xtj[:], scalar1=mv[:, 0:1], scalar2=rstd[:],
                                                                  op0=ALU.subtract, op1=ALU.mult),
                                 reads=[xk, "Bmv", "Brstd"], writes=["xb"])
                            for k in range(8):
                                S.op("pe", lambda e: e.transpose(pT[:, k, :], xb[:, k * 128:(k + 1) * 128], identb[:]),
                                     reads=["xb", "identb"], writes=["pT"])
                            for k in range(8):
                                S.op("act", lambda e: e.activation(out=hT[:, k, :], in_=pT[:, k, :], func=AF.Identity,
                                                                   bias=modT[:, 0 + k, s:s + 1], scale=modT[:, 8 + k, s:s + 1]),
                                     reads=["pT", "modT"], writes=[("hT", k)])
                        gu2 = [gu, SB(esB, "gu_b", [128, 256], BF16)]
                        gv2 = [gv, SB(esB, "gv_b", [128, 256], F32)]
                        stv2 = [stv, SB(esB, "stv_b", [128, 4, 6], F32)]
                        mvv2 = [mvv, SB(esB, "mvv_b", [128, 4, 2], F32)]
                        rsv2 = [rsv, SB(esB, "rsv_b", [128, 4], F32)]
                        vn2 = [vn, SB(esB, "vn_b", [128, 256], F32)]
                        vh2 = [vh, SB(esB, "vh_b", [128, 256], BF16)]
                        sgo2 = [sgo, SB(esB, "sgo_b", [128, 256], BF16)]
                        sq2 = [sq, SB(esB, "sq_b", [128, 640], F32)]
                        zq2 = [SB(esB, "zq_a", [128, 640], F32), SB(esB, "zq_b", [128, 640], F32)]
                        ss2 = [ss, SB(esB, "ss_b", [128, 10], F32)]
                        qn2 = [qn, SB(esB, "qn_b", [128, 10, 64], F32)]
                        ta2 = [ta, SB(esB, "ta_b", [128, 10, 32], F32)]
                        tb2 = [tb, SB(esB, "tb_b", [128, 10, 32], F32)]
                        qr2 = [qr, SB(esB, "qr_b", [128, 640], BF16)]
                        psg2 = [psg, PS(esB, "psg_b", [128, 256], F32)]
                        ptr2 = [ptr, PS(esB, "ptr_b", [128, 7, 128], BF16)]

                        def mm_and_evac(j, t):
                            for cblk in range(3):
                                for k in range(8):
                                    S.op("pe", lambda e: e.matmul(pz[:, cblk * 512:(cblk + 1) * 512], lhsT=hT[:, k, :],
                                                                  rhs=winb[:, k, cblk * 512:(cblk + 1) * 512],
                                                                  start=(k == 0), stop=(k == 7)),
                                         reads=[("hT", k), "winb"], writes=[("pz", cblk)])
                            if j + 1 < NT:
                                front_B(j + 1)
                            S.op("act", lambda e: e.activation(out=p_all[:, j, :], in_=pz[:, 0:256], func=AF.Copy),
                                 reads=[("pz", 0)], writes=[("p_all", j)])
                            S.op("act", lambda e: e.activation(out=gu2[t][:], in_=pz[:, 256:512], func=AF.Gelu_apprx_tanh),
                                 reads=[("pz", 0)], writes=[("gu", t)])
                            S.op("act", lambda e: e.activation(out=gv2[t][:], in_=pz[:, 512:768], func=AF.Gelu_apprx_tanh),
                                 reads=[("pz", 1)], writes=[("gv", t)])
                            S.op("act", lambda e: e.activation(out=sq2[t][:], in_=pz[:, 768:1408], func=AF.Square),
                                 reads=[("pz", 1), ("pz", 2)], writes=[("sq", t)])
                            S.op("dve", lambda e: e.tensor_copy(out=zq2[t][:], in_=pz[:, 768:1408]),
                                 reads=[("pz", 1), ("pz", 2)], writes=[("zq", t)])
                            S.op("dve", lambda e: e.tensor_copy(out=v_all[:, j, :, 0:64],
                                                                in_=pz[:, 1408:1536].rearrange("p (k d) -> p k d", d=64)),
                                 reads=[("pz", 2)], writes=[("v", j)])

                        front_B(0)
                        for j0 in range(0, NTL, 2):
                            pair = [(j0, 0), (j0 + 1, 1)]
                            for (j, t) in pair:
                                mm_and_evac(j, t)
                            for h in range(4):
                                for (j, t) in pair:
                                    S.op("dve", lambda e: e.bn_stats(stv2[t][:, h, :], gv2[t][:, h * 64:(h + 1) * 64]),
                                         reads=[("gv", t)], writes=[("stv", t, h)])
                                for (j, t) in pair:
                                    S.op("dve", lambda e: e.bn_aggr(mvv2[t][:, h, :], stv2[t][:, h, :]),
                                         reads=[("stv", t, h)], writes=[("mvv", t)])
                            for (j, t) in pair:
                                S.op("act", lambda e: e.activation(out=rsv2[t][:], in_=mvv2[t][:, :, 1], func=AF.Sqrt, bias=epsc[:], scale=1.0),
                                     reads=[("mvv", t), "epsc"], writes=[("rsv", t)])
                            for (j, t) in pair:
                                S.op("dve", lambda e: e.tensor_reduce(out=ss2[t][:], in_=sq2[t][:].rearrange("p (h d) -> p h d", d=64),
                                                                      axis=AX.X, op=ALU.add),
                                     reads=[("sq", t)], writes=[("ss", t)])
                            for (j, t) in pair:
                                S.op("act", lambda e: e.activation(out=ss2[t][:], in_=ss2[t][:], func=AF.Sqrt, bias=epsc[:], scale=1.0 / 64),
                                     reads=[("ss", t), "epsc"], writes=[("ss", t)])
                            for (j, t) in pair:
                                S.op("dve", lambda e: e.reciprocal(rsv2[t][:], rsv2[t][:]), reads=[("rsv", t)], writes=[("rsv", t)])
                            for h in range(4):
                                for (j, t) in pair:
                                    S.op("dve", lambda e: e.tensor_scalar(out=vn2[t][:, h * 64:(h + 1) * 64], in0=gv2[t][:, h * 64:(h + 1) * 64],
                                                                          scalar1=mvv2[t][:, h, 0:1], scalar2=rsv2[t][:, h:h + 1],
                                                                          op0=ALU.subtract, op1=ALU.mult),
                                         reads=[("gv", t), ("mvv", t), ("rsv", t)], writes=[("vn", t)])
                            for (j, t) in pair:
                                S.op("dve", lambda e: e.tensor_tensor(out=vh2[t][:], in0=vn2[t][:], in1=sgug[:], op=ALU.mult),
                                     reads=[("vn", t), "sgug"], writes=[("vh", t)])
                            for (j, t) in pair:
                                for h in range(4):
                                    S.op("pe", lambda e: e.matmul(psg2[t][:, h * 64:(h + 1) * 64], lhsT=wsT[:, h, :],
                                                                  rhs=vh2[t][:, h * 64:(h + 1) * 64], start=True, stop=True),
                                         reads=["wsT", ("vh", t)], writes=[("psg", t)])
                            for (j, t) in pair:
                                S.op("dve", lambda e: e.reciprocal(ss2[t][:], ss2[t][:]), reads=[("ss", t)], writes=[("ss", t)])
                            for (j, t) in pair:
                                S.op("dve", lambda e: e.tensor_tensor(out=qn2[t][:], in0=zq2[t][:].rearrange("p (h d) -> p h d", d=64),
                                                                      in1=ss2[t][:].unsqueeze(2).broadcast_to([128, 10, 64]), op=ALU.mult),
                                     reads=[("zq", t), ("ss", t)], writes=[("qn", t)])
                            for (j, t) in pair:
                                S.op("dve", lambda e: e.tensor_tensor(out=qn2[t][:], in0=qn2[t][:],
                                                                      in1=qkg[:].rearrange("p (h d) -> p h d", d=64), op=ALU.mult),
                                     reads=[("qn", t), "qkg"], writes=[("qn", t)])
                            for h in range(4):
                                for (j, t) in pair:
                                    S.op("dve", lambda e: e.scalar_tensor_tensor(out=sgo2[t][:, h * 64:(h + 1) * 64],
                                                                                 in0=psg2[t][:, h * 64:(h + 1) * 64],
                                                                                 scalar=sbT[:, h:h + 1],
                                                                                 in1=gu2[t][:, h * 64:(h + 1) * 64],
                                                                                 op0=ALU.add, op1=ALU.mult),
                                         reads=[("psg", t), "sbT", ("gu", t)], writes=[("sgo", t)])
                            for (j, t) in pair:
                                for c in range(2):
                                    S.op("pe", lambda e: e.transpose(ptr2[t][:, c, :], sgo2[t][:, c * 128:(c + 1) * 128], identb[:]),
                                         reads=[("sgo", t), "identb"], writes=[("ptr", t)])
                            for (j, t) in pair:
                                tok = slice(j * 128, (j + 1) * 128)
                                S.op("act", lambda e: e.activation(out=mixps[:, 2:4, tok], in_=ptr2[t][:, 0:2, :], func=AF.Copy),
                                     reads=[("ptr", t)], writes=[("mixps_s", j)])
                            def dsts(t):
                                qdst = qr2[t][:, 0:512].rearrange("p (g k d) -> p k g d", g=4, k=2, d=64)
                                kdst = qr2[t][:, 512:640].rearrange("p (k d) -> p k d", d=64)
                                return qdst, kdst
                            if j0 < NTL:
                                for half_, op_ in ((0, ALU.subtract), (1, ALU.add)):
                                    for (j, t) in pair:
                                        cosb = cs[:, j:j + 1, 0:32].broadcast_to([128, 10, 32])
                                        xa_ = qn2[t][:, :, 0:32] if half_ == 0 else qn2[t][:, :, 32:64]
                                        S.op("dve", lambda e: e.tensor_tensor(out=ta2[t][:], in0=xa_, in1=cosb, op=ALU.mult),
                                             reads=[("qn", t), "cs"], writes=[("ta", t)])
                                    for (j, t) in pair:
                                        sinb = cs[:, j:j + 1, 32:64].broadcast_to([128, 10, 32])
                                        xb_ = qn2[t][:, :, 32:64] if half_ == 0 else qn2[t][:, :, 0:32]
                                        S.op("dve", lambda e: e.tensor_tensor(out=tb2[t][:], in0=xb_, in1=sinb, op=ALU.mult),
                                             reads=[("qn", t), "cs"], writes=[("tb", t)])
                                    for (j, t) in pair:
                                        qdst, kdst = dsts(t)
                                        dsl = slice(0, 32) if half_ == 0 else slice(32, 64)
                                        S.op("dve", lambda e: e.tensor_tensor(out=qdst[:, :, :, dsl],
                                                                              in0=ta2[t][:, 0:8, :].rearrange("p (k g) d -> p k g d", k=2),
                                                                              in1=tb2[t][:, 0:8, :].rearrange("p (k g) d -> p k g d", k=2),
                                                                              op=op_),
                                             reads=[("ta", t), ("tb", t)], writes=[("qr", t, half_, 0)])
                                        S.op("dve", lambda e: e.tensor_tensor(out=kdst[:, :, dsl], in0=ta2[t][:, 8:10, :], in1=tb2[t][:, 8:10, :],
                                                                              op=op_),
                                             reads=[("ta", t), ("tb", t)], writes=[("qr", t, half_, 1)])
                            else:
                                for (j, t) in pair:
                                    qdst, kdst = dsts(t)
                                    S.op("dve", lambda e: e.tensor_copy(out=qdst, in_=qn2[t][:, 0:8, :].rearrange("p (k g) d -> p k g d", k=2)),
                                         reads=[("qn", t)], writes=[("qr", t, 0, 0), ("qr", t, 1, 0)])
                                    S.op("dve", lambda e: e.tensor_copy(out=kdst, in_=qn2[t][:, 8:10, :]),
                                         reads=[("qn", t)], writes=[("qr", t, 0, 1), ("qr", t, 1, 1)])
                            for (j, t) in pair:
                                for c in range(5):
                                    S.op("pe", lambda e: e.transpose(ptr2[t][:, 2 + c, :], qr2[t][:, c * 128:(c + 1) * 128], identb[:]),
                                         reads=[("qr", t, 0, 0), ("qr", t, 0, 1), ("qr", t, 1, 0), ("qr", t, 1, 1), "identb"],
                                         writes=[("ptr", t)])
                            for (j, t) in pair:
                                tok = slice(j * 128, (j + 1) * 128)
                                S.op("act", lambda e: e.activation(out=qT_all[:, :, tok], in_=ptr2[t][:, 2:6, :], func=AF.Copy),
                                     reads=[("ptr", t)], writes=[("qT", j)])
                                for kv_ in range(2):
                                    S.op("act", lambda e: e.activation(out=kT_all[kv_ * 64:(kv_ + 1) * 64, kv_, tok],
                                                                       in_=ptr2[t][kv_ * 64:(kv_ + 1) * 64, 6, :], func=AF.Copy),
                                         reads=[("ptr", t), "kT_zero"], writes=[("kT", j, kv_)])
                        S.barrier()
                        for j in range(NTL, NT):
                            s = 0 if j < NTL else 1
                            tok = slice(j * 128, (j + 1) * 128)
                            for cblk in range(3):
                                for k in range(8):
                                    S.op("pe", lambda e: e.matmul(pz[:, cblk * 512:(cblk + 1) * 512], lhsT=hT[:, k, :],
                                                                  rhs=winb[:, k, cblk * 512:(cblk + 1) * 512],
                                                                  start=(k == 0), stop=(k == 7)),
                                         reads=[("hT", k), "winb"], writes=[("pz", cblk)])
                            if j + 1 < NT:
                                front_B(j + 1)
                            S.op("act", lambda e: e.activation(out=p_all[:, j, :], in_=pz[:, 0:256], func=AF.Copy),
                                 reads=[("pz", 0)], writes=[("p_all", j)])
                            S.op("act", lambda e: e.activation(out=gu[:], in_=pz[:, 256:512], func=AF.Gelu_apprx_tanh),
                                 reads=[("pz", 0)], writes=["gu"])
                            S.op("act", lambda e: e.activation(out=gv[:], in_=pz[:, 512:768], func=AF.Gelu_apprx_tanh),
                                 reads=[("pz", 1)], writes=["gv"])
                            for h in range(4):
                                S.op("dve", lambda e: e.bn_stats(stv[:, h, :], gv[:, h * 64:(h + 1) * 64]),
                                     reads=["gv"], writes=[("stv", h)])
                                S.op("dve", lambda e: e.bn_aggr(mvv[:, h, :], stv[:, h, :]),
                                     reads=[("stv", h)], writes=["mvv"])
                            S.op("act", lambda e: e.activation(out=rsv[:], in_=mvv[:, :, 1], func=AF.Sqrt, bias=epsc[:], scale=1.0),
                                 reads=["mvv", "epsc"], writes=["rsv"])
                            S.op("dve", lambda e: e.reciprocal(rsv[:], rsv[:]), reads=["rsv"], writes=["rsv"])
                            for h in range(4):
                                S.op("dve", lambda e: e.tensor_scalar(out=vn[:, h * 64:(h + 1) * 64], in0=gv[:, h * 64:(h + 1) * 64],
                                                                      scalar1=mvv[:, h, 0:1], scalar2=rsv[:, h:h + 1],
                                                                      op0=ALU.subtract, op1=ALU.mult),
                                     reads=["gv", "mvv", "rsv"], writes=["vn"])
                            S.op("dve", lambda e: e.tensor_tensor(out=vh[:], in0=vn[:], in1=sgug[:], op=ALU.mult),
                                 reads=["vn", "sgug"], writes=["vh"])
                            for h in range(4):
                                S.op("pe", lambda e: e.matmul(psg[:, h * 64:(h + 1) * 64], lhsT=wsT[:, h, :],
                                                              rhs=vh[:, h * 64:(h + 1) * 64], start=True, stop=True),
                                     reads=["wsT", "vh"], writes=["psg"])
                            for h in range(4):
                                S.op("dve", lambda e: e.scalar_tensor_tensor(out=sgo[:, h * 64:(h + 1) * 64],
                                                                             in0=psg[:, h * 64:(h + 1) * 64],
                                                                             scalar=sbT[:, h:h + 1],
                                                                             in1=gu[:, h * 64:(h + 1) * 64],
                                                                             op0=ALU.add, op1=ALU.mult),
                                     reads=["psg", "sbT", "gu"], writes=["sgo"])
                            for c in range(2):
                                S.op("pe", lambda e: e.transpose(ptr[:, c, :], sgo[:, c * 128:(c + 1) * 128], identb[:]),
                                     reads=["sgo", "identb"], writes=[("ptr", 0)])
                            S.op("act", lambda e: e.activation(out=mixps[:, 2:4, tok], in_=ptr[:, 0:2, :], func=AF.Copy),
                                 reads=[("ptr", 0)], writes=[("mixps_s", j)])
                            S.op("act", lambda e: e.activation(out=sq[:], in_=pz[:, 768:1408], func=AF.Square),
                                 reads=[("pz", 1), ("pz", 2)], writes=["sq"])
                            S.op("dve", lambda e: e.tensor_reduce(out=ss[:], in_=sq[:].rearrange("p (h d) -> p h d", d=64),
                                                                  axis=AX.X, op=ALU.add),
                                 reads=["sq"], writes=["ss"])
                            S.op("act", lambda e: e.activation(out=ss[:], in_=ss[:], func=AF.Sqrt, bias=epsc[:], scale=1.0 / 64),
                                 reads=["ss", "epsc"], writes=["ss"])
                            S.op("dve", lambda e: e.reciprocal(ss[:], ss[:]), reads=["ss"], writes=["ss"])
                            S.op("dve", lambda e: e.tensor_tensor(out=qn[:], in0=pz[:, 768:1408].rearrange("p (h d) -> p h d", d=64),
                                                                  in1=ss[:].unsqueeze(2).broadcast_to([128, 10, 64]), op=ALU.mult),
                                 reads=[("pz", 1), ("pz", 2), "ss"], writes=["qn"])
                            S.op("dve", lambda e: e.tensor_tensor(out=qn[:], in0=qn[:],
                                                                  in1=qkg[:].rearrange("p (h d) -> p h d", d=64), op=ALU.mult),
                                 reads=["qn", "qkg"], writes=["qn"])
                            qdst = qr[:, 0:512].rearrange("p (g k d) -> p k g d", g=4, k=2, d=64)
                            kdst = qr[:, 512:640].rearrange("p (k d) -> p k d", d=64)
                            qsrc = qn[:, 0:8, :].rearrange("p (k g) d -> p k g d", k=2)
                            ksrc = qn[:, 8:10, :]
                            if j < NTL:
                                cosb = cs[:, j:j + 1, 0:32].broadcast_to([128, 10, 32])
                                sinb = cs[:, j:j + 1, 32:64].broadcast_to([128, 10, 32])
                                x1 = qn[:, :, 0:32]
                                x2 = qn[:, :, 32:64]
                                S.op("dve", lambda e: e.tensor_tensor(out=ta[:], in0=x1, in1=cosb, op=ALU.mult),
                                     reads=["qn", "cs"], writes=["ta"])
                                S.op("dve", lambda e: e.tensor_tensor(out=tb[:], in0=x2, in1=sinb, op=ALU.mult),
                                     reads=["qn", "cs"], writes=["tb"])
                                S.op("dve", lambda e: e.tensor_tensor(out=qdst[:, :, :, 0:32],
                                                                      in0=ta[:, 0:8, :].rearrange("p (k g) d -> p k g d", k=2),
                                                                      in1=tb[:, 0:8, :].rearrange("p (k g) d -> p k g d", k=2),
                                                                      op=ALU.subtract),
                                     reads=["ta", "tb"], writes=["qr_a"])
                                S.op("dve", lambda e: e.tensor_tensor(out=kdst[:, :, 0:32], in0=ta[:, 8:10, :], in1=tb[:, 8:10, :],
                                                                      op=ALU.subtract),
                                     reads=["ta", "tb"], writes=["qr_b"])
                                S.op("dve", lambda e: e.tensor_tensor(out=ta[:], in0=x2, in1=cosb, op=ALU.mult),
                                     reads=["qn", "cs"], writes=["ta"])
                                S.op("dve", lambda e: e.tensor_tensor(out=tb[:], in0=x1, in1=sinb, op=ALU.mult),
                                     reads=["qn", "cs"], writes=["tb"])
                                S.op("dve", lambda e: e.tensor_tensor(out=qdst[:, :, :, 32:64],
                                                                      in0=ta[:, 0:8, :].rearrange("p (k g) d -> p k g d", k=2),
                                                                      in1=tb[:, 0:8, :].rearrange("p (k g) d -> p k g d", k=2),
                                                                      op=ALU.add),
                                     reads=["ta", "tb"], writes=["qr_c"])
                                S.op("dve", lambda e: e.tensor_tensor(out=kdst[:, :, 32:64], in0=ta[:, 8:10, :], in1=tb[:, 8:10, :],
                                                                      op=ALU.add),
                                     reads=["ta", "tb"], writes=["qr_d"])
                            else:
                                S.op("dve", lambda e: e.tensor_copy(out=qdst, in_=qsrc), reads=["qn"], writes=["qr_a", "qr_c"])
                                S.op("dve", lambda e: e.tensor_copy(out=kdst, in_=ksrc), reads=["qn"], writes=["qr_b", "qr_d"])
                            for c in range(5):
                                S.op("pe", lambda e: e.transpose(ptr[:, 2 + c, :], qr[:, c * 128:(c + 1) * 128], identb[:]),
                                     reads=["qr_a", "qr_b", "qr_c", "qr_d", "identb"], writes=[("ptr", 1)])
                            S.op("act", lambda e: e.activation(out=qT_all[:, :, tok], in_=ptr[:, 2:6, :], func=AF.Copy),
                                 reads=[("ptr", 1)], writes=[("qT", j)])
                            for kv_ in range(2):
                                S.op("act", lambda e: e.activation(out=kT_all[kv_ * 64:(kv_ + 1) * 64, kv_, tok],
                                                                   in_=ptr[kv_ * 64:(kv_ + 1) * 64, 6, :], func=AF.Copy),
                                     reads=[("ptr", 1), "kT_zero"], writes=[("kT", j, kv_)])
                            S.op("dve", lambda e: e.tensor_copy(out=v_all[:, j, :, 0:64],
                                                                in_=pz[:, 1408:1536].rearrange("p (k d) -> p k d", d=64)),
                                 reads=[("pz", 2)], writes=[("v", j)])
                    S.barrier()
                    esC = ExitStack()
                    with esC:
                        bandf = SB(esC, "bandf", [128, 4, 5, 128], F32)
                        band = SB(esC, "band", [128, 4, 5, 128], BF16)
                        wbd = SB(esC, "wbd", [128, 2, 128], BF16)
                        pscT = SB(esC, "pscT", [128, 2], F32)
                        pooledT = SB(esC, "pooledT", [128, 2, 128], BF16)
                        ppool = PS(esC, "ppool", [128, 4, 128], F32)
                        pp2 = PS(esC, "pp2", [128, 2, 128], F32)
                        S.dma("sp", lambda e: e.dma_start(out=bandf[:], in_=band_in), writes=["bandf"])
                        S.op("dve", lambda e: e.tensor_copy(out=band[:], in_=bandf[:]), reads=["bandf"], writes=["band"])
                        S.dma("pool", lambda e: e.dma_start(out=wbd[:], in_=wbd_in[ll].rearrange("c p q -> p c q")), writes=["wbd"])
                        S.dma("sp", lambda e: e.dma_start(out=pscT[:], in_=pscT_in[ll]), writes=["pscT"])
                        for j in range(NT):
                            tok = slice(j * 128, (j + 1) * 128)
                            lo_t, hi_t = (0, NTL - 1) if j < NTL else (NTL, NT - 1)
                            rels = [r for r in (-1, 0, 1) if lo_t <= j + r <= hi_t]
                            for c in range(2):
                                for bi in range(2):
                                    g = 2 * c + bi
                                    for ri, r in enumerate(rels):
                                        if r == -1:
                                            v = 0
                                        elif r == 1:
                                            v = 4
                                        else:
                                            v = 2 if j == lo_t else (3 if j == hi_t else 1)
                                        S.op("pe", lambda e: e.matmul(ppool[:, c * 2 + bi, :], lhsT=p_all[:, j + r, c * 128:(c + 1) * 128],
                                                                      rhs=band[:, g, v, :], start=(ri == 0), stop=(ri == len(rels) - 1)),
                                             reads=["band"], writes=["ppool"])
                                S.op("act", lambda e: e.activation(out=pooledT[0:64, c, :], in_=ppool[0:64, c * 2, :], func=AF.Copy),
                                     reads=["ppool"], writes=[("pooledT", c, 0)])
                                S.op("act", lambda e: e.activation(out=pooledT[64:128, c, :], in_=ppool[64:128, c * 2 + 1, :], func=AF.Copy),
                                     reads=["ppool"], writes=[("pooledT", c, 1)])
                            for c in range(2):
                                S.op("pe", lambda e: e.matmul(pp2[:, c, :], lhsT=wbd[:, c, :], rhs=pooledT[:, c, :], start=True, stop=True),
                                     reads=["wbd", ("pooledT", c, 0), ("pooledT", c, 1)], writes=["pp2"])
                            for c in range(2):
                                S.op("act", lambda e: e.activation(out=mixps[:, c, tok], in_=pp2[:, c, :], func=AF.Copy,
                                                                   scale=pscT[:, c:c + 1]),
                                     reads=["pp2", "pscT"], writes=[("mixps_p", j)])
                    S.barrier()
                if debug and ll == 0:
                    S.dma("sp", lambda e: e.dma_start(out=dbg_mix, in_=mixps[:]), writes=["dbgmix"])
                esD = ExitStack()
                with esD:
                    woutb = SB(esD, "woutb", [128, 8, D], BF16)
                    lnr = SB(esD, "lnr", [128, 2, D], F32)
                    wr = SB(esD, "wr", [128, 8, NE], F32)
                    PT = [SB(esD, "PT%d" % i, [128, 512], BF16) for i in range(3)]
                    attn_tok = [SB(esD, "attn_tok%d" % i, [128, 4, 512], BF16) for i in range(2)]
                    rden = SB(esD, "rden", [128, 4], F32)
                    mixA = SB(esD, "mixA", [128, 4, 128], BF16)
                    bufA = SB(esD, "bufA", [128, D], F32)
                    bufB = SB(esD, "bufB", [128, D], F32)
                    xhb = SB(esD, "xhb", [128, D], BF16)
                    h2T = SB(esD, "h2T", [128, 8, 128], F32)
                    st = SB(esD, "stE", [128, 2, 6], F32)
                    mv = SB(esD, "mvE", [128, 2], F32)
                    rstd = SB(esD, "rstdE", [128, 1], F32)
                    mx = SB(esD, "mx", [128, 1], F32)
                    sm = SB(esD, "sm", [128, 1], F32)
                    ex = SB(esD, "ex", [128, NE], F32)
                    g1rows = build_gate_rows(esD, 16, "g1r")
                    pS = [PS(esD, "pS%d" % i, [128, 512], F32) for i in range(3)]
                    pO = [PS(esD, "pO%d" % i, [128, 4, 128], F32) for i in range(2)]
                    ptr = PS(esD, "ptrD", [128, 4, 128], BF16)
                    pbig = PS(esD, "pbig", [128, D], F32)
                    pr = pbig[:, 0:NE]
                    S.dma("pool", lambda e: e.dma_start(out=woutb[:], in_=w_out[ll].rearrange("(k p) c -> p k c", p=128)),
                          writes=["woutb"])
                    S.dma("sp", lambda e: e.dma_start(out=lnr[:], in_=ln_in[ll:ll + 1, 0:2, :].broadcast_to([128, 2, D])),
                          writes=["lnr"])
                    S.dma("sp", lambda e: e.dma_start(out=wr[:], in_=wr_in[ll]), writes=["wr"])

                    def rstd_act(tag):
                        S.op("act", lambda e: e.activation(out=rstd[:], in_=mv[:, 1:2], func=AF.Ln, bias=epsc[:], scale=1.0),
                             reads=[tag + "mv", "epsc"], writes=[tag + "rstd"])
                        S.op("act", lambda e: e.activation(out=rstd[:], in_=rstd[:], func=AF.Exp, scale=-0.5),
                             reads=[tag + "rstd"], writes=[tag + "rstd"])

                    def stats_dve(src, key_src, tag):
                        S.op("dve", lambda e: e.bn_stats(st[:, 0, :], src[:, 0:512]), reads=[key_src], writes=[tag + "st0"])
                        S.op("dve", lambda e: e.bn_stats(st[:, 1, :], src[:, 512:1024]), reads=[key_src], writes=[tag + "st1"])
                        S.op("dve", lambda e: e.bn_aggr(mv[:], st[:]), reads=[tag + "st0", tag + "st1"], writes=[tag + "mv"])

                    def make_ef_stages(j, qs, par):
                        s = 0 if j < NTL else 1
                        tok = slice(j * 128, (j + 1) * 128)
                        at = attn_tok[par]

                        def st0():
                            for cc in range(4):
                                S.op("pe", lambda e: e.transpose(ptr[:, cc, :], at[:, qs, cc * 128:(cc + 1) * 128], identb[:]),
                                     reads=[("attn_tok", par, qs), "identb"], writes=["ptrD"])
                            S.dma("sp", lambda e: e.dma_start(out=bufA[:], in_=src_tile(ll, j)), reads=[src_key(ll, j)], writes=["bufA"])

                        def st1():
                            S.op("act", lambda e: e.activation(out=mixA[:], in_=ptr[:], func=AF.Copy), reads=["ptrD"], writes=["mixA"])

                        def st2():
                            for half in range(2):
                                for c8 in range(8):
                                    lhs = mixps[:, c8, tok] if c8 < 4 else mixA[:, c8 - 4, :]
                                    S.op("pe", lambda e: e.matmul(pbig[:, half * 512:(half + 1) * 512], lhsT=lhs,
                                                                  rhs=woutb[:, c8, half * 512:(half + 1) * 512],
                                                                  start=(c8 == 0), stop=(c8 == 7)),
                                         reads=["mixA", "woutb"], writes=["pbig"])

                        def st3():
                            S.op("dve", lambda e: e.tensor_tensor(out=bufB[:], in0=pbig[:], in1=g1rows[s][:], op=ALU.mult),
                                 reads=["pbig", "g1r_%d" % s], writes=["bufB"])
                            S.op("dve", lambda e: e.scalar_tensor_tensor(out=bufA[:], in0=bufA[:], scalar=ALPHA, in1=bufB[:],
                                                                         op0=ALU.mult, op1=ALU.add),
                                 reads=["bufA", "bufB"], writes=["bufA"])
                            stats_dve(bufA, "bufA", "E")

                        def st4():
                            rstd_act("E")

                        def st5():
                            S.op("dve", lambda e: e.tensor_scalar(out=bufB[:], in0=bufA[:], scalar1=mv[:, 0:1], scalar2=rstd[:],
                                                                  op0=ALU.subtract, op1=ALU.mult),
                                 reads=["bufA", "Emv", "Erstd"], writes=["bufB"])
                            S.op("dve", lambda e: e.tensor_tensor(out=bufB[:], in0=bufB[:], in1=lnr[:, 0, :], op=ALU.mult),
                                 reads=["bufB", "lnr"], writes=["bufB"])
                            S.op("dve", lambda e: e.tensor_tensor(out=bufB[:], in0=bufB[:], in1=lnr[:, 1, :], op=ALU.add),
                                 reads=["bufB", "lnr"], writes=["bufB"])
                            S.op("dve", lambda e: e.tensor_scalar(out=bufA[:], in0=bufB[:], scalar1=ALPHA, scalar2=None, op0=ALU.mult),
                                 reads=["bufB"], writes=["bufA"])
                            S.dma("sp", lambda e: e.dma_start(out=macc[tok, :], in_=bufA[:]), reads=["bufA"], writes=[("macc", j)])
                            if debug and ll == 0:
                                S.dma("sp", lambda e: e.dma_start(out=dbg_x1[tok, :], in_=bufA[:]), reads=["bufA"], writes=[("dbgx1", j)])
                            stats_dve(bufB, "bufB", "E")

                        def st6():
                            rstd_act("E")

                        def st7():
                            S.op("dve", lambda e: e.tensor_scalar(out=bufB[:], in0=bufB[:], scalar1=mv[:, 0:1], scalar2=rstd[:],
                                                                  op0=ALU.subtract, op1=ALU.mult),
                                 reads=["bufB", "Emv", "Erstd"], writes=["bufB"])
                            S.op("dve", lambda e: e.tensor_copy(out=xhb[:], in_=bufB[:]), reads=["bufB"], writes=["xhb"])
                            S.dma("sp", lambda e: e.dma_start(out=xh2[tok, :], in_=xhb[:]), reads=["xhb"], writes=[("xh2", j)])

                        def st8():
                            for k in range(8):
                                S.op("pe", lambda e: e.transpose(pbig[:, k * 128:(k + 1) * 128], bufB[:, k * 128:(k + 1) * 128], identf[:]),
                                     reads=["bufB", "identf"], writes=["pbig"])

                        def st9():
                            for k in range(8):
                                S.op("dve", lambda e: e.tensor_scalar(out=h2T[:, k, :], in0=pbig[:, k * 128:(k + 1) * 128],
                                                                      scalar1=modT[:, 32 + k, s:s + 1], scalar2=modT[:, 24 + k, s:s + 1],
                                                                      op0=ALU.mult, op1=ALU.add),
                                     reads=["pbig", "modT"], writes=[("h2T", k)])

                        def st10():
                            for k in range(8):
                                S.op("pe", lambda e: e.matmul(pr, lhsT=h2T[:, k, :], rhs=wr[:, k, :], start=(k == 0), stop=(k == 7)),
                                     reads=[("h2T", k), "wr"], writes=["pbig"])

                        def st11():
                            S.op("dve", lambda e: e.reduce_max(out=mx[:], in_=pr, axis=AX.X), reads=["pbig"], writes=["mx"])
                            S.op("dve", lambda e: e.tensor_scalar(out=mx[:], in0=mx[:], scalar1=-1.0, scalar2=None, op0=ALU.mult),
                                 reads=["mx"], writes=["mx"])

                        def st12():
                            S.op("act", lambda e: e.activation(out=ex[:], in_=pr, func=AF.Exp, bias=mx[:], scale=1.0, accum_out=sm[:]),
                                 reads=["pbig", "mx"], writes=["ex", "sm"])

                        def st13():
                            S.op("dve", lambda e: e.reciprocal(sm[:], sm[:]), reads=["sm"], writes=["sm"])
                            S.op("dve", lambda e: e.tensor_scalar(out=aff_all[:, j, :], in0=ex[:], scalar1=sm[:], scalar2=None, op0=ALU.mult),
                                 reads=["ex", "sm"], writes=[("aff", j)])

                        return [st0, st1, st2, st3, st4, st5, st6, st7, st8, st9, st10, st11, st12, st13]

                    blocks = [(NTL, 2)] + [(qb * 4, 4) for qb in range(8)]
                    nS = 0
                    nO = 0
                    pending = []
                    for bi, (t0, ntile) in enumerate(blocks):
                        par = bi % 2
                        N = ntile * 128
                        qtok = slice(t0 * 128, t0 * 128 + N)
                        kts = list(range(NT)) if t0 < NTL else [NTL, NTL + 1]
                        steps = [(c, kv, ki, kt) for c in range(4) for kv in range(2) for ki, kt in enumerate(kts)]
                        spacing = max(1, (len(steps) - 4) // (len(pending) + 1))

                        def emit_S(i):
                            c, kv, ki, kt = steps[i]
                            pSb = pS[(nS + i) % 3]
                            pSk = "pS%d" % ((nS + i) % 3)
                            PTb = PT[(nS + i) % 3]
                            PTk = "PT%d" % ((nS + i) % 3)
                            S.op("pe", lambda e: e.matmul(pSb[:, 0:N], lhsT=kT_all[:, kv, kt * 128:(kt + 1) * 128],
                                                          rhs=qT_all[:, c, qtok], start=True, stop=True),
                                 writes=[pSk])
                            S.op("act", lambda e: e.activation(out=PTb[:, 0:N], in_=pSb[:, 0:N], func=AF.Exp, scale=0.125),
                                 reads=[pSk], writes=[PTk])

                        emit_S(0)
                        emit_S(1)
                        for i, (c, kv, ki, kt) in enumerate(steps):
                            h = kv * 4 + c
                            if i + 2 < len(steps):
                                emit_S(i + 2)
                            if pending and i % spacing == spacing - 1:
                                pending.pop(0)()
                            if ki == 0:
                                nO += 1
                                if nO == 1:
                                    S.op("dve", lambda e: e.memset(pO[nO % 2][:], 0.0), writes=["pO%d" % (nO % 2)])
                            if ki == 1:
                                S.op("dve", lambda e: e.memset(pO[(nO + 1) % 2][:], 0.0), writes=["pO%d" % ((nO + 1) % 2)])
                            pOb = pO[nO % 2]
                            pOk = "pO%d" % (nO % 2)
                            PTb = PT[(nS + i) % 3]
                            PTk = "PT%d" % ((nS + i) % 3)
                            for qs in range(ntile):
                                S.op("pe", lambda e: e.matmul(pOb[:, qs, 0:65], lhsT=PTb[:, qs * 128:(qs + 1) * 128],
                                                              rhs=v_all[:, kt, kv, :], start=False, stop=(ki == len(kts) - 1)),
                                     reads=[PTk], writes=[pOk])
                            if ki == len(kts) - 1:
                                S.op("dve", lambda e: e.reciprocal(rden[:, 0:ntile], pOb[:, 0:ntile, 64]),
                                     reads=[pOk], writes=["rden"])
                                for qs in range(ntile):
                                    S.op("dve", lambda e: e.tensor_scalar(out=attn_tok[par][:, qs, h * 64:(h + 1) * 64], in0=pOb[:, qs, 0:64],
                                                                          scalar1=rden[:, qs:qs + 1], scalar2=None, op0=ALU.mult),
                                         reads=[pOk, "rden"], writes=[("attn_tok", par, qs)])
                        nS += len(steps)
                        while pending:
                            pending.pop(0)()
                        for qs in range(ntile):
                            pending.extend(make_ef_stages(t0 + qs, qs, par))
                    while pending:
                        pending.pop(0)()
                S.barrier()
            if debug and ll == 0:
                S.dma("sp", lambda e: e.dma_start(out=dbg_aff, in_=aff_all[:]), writes=["dbgaff"])
            esGH = ExitStack()
            with esGH:
                idx_i = SB(esGH, "idx_i", [128, NE, 5], I32)
                gate_s = SB(esGH, "gate_s", [128, NE, 5], F32)
                esG = ExitStack()
                with esG:
                    lo = SB(esG, "lo", [128, 2, NE], F32)
                    hi = SB(esG, "hi", [128, 2, NE], F32)
                    mid = SB(esG, "mid", [128, 2, NE], F32)
                    capv = SB(esG, "capv", [128, 2, NE], F32)
                    cmp_ = SB(esG, "cmp", [128, NT, NE], F32)
                    cnt = SB(esG, "cnt", [128, 2, NE], F32)
                    ge = SB(esG, "ge", [128, 2, NE], F32)
                    d1 = SB(esG, "d1", [128, 2, NE], F32)
                    mask = SB(esG, "mask", [128, NT, NE], F32)
                    tg = SB(esG, "tg", [128, NT, NE, 5], BF16)
                    gr1 = SB(esG, "gr1", [128, NT, NE], F32)
                    gr2 = SB(esG, "gr2", [128, NT, NE], F32)
                    lst = SB(esG, "lst", [128, 5, 5], F32)
                    pos = SB(esG, "pos", [128, NT, NE], F32)
                    off = SB(esG, "off", [128, NT, NE], F32)
                    tot = SB(esG, "tot", [128, NT, NE], F32)
                    oh = [SB(esG, "oh%d" % i, [128, 512], BF16) for i in range(4)]
                    idx_f = SB(esG, "idx_f", [128, NE, 5], F32)
                    ptot = PS(esG, "ptot", [128, 2, NE], F32)
                    pwl = PS(esG, "pwl", [128, 512], F32)
                    pwc = PS(esG, "pwc", [128, 32], F32)
                    ptl = PS(esG, "ptl", [128, 512], F32)
                    ptc = PS(esG, "ptc", [128, 32], F32)
                    plist = [PS(esG, "plist0", [128, 5, 128], F32)]
                    S.op("pool", lambda e: e.memset(lo[:], 0.0), writes=["lo"])
                    S.op("pool", lambda e: e.memset(hi[:], 1.0), writes=["hi"])
                    S.op("pool", lambda e: e.memset(capv[:, 0, :], float(CAP_L)), writes=["capv0"])
                    S.op("pool", lambda e: e.memset(capv[:, 1, :], float(CAP_C)), writes=["capv1"])
                    aff_l = aff_all[:, 0:NTL, :]
                    aff_c = aff_all[:, NTL:NT, :]
                    for it in range(30):
                        S.op("dve", lambda e: e.tensor_tensor(out=mid[:], in0=lo[:], in1=hi[:], op=ALU.add),
                             reads=["lo", "hi"], writes=["mid"])
                        S.op("dve", lambda e: e.tensor_scalar(out=mid[:], in0=mid[:], scalar1=0.5, scalar2=None, op0=ALU.mult),
                             reads=["mid"], writes=["mid"])
                        S.op("dve", lambda e: e.tensor_tensor(out=cmp_[:, 0:NTL, :], in0=aff_l,
                                                              in1=mid[:, 0:1, :].broadcast_to([128, NTL, NE]), op=ALU.is_ge),
                             reads=["mid"], writes=["cmp_l"])
                        S.op("dve", lambda e: e.tensor_tensor(out=cmp_[:, NTL:NT, :], in0=aff_c,
                                                              in1=mid[:, 1:2, :].broadcast_to([128, 2, NE]), op=ALU.is_ge),
                             reads=["mid"], writes=["cmp_c"])
                        S.op("dve", lambda e: e.tensor_reduce(out=cnt[:, 0, :], in_=cmp_[:, 0:NTL, :].rearrange("p j e -> p e j"),
                                                              axis=AX.X, op=ALU.add),
                             reads=["cmp_l"], writes=["cnt0"])
                        S.op("dve", lambda e: e.tensor_reduce(out=cnt[:, 1, :], in_=cmp_[:, NTL:NT, :].rearrange("p j e -> p e j"),
                                                              axis=AX.X, op=ALU.add),
                             reads=["cmp_c"], writes=["cnt1"])
                        S.op("pe", lambda e: e.matmul(ptot[:], lhsT=ones_f[:], rhs=cnt[:], start=True, stop=True),
                             reads=["cnt0", "cnt1", "ones_f"], writes=["ptot"])
                        S.op("dve", lambda e: e.tensor_tensor(out=ge[:], in0=ptot[:], in1=capv[:], op=ALU.is_ge),
                             reads=["ptot", "capv0", "capv1"], writes=["ge"])
                        S.op("dve", lambda e: e.tensor_tensor(out=d1[:], in0=mid[:], in1=lo[:], op=ALU.subtract),
                             reads=["mid", "lo"], writes=["d1"])
                        S.op("dve", lambda e: e.tensor_tensor(out=d1[:], in0=d1[:], in1=ge[:], op=ALU.mult),
                             reads=["d1", "ge"], writes=["d1"])
                        S.op("dve", lambda e: e.tensor_tensor(out=lo[:], in0=lo[:], in1=d1[:], op=ALU.add),
                             reads=["d1", "lo"], writes=["lo"])
                        S.op("dve", lambda e: e.tensor_tensor(out=d1[:], in0=hi[:], in1=mid[:], op=ALU.subtract),
                             reads=["mid", "hi"], writes=["d1"])
                        S.op("dve", lambda e: e.tensor_tensor(out=d1[:], in0=d1[:], in1=ge[:], op=ALU.mult),
                             reads=["d1", "ge"], writes=["d1"])
                        S.op("dve", lambda e: e.tensor_tensor(out=hi[:], in0=mid[:], in1=d1[:], op=ALU.add),
                             reads=["d1", "mid"], writes=["hi"])
                    S.op("dve", lambda e: e.tensor_tensor(out=mask[:, 0:NTL, :], in0=aff_l,
                                                          in1=lo[:, 0:1, :].broadcast_to([128, NTL, NE]), op=ALU.is_ge),
                         reads=["lo"], writes=["mask_l"])
                    S.op("dve", lambda e: e.tensor_tensor(out=mask[:, NTL:NT, :], in0=aff_c,
                                                          in1=lo[:, 1:2, :].broadcast_to([128, 2, NE]), op=ALU.is_ge),
                         reads=["lo"], writes=["mask_c"])
                    S.op("dve", lambda e: e.tensor_copy(out=tg[:, :, :, 0], in_=tokA[:].unsqueeze(2).broadcast_to([128, NT, NE])),
                         reads=["tokA"], writes=["tg0"])
                    S.op("dve", lambda e: e.tensor_copy(out=tg[:, :, :, 1], in_=tokB[:].unsqueeze(2).broadcast_to([128, NT, NE])),
                         reads=["tokB"], writes=["tg0"])
                    S.op("dve", lambda e: e.tensor_copy(out=tg[:, :, :, 2], in_=aff_all[:]), writes=["tg1"])
                    S.op("dve", lambda e: e.tensor_tensor(out=gr1[:], in0=aff_all[:], in1=tg[:, :, :, 2], op=ALU.subtract),
                         reads=["tg1"], writes=["gr1"])
                    S.op("dve", lambda e: e.tensor_copy(out=tg[:, :, :, 3], in_=gr1[:]), reads=["gr1"], writes=["tg1"])
                    S.op("dve", lambda e: e.tensor_tensor(out=gr2[:], in0=gr1[:], in1=tg[:, :, :, 3], op=ALU.subtract),
                         reads=["gr1", "tg1"], writes=["gr2"])
                    S.op("dve", lambda e: e.tensor_copy(out=tg[:, :, :, 4], in_=gr2[:]), reads=["gr2"], writes=["tg1"])
                    mk2 = mask[:].rearrange("p j e -> p (j e)")
                    S.op("pe", lambda e: e.matmul(pwl[:], lhsT=lmat[:], rhs=mk2[:, 0:512], start=True, stop=True),
                         reads=["mask_l", "lmat"], writes=["pwl"])
                    S.op("pe", lambda e: e.matmul(pwc[:], lhsT=lmat[:], rhs=mk2[:, 512:544], start=True, stop=True),
                         reads=["mask_c", "lmat"], writes=["pwc"])
                    S.op("pe", lambda e: e.matmul(ptl[:], lhsT=ones_f[:], rhs=mk2[:, 0:512], start=True, stop=True),
                         reads=["mask_l", "ones_f"], writes=["ptl"])
                    S.op("pe", lambda e: e.matmul(ptc[:], lhsT=ones_f[:], rhs=mk2[:, 512:544], start=True, stop=True),
                         reads=["mask_c", "ones_f"], writes=["ptc"])
                    pos2 = pos[:].rearrange("p j e -> p (j e)")
                    tot2 = tot[:].rearrange("p j e -> p (j e)")
                    S.op("act", lambda e: e.activation(out=pos2[:, 0:512], in_=pwl[:], func=AF.Copy), reads=["pwl"], writes=["pos_l"])
                    S.op("act", lambda e: e.activation(out=pos2[:, 512:544], in_=pwc[:], func=AF.Copy), reads=["pwc"], writes=["pos_c"])
                    S.op("act", lambda e: e.activation(out=tot2[:, 0:512], in_=ptl[:], func=AF.Copy), reads=["ptl"], writes=["tot"])
                    S.op("act", lambda e: e.activation(out=tot2[:, 512:544], in_=ptc[:], func=AF.Copy), reads=["ptc"], writes=["tot"])
                    S.op("pool", lambda e: e.memset(off[:], 0.0), writes=["off"])
                    for j in range(1, NTL):
                        S.op("dve", lambda e: e.tensor_tensor(out=off[:, j, :], in0=off[:, j - 1, :], in1=tot[:, j - 1, :], op=ALU.add),
                             reads=["off", "tot"], writes=["off"])
                    S.op("dve", lambda e: e.tensor_copy(out=off[:, NTL + 1, :], in_=tot[:, NTL, :]), reads=["off", "tot"], writes=["off"])
                    S.op("dve", lambda e: e.tensor_tensor(out=pos[:], in0=pos[:], in1=off[:], op=ALU.add),
                         reads=["pos_l", "pos_c", "off"], writes=["pos_l", "pos_c"])
                    S.op("dve", lambda e: e.scalar_tensor_tensor(out=pos[:], in0=pos[:], scalar=-BIG, in1=mask[:], op0=ALU.add, op1=ALU.mult),
                         reads=["pos_l", "pos_c", "mask_l", "mask_c"], writes=["pos_l", "pos_c"])
                    S.op("dve", lambda e: e.tensor_scalar(out=pos[:], in0=pos[:], scalar1=BIG, scalar2=None, op0=ALU.add),
                         reads=["pos_l", "pos_c"], writes=["pos"])
                    n_oh = 0
                    for ex_ in range(NE):
                        pl = plist[0]
                        plk = "plist0"
                        S.op("dve", lambda e: e.memset(pl[:], 0.0), writes=[plk])
                        for j in range(NT):
                            ohb = oh[n_oh % 4]
                            ohk = "oh%d" % (n_oh % 4)
                            eng_ = "dve"
                            n_oh += 1
                            if j < NTL:
                                S.op(eng_, lambda e: e.tensor_scalar(out=ohb[:], in0=iota_row[:], scalar1=pos[:, j, ex_:ex_ + 1],
                                                                     scalar2=None, op0=ALU.is_equal),
                                     reads=["iota_row", "pos"], writes=[ohk])
                                for st_ in range(4):
                                    S.op("pe", lambda e: e.matmul(pl[:, st_, 0:5], lhsT=ohb[:, st_ * 128:(st_ + 1) * 128],
                                                                  rhs=tg[:, j, ex_, :], start=False, stop=(j == NTL - 1)),
                                         reads=[ohk, "tg0", "tg1"], writes=[plk])
                            else:
                                S.op(eng_, lambda e: e.tensor_scalar(out=ohb[:, 0:128], in0=iota_row[:, 0:128], scalar1=pos[:, j, ex_:ex_ + 1],
                                                                     scalar2=None, op0=ALU.is_equal),
                                     reads=["iota_row", "pos"], writes=[ohk])
                                S.op("pe", lambda e: e.matmul(pl[:, 4, 0:5], lhsT=ohb[:, 0:128], rhs=tg[:, j, ex_, :],
                                                              start=False, stop=(j == NT - 1)),
                                     reads=[ohk, "tg0", "tg1"], writes=[plk])
                        S.op("act", lambda e: e.activation(out=lst[:], in_=pl[:, :, 0:5], func=AF.Copy),
                             reads=[plk], writes=["lst"])
                        S.op("dve", lambda e: e.tensor_tensor(out=idx_f[:, ex_, :], in0=lst[:, :, 0], in1=lst[:, :, 1], op=ALU.add),
                             reads=["lst"], writes=["idx_f"])
                        S.op("dve", lambda e: e.tensor_tensor(out=gate_s[:, ex_, :], in0=lst[:, :, 2], in1=lst[:, :, 3], op=ALU.add),
                             reads=["lst"], writes=["gate_s"])
                        S.op("dve", lambda e: e.tensor_tensor(out=gate_s[:, ex_, :], in0=gate_s[:, ex_, :], in1=lst[:, :, 4], op=ALU.add),
                             reads=["lst", "gate_s"], writes=["gate_s"])
                    S.op("dve", lambda e: e.tensor_copy(out=idx_i[:], in_=idx_f[:]), reads=["idx_f"], writes=["idx_i"])
                    if debug and ll == 0:
                        S.dma("sp", lambda e: e.dma_start(out=dbg_idx, in_=idx_i[:]), reads=["idx_i"], writes=["dbgidx"])
                        S.dma("sp", lambda e: e.dma_start(out=dbg_gate, in_=gate_s[:]), reads=["gate_s"], writes=["dbggate"])
                        S.dma("sp", lambda e: e.dma_start(out=dbg_pos, in_=pos[:]), reads=["pos"], writes=["dbgpos"])
                S.barrier()
                esH = ExitStack()
                with esH:
                    g2rows = build_gate_rows(esH, 40, "g2r")
                    wb = [[SB(esH, "wb%d_%d" % (m, i), [128, 8, D], BF16) for m in range(3)] for i in range(2)]
                    xg = [[SB(esH, "xg%d_%d" % (st_, i), [128, D], BF16) for st_ in range(5)] for i in range(2)]
                    xgT = [SB(esH, "xgT%d" % i, [128, 8, 544], BF16) for i in range(2)]
                    hidT = SB(esH, "hidT", [128, 8, 544], BF16)
                    s1 = SB(esH, "s1", [128, 544], F32)
                    ysc = [SB(esH, "ysc%d" % i, [128, D], F32) for i in range(4)]
                    pxT = [PS(esH, "pxT%d" % i, [128, 8, 128], BF16) for i in range(1)]
                    ph1 = [PS(esH, "ph1_%d" % i, [128, 512], F32) for i in range(2)]
                    ph3 = [PS(esH, "ph3_%d" % i, [128, 512], F32) for i in range(2)]
                    phc = PS(esH, "phc", [128, 2, 32], F32)
                    py = [PS(esH, "pyH%d" % i, [128, 512], F32) for i in range(2)]
                    wsrc = (w1, w3, w2)

                    def issue_loads(ex_):
                        i = ex_ % 2
                        for st_ in range(5):
                            npart = 128 if st_ < 4 else CAP_C
                            S.dma("pool", lambda e: e.indirect_dma_start(
                                out=xg[i][st_][0:npart, :], out_offset=None, in_=xh2[:, :],
                                in_offset=bass.IndirectOffsetOnAxis(ap=idx_i[0:npart, ex_, st_:st_ + 1], axis=0),
                                bounds_check=bcreg, oob_is_err=False),
                                reads=["idx_i"] + [("xh2", j) for j in range(NT)], writes=["xg%d_%d" % (st_, i)])
                        for m in range(3):
                            S.dma("pool", lambda e: e.dma_start(out=wb[i][m][:], in_=wsrc[m][ll, ex_].rearrange("(k p) c -> p k c", p=128)),
                                  writes=["wb%d_%d" % (m, i)])

                    def prep_tile(ex_, st_):
                        i = ex_ % 2
                        npart = 128 if st_ < 4 else CAP_C
                        s = 0 if st_ < 4 else 1
                        c0 = st_ * 128
                        npx[0] += 1
                        pb = pxT[0]
                        pk = "pxT0"
                        for k in range(8):
                            S.op("pe", lambda e: e.transpose(pb[:, k, 0:npart], xg[i][st_][0:npart, k * 128:(k + 1) * 128],
                                                             identb[0:npart, 0:npart]),
                                 reads=["xg%d_%d" % (st_, i), "identb"], writes=[pk])
                        for k in range(8):
                            if k % 2 == 0:
                                S.op("act", lambda e: e.activation(out=xgT[i][:, k, c0:c0 + npart], in_=pb[:, k, 0:npart], func=AF.Identity,
                                                                   bias=modT[:, 24 + k, s:s + 1], scale=modT[:, 32 + k, s:s + 1]),
                                     reads=[pk, "modT"], writes=["xgT%d" % i])
                            else:
                                S.op("dve", lambda e: e.tensor_scalar(out=xgT[i][:, k, c0:c0 + npart], in0=pb[:, k, 0:npart],
                                                                      scalar1=modT[:, 32 + k, s:s + 1], scalar2=modT[:, 24 + k, s:s + 1],
                                                                      op0=ALU.mult, op1=ALU.add),
                                     reads=[pk, "modT"], writes=["xgT%d" % i])

                    npx = [0]
                    issue_loads(0)
                    for st_ in range(5):
                        prep_tile(0, st_)
                    nys = 0
                    npy = 0
                    for ex_ in range(NE):
                        i = ex_ % 2
                        xgTi = xgT[i]
                        xk_ = "xgT%d" % i
                        if ex_ + 1 < NE:
                            issue_loads(ex_ + 1)
                        for fc in range(8):
                            fb = fc % 2
                            for m, ph in ((0, ph1[fb]), (1, ph3[fb])):
                                for k in range(8):
                                    S.op("pe", lambda e: e.matmul(ph[:], lhsT=wb[i][m][:, k, fc * 128:(fc + 1) * 128], rhs=xgTi[:, k, 0:512],
                                                                  start=(k == 0), stop=(k == 7)),
                                         reads=[xk_, "wb%d_%d" % (m, i)], writes=["ph%d_%d" % (m, fb)])
                                for k in range(8):
                                    S.op("pe", lambda e: e.matmul(phc[:, m, :], lhsT=wb[i][m][:, k, fc * 128:(fc + 1) * 128], rhs=xgTi[:, k, 512:544],
                                                                  start=(k == 0), stop=(k == 7)),
                                         reads=[xk_, "wb%d_%d" % (m, i)], writes=["phc"])
                            if ex_ + 1 < NE and fc < 5:
                                prep_tile(ex_ + 1, fc)
                            S.op("act", lambda e: e.activation(out=s1[:, 0:512], in_=ph1[fb][:], func=AF.Silu), reads=["ph0_%d" % fb], writes=["s1a"])
                            S.op("act", lambda e: e.activation(out=s1[:, 512:544], in_=phc[:, 0, :], func=AF.Silu), reads=["phc"], writes=["s1b"])
                            S.op("dve", lambda e: e.tensor_tensor(out=hidT[:, fc, 0:512], in0=s1[:, 0:512], in1=ph3[fb][:], op=ALU.mult),
                                 reads=["s1a", "ph1_%d" % fb], writes=["hidT"])
                            S.op("dve", lambda e: e.tensor_tensor(out=hidT[:, fc, 512:544], in0=s1[:, 512:544], in1=phc[:, 1, :], op=ALU.mult),
                                 reads=["s1b", "phc"], writes=["hidT"])
                        for st_ in range(5):
                            npart = 128 if st_ < 4 else CAP_C
                            s = 0 if st_ < 4 else 1
                            c0 = st_ * 128
                            yb = ysc[nys % 4]
                            yk = "ysc%d" % (nys % 4)
                            nys += 1
                            for half in range(2):
                                pyb = py[npy % 2]
                                pyk = "pyH%d" % (npy % 2)
                                npy += 1
                                for fc in range(8):
                                    S.op("pe", lambda e: e.matmul(pyb[0:npart, :], lhsT=hidT[:, fc, c0:c0 + npart],
                                                                  rhs=wb[i][2][:, fc, half * 512:(half + 1) * 512],
                                                                  start=(fc == 0), stop=(fc == 7)),
                                         reads=["hidT", "wb2_%d" % i], writes=[pyk])
                                S.op("dve", lambda e: e.scalar_tensor_tensor(out=yb[0:npart, half * 512:(half + 1) * 512], in0=pyb[0:npart, :],
                                                                             scalar=gate_s[0:npart, ex_, st_:st_ + 1],
                                                                             in1=g2rows[s][0:npart, half * 512:(half + 1) * 512],
                                                                             op0=ALU.mult, op1=ALU.mult),
                                     reads=[pyk, "gate_s", "g2r_%d" % s], writes=[(yk, half)])
                            S.dma("pool", lambda e: e.indirect_dma_start(
                                out=macc[:, :], out_offset=bass.IndirectOffsetOnAxis(ap=idx_i[0:npart, ex_, st_:st_ + 1], axis=0),
                                in_=yb[0:npart, :], in_offset=None, bounds_check=bcreg, oob_is_err=False,
                                compute_op=ALU.add),
                                reads=[(yk, 0), (yk, 1), "idx_i"] + [("msc", ex_ - 1, k) for k in range(5)],
                                writes=[("msc", ex_, st_)])
                S.barrier()
            if debug and ll == 0:
                S.dma("sp", lambda e: e.dma_start(out=dbg_macc, in_=macc), writes=["dbgmacc"])
                S.barrier()
            esI = ExitStack()
            with esI:
                lnr2 = SB(esI, "lnr2", [128, 2, D], F32)
                mt = [SB(esI, "mt%d" % i, [128, D], F32) for i in range(4)]
                xo = [SB(esI, "xo%d" % i, [128, D], F32) for i in range(4)]
                st = SB(esI, "stI", [128, 2, 6], F32)
                mv = SB(esI, "mvI", [128, 2], F32)
                rstd = SB(esI, "rstdI", [128, 1], F32)
                nmr = SB(esI, "nmr", [128, 1], F32)
                S.dma("sp", lambda e: e.dma_start(out=lnr2[:], in_=ln_in[ll:ll + 1, 2:4, :].broadcast_to([128, 2, D])),
                      writes=["lnr2"])
                stI = [st, SB(esI, "stI_b", [128, 2, 6], F32)]
                mvI = [mv, SB(esI, "mvI_b", [128, 2], F32)]
                rsI = [rstd, SB(esI, "rstdI_b", [128, 1], F32)]
                nmI = [nmr, SB(esI, "nmr_b", [128, 1], F32)]
                for j0 in range(0, NT, 2):
                    pair = [(j0, 0), (j0 + 1, 1)]
                    bo = (j0 // 2 % 2) * 2
                    for (j, t) in pair:
                        tok = slice(j * 128, (j + 1) * 128)
                        S.dma("sp", lambda e: e.dma_start(out=mt[bo + t][:], in_=macc[tok, :]), reads=[("macc", j)], writes=[("mt", bo + t)])
                    for c_ in range(2):
                        for (j, t) in pair:
                            S.op("dve", lambda e: e.bn_stats(stI[t][:, c_, :], mt[bo + t][:, c_ * 512:(c_ + 1) * 512]),
                                 reads=[("mt", bo + t)], writes=[("Ist", t, c_)])
                    for (j, t) in pair:
                        S.op("dve", lambda e: e.bn_aggr(mvI[t][:], stI[t][:]), reads=[("Ist", t, 0), ("Ist", t, 1)], writes=[("Imv", t)])
                    for (j, t) in pair:
                        S.op("act", lambda e: e.activation(out=rsI[t][:], in_=mvI[t][:, 1:2], func=AF.Sqrt, bias=epsc[:], scale=1.0),
                             reads=[("Imv", t), "epsc"], writes=[("Irs", t)])
                    for (j, t) in pair:
                        S.op("dve", lambda e: e.reciprocal(rsI[t][:], rsI[t][:]), reads=[("Irs", t)], writes=[("Irs", t)])
                    for (j, t) in pair:
                        S.op("dve", lambda e: e.scalar_tensor_tensor(out=nmI[t][:], in0=mvI[t][:, 0:1], scalar=-1.0, in1=rsI[t][:],
                                                                     op0=ALU.mult, op1=ALU.mult),
                             reads=[("Imv", t), ("Irs", t)], writes=[("Inm", t)])
                    for (j, t) in pair:
                        S.op("act", lambda e: e.activation(out=xo[bo + t][:], in_=mt[bo + t][:], func=AF.Identity, bias=nmI[t][:], scale=rsI[t][:]),
                             reads=[("mt", bo + t), ("Inm", t), ("Irs", t)], writes=[("xo", bo + t)])
                    for (j, t) in pair:
                        S.op("dve", lambda e: e.tensor_tensor(out=xo[bo + t][:], in0=xo[bo + t][:], in1=lnr2[:, 0, :], op=ALU.mult),
                             reads=[("xo", bo + t), "lnr2"], writes=[("xo", bo + t)])
                    for (j, t) in pair:
                        S.op("dve", lambda e: e.tensor_tensor(out=xo[bo + t][:], in0=xo[bo + t][:], in1=lnr2[:, 1, :], op=ALU.add),
                             reads=[("xo", bo + t), "lnr2"], writes=[("xo", bo + t)])
                    for (j, t) in pair:
                        S.dma("sp", lambda e: e.dma_start(out=dst_tile(ll, j), in_=xo[bo + t][:]), reads=[("xo", bo + t)], writes=[dst_key(ll, j)])
            S.barrier()
        print("instructions", S.n_inst, "waits", S.n_wait, flush=True)
    return nc


def _band_tables():
    wins = (2, 4, 8, 16)
    Ls = 384
    band = np.zeros((128, 4, 5, 128), np.float32)
    for g, w in enumerate(wins):
        A = np.zeros((Ls, Ls), np.float64)
        for t in range(Ls):
            lo = min(max(t - w // 2, 0), Ls)
            hi = min(max(t + w // 2, 0), Ls)
            A[t, lo:hi] = 1.0 / (hi - lo)
            A[t, t] -= 1.0
        AT = A.T
        band[:, g, 0, :] = AT[0:128, 128:256]
        band[:, g, 1, :] = AT[128:256, 128:256]
        band[:, g, 2, :] = AT[0:128, 0:128]
        band[:, g, 3, :] = AT[256:384, 256:384]
        band[:, g, 4, :] = AT[256:384, 128:256]
    return band


def _rope_tables():
    t = np.arange(SEQ)
    r = (t // 64).astype(np.float32)
    col = (t % 64).astype(np.float32)
    inv = (np.float32(10000.0) ** (-np.arange(16, dtype=np.float32) / np.float32(16))).astype(np.float32)
    ang = np.concatenate([r[:, None] * inv, col[:, None] * inv], axis=-1).astype(np.float32)
    cs = np.concatenate([np.cos(ang), np.sin(ang)], axis=-1).astype(np.float32)
    return np.ascontiguousarray(cs.reshape(NTL, 128, 64).transpose(1, 0, 2))


def _prep_common(inp, layers):
    L = len(layers)
    sl = lambda a: np.ascontiguousarray(np.asarray(a)[layers])
    pool_w = sl(inp["pool_w"])
    wbd = np.zeros((L, 2, 128, 128), np.float32)
    for c in range(2):
        for gi in range(2):
            wbd[:, c, gi * 64:(gi + 1) * 64, gi * 64:(gi + 1) * 64] = pool_w[:, 2 * c + gi]
    com = {
        "w_mod": sl(inp["w_mod"]),
        "b_modT": np.ascontiguousarray(sl(inp["b_mod"]).reshape(L, 48, 128).transpose(0, 2, 1)),
        "w_in": sl(inp["w_in"]),
        "wbd": wbd,
        "pscT": np.ascontiguousarray(sl(inp["pool_scale"]).reshape(L, 2, 128).transpose(0, 2, 1)),
        "band": _band_tables(),
        "sgu_g": np.ascontiguousarray(sl(inp["sgu_g"]).reshape(L, 256)),
        "sgu_wT": np.ascontiguousarray(sl(inp["sgu_w"]).transpose(0, 3, 1, 2)),
        "sgu_bT": np.ascontiguousarray(sl(inp["sgu_b"]).transpose(0, 2, 1)),
        "qkg": np.ascontiguousarray(np.concatenate([np.tile(sl(inp["q_g"]), (1, 8)), np.tile(sl(inp["k_g"]), (1, 2))], axis=1)),
        "w_out": sl(inp["w_out"]),
        "ln": np.ascontiguousarray(np.stack([sl(inp["ln1_g"]), sl(inp["ln1_b"]), sl(inp["ln2_g"]), sl(inp["ln2_b"])], axis=1)),
        "w_router": np.ascontiguousarray(sl(inp["w_router"]).reshape(L, 8, 128, NE).transpose(0, 2, 1, 3)),
        "w1": sl(inp["w1"]), "w3": sl(inp["w3"]), "w2": sl(inp["w2"]),
        "cs": _rope_tables(),
    }
    return com


_NC_CACHE = {}


def _run(inp, x, ctx, layers, n_cores=8):
    L = len(layers)
    if L not in _NC_CACHE:
        _NC_CACHE[L] = build(L)
    nc = _NC_CACHE[L]
    com = _prep_common(inp, layers)
    c = np.asarray(inp["c"], np.float32)
    c_ctx = np.asarray(inp["c_ctx"], np.float32)
    in_maps = []
    for b in range(n_cores):
        cond = np.stack([c[b].reshape(8, 128).T, c_ctx.reshape(8, 128).T], axis=-1)
        m = dict(com)
        m["x"] = np.ascontiguousarray(x[b])
        m["ctx"] = np.ascontiguousarray(ctx[b])
        m["cond"] = np.ascontiguousarray(cond.astype(np.float32))
        in_maps.append(m)
    res = run_bass_kernel_spmd(nc, in_maps, core_ids=list(range(n_cores)))
    xo = np.stack([np.asarray(r["out"]) for r in res.results], 0)
    co = np.stack([np.asarray(r["ctx_out"]) for r in res.results], 0)
    return xo, co


LAYERS_PER_LAUNCH = 4


def kernel(**inputs):
    inp = {k: np.asarray(v) for k, v in inputs.items()}
    x = np.asarray(inp["x"], np.float32)
    ctx = np.asarray(inp["ctx"], np.float32)
    for l0 in range(0, DEPTH, LAYERS_PER_LAUNCH):
        x, ctx = _run(inp, x, ctx, list(range(l0, l0 + LAYERS_PER_LAUNCH)))
    return x.astype(np.float32)
```

```python
import numpy as np
from contextlib import ExitStack
import concourse.bass as bass
import concourse.mybir as mybir
from concourse.bass_utils import run_bass_kernel_spmd

F32 = mybir.dt.float32
BF16 = mybir.dt.bfloat16
I32 = mybir.dt.int32
AF = mybir.ActivationFunctionType
ALU = mybir.AluOpType
AX = mybir.AxisListType

DEPTH = 4
D = 1024
SEQ = 4096
CTX = 256
NT = 34
NTL = 32
NTOK = SEQ + CTX
NE = 16
CAP_L = 512
CAP_C = 32
ALPHA = float((2 * DEPTH) ** 0.25)
EPS = 1e-6
BIG = 100000.0


class Sched:
    def __init__(self, nc, es, n_dma_sems=48):
        self.nc = nc
        self.eng = {"pe": nc.tensor, "act": nc.scalar, "dve": nc.vector, "pool": nc.gpsimd, "sp": nc.sync}
        self.sem = {k: es.enter_context(nc.semaphore("prog_" + k)) for k in self.eng}
        self.cnt = {k: 0 for k in self.eng}
        self.dsem = [es.enter_context(nc.semaphore("dma%d" % i)) for i in range(n_dma_sems)]
        self.dtot = [0] * n_dma_sems
        self.dnext = 0
        self.dnext_sw = 0
        self.waited = {k: {} for k in self.eng}
        self.res = {}
        self.n_inst = 0
        self.n_wait = 0

    def _semobj(self, key):
        return self.sem[key] if isinstance(key, str) else self.dsem[key]

    def _collect(self, reads, writes):
        deps = {}

        def add(d):
            if d is None:
                return
            k, v = d
            if deps.get(k, 0) < v:
                deps[k] = v
        for r in reads:
            e = self.res.get(r)
            if e is not None:
                add(e["w"])
        for w in writes:
            e = self.res.get(w)
            if e is not None:
                add(e["w"])
                for k, v in e["r"].items():
                    add((k, v))
        return deps

    def _wait(self, F, deps, skip_self=False):
        for k, v in deps.items():
            if skip_self and k == F:
                continue
            if self.waited[F].get(k, 0) < v:
                self.eng[F].wait_ge(self._semobj(k), v)
                self.waited[F][k] = v
                self.n_wait += 1

    def _update(self, dep, reads, writes):
        k, v = dep
        for r in reads:
            e = self.res.setdefault(r, {"w": None, "r": {}})
            if e["r"].get(k, 0) < v:
                e["r"][k] = v
        for w in writes:
            self.res[w] = {"w": dep, "r": {}}

    def op(self, F, fn, reads=(), writes=(), skip_self=None):
        if skip_self is None:
            skip_self = (F == "pe")
        deps = self._collect(reads, writes)
        self._wait(F, deps, skip_self=skip_self)
        inst = fn(self.eng[F])
        self.cnt[F] += 1
        inst.then_inc(self.sem[F], 1)
        self._update((F, self.cnt[F]), reads, writes)
        self.n_inst += 1
        return inst

    def dma(self, Q, fn, reads=(), writes=()):
        deps = self._collect(reads, writes)
        self._wait(Q, deps)
        half = len(self.dsem) // 2
        if Q == "pool":
            i = half + self.dnext_sw
            self.dnext_sw = (self.dnext_sw + 1) % (len(self.dsem) - half)
        else:
            i = self.dnext
            self.dnext = (self.dnext + 1) % half
        if self.dtot[i] > 0 and self.waited[Q].get(i, 0) < self.dtot[i]:
            self.eng[Q].wait_ge(self.dsem[i], self.dtot[i])
            self.waited[Q][i] = self.dtot[i]
            self.n_wait += 1
        inst = fn(self.eng[Q])
        self.dtot[i] += 16
        inst.then_inc(self.dsem[i], 16)
        self._update((i, self.dtot[i]), reads, writes)
        self.n_inst += 1
        return inst

    def barrier(self):
        for F in self.eng:
            for i, t in enumerate(self.dtot):
                if t > 0 and self.waited[F].get(i, 0) < t:
                    self.eng[F].wait_ge(self.dsem[i], t)
                    self.waited[F][i] = t
            for k in self.eng:
                if k != F and self.cnt[k] > 0 and self.waited[F].get(k, 0) < self.cnt[k]:
                    self.eng[F].wait_ge(self.sem[k], self.cnt[k])
                    self.waited[F][k] = self.cnt[k]
        self.res = {}


def build(L, debug=False):
    nc = bass.Bass("TRN2", target_bir_lowering=False)

    def DT(name, shape, dt=F32, kind="ExternalInput"):
        return nc.dram_tensor(name, shape, dt, kind=kind).ap()

    x_in = DT("x", [SEQ, D])
    ctx_in = DT("ctx", [CTX, D])
    cond_in = DT("cond", [128, 8, 2])
    w_mod = DT("w_mod", [L, D, 6 * D])
    b_modT = DT("b_modT", [L, 128, 48])
    w_in = DT("w_in", [L, D, 1536])
    wbd_in = DT("wbd", [L, 2, 128, 128])
    pscT_in = DT("pscT", [L, 128, 2])
    band_in = DT("band", [128, 4, 5, 128])
    sgug_in = DT("sgu_g", [L, 256])
    sguwT_in = DT("sgu_wT", [L, 128, 4, 128])
    sgubT_in = DT("sgu_bT", [L, 128, 4])
    qkg_in = DT("qkg", [L, 640])
    w_out = DT("w_out", [L, D, D])
    ln_in = DT("ln", [L, 4, D])
    wr_in = DT("w_router", [L, 128, 8, NE])
    w1 = DT("w1", [L, NE, D, D])
    w3 = DT("w3", [L, NE, D, D])
    w2 = DT("w2", [L, NE, D, D])
    cs_in = DT("cs", [128, NTL, 64])
    out = DT("out", [SEQ, D], kind="ExternalOutput")
    ctx_out = DT("ctx_out", [CTX, D], kind="ExternalOutput")
    if debug:
        dbg_x1 = DT("dbg_x1", [NTOK, D], kind="ExternalOutput")
        dbg_aff = DT("dbg_aff", [128, NT, NE], kind="ExternalOutput")
        dbg_mix = DT("dbg_mix", [128, 4, NTOK], BF16, kind="ExternalOutput")
        dbg_idx = DT("dbg_idx", [128, NE, 5], I32, kind="ExternalOutput")
        dbg_gate = DT("dbg_gate", [128, NE, 5], kind="ExternalOutput")
        dbg_macc = DT("dbg_macc", [NTOK, D], kind="ExternalOutput")
        dbg_pos = DT("dbg_pos", [128, NT, NE], kind="ExternalOutput")
    xs = DT("xs", [NTOK, D], kind="Internal")
    macc = DT("macc", [NTOK, D], kind="Internal")
    xh2 = DT("xh2", [NTOK, D], BF16, kind="Internal")

    def src_tile(ll, j):
        if ll == 0:
            return x_in[j * 128:(j + 1) * 128, :] if j < NTL else ctx_in[(j - NTL) * 128:(j - NTL + 1) * 128, :]
        return xs[j * 128:(j + 1) * 128, :]

    def dst_tile(ll, j):
        if ll == L - 1:
            return out[j * 128:(j + 1) * 128, :] if j < NTL else ctx_out[(j - NTL) * 128:(j - NTL + 1) * 128, :]
        return xs[j * 128:(j + 1) * 128, :]

    def src_key(ll, j):
        return ("xin", j) if ll == 0 else ("xs", j)

    def dst_key(ll, j):
        return ("xout", j) if ll == L - 1 else ("xs", j)

    es0 = ExitStack()
    with es0:
        S = Sched(nc, es0)
        bcreg = es0.enter_context(nc.gpsimd.register("bcreg"))
        nc.gpsimd.reg_mov(bcreg, NTOK - 1)

        uid = [0]

        def SB(es, name, shape, dt):
            uid[0] += 1
            return es.enter_context(nc.sbuf_tensor("s%d_%s" % (uid[0], name), shape, dt))

        def PS(es, name, shape, dt):
            uid[0] += 1
            return es.enter_context(nc.psum_tensor("p%d_%s" % (uid[0], name), shape, dt))

        identf = SB(es0, "identf", [128, 128], F32)
        identb = SB(es0, "identb", [128, 128], BF16)
        ones_f = SB(es0, "ones_f", [128, 128], F32)
        lmat = SB(es0, "lmat", [128, 128], F32)
        iota_row = SB(es0, "iota_row", [128, 512], mybir.dt.float16)
        tokid = SB(es0, "tokid", [128, NT], F32)
        jmp = SB(es0, "jmp", [128, 128], F32)
        modT = SB(es0, "modT", [128, 48, 2], F32)
        aff_all = SB(es0, "aff_all", [128, NT, NE], F32)
        epsc = SB(es0, "epsc", [128, 1], F32)
        S.op("pool", lambda e: e.memset(epsc[:], EPS), writes=["epsc"])
        S.op("pool", lambda e: e.iota(jmp[:], [[1, 128]], base=0, channel_multiplier=-1,
                                      allow_small_or_imprecise_dtypes=True), writes=["jmp"])
        S.op("pool", lambda e: e.tensor_single_scalar(out=identf[:], in_=jmp[:], scalar=0.0, op=ALU.is_equal),
             reads=["jmp"], writes=["identf"])
        S.op("pool", lambda e: e.tensor_single_scalar(out=identb[:], in_=jmp[:], scalar=0.0, op=ALU.is_equal),
             reads=["jmp"], writes=["identb"])
        S.op("pool", lambda e: e.tensor_single_scalar(out=lmat[:], in_=jmp[:], scalar=0.0, op=ALU.is_gt),
             reads=["jmp"], writes=["lmat"])
        S.op("pool", lambda e: e.memset(ones_f[:], 1.0), writes=["ones_f"])
        S.op("pool", lambda e: e.iota(iota_row[:], [[1, 512]], base=0, channel_multiplier=0,
                                      allow_small_or_imprecise_dtypes=True), writes=["iota_row"])
        S.op("pool", lambda e: e.iota(tokid[:], [[128, NT]], base=0, channel_multiplier=1,
                                      allow_small_or_imprecise_dtypes=True), writes=["tokid"])
        tokA = SB(es0, "tokA", [128, NT], F32)
        tokB = SB(es0, "tokB", [128, 1], F32)
        S.op("pool", lambda e: e.iota(tokA[:], [[128, NT]], base=0, channel_multiplier=0,
                                      allow_small_or_imprecise_dtypes=True), writes=["tokA"])
        S.op("pool", lambda e: e.iota(tokB[:], [[0, 1]], base=0, channel_multiplier=1,
                                      allow_small_or_imprecise_dtypes=True), writes=["tokB"])

        def ln_stats(es_tiles, src, key_src, tag):
            st, mv, rstd = es_tiles
            S.op("dve", lambda e: e.bn_stats(st[:, 0, :], src[:, 0:512]), reads=[key_src], writes=[tag + "st0"])
            S.op("dve", lambda e: e.bn_stats(st[:, 1, :], src[:, 512:1024]), reads=[key_src], writes=[tag + "st1"])
            S.op("dve", lambda e: e.bn_aggr(mv[:], st[:]), reads=[tag + "st0", tag + "st1"], writes=[tag + "mv"])
            S.op("act", lambda e: e.activation(out=rstd[:], in_=mv[:, 1:2], func=AF.Sqrt, bias=epsc[:], scale=1.0),
                 reads=[tag + "mv", "epsc"], writes=[tag + "rstd"])
            S.op("dve", lambda e: e.reciprocal(rstd[:], rstd[:]), reads=[tag + "rstd"], writes=[tag + "rstd"])

        for ll in range(L):
            def emit_phase_A(la, esA_, qa):
                condT = SB(esA_, "condT", [128, 8, 2], F32)
                bmT = SB(esA_, "bmT", [128, 48], F32)
                wblk = [SB(esA_, "wblk%d" % i, [128, 8, 512], F32) for i in range(2)]
                pmod = PS(esA_, "pmod", [128, 48, 2], F32)
                S.dma("sp", lambda e: e.dma_start(out=condT[:], in_=cond_in), writes=["condT"])
                S.dma("sp", lambda e: e.dma_start(out=bmT[:], in_=b_modT[la]), writes=["bmT"])
                S.op("act", lambda e: e.activation(out=condT[:], in_=condT[:], func=AF.Silu),
                     reads=["condT"], writes=["condT"])
                wm = w_mod[la].rearrange("(k p) c -> p k c", p=128)
                for cb in range(12):
                    wb_ = wblk[cb % 2]
                    wk = "wblk%d" % (cb % 2)
                    S.dma(qa, lambda e: e.dma_start(out=wb_[:], in_=wm[:, :, cb * 512:(cb + 1) * 512]), writes=[wk])
                    for sub in range(4):
                        j = cb * 4 + sub
                        for k in range(8):
                            S.op("pe", lambda e: e.matmul(pmod[:, j, :], lhsT=wb_[:, k, sub * 128:(sub + 1) * 128],
                                                          rhs=condT[:, k, :], start=(k == 0), stop=(k == 7)),
                                 reads=[wk, "condT"], writes=["pmod"])
                def finish_A():
                    S.op("dve", lambda e: e.tensor_tensor(out=modT[:], in0=pmod[:],
                                                          in1=bmT[:].unsqueeze(2).broadcast_to([128, 48, 2]), op=ALU.add),
                         reads=["pmod", "bmT"], writes=["modT"])
                    for base in (8, 32):
                        S.op("dve", lambda e: e.tensor_scalar_add(modT[:, base:base + 8, :], modT[:, base:base + 8, :], 1.0),
                             reads=["modT"], writes=["modT"])
                return finish_A
            if ll == 0:
                esA0 = ExitStack()
                with esA0:
                    emit_phase_A(0, esA0, "sp")()
                    S.barrier()

            def build_gate_rows(es, vbase, tagname):
                rows = [SB(es, "%s_%d" % (tagname, s), [128, D], F32) for s in range(2)]
                est = ExitStack()
                with est:
                  diag = [SB(est, "%s_dg%d" % (tagname, i), [128, 128], F32) for i in range(2)]
                  pg = PS(est, tagname + "_pg", [128, D], F32)
                  n = 0
                  for s in range(2):
                    for dc in range(8):
                        dg = diag[n % 2]
                        dk = "%s_dg%d" % (tagname, n % 2)
                        n += 1
                        S.op("dve", lambda e: e.tensor_scalar(out=dg[:], in0=identf[:], scalar1=modT[:, vbase + dc, s:s + 1],
                                                              scalar2=None, op0=ALU.mult),
                             reads=["identf", "modT"], writes=[dk])
                        S.op("pe", lambda e: e.matmul(pg[:, dc * 128:(dc + 1) * 128], lhsT=ones_f[:], rhs=dg[:],
                                                      start=True, stop=True),
                             reads=[dk, "ones_f"], writes=[tagname + "_pg"])
                    S.op("act", lambda e: e.activation(out=rows[s][:], in_=pg[:], func=AF.Copy),
                         reads=[tagname + "_pg"], writes=["%s_%d" % (tagname, s)])
                  S.barrier()
                return rows

            esBE = ExitStack()
            with esBE:
                mixps = SB(esBE, "mixps", [128, 4, NTOK], BF16)
                qT_all = SB(esBE, "qT_all", [128, 4, NTOK], BF16)
                kT_all = SB(esBE, "kT_all", [128, 2, NTOK], BF16)
                S.op("pool", lambda e: e.memset(kT_all[:], 0.0), writes=["kT_zero"])
                v_all = SB(esBE, "v_all", [128, NT, 2, 65], BF16)
                S.op("pool", lambda e: e.memset(v_all[:, :, :, 64:65], 1.0), writes=["v_ones"])
                esBC = ExitStack()
                with esBC:
                    p_all = SB(esBC, "p_all", [128, NT, 256], BF16)
                    esB = ExitStack()
                    with esB:
                        winb = SB(esB, "winb", [128, 8, 1536], BF16)
                        cs = SB(esB, "cs", [128, NTL, 64], F32)
                        sgug = SB(esB, "sgug", [128, 256], F32)
                        wsT = SB(esB, "wsT", [128, 4, 128], BF16)
                        sbT = SB(esB, "sbT", [128, 4], F32)
                        qkg = SB(esB, "qkg", [128, 640], F32)
                        xt = [SB(esB, "xt%d" % i, [128, D], F32) for i in range(2)]
                        st = SB(esB, "st", [128, 2, 6], F32)
                        mv = SB(esB, "mv", [128, 2], F32)
                        rstd = SB(esB, "rstd", [128, 1], F32)
                        xb = SB(esB, "xb", [128, D], BF16)
                        hT = SB(esB, "hT", [128, 8, 128], BF16)
                        gu = SB(esB, "gu", [128, 256], BF16)
                        gv = SB(esB, "gv", [128, 256], F32)
                        stv = SB(esB, "stv", [128, 4, 6], F32)
                        mvv = SB(esB, "mvv", [128, 4, 2], F32)
                        rsv = SB(esB, "rsv", [128, 4], F32)
                        vn = SB(esB, "vn", [128, 256], F32)
                        vh = SB(esB, "vh", [128, 256], BF16)
                        sgo = SB(esB, "sgo", [128, 256], BF16)
                        sq = SB(esB, "sq", [128, 640], F32)
                        ss = SB(esB, "ss", [128, 10], F32)
                        qn = SB(esB, "qn", [128, 10, 64], F32)
                        ta = SB(esB, "ta", [128, 10, 32], F32)
                        tb = SB(esB, "tb", [128, 10, 32], F32)
                        qr = SB(esB, "qr", [128, 640], BF16)
                        pT = PS(esB, "pT", [128, 8, 128], BF16)
                        pz = PS(esB, "pz", [128, 1536], F32)
                        psg = PS(esB, "psg", [128, 256], F32)
                        ptr = PS(esB, "ptr", [128, 7, 128], BF16)
                        S.dma("pool", lambda e: e.dma_start(out=winb[:], in_=w_in[ll].rearrange("(k p) c -> p k c", p=128)),
                              writes=["winb"])
                        S.dma("pool", lambda e: e.dma_start(out=wsT[:], in_=sguwT_in[ll]), writes=["wsT"])
                        S.dma("sp", lambda e: e.dma_start(out=cs[:], in_=cs_in), writes=["cs"])
                        S.dma("sp", lambda e: e.dma_start(out=sgug[:], in_=sgug_in[ll:ll + 1, :].broadcast_to([128, 256])),
                              writes=["sgug"])
                        S.dma("sp", lambda e: e.dma_start(out=qkg[:], in_=qkg_in[ll:ll + 1, :].broadcast_to([128, 640])),
                              writes=["qkg"])
                        S.dma("sp", lambda e: e.dma_start(out=sbT[:], in_=sgubT_in[ll]), writes=["sbT"])
                        def front_B(j):
                            s = 0 if j < NTL else 1
                            tok = slice(j * 128, (j + 1) * 128)
                            xtj = xt[j % 2]
                            xk = "xt%d" % (j % 2)
                            S.dma("sp", lambda e: e.dma_start(out=xtj[:], in_=src_tile(ll, j)),
                                  reads=[src_key(ll, j)], writes=[xk])
                            ln_stats((st, mv, rstd), xtj, xk, "B")
                            S.op("dve", lambda e: e.tensor_scalar(out=xb[:], in0=xtj[:], scalar1=mv[:, 0:1], scalar2=rstd[:],
                                                                  op0=ALU.subtract, op1=ALU.mult),
                                 reads=[xk, "Bmv", "Brstd"], writes=["xb"])
                            for k in range(8):
                                S.op("pe", lambda e: e.transpose(pT[:, k, :], xb[:, k * 128:(k + 1) * 128], identb[:]),
                                     reads=["xb", "identb"], writes=["pT"])
                            for k in range(8):
                                S.op("act", lambda e: e.activation(out=hT[:, k, :], in_=pT[:, k, :], func=AF.Identity,
                                                                   bias=modT[:, 0 + k, s:s + 1], scale=modT[:, 8 + k, s:s + 1]),
                                     reads=["pT", "modT"], writes=[("hT", k)])
                        gu2 = [gu, SB(esB, "gu_b", [128, 256], BF16)]
                        gv2 = [gv, SB(esB, "gv_b", [128, 256], F32)]
                        stv2 = [stv, SB(esB, "stv_b", [128, 4, 6], F32)]
                        mvv2 = [mvv, SB(esB, "mvv_b", [128, 4, 2], F32)]
                        rsv2 = [rsv, SB(esB, "rsv_b", [128, 4], F32)]
                        vn2 = [vn, SB(esB, "vn_b", [128, 256], F32)]
                        vh2 = [vh, SB(esB, "vh_b", [128, 256], BF16)]
                        sgo2 = [sgo, SB(esB, "sgo_b", [128, 256], BF16)]
                        sq2 = [sq, SB(esB, "sq_b", [128, 640], F32)]
                        zq2 = [SB(esB, "zq_a", [128, 640], F32), SB(esB, "zq_b", [128, 640], F32)]
                        ss2 = [ss, SB(esB, "ss_b", [128, 10], F32)]
                        qn2 = [qn, SB(esB, "qn_b", [128, 10, 64], F32)]
                        ta2 = [ta, SB(esB, "ta_b", [128, 10, 32], F32)]
                        tb2 = [tb, SB(esB, "tb_b", [128, 10, 32], F32)]
                        qr2 = [qr, SB(esB, "qr_b", [128, 640], BF16)]
                        psg2 = [psg, PS(esB, "psg_b", [128, 256], F32)]
                        ptr2 = [ptr, PS(esB, "ptr_b", [128, 7, 128], BF16)]

                        def mm_and_evac(j, t):
                            for cblk in range(3):
                                for k in range(8):
                                    S.op("pe", lambda e: e.matmul(pz[:, cblk * 512:(cblk + 1) * 512], lhsT=hT[:, k, :],
                                                                  rhs=winb[:, k, cblk * 512:(cblk + 1) * 512],
                                                                  start=(k == 0), stop=(k == 7)),
                                         reads=[("hT", k), "winb"], writes=[("pz", cblk)])
                            if j + 1 < NT:
                                front_B(j + 1)
                            S.op("act", lambda e: e.activation(out=p_all[:, j, :], in_=pz[:, 0:256], func=AF.Copy),
                                 reads=[("pz", 0)], writes=[("p_all", j)])
                            S.op("act", lambda e: e.activation(out=gu2[t][:], in_=pz[:, 256:512], func=AF.Gelu_apprx_tanh),
                                 reads=[("pz", 0)], writes=[("gu", t)])
                            S.op("act", lambda e: e.activation(out=gv2[t][:], in_=pz[:, 512:768], func=AF.Gelu_apprx_tanh),
                                 reads=[("pz", 1)], writes=[("gv", t)])
                            S.op("act", lambda e: e.activation(out=sq2[t][:], in_=pz[:, 768:1408], func=AF.Square),
                                 reads=[("pz", 1), ("pz", 2)], writes=[("sq", t)])
                            S.op("dve", lambda e: e.tensor_copy(out=zq2[t][:], in_=pz[:, 768:1408]),
                                 reads=[("pz", 1), ("pz", 2)], writes=[("zq", t)])
                            S.op("dve", lambda e: e.tensor_copy(out=v_all[:, j, :, 0:64],
                                                                in_=pz[:, 1408:1536].rearrange("p (k d) -> p k d", d=64)),
                                 reads=[("pz", 2)], writes=[("v", j)])

                        front_B(0)
                        for j0 in range(0, NTL, 2):
                            pair = [(j0, 0), (j0 + 1, 1)]
                            for (j, t) in pair:
                                mm_and_evac(j, t)
                            for h in range(4):
                                for (j, t) in pair:
                                    S.op("dve", lambda e: e.bn_stats(stv2[t][:, h, :], gv2[t][:, h * 64:(h + 1) * 64]),
                                         reads=[("gv", t)], writes=[("stv", t, h)])
                                for (j, t) in pair:
                                    S.op("dve", lambda e: e.bn_aggr(mvv2[t][:, h, :], stv2[t][:, h, :]),
                                         reads=[("stv", t, h)], writes=[("mvv", t)])
                            for (j, t) in pair:
                                S.op("act", lambda e: e.activation(out=rsv2[t][:], in_=mvv2[t][:, :, 1], func=AF.Sqrt, bias=epsc[:], scale=1.0),
                                     reads=[("mvv", t), "epsc"], writes=[("rsv", t)])
                            for (j, t) in pair:
                                S.op("dve", lambda e: e.tensor_reduce(out=ss2[t][:], in_=sq2[t][:].rearrange("p (h d) -> p h d", d=64),
                                                                      axis=AX.X, op=ALU.add),
                                     reads=[("sq", t)], writes=[("ss", t)])
                            for (j, t) in pair:
                                S.op("act", lambda e: e.activation(out=ss2[t][:], in_=ss2[t][:], func=AF.Sqrt, bias=epsc[:], scale=1.0 / 64),
                                     reads=[("ss", t), "epsc"], writes=[("ss", t)])
                            for (j, t) in pair:
                                S.op("dve", lambda e: e.reciprocal(rsv2[t][:], rsv2[t][:]), reads=[("rsv", t)], writes=[("rsv", t)])
                            for h in range(4):
                                for (j, t) in pair:
                                    S.op("dve", lambda e: e.tensor_scalar(out=vn2[t][:, h * 64:(h + 1) * 64], in0=gv2[t][:, h * 64:(h + 1) * 64],
                                                                          scalar1=mvv2[t][:, h, 0:1], scalar2=rsv2[t][:, h:h + 1],
                                                                          op0=ALU.subtract, op1=ALU.mult),
                                         reads=[("gv", t), ("mvv", t), ("rsv", t)], writes=[("vn", t)])
                            for (j, t) in pair:
                                S.op("dve", lambda e: e.tensor_tensor(out=vh2[t][:], in0=vn2[t][:], in1=sgug[:], op=ALU.mult),
                                     reads=[("vn", t), "sgug"], writes=[("vh", t)])
                            for (j, t) in pair:
                                for h in range(4):
                                    S.op("pe", lambda e: e.matmul(psg2[t][:, h * 64:(h + 1) * 64], lhsT=wsT[:, h, :],
                                                                  rhs=vh2[t][:, h * 64:(h + 1) * 64], start=True, stop=True),
                                         reads=["wsT", ("vh", t)], writes=[("psg", t)])
                            for (j, t) in pair:
                                S.op("dve", lambda e: e.reciprocal(ss2[t][:], ss2[t][:]), reads=[("ss", t)], writes=[("ss", t)])
                            for (j, t) in pair:
                                S.op("dve", lambda e: e.tensor_tensor(out=qn2[t][:], in0=zq2[t][:].rearrange("p (h d) -> p h d", d=64),
                                                                      in1=ss2[t][:].unsqueeze(2).broadcast_to([128, 10, 64]), op=ALU.mult),
                                     reads=[("zq", t), ("ss", t)], writes=[("qn", t)])
                            for (j, t) in pair:
                                S.op("dve", lambda e: e.tensor_tensor(out=qn2[t][:], in0=qn2[t][:],
                                                                      in1=qkg[:].rearrange("p (h d) -> p h d", d=64), op=ALU.mult),
                                     reads=[("qn", t), "qkg"], writes=[("qn", t)])
                            for h in range(4):
                                for (j, t) in pair:
                                    S.op("dve", lambda e: e.scalar_tensor_tensor(out=sgo2[t][:, h * 64:(h + 1) * 64],
                                                                                 in0=psg2[t][:, h * 64:(h + 1) * 64],
                                                                                 scalar=sbT[:, h:h + 1],
                                                                                 in1=gu2[t][:, h * 64:(h + 1) * 64],
                                                                                 op0=ALU.add, op1=ALU.mult),
                                         reads=[("psg", t), "sbT", ("gu", t)], writes=[("sgo", t)])
                            for (j, t) in pair:
                                for c in range(2):
                                    S.op("pe", lambda e: e.transpose(ptr2[t][:, c, :], sgo2[t][:, c * 128:(c + 1) * 128], identb[:]),
                                         reads=[("sgo", t), "identb"], writes=[("ptr", t)])
                            for (j, t) in pair:
                                tok = slice(j * 128, (j + 1) * 128)
                                S.op("act", lambda e: e.activation(out=mixps[:, 2:4, tok], in_=ptr2[t][:, 0:2, :], func=AF.Copy),
                                     reads=[("ptr", t)], writes=[("mixps_s", j)])
                            def dsts(t):
                                qdst = qr2[t][:, 0:512].rearrange("p (g k d) -> p k g d", g=4, k=2, d=64)
                                kdst = qr2[t][:, 512:640].rearrange("p (k d) -> p k d", d=64)
                                return qdst, kdst
                            if j0 < NTL:
                                for half_, op_ in ((0, ALU.subtract), (1, ALU.add)):
                                    for (j, t) in pair:
                                        cosb = cs[:, j:j + 1, 0:32].broadcast_to([128, 10, 32])
                                        xa_ = qn2[t][:, :, 0:32] if half_ == 0 else qn2[t][:, :, 32:64]
                                        S.op("dve", lambda e: e.tensor_tensor(out=ta2[t][:], in0=xa_, in1=cosb, op=ALU.mult),
                                             reads=[("qn", t), "cs"], writes=[("ta", t)])
                                    for (j, t) in pair:
                                        sinb = cs[:, j:j + 1, 32:64].broadcast_to([128, 10, 32])
                                        xb_ = qn2[t][:, :, 32:64] if half_ == 0 else qn2[t][:, :, 0:32]
                                        S.op("dve", lambda e: e.tensor_tensor(out=tb2[t][:], in0=xb_, in1=sinb, op=ALU.mult),
                                             reads=[("qn", t), "cs"], writes=[("tb", t)])
                                    for (j, t) in pair:
                                        qdst, kdst = dsts(t)
                                        dsl = slice(0, 32) if half_ == 0 else slice(32, 64)
                                        S.op("dve", lambda e: e.tensor_tensor(out=qdst[:, :, :, dsl],
                                                                              in0=ta2[t][:, 0:8, :].rearrange("p (k g) d -> p k g d", k=2),
                                                                              in1=tb2[t][:, 0:8, :].rearrange("p (k g) d -> p k g d", k=2),
                                                                              op=op_),
                                             reads=[("ta", t), ("tb", t)], writes=[("qr", t, half_, 0)])
                                        S.op("dve", lambda e: e.tensor_tensor(out=kdst[:, :, dsl], in0=ta2[t][:, 8:10, :], in1=tb2[t][:, 8:10, :],
                                                                              op=op_),
                                             reads=[("ta", t), ("tb", t)], writes=[("qr", t, half_, 1)])
                            else:
                                for (j, t) in pair:
                                    qdst, kdst = dsts(t)
                                    S.op("dve", lambda e: e.tensor_copy(out=qdst, in_=qn2[t][:, 0:8, :].rearrange("p (k g) d -> p k g d", k=2)),
                                         reads=[("qn", t)], writes=[("qr", t, 0, 0), ("qr", t, 1, 0)])
                                    S.op("dve", lambda e: e.tensor_copy(out=kdst, in_=qn2[t][:, 8:10, :]),
                                         reads=[("qn", t)], writes=[("qr", t, 0, 1), ("qr", t, 1, 1)])
                            for (j, t) in pair:
                                for c in range(5):
                                    S.op("pe", lambda e: e.transpose(ptr2[t][:, 2 + c, :], qr2[t][:, c * 128:(c + 1) * 128], identb[:]),
                                         reads=[("qr", t, 0, 0), ("qr", t, 0, 1), ("qr", t, 1, 0), ("qr", t, 1, 1), "identb"],
                                         writes=[("ptr", t)])
                            for (j, t) in pair:
                                tok = slice(j * 128, (j + 1) * 128)
                                S.op("act", lambda e: e.activation(out=qT_all[:, :, tok], in_=ptr2[t][:, 2:6, :], func=AF.Copy),
                                     reads=[("ptr", t)], writes=[("qT", j)])
                                for kv_ in range(2):
                                    S.op("act", lambda e: e.activation(out=kT_all[kv_ * 64:(kv_ + 1) * 64, kv_, tok],
                                                                       in_=ptr2[t][kv_ * 64:(kv_ + 1) * 64, 6, :], func=AF.Copy),
                                         reads=[("ptr", t), "kT_zero"], writes=[("kT", j, kv_)])
                        S.barrier()
                        for j in range(NTL, NT):
                            s = 0 if j < NTL else 1
                            tok = slice(j * 128, (j + 1) * 128)
                            for cblk in range(3):
                                for k in range(8):
                                    S.op("pe", lambda e: e.matmul(pz[:, cblk * 512:(cblk + 1) * 512], lhsT=hT[:, k, :],
                                                                  rhs=winb[:, k, cblk * 512:(cblk + 1) * 512],
                                                                  start=(k == 0), stop=(k == 7)),
                                         reads=[("hT", k), "winb"], writes=[("pz", cblk)])
                            if j + 1 < NT:
                                front_B(j + 1)
                            S.op("act", lambda e: e.activation(out=p_all[:, j, :], in_=pz[:, 0:256], func=AF.Copy),
                                 reads=[("pz", 0)], writes=[("p_all", j)])
                            S.op("act", lambda e: e.activation(out=gu[:], in_=pz[:, 256:512], func=AF.Gelu_apprx_tanh),
                                 reads=[("pz", 0)], writes=["gu"])
                            S.op("act", lambda e: e.activation(out=gv[:], in_=pz[:, 512:768], func=AF.Gelu_apprx_tanh),
                                 reads=[("pz", 1)], writes=["gv"])
                            for h in range(4):
                                S.op("dve", lambda e: e.bn_stats(stv[:, h, :], gv[:, h * 64:(h + 1) * 64]),
                                     reads=["gv"], writes=[("stv", h)])
                                S.op("dve", lambda e: e.bn_aggr(mvv[:, h, :], stv[:, h, :]),
                                     reads=[("stv", h)], writes=["mvv"])
                            S.op("act", lambda e: e.activation(out=rsv[:], in_=mvv[:, :, 1], func=AF.Sqrt, bias=epsc[:], scale=1.0),
                                 reads=["mvv", "epsc"], writes=["rsv"])
                            S.op("dve", lambda e: e.reciprocal(rsv[:], rsv[:]), reads=["rsv"], writes=["rsv"])
                            for h in range(4):
                                S.op("dve", lambda e: e.tensor_scalar(out=vn[:, h * 64:(h + 1) * 64], in0=gv[:, h * 64:(h + 1) * 64],
                                                                      scalar1=mvv[:, h, 0:1], scalar2=rsv[:, h:h + 1],
                                                                      op0=ALU.subtract, op1=ALU.mult),
                                     reads=["gv", "mvv", "rsv"], writes=["vn"])
                            S.op("dve", lambda e: e.tensor_tensor(out=vh[:], in0=vn[:], in1=sgug[:], op=ALU.mult),
                                 reads=["vn", "sgug"], writes=["vh"])
                            for h in range(4):
                                S.op("pe", lambda e: e.matmul(psg[:, h * 64:(h + 1) * 64], lhsT=wsT[:, h, :],
                                                              rhs=vh[:, h * 64:(h + 1) * 64], start=True, stop=True),
                                     reads=["wsT", "vh"], writes=["psg"])
                            for h in range(4):
                                S.op("dve", lambda e: e.scalar_tensor_tensor(out=sgo[:, h * 64:(h + 1) * 64],
                                                                             in0=psg[:, h * 64:(h + 1) * 64],
                                                                             scalar=sbT[:, h:h + 1],
                                                                             in1=gu[:, h * 64:(h + 1) * 64],
                                                                             op0=ALU.add, op1=ALU.mult),
                                     reads=["psg", "sbT", "gu"], writes=["sgo"])
                            for c in range(2):
                                S.op("pe", lambda e: e.transpose(ptr[:, c, :], sgo[:, c * 128:(c + 1) * 128], identb[:]),
                                     reads=["sgo", "identb"], writes=[("ptr", 0)])
                            S.op("act", lambda e: e.activation(out=mixps[:, 2:4, tok], in_=ptr[:, 0:2, :], func=AF.Copy),
                                 reads=[("ptr", 0)], writes=[("mixps_s", j)])
                            S.op("act", lambda e: e.activation(out=sq[:], in_=pz[:, 768:1408], func=AF.Square),
                                 reads=[("pz", 1), ("pz", 2)], writes=["sq"])
                            S.op("dve", lambda e: e.tensor_reduce(out=ss[:], in_=sq[:].rearrange("p (h d) -> p h d", d=64),
                                                                  axis=AX.X, op=ALU.add),
                                 reads=["sq"], writes=["ss"])
                            S.op("act", lambda e: e.activation(out=ss[:], in_=ss[:], func=AF.Sqrt, bias=epsc[:], scale=1.0 / 64),
                                 reads=["ss", "epsc"], writes=["ss"])
                            S.op("dve", lambda e: e.reciprocal(ss[:], ss[:]), reads=["ss"], writes=["ss"])
                            S.op("dve", lambda e: e.tensor_tensor(out=qn[:], in0=pz[:, 768:1408].rearrange("p (h d) -> p h d", d=64),
                                                                  in1=ss[:].unsqueeze(2).broadcast_to([128, 10, 64]), op=ALU.mult),
                                 reads=[("pz", 1), ("pz", 2), "ss"], writes=["qn"])
                            S.op("dve", lambda e: e.tensor_tensor(out=qn[:], in0=qn[:],
                                                                  in1=qkg[:].rearrange("p (h d) -> p h d", d=64), op=ALU.mult),
                                 reads=["qn", "qkg"], writes=["qn"])
                            qdst = qr[:, 0:512].rearrange("p (g k d) -> p k g d", g=4, k=2, d=64)
                            kdst = qr[:, 512:640].rearrange("p (k d) -> p k d", d=64)
                            qsrc = qn[:, 0:8, :].rearrange("p (k g) d -> p k g d", k=2)
                            ksrc = qn[:, 8:10, :]
                            if j < NTL:
                                cosb = cs[:, j:j + 1, 0:32].broadcast_to([128, 10, 32])
                                sinb = cs[:, j:j + 1, 32:64].broadcast_to([128, 10, 32])
                                x1 = qn[:, :, 0:32]
                                x2 = qn[:, :, 32:64]
                                S.op("dve", lambda e: e.tensor_tensor(out=ta[:], in0=x1, in1=cosb, op=ALU.mult),
                                     reads=["qn", "cs"], writes=["ta"])
                                S.op("dve", lambda e: e.tensor_tensor(out=tb[:], in0=x2, in1=sinb, op=ALU.mult),
                                     reads=["qn", "cs"], writes=["tb"])
                                S.op("dve", lambda e: e.tensor_tensor(out=qdst[:, :, :, 0:32],
                                                                      in0=ta[:, 0:8, :].rearrange("p (k g) d -> p k g d", k=2),
                                                                      in1=tb[:, 0:8, :].rearrange("p (k g) d -> p k g d", k=2),
                                                                      op=ALU.subtract),
                                     reads=["ta", "tb"], writes=["qr_a"])
                                S.op("dve", lambda e: e.tensor_tensor(out=kdst[:, :, 0:32], in0=ta[:, 8:10, :], in1=tb[:, 8:10, :],
                                                                      op=ALU.subtract),
                                     reads=["ta", "tb"], writes=["qr_b"])
                                S.op("dve", lambda e: e.tensor_tensor(out=ta[:], in0=x2, in1=cosb, op=ALU.mult),
                                     reads=["qn", "cs"], writes=["ta"])
                                S.op("dve", lambda e: e.tensor_tensor(out=tb[:], in0=x1, in1=sinb, op=ALU.mult),
                                     reads=["qn", "cs"], writes=["tb"])
                                S.op("dve", lambda e: e.tensor_tensor(out=qdst[:, :, :, 32:64],
                                                                      in0=ta[:, 0:8, :].rearrange("p (k g) d -> p k g d", k=2),
                                                                      in1=tb[:, 0:8, :].rearrange("p (k g) d -> p k g d", k=2),
                                                                      op=ALU.add),
                                     reads=["ta", "tb"], writes=["qr_c"])
                                S.op("dve", lambda e: e.tensor_tensor(out=kdst[:, :, 32:64], in0=ta[:, 8:10, :], in1=tb[:, 8:10, :],
                                                                      op=ALU.add),
                                     reads=["ta", "tb"], writes=["qr_d"])
                            else:
                                S.op("dve", lambda e: e.tensor_copy(out=qdst, in_=qsrc), reads=["qn"], writes=["qr_a", "qr_c"])
                                S.op("dve", lambda e: e.tensor_copy(out=kdst, in_=ksrc), reads=["qn"], writes=["qr_b", "qr_d"])
                            for c in range(5):
                                S.op("pe", lambda e: e.transpose(ptr[:, 2 + c, :], qr[:, c * 128:(c + 1) * 128], identb[:]),
                                     reads=["qr_a", "qr_b", "qr_c", "qr_d", "identb"], writes=[("ptr", 1)])
                            S.op("act", lambda e: e.activation(out=qT_all[:, :, tok], in_=ptr[:, 2:6, :], func=AF.Copy),
                                 reads=[("ptr", 1)], writes=[("qT", j)])
                            for kv_ in range(2):
                                S.op("act", lambda e: e.activation(out=kT_all[kv_ * 64:(kv_ + 1) * 64, kv_, tok],
                                                                   in_=ptr[kv_ * 64:(kv_ + 1) * 64, 6, :], func=AF.Copy),
                                     reads=[("ptr", 1), "kT_zero"], writes=[("kT", j, kv_)])
                            S.op("dve", lambda e: e.tensor_copy(out=v_all[:, j, :, 0:64],
                                                                in_=pz[:, 1408:1536].rearrange("p (k d) -> p k d", d=64)),
                                 reads=[("pz", 2)], writes=[("v", j)])
                    S.barrier()
                    esC = ExitStack()
                    with esC:
                        bandf = SB(esC, "bandf", [128, 4, 5, 128], F32)
                        band = SB(esC, "band", [128, 4, 5, 128], BF16)
                        wbd = SB(esC, "wbd", [128, 2, 128], BF16)
                        pscT = SB(esC, "pscT", [128, 2], F32)
                        pooledT = SB(esC, "pooledT", [128, 2, 128], BF16)
                        ppool = PS(esC, "ppool", [128, 4, 128], F32)
                        pp2 = PS(esC, "pp2", [128, 2, 128], F32)
                        S.dma("sp", lambda e: e.dma_start(out=bandf[:], in_=band_in), writes=["bandf"])
                        S.op("dve", lambda e: e.tensor_copy(out=band[:], in_=bandf[:]), reads=["bandf"], writes=["band"])
                        S.dma("pool", lambda e: e.dma_start(out=wbd[:], in_=wbd_in[ll].rearrange("c p q -> p c q")), writes=["wbd"])
                        S.dma("sp", lambda e: e.dma_start(out=pscT[:], in_=pscT_in[ll]), writes=["pscT"])
                        for j in range(NT):
                            tok = slice(j * 128, (j + 1) * 128)
                            lo_t, hi_t = (0, NTL - 1) if j < NTL else (NTL, NT - 1)
                            rels = [r for r in (-1, 0, 1) if lo_t <= j + r <= hi_t]
                            for c in range(2):
                                for bi in range(2):
                                    g = 2 * c + bi
                                    for ri, r in enumerate(rels):
                                        if r == -1:
                                            v = 0
                                        elif r == 1:
                                            v = 4
                                        else:
                                            v = 2 if j == lo_t else (3 if j == hi_t else 1)
                                        S.op("pe", lambda e: e.matmul(ppool[:, c * 2 + bi, :], lhsT=p_all[:, j + r, c * 128:(c + 1) * 128],
                                                                      rhs=band[:, g, v, :], start=(ri == 0), stop=(ri == len(rels) - 1)),
                                             reads=["band"], writes=["ppool"])
                                S.op("act", lambda e: e.activation(out=pooledT[0:64, c, :], in_=ppool[0:64, c * 2, :], func=AF.Copy),
                                     reads=["ppool"], writes=[("pooledT", c, 0)])
                                S.op("act", lambda e: e.activation(out=pooledT[64:128, c, :], in_=ppool[64:128, c * 2 + 1, :], func=AF.Copy),
                                     reads=["ppool"], writes=[("pooledT", c, 1)])
                            for c in range(2):
                                S.op("pe", lambda e: e.matmul(pp2[:, c, :], lhsT=wbd[:, c, :], rhs=pooledT[:, c, :], start=True, stop=True),
                                     reads=["wbd", ("pooledT", c, 0), ("pooledT", c, 1)], writes=["pp2"])
                            for c in range(2):
                                S.op("act", lambda e: e.activation(out=mixps[:, c, tok], in_=pp2[:, c, :], func=AF.Copy,
                                                                   scale=pscT[:, c:c + 1]),
                                     reads=["pp2", "pscT"], writes=[("mixps_p", j)])
                    S.barrier()
                if debug and ll == 0:
                    S.dma("sp", lambda e: e.dma_start(out=dbg_mix, in_=mixps[:]), writes=["dbgmix"])
                esD = ExitStack()
                with esD:
                    woutb = SB(esD, "woutb", [128, 8, D], BF16)
                    lnr = SB(esD, "lnr", [128, 2, D], F32)
                    wr = SB(esD, "wr", [128, 8, NE], F32)
                    PT = [SB(esD, "PT%d" % i, [128, 512], BF16) for i in range(3)]
                    attn_tok = [SB(esD, "attn_tok%d" % i, [128, 4, 512], BF16) for i in range(2)]
                    rden = SB(esD, "rden", [128, 4], F32)
                    mixA = SB(esD, "mixA", [128, 4, 128], BF16)
                    bufA = SB(esD, "bufA", [128, D], F32)
                    bufB = SB(esD, "bufB", [128, D], F32)
                    xhb = SB(esD, "xhb", [128, D], BF16)
                    h2T = SB(esD, "h2T", [128, 8, 128], F32)
                    st = SB(esD, "stE", [128, 2, 6], F32)
                    mv = SB(esD, "mvE", [128, 2], F32)
                    rstd = SB(esD, "rstdE", [128, 1], F32)
                    mx = SB(esD, "mx", [128, 1], F32)
                    sm = SB(esD, "sm", [128, 1], F32)
                    ex = SB(esD, "ex", [128, NE], F32)
                    g1rows = build_gate_rows(esD, 16, "g1r")
                    pS = [PS(esD, "pS%d" % i, [128, 512], F32) for i in range(3)]
                    pO = [PS(esD, "pO%d" % i, [128, 4, 128], F32) for i in range(2)]
                    ptr = PS(esD, "ptrD", [128, 4, 128], BF16)
                    pbig = PS(esD, "pbig", [128, D], F32)
                    pr = pbig[:, 0:NE]
                    S.dma("pool", lambda e: e.dma_start(out=woutb[:], in_=w_out[ll].rearrange("(k p) c -> p k c", p=128)),
                          writes=["woutb"])
                    S.dma("sp", lambda e: e.dma_start(out=lnr[:], in_=ln_in[ll:ll + 1, 0:2, :].broadcast_to([128, 2, D])),
                          writes=["lnr"])
                    S.dma("sp", lambda e: e.dma_start(out=wr[:], in_=wr_in[ll]), writes=["wr"])

                    def rstd_act(tag):
                        S.op("act", lambda e: e.activation(out=rstd[:], in_=mv[:, 1:2], func=AF.Ln, bias=epsc[:], scale=1.0),
                             reads=[tag + "mv", "epsc"], writes=[tag + "rstd"])
                        S.op("act", lambda e: e.activation(out=rstd[:], in_=rstd[:], func=AF.Exp, scale=-0.5),
                             reads=[tag + "rstd"], writes=[tag + "rstd"])

                    def stats_dve(src, key_src, tag):
                        S.op("dve", lambda e: e.bn_stats(st[:, 0, :], src[:, 0:512]), reads=[key_src], writes=[tag + "st0"])
                        S.op("dve", lambda e: e.bn_stats(st[:, 1, :], src[:, 512:1024]), reads=[key_src], writes=[tag + "st1"])
                        S.op("dve", lambda e: e.bn_aggr(mv[:], st[:]), reads=[tag + "st0", tag + "st1"], writes=[tag + "mv"])

                    def make_ef_stages(j, qs, par):
                        s = 0 if j < NTL else 1
                        tok = slice(j * 128, (j + 1) * 128)
                        at = attn_tok[par]

                        def st0():
                            for cc in range(4):
                                S.op("pe", lambda e: e.transpose(ptr[:, cc, :], at[:, qs, cc * 128:(cc + 1) * 128], identb[:]),
                                     reads=[("attn_tok", par, qs), "identb"], writes=["ptrD"])
                            S.dma("sp", lambda e: e.dma_start(out=bufA[:], in_=src_tile(ll, j)), reads=[src_key(ll, j)], writes=["bufA"])

                        def st1():
                            S.op("act", lambda e: e.activation(out=mixA[:], in_=ptr[:], func=AF.Copy), reads=["ptrD"], writes=["mixA"])

                        def st2():
                            for half in range(2):
                                for c8 in range(8):
                                    lhs = mixps[:, c8, tok] if c8 < 4 else mixA[:, c8 - 4, :]
                                    S.op("pe", lambda e: e.matmul(pbig[:, half * 512:(half + 1) * 512], lhsT=lhs,
                                                                  rhs=woutb[:, c8, half * 512:(half + 1) * 512],
                                                                  start=(c8 == 0), stop=(c8 == 7)),
                                         reads=["mixA", "woutb"], writes=["pbig"])

                        def st3():
                            S.op("dve", lambda e: e.tensor_tensor(out=bufB[:], in0=pbig[:], in1=g1rows[s][:], op=ALU.mult),
                                 reads=["pbig", "g1r_%d" % s], writes=["bufB"])
                            S.op("dve", lambda e: e.scalar_tensor_tensor(out=bufA[:], in0=bufA[:], scalar=ALPHA, in1=bufB[:],
                                                                         op0=ALU.mult, op1=ALU.add),
                                 reads=["bufA", "bufB"], writes=["bufA"])
                            stats_dve(bufA, "bufA", "E")

                        def st4():
                            rstd_act("E")

                        def st5():
                            S.op("dve", lambda e: e.tensor_scalar(out=bufB[:], in0=bufA[:], scalar1=mv[:, 0:1], scalar2=rstd[:],
                                                                  op0=ALU.subtract, op1=ALU.mult),
                                 reads=["bufA", "Emv", "Erstd"], writes=["bufB"])
                            S.op("dve", lambda e: e.tensor_tensor(out=bufB[:], in0=bufB[:], in1=lnr[:, 0, :], op=ALU.mult),
                                 reads=["bufB", "lnr"], writes=["bufB"])
                            S.op("dve", lambda e: e.tensor_tensor(out=bufB[:], in0=bufB[:], in1=lnr[:, 1, :], op=ALU.add),
                                 reads=["bufB", "lnr"], writes=["bufB"])
                            S.op("dve", lambda e: e.tensor_scalar(out=bufA[:], in0=bufB[:], scalar1=ALPHA, scalar2=None, op0=ALU.mult),
                                 reads=["bufB"], writes=["bufA"])
                            S.dma("sp", lambda e: e.dma_start(out=macc[tok, :], in_=bufA[:]), reads=["bufA"], writes=[("macc", j)])
                            if debug and ll == 0:
                                S.dma("sp", lambda e: e.dma_start(out=dbg_x1[tok, :], in_=bufA[:]), reads=["bufA"], writes=[("dbgx1", j)])
                            stats_dve(bufB, "bufB", "E")

                        def st6():
                            rstd_act("E")

                        def st7():
                            S.op("dve", lambda e: e.tensor_scalar(out=bufB[:], in0=bufB[:], scalar1=mv[:, 0:1], scalar2=rstd[:],
                                                                  op0=ALU.subtract, op1=ALU.mult),
                                 reads=["bufB", "Emv", "Erstd"], writes=["bufB"])
                            S.op("dve", lambda e: e.tensor_copy(out=xhb[:], in_=bufB[:]), reads=["bufB"], writes=["xhb"])
                            S.dma("sp", lambda e: e.dma_start(out=xh2[tok, :], in_=xhb[:]), reads=["xhb"], writes=[("xh2", j)])

                        def st8():
                            for k in range(8):
                                S.op("pe", lambda e: e.transpose(pbig[:, k * 128:(k + 1) * 128], bufB[:, k * 128:(k + 1) * 128], identf[:]),
                                     reads=["bufB", "identf"], writes=["pbig"])

                        def st9():
                            for k in range(8):
                                S.op("dve", lambda e: e.tensor_scalar(out=h2T[:, k, :], in0=pbig[:, k * 128:(k + 1) * 128],
                                                                      scalar1=modT[:, 32 + k, s:s + 1], scalar2=modT[:, 24 + k, s:s + 1],
                                                                      op0=ALU.mult, op1=ALU.add),
                                     reads=["pbig", "modT"], writes=[("h2T", k)])

                        def st10():
                            for k in range(8):
                                S.op("pe", lambda e: e.matmul(pr, lhsT=h2T[:, k, :], rhs=wr[:, k, :], start=(k == 0), stop=(k == 7)),
                                     reads=[("h2T", k), "wr"], writes=["pbig"])

                        def st11():
                            S.op("dve", lambda e: e.reduce_max(out=mx[:], in_=pr, axis=AX.X), reads=["pbig"], writes=["mx"])
                            S.op("dve", lambda e: e.tensor_scalar(out=mx[:], in0=mx[:], scalar1=-1.0, scalar2=None, op0=ALU.mult),
                                 reads=["mx"], writes=["mx"])

                        def st12():
                            S.op("act", lambda e: e.activation(out=ex[:], in_=pr, func=AF.Exp, bias=mx[:], scale=1.0, accum_out=sm[:]),
                                 reads=["pbig", "mx"], writes=["ex", "sm"])

                        def st13():
                            S.op("dve", lambda e: e.reciprocal(sm[:], sm[:]), reads=["sm"], writes=["sm"])
                            S.op("dve", lambda e: e.tensor_scalar(out=aff_all[:, j, :], in0=ex[:], scalar1=sm[:], scalar2=None, op0=ALU.mult),
                                 reads=["ex", "sm"], writes=[("aff", j)])

                        return [st0, st1, st2, st3, st4, st5, st6, st7, st8, st9, st10, st11, st12, st13]

                    blocks = [(NTL, 2)] + [(qb * 4, 4) for qb in range(8)]
                    nS = 0
                    nO = 0
                    pending = []
                    for bi, (t0, ntile) in enumerate(blocks):
                        par = bi % 2
                        N = ntile * 128
                        qtok = slice(t0 * 128, t0 * 128 + N)
                        kts = list(range(NT)) if t0 < NTL else [NTL, NTL + 1]
                        steps = [(c, kv, ki, kt) for c in range(4) for kv in range(2) for ki, kt in enumerate(kts)]
                        spacing = max(1, (len(steps) - 4) // (len(pending) + 1))

                        def emit_S(i):
                            c, kv, ki, kt = steps[i]
                            pSb = pS[(nS + i) % 3]
                            pSk = "pS%d" % ((nS + i) % 3)
                            PTb = PT[(nS + i) % 3]
                            PTk = "PT%d" % ((nS + i) % 3)
                            S.op("pe", lambda e: e.matmul(pSb[:, 0:N], lhsT=kT_all[:, kv, kt * 128:(kt + 1) * 128],
                                                          rhs=qT_all[:, c, qtok], start=True, stop=True),
                                 writes=[pSk])
                            S.op("act", lambda e: e.activation(out=PTb[:, 0:N], in_=pSb[:, 0:N], func=AF.Exp, scale=0.125),
                                 reads=[pSk], writes=[PTk])

                        emit_S(0)
                        emit_S(1)
                        for i, (c, kv, ki, kt) in enumerate(steps):
                            h = kv * 4 + c
                            if i + 2 < len(steps):
                                emit_S(i + 2)
                            if pending and i % spacing == spacing - 1:
                                pending.pop(0)()
                            if ki == 0:
                                nO += 1
                                if nO == 1:
                                    S.op("dve", lambda e: e.memset(pO[nO % 2][:], 0.0), writes=["pO%d" % (nO % 2)])
                            if ki == 1:
                                S.op("dve", lambda e: e.memset(pO[(nO + 1) % 2][:], 0.0), writes=["pO%d" % ((nO + 1) % 2)])
                            pOb = pO[nO % 2]
                            pOk = "pO%d" % (nO % 2)
                            PTb = PT[(nS + i) % 3]
                            PTk = "PT%d" % ((nS + i) % 3)
                            for qs in range(ntile):
                                S.op("pe", lambda e: e.matmul(pOb[:, qs, 0:65], lhsT=PTb[:, qs * 128:(qs + 1) * 128],
                                                              rhs=v_all[:, kt, kv, :], start=False, stop=(ki == len(kts) - 1)),
                                     reads=[PTk], writes=[pOk])
                            if ki == len(kts) - 1:
                                S.op("dve", lambda e: e.reciprocal(rden[:, 0:ntile], pOb[:, 0:ntile, 64]),
                                     reads=[pOk], writes=["rden"])
                                for qs in range(ntile):
                                    S.op("dve", lambda e: e.tensor_scalar(out=attn_tok[par][:, qs, h * 64:(h + 1) * 64], in0=pOb[:, qs, 0:64],
                                                                          scalar1=rden[:, qs:qs + 1], scalar2=None, op0=ALU.mult),
                                         reads=[pOk, "rden"], writes=[("attn_tok", par, qs)])
                        nS += len(steps)
                        while pending:
                            pending.pop(0)()
                        for qs in range(ntile):
                            pending.extend(make_ef_stages(t0 + qs, qs, par))
                    while pending:
                        pending.pop(0)()
                S.barrier()
            if debug and ll == 0:
                S.dma("sp", lambda e: e.dma_start(out=dbg_aff, in_=aff_all[:]), writes=["dbgaff"])
            esGH = ExitStack()
            with esGH:
                idx_i = SB(esGH, "idx_i", [128, NE, 5], I32)
                gate_s = SB(esGH, "gate_s", [128, NE, 5], F32)
                esG = ExitStack()
                with esG:
                    lo = SB(esG, "lo", [128, 2, NE], F32)
                    hi = SB(esG, "hi", [128, 2, NE], F32)
                    mid = SB(esG, "mid", [128, 2, NE], F32)
                    capv = SB(esG, "capv", [128, 2, NE], F32)
                    cmp_ = SB(esG, "cmp", [128, NT, NE], F32)
                    cnt = SB(esG, "cnt", [128, 2, NE], F32)
                    ge = SB(esG, "ge", [128, 2, NE], F32)
                    d1 = SB(esG, "d1", [128, 2, NE], F32)
                    mask = SB(esG, "mask", [128, NT, NE], F32)
                    tg = SB(esG, "tg", [128, NT, NE, 5], BF16)
                    gr1 = SB(esG, "gr1", [128, NT, NE], F32)
                    gr2 = SB(esG, "gr2", [128, NT, NE], F32)
                    lst = SB(esG, "lst", [128, 5, 5], F32)
                    pos = SB(esG, "pos", [128, NT, NE], F32)
                    off = SB(esG, "off", [128, NT, NE], F32)
                    tot = SB(esG, "tot", [128, NT, NE], F32)
                    oh = [SB(esG, "oh%d" % i, [128, 512], BF16) for i in range(4)]
                    idx_f = SB(esG, "idx_f", [128, NE, 5], F32)
                    ptot = PS(esG, "ptot", [128, 2, NE], F32)
                    pwl = PS(esG, "pwl", [128, 512], F32)
                    pwc = PS(esG, "pwc", [128, 32], F32)
                    ptl = PS(esG, "ptl", [128, 512], F32)
                    ptc = PS(esG, "ptc", [128, 32], F32)
                    plist = [PS(esG, "plist0", [128, 5, 128], F32)]
                    S.op("pool", lambda e: e.memset(lo[:], 0.0), writes=["lo"])
                    S.op("pool", lambda e: e.memset(hi[:], 1.0), writes=["hi"])
                    S.op("pool", lambda e: e.memset(capv[:, 0, :], float(CAP_L)), writes=["capv0"])
                    S.op("pool", lambda e: e.memset(capv[:, 1, :], float(CAP_C)), writes=["capv1"])
                    aff_l = aff_all[:, 0:NTL, :]
                    aff_c = aff_all[:, NTL:NT, :]
                    for it in range(30):
                        S.op("dve", lambda e: e.tensor_tensor(out=mid[:], in0=lo[:], in1=hi[:], op=ALU.add),
                             reads=["lo", "hi"], writes=["mid"])
                        S.op("dve", lambda e: e.tensor_scalar(out=mid[:], in0=mid[:], scalar1=0.5, scalar2=None, op0=ALU.mult),
                             reads=["mid"], writes=["mid"])
                        S.op("dve", lambda e: e.tensor_tensor(out=cmp_[:, 0:NTL, :], in0=aff_l,
                                                              in1=mid[:, 0:1, :].broadcast_to([128, NTL, NE]), op=ALU.is_ge),
                             reads=["mid"], writes=["cmp_l"])
                        S.op("dve", lambda e: e.tensor_tensor(out=cmp_[:, NTL:NT, :], in0=aff_c,
                                                              in1=mid[:, 1:2, :].broadcast_to([128, 2, NE]), op=ALU.is_ge),
                             reads=["mid"], writes=["cmp_c"])
                        S.op("dve", lambda e: e.tensor_reduce(out=cnt[:, 0, :], in_=cmp_[:, 0:NTL, :].rearrange("p j e -> p e j"),
                                                              axis=AX.X, op=ALU.add),
                             reads=["cmp_l"], writes=["cnt0"])
                        S.op("dve", lambda e: e.tensor_reduce(out=cnt[:, 1, :], in_=cmp_[:, NTL:NT, :].rearrange("p j e -> p e j"),
                                                              axis=AX.X, op=ALU.add),
                             reads=["cmp_c"], writes=["cnt1"])
                        S.op("pe", lambda e: e.matmul(ptot[:], lhsT=ones_f[:], rhs=cnt[:], start=True, stop=True),
                             reads=["cnt0", "cnt1", "ones_f"], writes=["ptot"])
                        S.op("dve", lambda e: e.tensor_tensor(out=ge[:], in0=ptot[:], in1=capv[:], op=ALU.is_ge),
                             reads=["ptot", "capv0", "capv1"], writes=["ge"])
                        S.op("dve", lambda e: e.tensor_tensor(out=d1[:], in0=mid[:], in1=lo[:], op=ALU.subtract),
                             reads=["mid", "lo"], writes=["d1"])
                        S.op("dve", lambda e: e.tensor_tensor(out=d1[:], in0=d1[:], in1=ge[:], op=ALU.mult),
                             reads=["d1", "ge"], writes=["d1"])
                        S.op("dve", lambda e: e.tensor_tensor(out=lo[:], in0=lo[:], in1=d1[:], op=ALU.add),
                             reads=["d1", "lo"], writes=["lo"])
                        S.op("dve", lambda e: e.tensor_tensor(out=d1[:], in0=hi[:], in1=mid[:], op=ALU.subtract),
                             reads=["mid", "hi"], writes=["d1"])
                        S.op("dve", lambda e: e.tensor_tensor(out=d1[:], in0=d1[:], in1=ge[:], op=ALU.mult),
                             reads=["d1", "ge"], writes=["d1"])
                        S.op("dve", lambda e: e.tensor_tensor(out=hi[:], in0=mid[:], in1=d1[:], op=ALU.add),
                             reads=["d1", "mid"], writes=["hi"])
                    S.op("dve", lambda e: e.tensor_tensor(out=mask[:, 0:NTL, :], in0=aff_l,
                                                          in1=lo[:, 0:1, :].broadcast_to([128, NTL, NE]), op=ALU.is_ge),
                         reads=["lo"], writes=["mask_l"])
                    S.op("dve", lambda e: e.tensor_tensor(out=mask[:, NTL:NT, :], in0=aff_c,
                                                          in1=lo[:, 1:2, :].broadcast_to([128, 2, NE]), op=ALU.is_ge),
                         reads=["lo"], writes=["mask_c"])
                    S.op("dve", lambda e: e.tensor_copy(out=tg[:, :, :, 0], in_=tokA[:].unsqueeze(2).broadcast_to([128, NT, NE])),
                         reads=["tokA"], writes=["tg0"])
                    S.op("dve", lambda e: e.tensor_copy(out=tg[:, :, :, 1], in_=tokB[:].unsqueeze(2).broadcast_to([128, NT, NE])),
                         reads=["tokB"], writes=["tg0"])
                    S.op("dve", lambda e: e.tensor_copy(out=tg[:, :, :, 2], in_=aff_all[:]), writes=["tg1"])
                    S.op("dve", lambda e: e.tensor_tensor(out=gr1[:], in0=aff_all[:], in1=tg[:, :, :, 2], op=ALU.subtract),
                         reads=["tg1"], writes=["gr1"])
                    S.op("dve", lambda e: e.tensor_copy(out=tg[:, :, :, 3], in_=gr1[:]), reads=["gr1"], writes=["tg1"])
                    S.op("dve", lambda e: e.tensor_tensor(out=gr2[:], in0=gr1[:], in1=tg[:, :, :, 3], op=ALU.subtract),
                         reads=["gr1", "tg1"], writes=["gr2"])
                    S.op("dve", lambda e: e.tensor_copy(out=tg[:, :, :, 4], in_=gr2[:]), reads=["gr2"], writes=["tg1"])
                    mk2 = mask[:].rearrange("p j e -> p (j e)")
                    S.op("pe", lambda e: e.matmul(pwl[:], lhsT=lmat[:], rhs=mk2[:, 0:512], start=True, stop=True),
                         reads=["mask_l", "lmat"], writes=["pwl"])
                    S.op("pe", lambda e: e.matmul(pwc[:], lhsT=lmat[:], rhs=mk2[:, 512:544], start=True, stop=True),
                         reads=["mask_c", "lmat"], writes=["pwc"])
                    S.op("pe", lambda e: e.matmul(ptl[:], lhsT=ones_f[:], rhs=mk2[:, 0:512], start=True, stop=True),
                         reads=["mask_l", "ones_f"], writes=["ptl"])
                    S.op("pe", lambda e: e.matmul(ptc[:], lhsT=ones_f[:], rhs=mk2[:, 512:544], start=True, stop=True),
                         reads=["mask_c", "ones_f"], writes=["ptc"])
                    pos2 = pos[:].rearrange("p j e -> p (j e)")
                    tot2 = tot[:].rearrange("p j e -> p (j e)")
                    S.op("act", lambda e: e.activation(out=pos2[:, 0:512], in_=pwl[:], func=AF.Copy), reads=["pwl"], writes=["pos_l"])
                    S.op("act", lambda e: e.activation(out=pos2[:, 512:544], in_=pwc[:], func=AF.Copy), reads=["pwc"], writes=["pos_c"])
                    S.op("act", lambda e: e.activation(out=tot2[:, 0:512], in_=ptl[:], func=AF.Copy), reads=["ptl"], writes=["tot"])
                    S.op("act", lambda e: e.activation(out=tot2[:, 512:544], in_=ptc[:], func=AF.Copy), reads=["ptc"], writes=["tot"])
                    S.op("pool", lambda e: e.memset(off[:], 0.0), writes=["off"])
                    for j in range(1, NTL):
                        S.op("dve", lambda e: e.tensor_tensor(out=off[:, j, :], in0=off[:, j - 1, :], in1=tot[:, j - 1, :], op=ALU.add),
                             reads=["off", "tot"], writes=["off"])
                    S.op("dve", lambda e: e.tensor_copy(out=off[:, NTL + 1, :], in_=tot[:, NTL, :]), reads=["off", "tot"], writes=["off"])
                    S.op("dve", lambda e: e.tensor_tensor(out=pos[:], in0=pos[:], in1=off[:], op=ALU.add),
                         reads=["pos_l", "pos_c", "off"], writes=["pos_l", "pos_c"])
                    S.op("dve", lambda e: e.scalar_tensor_tensor(out=pos[:], in0=pos[:], scalar=-BIG, in1=mask[:], op0=ALU.add, op1=ALU.mult),
                         reads=["pos_l", "pos_c", "mask_l", "mask_c"], writes=["pos_l", "pos_c"])
                    S.op("dve", lambda e: e.tensor_scalar(out=pos[:], in0=pos[:], scalar1=BIG, scalar2=None, op0=ALU.add),
                         reads=["pos_l", "pos_c"], writes=["pos"])
                    n_oh = 0
                    for ex_ in range(NE):
                        pl = plist[0]
                        plk = "plist0"
                        S.op("dve", lambda e: e.memset(pl[:], 0.0), writes=[plk])
                        for j in range(NT):
                            ohb = oh[n_oh % 4]
                            ohk = "oh%d" % (n_oh % 4)
                            eng_ = "dve"
                            n_oh += 1
                            if j < NTL:
                                S.op(eng_, lambda e: e.tensor_scalar(out=ohb[:], in0=iota_row[:], scalar1=pos[:, j, ex_:ex_ + 1],
                                                                     scalar2=None, op0=ALU.is_equal),
                                     reads=["iota_row", "pos"], writes=[ohk])
                                for st_ in range(4):
                                    S.op("pe", lambda e: e.matmul(pl[:, st_, 0:5], lhsT=ohb[:, st_ * 128:(st_ + 1) * 128],
                                                                  rhs=tg[:, j, ex_, :], start=False, stop=(j == NTL - 1)),
                                         reads=[ohk, "tg0", "tg1"], writes=[plk])
                            else:
                                S.op(eng_, lambda e: e.tensor_scalar(out=ohb[:, 0:128], in0=iota_row[:, 0:128], scalar1=pos[:, j, ex_:ex_ + 1],
                                                                     scalar2=None, op0=ALU.is_equal),
                                     reads=["iota_row", "pos"], writes=[ohk])
                                S.op("pe", lambda e: e.matmul(pl[:, 4, 0:5], lhsT=ohb[:, 0:128], rhs=tg[:, j, ex_, :],
                                                              start=False, stop=(j == NT - 1)),
                                     reads=[ohk, "tg0", "tg1"], writes=[plk])
                        S.op("act", lambda e: e.activation(out=lst[:], in_=pl[:, :, 0:5], func=AF.Copy),
                             reads=[plk], writes=["lst"])
                        S.op("dve", lambda e: e.tensor_tensor(out=idx_f[:, ex_, :], in0=lst[:, :, 0], in1=lst[:, :, 1], op=ALU.add),
                             reads=["lst"], writes=["idx_f"])
                        S.op("dve", lambda e: e.tensor_tensor(out=gate_s[:, ex_, :], in0=lst[:, :, 2], in1=lst[:, :, 3], op=ALU.add),
                             reads=["lst"], writes=["gate_s"])
                        S.op("dve", lambda e: e.tensor_tensor(out=gate_s[:, ex_, :], in0=gate_s[:, ex_, :], in1=lst[:, :, 4], op=ALU.add),
                             reads=["lst", "gate_s"], writes=["gate_s"])
                    S.op("dve", lambda e: e.tensor_copy(out=idx_i[:], in_=idx_f[:]), reads=["idx_f"], writes=["idx_i"])
                    if debug and ll == 0:
                        S.dma("sp", lambda e: e.dma_start(out=dbg_idx, in_=idx_i[:]), reads=["idx_i"], writes=["dbgidx"])
                        S.dma("sp", lambda e: e.dma_start(out=dbg_gate, in_=gate_s[:]), reads=["gate_s"], writes=["dbggate"])
                        S.dma("sp", lambda e: e.dma_start(out=dbg_pos, in_=pos[:]), reads=["pos"], writes=["dbgpos"])
                S.barrier()
                esH = ExitStack()
                with esH:
                    g2rows = build_gate_rows(esH, 40, "g2r")
                    wb = [[SB(esH, "wb%d_%d" % (m, i), [128, 8, D], BF16) for m in range(3)] for i in range(2)]
                    xg = [[SB(esH, "xg%d_%d" % (st_, i), [128, D], BF16) for st_ in range(5)] for i in range(2)]
                    xgT = [SB(esH, "xgT%d" % i, [128, 8, 544], BF16) for i in range(2)]
                    hidT = SB(esH, "hidT", [128, 8, 544], BF16)
                    s1 = SB(esH, "s1", [128, 544], F32)
                    ysc = [SB(esH, "ysc%d" % i, [128, D], F32) for i in range(4)]
                    pxT = [PS(esH, "pxT%d" % i, [128, 8, 128], BF16) for i in range(1)]
                    ph1 = [PS(esH, "ph1_%d" % i, [128, 512], F32) for i in range(2)]
                    ph3 = [PS(esH, "ph3_%d" % i, [128, 512], F32) for i in range(2)]
                    phc = PS(esH, "phc", [128, 2, 32], F32)
                    py = [PS(esH, "pyH%d" % i, [128, 512], F32) for i in range(2)]
                    wsrc = (w1, w3, w2)

                    def issue_loads(ex_):
                        i = ex_ % 2
                        for st_ in range(5):
                            npart = 128 if st_ < 4 else CAP_C
                            S.dma("pool", lambda e: e.indirect_dma_start(
                                out=xg[i][st_][0:npart, :], out_offset=None, in_=xh2[:, :],
                                in_offset=bass.IndirectOffsetOnAxis(ap=idx_i[0:npart, ex_, st_:st_ + 1], axis=0),
                                bounds_check=bcreg, oob_is_err=False),
                                reads=["idx_i"] + [("xh2", j) for j in range(NT)], writes=["xg%d_%d" % (st_, i)])
                        for m in range(3):
                            S.dma("pool", lambda e: e.dma_start(out=wb[i][m][:], in_=wsrc[m][ll, ex_].rearrange("(k p) c -> p k c", p=128)),
                                  writes=["wb%d_%d" % (m, i)])

                    def prep_tile(ex_, st_):
                        i = ex_ % 2
                        npart = 128 if st_ < 4 else CAP_C
                        s = 0 if st_ < 4 else 1
                        c0 = st_ * 128
                        npx[0] += 1
                        pb = pxT[0]
                        pk = "pxT0"
                        for k in range(8):
                            S.op("pe", lambda e: e.transpose(pb[:, k, 0:npart], xg[i][st_][0:npart, k * 128:(k + 1) * 128],
                                                             identb[0:npart, 0:npart]),
                                 reads=["xg%d_%d" % (st_, i), "identb"], writes=[pk])
                        for k in range(8):
                            if k % 2 == 0:
                                S.op("act", lambda e: e.activation(out=xgT[i][:, k, c0:c0 + npart], in_=pb[:, k, 0:npart], func=AF.Identity,
                                                                   bias=modT[:, 24 + k, s:s + 1], scale=modT[:, 32 + k, s:s + 1]),
                                     reads=[pk, "modT"], writes=["xgT%d" % i])
                            else:
                                S.op("dve", lambda e: e.tensor_scalar(out=xgT[i][:, k, c0:c0 + npart], in0=pb[:, k, 0:npart],
                                                                      scalar1=modT[:, 32 + k, s:s + 1], scalar2=modT[:, 24 + k, s:s + 1],
                                                                      op0=ALU.mult, op1=ALU.add),
                                     reads=[pk, "modT"], writes=["xgT%d" % i])

                    npx = [0]
                    issue_loads(0)
                    for st_ in range(5):
                        prep_tile(0, st_)
                    nys = 0
                    npy = 0
                    for ex_ in range(NE):
                        i = ex_ % 2
                        xgTi = xgT[i]
                        xk_ = "xgT%d" % i
                        if ex_ + 1 < NE:
                            issue_loads(ex_ + 1)
                        for fc in range(8):
                            fb = fc % 2
                            for m, ph in ((0, ph1[fb]), (1, ph3[fb])):
                                for k in range(8):
                                    S.op("pe", lambda e: e.matmul(ph[:], lhsT=wb[i][m][:, k, fc * 128:(fc + 1) * 128], rhs=xgTi[:, k, 0:512],
                                                                  start=(k == 0), stop=(k == 7)),
                                         reads=[xk_, "wb%d_%d" % (m, i)], writes=["ph%d_%d" % (m, fb)])
                                for k in range(8):
                                    S.op("pe", lambda e: e.matmul(phc[:, m, :], lhsT=wb[i][m][:, k, fc * 128:(fc + 1) * 128], rhs=xgTi[:, k, 512:544],
                                                                  start=(k == 0), stop=(k == 7)),
                                         reads=[xk_, "wb%d_%d" % (m, i)], writes=["phc"])
                            if ex_ + 1 < NE and fc < 5:
                                prep_tile(ex_ + 1, fc)
                            S.op("act", lambda e: e.activation(out=s1[:, 0:512], in_=ph1[fb][:], func=AF.Silu), reads=["ph0_%d" % fb], writes=["s1a"])
                            S.op("act", lambda e: e.activation(out=s1[:, 512:544], in_=phc[:, 0, :], func=AF.Silu), reads=["phc"], writes=["s1b"])
                            S.op("dve", lambda e: e.tensor_tensor(out=hidT[:, fc, 0:512], in0=s1[:, 0:512], in1=ph3[fb][:], op=ALU.mult),
                                 reads=["s1a", "ph1_%d" % fb], writes=["hidT"])
                            S.op("dve", lambda e: e.tensor_tensor(out=hidT[:, fc, 512:544], in0=s1[:, 512:544], in1=phc[:, 1, :], op=ALU.mult),
                                 reads=["s1b", "phc"], writes=["hidT"])
                        for st_ in range(5):
                            npart = 128 if st_ < 4 else CAP_C
                            s = 0 if st_ < 4 else 1
                            c0 = st_ * 128
                            yb = ysc[nys % 4]
                            yk = "ysc%d" % (nys % 4)
                            nys += 1
                            for half in range(2):
                                pyb = py[npy % 2]
                                pyk = "pyH%d" % (npy % 2)
                                npy += 1
                                for fc in range(8):
                                    S.op("pe", lambda e: e.matmul(pyb[0:npart, :], lhsT=hidT[:, fc, c0:c0 + npart],
                                                                  rhs=wb[i][2][:, fc, half * 512:(half + 1) * 512],
                                                                  start=(fc == 0), stop=(fc == 7)),
                                         reads=["hidT", "wb2_%d" % i], writes=[pyk])
                                S.op("dve", lambda e: e.scalar_tensor_tensor(out=yb[0:npart, half * 512:(half + 1) * 512], in0=pyb[0:npart, :],
                                                                             scalar=gate_s[0:npart, ex_, st_:st_ + 1],
                                                                             in1=g2rows[s][0:npart, half * 512:(half + 1) * 512],
                                                                             op0=ALU.mult, op1=ALU.mult),
                                     reads=[pyk, "gate_s", "g2r_%d" % s], writes=[(yk, half)])
                            S.dma("pool", lambda e: e.indirect_dma_start(
                                out=macc[:, :], out_offset=bass.IndirectOffsetOnAxis(ap=idx_i[0:npart, ex_, st_:st_ + 1], axis=0),
                                in_=yb[0:npart, :], in_offset=None, bounds_check=bcreg, oob_is_err=False,
                                compute_op=ALU.add),
                                reads=[(yk, 0), (yk, 1), "idx_i"] + [("msc", ex_ - 1, k) for k in range(5)],
                                writes=[("msc", ex_, st_)])
                S.barrier()
            if debug and ll == 0:
                S.dma("sp", lambda e: e.dma_start(out=dbg_macc, in_=macc), writes=["dbgmacc"])
                S.barrier()
            esI = ExitStack()
            with esI:
                finish_next_A = None
                if ll + 1 < L:
                    finish_next_A = emit_phase_A(ll + 1, esI, "pool")
                lnr2 = SB(esI, "lnr2", [128, 2, D], F32)
                mt = [SB(esI, "mt%d" % i, [128, D], F32) for i in range(4)]
                xo = [SB(esI, "xo%d" % i, [128, D], F32) for i in range(4)]
                st = SB(esI, "stI", [128, 2, 6], F32)
                mv = SB(esI, "mvI", [128, 2], F32)
                rstd = SB(esI, "rstdI", [128, 1], F32)
                nmr = SB(esI, "nmr", [128, 1], F32)
                S.dma("sp", lambda e: e.dma_start(out=lnr2[:], in_=ln_in[ll:ll + 1, 2:4, :].broadcast_to([128, 2, D])),
                      writes=["lnr2"])
                stI = [st, SB(esI, "stI_b", [128, 2, 6], F32)]
                mvI = [mv, SB(esI, "mvI_b", [128, 2], F32)]
                rsI = [rstd, SB(esI, "rstdI_b", [128, 1], F32)]
                nmI = [nmr, SB(esI, "nmr_b", [128, 1], F32)]
                for j0 in range(0, NT, 2):
                    pair = [(j0, 0), (j0 + 1, 1)]
                    bo = (j0 // 2 % 2) * 2
                    for (j, t) in pair:
                        tok = slice(j * 128, (j + 1) * 128)
                        S.dma("sp", lambda e: e.dma_start(out=mt[bo + t][:], in_=macc[tok, :]), reads=[("macc", j)], writes=[("mt", bo + t)])
                    for c_ in range(2):
                        for (j, t) in pair:
                            S.op("dve", lambda e: e.bn_stats(stI[t][:, c_, :], mt[bo + t][:, c_ * 512:(c_ + 1) * 512]),
                                 reads=[("mt", bo + t)], writes=[("Ist", t, c_)])
                    for (j, t) in pair:
                        S.op("dve", lambda e: e.bn_aggr(mvI[t][:], stI[t][:]), reads=[("Ist", t, 0), ("Ist", t, 1)], writes=[("Imv", t)])
                    for (j, t) in pair:
                        S.op("act", lambda e: e.activation(out=rsI[t][:], in_=mvI[t][:, 1:2], func=AF.Sqrt, bias=epsc[:], scale=1.0),
                             reads=[("Imv", t), "epsc"], writes=[("Irs", t)])
                    for (j, t) in pair:
                        S.op("dve", lambda e: e.reciprocal(rsI[t][:], rsI[t][:]), reads=[("Irs", t)], writes=[("Irs", t)])
                    for (j, t) in pair:
                        S.op("dve", lambda e: e.scalar_tensor_tensor(out=nmI[t][:], in0=mvI[t][:, 0:1], scalar=-1.0, in1=rsI[t][:],
                                                                     op0=ALU.mult, op1=ALU.mult),
                             reads=[("Imv", t), ("Irs", t)], writes=[("Inm", t)])
                    for (j, t) in pair:
                        S.op("act", lambda e: e.activation(out=xo[bo + t][:], in_=mt[bo + t][:], func=AF.Identity, bias=nmI[t][:], scale=rsI[t][:]),
                             reads=[("mt", bo + t), ("Inm", t), ("Irs", t)], writes=[("xo", bo + t)])
                    for (j, t) in pair:
                        S.op("dve", lambda e: e.tensor_tensor(out=xo[bo + t][:], in0=xo[bo + t][:], in1=lnr2[:, 0, :], op=ALU.mult),
                             reads=[("xo", bo + t), "lnr2"], writes=[("xo", bo + t)])
                    for (j, t) in pair:
                        S.op("dve", lambda e: e.tensor_tensor(out=xo[bo + t][:], in0=xo[bo + t][:], in1=lnr2[:, 1, :], op=ALU.add),
                             reads=[("xo", bo + t), "lnr2"], writes=[("xo", bo + t)])
                    for (j, t) in pair:
                        S.dma("sp", lambda e: e.dma_start(out=dst_tile(ll, j), in_=xo[bo + t][:]), reads=[("xo", bo + t)], writes=[dst_key(ll, j)])
                if finish_next_A is not None:
                    finish_next_A()
            S.barrier()
        print("instructions", S.n_inst, "waits", S.n_wait, flush=True)
    return nc


def _band_tables():
    wins = (2, 4, 8, 16)
    Ls = 384
    band = np.zeros((128, 4, 5, 128), np.float32)
    for g, w in enumerate(wins):
        A = np.zeros((Ls, Ls), np.float64)
        for t in range(Ls):
            lo = min(max(t - w // 2, 0), Ls)
            hi = min(max(t + w // 2, 0), Ls)
            A[t, lo:hi] = 1.0 / (hi - lo)
            A[t, t] -= 1.0
        AT = A.T
        band[:, g, 0, :] = AT[0:128, 128:256]
        band[:, g, 1, :] = AT[128:256, 128:256]
        band[:, g, 2, :] = AT[0:128, 0:128]
        band[:, g, 3, :] = AT[256:384, 256:384]
        band[:, g, 4, :] = AT[256:384, 128:256]
    return band


def _rope_tables():
    t = np.arange(SEQ)
    r = (t // 64).astype(np.float32)
    col = (t % 64).astype(np.float32)
    inv = (np.float32(10000.0) ** (-np.arange(16, dtype=np.float32) / np.float32(16))).astype(np.float32)
    ang = np.concatenate([r[:, None] * inv, col[:, None] * inv], axis=-1).astype(np.float32)
    cs = np.concatenate([np.cos(ang), np.sin(ang)], axis=-1).astype(np.float32)
    return np.ascontiguousarray(cs.reshape(NTL, 128, 64).transpose(1, 0, 2))


def _prep_common(inp, layers):
    L = len(layers)
    sl = lambda a: np.ascontiguousarray(np.asarray(a)[layers])
    pool_w = sl(inp["pool_w"])
    wbd = np.zeros((L, 2, 128, 128), np.float32)
    for c in range(2):
        for gi in range(2):
            wbd[:, c, gi * 64:(gi + 1) * 64, gi * 64:(gi + 1) * 64] = pool_w[:, 2 * c + gi]
    com = {
        "w_mod": sl(inp["w_mod"]),
        "b_modT": np.ascontiguousarray(sl(inp["b_mod"]).reshape(L, 48, 128).transpose(0, 2, 1)),
        "w_in": sl(inp["w_in"]),
        "wbd": wbd,
        "pscT": np.ascontiguousarray(sl(inp["pool_scale"]).reshape(L, 2, 128).transpose(0, 2, 1)),
        "band": _band_tables(),
        "sgu_g": np.ascontiguousarray(sl(inp["sgu_g"]).reshape(L, 256)),
        "sgu_wT": np.ascontiguousarray(sl(inp["sgu_w"]).transpose(0, 3, 1, 2)),
        "sgu_bT": np.ascontiguousarray(sl(inp["sgu_b"]).transpose(0, 2, 1)),
        "qkg": np.ascontiguousarray(np.concatenate([np.tile(sl(inp["q_g"]), (1, 8)), np.tile(sl(inp["k_g"]), (1, 2))], axis=1)),
        "w_out": sl(inp["w_out"]),
        "ln": np.ascontiguousarray(np.stack([sl(inp["ln1_g"]), sl(inp["ln1_b"]), sl(inp["ln2_g"]), sl(inp["ln2_b"])], axis=1)),
        "w_router": np.ascontiguousarray(sl(inp["w_router"]).reshape(L, 8, 128, NE).transpose(0, 2, 1, 3)),
        "w1": sl(inp["w1"]), "w3": sl(inp["w3"]), "w2": sl(inp["w2"]),
        "cs": _rope_tables(),
    }
    return com


_NC_CACHE = {}


def _run(inp, x, ctx, layers, n_cores=8):
    L = len(layers)
    if L not in _NC_CACHE:
        _NC_CACHE[L] = build(L)
    nc = _NC_CACHE[L]
    com = _prep_common(inp, layers)
    c = np.asarray(inp["c"], np.float32)
    c_ctx = np.asarray(inp["c_ctx"], np.float32)
    in_maps = []
    for b in range(n_cores):
        cond = np.stack([c[b].reshape(8, 128).T, c_ctx.reshape(8, 128).T], axis=-1)
        m = dict(com)
        m["x"] = np.ascontiguousarray(x[b])
        m["ctx"] = np.ascontiguousarray(ctx[b])
        m["cond"] = np.ascontiguousarray(cond.astype(np.float32))
        in_maps.append(m)
    res = run_bass_kernel_spmd(nc, in_maps, core_ids=list(range(n_cores)))
    xo = np.stack([np.asarray(r["out"]) for r in res.results], 0)
    co = np.stack([np.asarray(r["ctx_out"]) for r in res.results], 0)
    return xo, co


LAYERS_PER_LAUNCH = 4


def kernel(**inputs):
    inp = {k: np.asarray(v) for k, v in inputs.items()}
    x = np.asarray(inp["x"], np.float32)
    ctx = np.asarray(inp["ctx"], np.float32)
    for l0 in range(0, DEPTH, LAYERS_PER_LAUNCH):
        x, ctx = _run(inp, x, ctx, list(range(l0, l0 + LAYERS_PER_LAUNCH)))
    return x.astype(np.float32)
```

```python
import numpy as np
from contextlib import ExitStack
import concourse.bass as bass
import concourse.mybir as mybir
from concourse.bass_utils import run_bass_kernel_spmd

F32 = mybir.dt.float32
BF16 = mybir.dt.bfloat16
I32 = mybir.dt.int32
AF = mybir.ActivationFunctionType
ALU = mybir.AluOpType
AX = mybir.AxisListType

DEPTH = 4
D = 1024
SEQ = 4096
CTX = 256
NT = 34
NTL = 32
NTOK = SEQ + CTX
NE = 16
CAP_L = 512
CAP_C = 32
ALPHA = float((2 * DEPTH) ** 0.25)
EPS = 1e-6
BIG = 100000.0


class Sched:
    def __init__(self, nc, es, n_dma_sems=48):
        self.nc = nc
        self.eng = {"pe": nc.tensor, "act": nc.scalar, "dve": nc.vector, "pool": nc.gpsimd, "sp": nc.sync}
        self.sem = {k: es.enter_context(nc.semaphore("prog_" + k)) for k in self.eng}
        self.cnt = {k: 0 for k in self.eng}
        self.dsem = [es.enter_context(nc.semaphore("dma%d" % i)) for i in range(n_dma_sems)]
        self.dtot = [0] * n_dma_sems
        self.dnext = 0
        self.dnext_sw = 0
        self.waited = {k: {} for k in self.eng}
        self.res = {}
        self.n_inst = 0
        self.n_wait = 0

    def _semobj(self, key):
        return self.sem[key] if isinstance(key, str) else self.dsem[key]

    def _collect(self, reads, writes):
        deps = {}

        def add(d):
            if d is None:
                return
            k, v = d
            if deps.get(k, 0) < v:
                deps[k] = v
        for r in reads:
            e = self.res.get(r)
            if e is not None:
                add(e["w"])
        for w in writes:
            e = self.res.get(w)
            if e is not None:
                add(e["w"])
                for k, v in e["r"].items():
                    add((k, v))
        return deps

    def _wait(self, F, deps, skip_self=False):
        for k, v in deps.items():
            if skip_self and k == F:
                continue
            if self.waited[F].get(k, 0) < v:
                self.eng[F].wait_ge(self._semobj(k), v)
                self.waited[F][k] = v
                self.n_wait += 1

    def _update(self, dep, reads, writes):
        k, v = dep
        for r in reads:
            e = self.res.setdefault(r, {"w": None, "r": {}})
            if e["r"].get(k, 0) < v:
                e["r"][k] = v
        for w in writes:
            self.res[w] = {"w": dep, "r": {}}

    def op(self, F, fn, reads=(), writes=(), skip_self=None):
        if skip_self is None:
            skip_self = (F == "pe")
        deps = self._collect(reads, writes)
        self._wait(F, deps, skip_self=skip_self)
        inst = fn(self.eng[F])
        self.cnt[F] += 1
        inst.then_inc(self.sem[F], 1)
        self._update((F, self.cnt[F]), reads, writes)
        self.n_inst += 1
        return inst

    def dma(self, Q, fn, reads=(), writes=()):
        deps = self._collect(reads, writes)
        self._wait(Q, deps)
        half = len(self.dsem) // 2
        if Q == "pool":
            i = half + self.dnext_sw
            self.dnext_sw = (self.dnext_sw + 1) % (len(self.dsem) - half)
        else:
            i = self.dnext
            self.dnext = (self.dnext + 1) % half
        if self.dtot[i] > 0 and self.waited[Q].get(i, 0) < self.dtot[i]:
            self.eng[Q].wait_ge(self.dsem[i], self.dtot[i])
            self.waited[Q][i] = self.dtot[i]
            self.n_wait += 1
        inst = fn(self.eng[Q])
        self.dtot[i] += 16
        inst.then_inc(self.dsem[i], 16)
        self._update((i, self.dtot[i]), reads, writes)
        self.n_inst += 1
        return inst

    def barrier(self):
        for F in self.eng:
            for i, t in enumerate(self.dtot):
                if t > 0 and self.waited[F].get(i, 0) < t:
                    self.eng[F].wait_ge(self.dsem[i], t)
                    self.waited[F][i] = t
            for k in self.eng:
                if k != F and self.cnt[k] > 0 and self.waited[F].get(k, 0) < self.cnt[k]:
                    self.eng[F].wait_ge(self.sem[k], self.cnt[k])
                    self.waited[F][k] = self.cnt[k]
        self.res = {}


def build(L, debug=False):
    nc = bass.Bass("TRN2", target_bir_lowering=False)

    def DT(name, shape, dt=F32, kind="ExternalInput"):
        return nc.dram_tensor(name, shape, dt, kind=kind).ap()

    x_in = DT("x", [SEQ, D])
    ctx_in = DT("ctx", [CTX, D])
    cond_in = DT("cond", [128, 8, 2])
    w_mod = DT("w_mod", [L, D, 6 * D])
    b_modT = DT("b_modT", [L, 128, 48])
    w_in = DT("w_in", [L, D, 1536])
    wbd_in = DT("wbd", [L, 2, 128, 128])
    pscT_in = DT("pscT", [L, 128, 2])
    band_in = DT("band", [128, 4, 5, 128])
    sgug_in = DT("sgu_g", [L, 256])
    sguwT_in = DT("sgu_wT", [L, 128, 4, 128])
    sgubT_in = DT("sgu_bT", [L, 128, 4])
    qkg_in = DT("qkg", [L, 640])
    w_out = DT("w_out", [L, D, D])
    ln_in = DT("ln", [L, 4, D])
    wr_in = DT("w_router", [L, 128, 8, NE])
    w1 = DT("w1", [L, NE, D, D])
    w3 = DT("w3", [L, NE, D, D])
    w2 = DT("w2", [L, NE, D, D])
    cs_in = DT("cs", [128, NTL, 64])
    out = DT("out", [SEQ, D], kind="ExternalOutput")
    ctx_out = DT("ctx_out", [CTX, D], kind="ExternalOutput")
    if debug:
        dbg_x1 = DT("dbg_x1", [NTOK, D], kind="ExternalOutput")
        dbg_aff = DT("dbg_aff", [128, NT, NE], kind="ExternalOutput")
        dbg_mix = DT("dbg_mix", [128, 4, NTOK], BF16, kind="ExternalOutput")
        dbg_idx = DT("dbg_idx", [128, NE, 5], I32, kind="ExternalOutput")
        dbg_gate = DT("dbg_gate", [128, NE, 5], kind="ExternalOutput")
        dbg_macc = DT("dbg_macc", [NTOK, D], kind="ExternalOutput")
        dbg_pos = DT("dbg_pos", [128, NT, NE], kind="ExternalOutput")
    xs = DT("xs", [NTOK, D], kind="Internal")
    macc = DT("macc", [NTOK, D], kind="Internal")
    xh2 = DT("xh2", [NTOK, D], BF16, kind="Internal")

    def src_tile(ll, j):
        if ll == 0:
            return x_in[j * 128:(j + 1) * 128, :] if j < NTL else ctx_in[(j - NTL) * 128:(j - NTL + 1) * 128, :]
        return xs[j * 128:(j + 1) * 128, :]

    def dst_tile(ll, j):
        if ll == L - 1:
            return out[j * 128:(j + 1) * 128, :] if j < NTL else ctx_out[(j - NTL) * 128:(j - NTL + 1) * 128, :]
        return xs[j * 128:(j + 1) * 128, :]

    def src_key(ll, j):
        return ("xin", j) if ll == 0 else ("xs", j)

    def dst_key(ll, j):
        return ("xout", j) if ll == L - 1 else ("xs", j)

    es0 = ExitStack()
    with es0:
        S = Sched(nc, es0)
        bcreg = es0.enter_context(nc.gpsimd.register("bcreg"))
        nc.gpsimd.reg_mov(bcreg, NTOK - 1)

        uid = [0]

        def SB(es, name, shape, dt):
            uid[0] += 1
            return es.enter_context(nc.sbuf_tensor("s%d_%s" % (uid[0], name), shape, dt))

        def PS(es, name, shape, dt):
            uid[0] += 1
            return es.enter_context(nc.psum_tensor("p%d_%s" % (uid[0], name), shape, dt))

        identf = SB(es0, "identf", [128, 128], F32)
        identb = SB(es0, "identb", [128, 128], BF16)
        ones_f = SB(es0, "ones_f", [128, 128], F32)
        lmat = SB(es0, "lmat", [128, 128], F32)
        iota_row = SB(es0, "iota_row", [128, 512], mybir.dt.float16)
        tokid = SB(es0, "tokid", [128, NT], F32)
        jmp = SB(es0, "jmp", [128, 128], F32)
        modT = SB(es0, "modT", [128, 48, 2], F32)
        aff_all = SB(es0, "aff_all", [128, NT, NE], F32)
        epsc = SB(es0, "epsc", [128, 1], F32)
        S.op("pool", lambda e: e.memset(epsc[:], EPS), writes=["epsc"])
        S.op("pool", lambda e: e.iota(jmp[:], [[1, 128]], base=0, channel_multiplier=-1,
                                      allow_small_or_imprecise_dtypes=True), writes=["jmp"])
        S.op("pool", lambda e: e.tensor_single_scalar(out=identf[:], in_=jmp[:], scalar=0.0, op=ALU.is_equal),
             reads=["jmp"], writes=["identf"])
        S.op("pool", lambda e: e.tensor_single_scalar(out=identb[:], in_=jmp[:], scalar=0.0, op=ALU.is_equal),
             reads=["jmp"], writes=["identb"])
        S.op("pool", lambda e: e.tensor_single_scalar(out=lmat[:], in_=jmp[:], scalar=0.0, op=ALU.is_gt),
             reads=["jmp"], writes=["lmat"])
        S.op("pool", lambda e: e.memset(ones_f[:], 1.0), writes=["ones_f"])
        S.op("pool", lambda e: e.iota(iota_row[:], [[1, 512]], base=0, channel_multiplier=0,
                                      allow_small_or_imprecise_dtypes=True), writes=["iota_row"])
        S.op("pool", lambda e: e.iota(tokid[:], [[128, NT]], base=0, channel_multiplier=1,
                                      allow_small_or_imprecise_dtypes=True), writes=["tokid"])
        tokA = SB(es0, "tokA", [128, NT], F32)
        tokB = SB(es0, "tokB", [128, 1], F32)
        S.op("pool", lambda e: e.iota(tokA[:], [[128, NT]], base=0, channel_multiplier=0,
                                      allow_small_or_imprecise_dtypes=True), writes=["tokA"])
        S.op("pool", lambda e: e.iota(tokB[:], [[0, 1]], base=0, channel_multiplier=1,
                                      allow_small_or_imprecise_dtypes=True), writes=["tokB"])

        def ln_stats(es_tiles, src, key_src, tag):
            st, mv, rstd = es_tiles
            S.op("dve", lambda e: e.bn_stats(st[:, 0, :], src[:, 0:512]), reads=[key_src], writes=[tag + "st0"])
            S.op("dve", lambda e: e.bn_stats(st[:, 1, :], src[:, 512:1024]), reads=[key_src], writes=[tag + "st1"])
            S.op("dve", lambda e: e.bn_aggr(mv[:], st[:]), reads=[tag + "st0", tag + "st1"], writes=[tag + "mv"])
            S.op("act", lambda e: e.activation(out=rstd[:], in_=mv[:, 1:2], func=AF.Sqrt, bias=epsc[:], scale=1.0),
                 reads=[tag + "mv", "epsc"], writes=[tag + "rstd"])
            S.op("dve", lambda e: e.reciprocal(rstd[:], rstd[:]), reads=[tag + "rstd"], writes=[tag + "rstd"])

        for ll in range(L):
            def emit_phase_A(la, esA_, qa):
                condT = SB(esA_, "condT", [128, 8, 2], F32)
                bmT = SB(esA_, "bmT", [128, 48], F32)
                wblk = [SB(esA_, "wblk%d" % i, [128, 8, 512], F32) for i in range(2)]
                pmod = PS(esA_, "pmod", [128, 48, 2], F32)
                S.dma("sp", lambda e: e.dma_start(out=condT[:], in_=cond_in), writes=["condT"])
                S.dma("sp", lambda e: e.dma_start(out=bmT[:], in_=b_modT[la]), writes=["bmT"])
                S.op("act", lambda e: e.activation(out=condT[:], in_=condT[:], func=AF.Silu),
                     reads=["condT"], writes=["condT"])
                wm = w_mod[la].rearrange("(k p) c -> p k c", p=128)
                for cb in range(12):
                    wb_ = wblk[cb % 2]
                    wk = "wblk%d" % (cb % 2)
                    S.dma(qa, lambda e: e.dma_start(out=wb_[:], in_=wm[:, :, cb * 512:(cb + 1) * 512]), writes=[wk])
                    for sub in range(4):
                        j = cb * 4 + sub
                        for k in range(8):
                            S.op("pe", lambda e: e.matmul(pmod[:, j, :], lhsT=wb_[:, k, sub * 128:(sub + 1) * 128],
                                                          rhs=condT[:, k, :], start=(k == 0), stop=(k == 7)),
                                 reads=[wk, "condT"], writes=["pmod"])
                def finish_A():
                    S.op("dve", lambda e: e.tensor_tensor(out=modT[:], in0=pmod[:],
                                                          in1=bmT[:].unsqueeze(2).broadcast_to([128, 48, 2]), op=ALU.add),
                         reads=["pmod", "bmT"], writes=["modT"])
                    for base in (8, 32):
                        S.op("dve", lambda e: e.tensor_scalar_add(modT[:, base:base + 8, :], modT[:, base:base + 8, :], 1.0),
                             reads=["modT"], writes=["modT"])
                return finish_A
            if ll == 0:
                esA0 = ExitStack()
                with esA0:
                    emit_phase_A(0, esA0, "sp")()
                    S.barrier()

            def build_gate_rows(es, vbase, tagname):
                rows = [SB(es, "%s_%d" % (tagname, s), [128, D], F32) for s in range(2)]
                est = ExitStack()
                with est:
                  diag = [SB(est, "%s_dg%d" % (tagname, i), [128, 128], F32) for i in range(2)]
                  pg = PS(est, tagname + "_pg", [128, D], F32)
                  n = 0
                  for s in range(2):
                    for dc in range(8):
                        dg = diag[n % 2]
                        dk = "%s_dg%d" % (tagname, n % 2)
                        n += 1
                        S.op("dve", lambda e: e.tensor_scalar(out=dg[:], in0=identf[:], scalar1=modT[:, vbase + dc, s:s + 1],
                                                              scalar2=None, op0=ALU.mult),
                             reads=["identf", "modT"], writes=[dk])
                        S.op("pe", lambda e: e.matmul(pg[:, dc * 128:(dc + 1) * 128], lhsT=ones_f[:], rhs=dg[:],
                                                      start=True, stop=True),
                             reads=[dk, "ones_f"], writes=[tagname + "_pg"])
                    S.op("act", lambda e: e.activation(out=rows[s][:], in_=pg[:], func=AF.Copy),
                         reads=[tagname + "_pg"], writes=["%s_%d" % (tagname, s)])
                  S.barrier()
                return rows

            esBE = ExitStack()
            with esBE:
                mixps = SB(esBE, "mixps", [128, 4, NTOK], BF16)
                qT_all = SB(esBE, "qT_all", [128, 4, NTOK], BF16)
                kT_all = SB(esBE, "kT_all", [128, 2, NTOK], BF16)
                S.op("pool", lambda e: e.memset(kT_all[:], 0.0), writes=["kT_zero"])
                v_all = SB(esBE, "v_all", [128, NT, 2, 65], BF16)
                S.op("pool", lambda e: e.memset(v_all[:, :, :, 64:65], 1.0), writes=["v_ones"])
                esBC = ExitStack()
                with esBC:
                    p_all = SB(esBC, "p_all", [128, NT, 256], BF16)
                    esB = ExitStack()
                    with esB:
                        winb = SB(esB, "winb", [128, 8, 1536], BF16)
                        cs = SB(esB, "cs", [128, NTL, 64], F32)
                        sgug = SB(esB, "sgug", [128, 256], F32)
                        wsT = SB(esB, "wsT", [128, 4, 128], BF16)
                        sbT = SB(esB, "sbT", [128, 4], F32)
                        qkg = SB(esB, "qkg", [128, 640], F32)
                        xt = [SB(esB, "xt%d" % i, [128, D], F32) for i in range(2)]
                        st = SB(esB, "st", [128, 2, 6], F32)
                        mv = SB(esB, "mv", [128, 2], F32)
                        rstd = SB(esB, "rstd", [128, 1], F32)
                        xb = SB(esB, "xb", [128, D], BF16)
                        hT = SB(esB, "hT", [128, 8, 128], BF16)
                        gu = SB(esB, "gu", [128, 256], BF16)
                        gv = SB(esB, "gv", [128, 256], F32)
                        stv = SB(esB, "stv", [128, 4, 6], F32)
                        mvv = SB(esB, "mvv", [128, 4, 2], F32)
                        rsv = SB(esB, "rsv", [128, 4], F32)
                        vn = SB(esB, "vn", [128, 256], F32)
                        vh = SB(esB, "vh", [128, 256], BF16)
                        sgo = SB(esB, "sgo", [128, 256], BF16)
                        sq = SB(esB, "sq", [128, 640], F32)
                        ss = SB(esB, "ss", [128, 10], F32)
                        qn = SB(esB, "qn", [128, 10, 64], F32)
                        ta = SB(esB, "ta", [128, 10, 32], F32)
                        tb = SB(esB, "tb", [128, 10, 32], F32)
                        qr = SB(esB, "qr", [128, 640], BF16)
                        pT = PS(esB, "pT", [128, 8, 128], BF16)
                        pz = PS(esB, "pz", [128, 1536], F32)
                        psg = PS(esB, "psg", [128, 256], F32)
                        ptr = PS(esB, "ptr", [128, 7, 128], BF16)
                        S.dma("pool", lambda e: e.dma_start(out=winb[:], in_=w_in[ll].rearrange("(k p) c -> p k c", p=128)),
                              writes=["winb"])
                        S.dma("pool", lambda e: e.dma_start(out=wsT[:], in_=sguwT_in[ll]), writes=["wsT"])
                        S.dma("sp", lambda e: e.dma_start(out=cs[:], in_=cs_in), writes=["cs"])
                        S.dma("sp", lambda e: e.dma_start(out=sgug[:], in_=sgug_in[ll:ll + 1, :].broadcast_to([128, 256])),
                              writes=["sgug"])
                        S.dma("sp", lambda e: e.dma_start(out=qkg[:], in_=qkg_in[ll:ll + 1, :].broadcast_to([128, 640])),
                              writes=["qkg"])
                        S.dma("sp", lambda e: e.dma_start(out=sbT[:], in_=sgubT_in[ll]), writes=["sbT"])
                        def front_B(j):
                            s = 0 if j < NTL else 1
                            tok = slice(j * 128, (j + 1) * 128)
                            xtj = xt[j % 2]
                            xk = "xt%d" % (j % 2)
                            S.dma("sp", lambda e: e.dma_start(out=xtj[:], in_=src_tile(ll, j)),
                                  reads=[src_key(ll, j)], writes=[xk])
                            ln_stats((st, mv, rstd), xtj, xk, "B")
                            S.op("dve", lambda e: e.tensor_scalar(out=xb[:], in0=xtj[:], scalar1=mv[:, 0:1], scalar2=rstd[:],
                                                                  op0=ALU.subtract, op1=ALU.mult),
                                 reads=[xk, "Bmv", "Brstd"], writes=["xb"])
                            for k in range(8):
                                S.op("pe", lambda e: e.transpose(pT[:, k, :], xb[:, k * 128:(k + 1) * 128], identb[:]),
                                     reads=["xb", "identb"], writes=["pT"])
                            for k in range(8):
                                S.op("act", lambda e: e.activation(out=hT[:, k, :], in_=pT[:, k, :], func=AF.Identity,
                                                                   bias=modT[:, 0 + k, s:s + 1], scale=modT[:, 8 + k, s:s + 1]),
                                     reads=["pT", "modT"], writes=[("hT", k)])
                        gu2 = [gu, SB(esB, "gu_b", [128, 256], BF16)]
                        gv2 = [gv, SB(esB, "gv_b", [128, 256], F32)]
                        stv2 = [stv, SB(esB, "stv_b", [128, 4, 6], F32)]
                        mvv2 = [mvv, SB(esB, "mvv_b", [128, 4, 2], F32)]
                        rsv2 = [rsv, SB(esB, "rsv_b", [128, 4], F32)]
                        vn2 = [vn, SB(esB, "vn_b", [128, 256], F32)]
                        vh2 = [vh, SB(esB, "vh_b", [128, 256], BF16)]
                        sgo2 = [sgo, SB(esB, "sgo_b", [128, 256], BF16)]
                        sq2 = [sq, SB(esB, "sq_b", [128, 640], F32)]
                        zq2 = [SB(esB, "zq_a", [128, 640], F32), SB(esB, "zq_b", [128, 640], F32)]
                        ss2 = [ss, SB(esB, "ss_b", [128, 10], F32)]
                        qn2 = [qn, SB(esB, "qn_b", [128, 10, 64], F32)]
                        ta2 = [ta, SB(esB, "ta_b", [128, 10, 32], F32)]
                        tb2 = [tb, SB(esB, "tb_b", [128, 10, 32], F32)]
                        qr2 = [qr, SB(esB, "qr_b", [128, 640], BF16)]
                        psg2 = [psg, PS(esB, "psg_b", [128, 256], F32)]
                        ptr2 = [ptr, PS(esB, "ptr_b", [128, 7, 128], BF16)]

                        def mm_and_evac(j, t):
                            for cblk in range(3):
                                for k in range(8):
                                    S.op("pe", lambda e: e.matmul(pz[:, cblk * 512:(cblk + 1) * 512], lhsT=hT[:, k, :],
                                                                  rhs=winb[:, k, cblk * 512:(cblk + 1) * 512],
                                                                  start=(k == 0), stop=(k == 7)),
                                         reads=[("hT", k), "winb"], writes=[("pz", cblk)])
                            if j + 1 < NT:
                                front_B(j + 1)
                            S.op("act", lambda e: e.activation(out=p_all[:, j, :], in_=pz[:, 0:256], func=AF.Copy),
                                 reads=[("pz", 0)], writes=[("p_all", j)])
                            S.op("act", lambda e: e.activation(out=gu2[t][:], in_=pz[:, 256:512], func=AF.Gelu_apprx_tanh),
                                 reads=[("pz", 0)], writes=[("gu", t)])
                            S.op("act", lambda e: e.activation(out=gv2[t][:], in_=pz[:, 512:768], func=AF.Gelu_apprx_tanh),
                                 reads=[("pz", 1)], writes=[("gv", t)])
                            S.op("act", lambda e: e.activation(out=sq2[t][:], in_=pz[:, 768:1408], func=AF.Square),
                                 reads=[("pz", 1), ("pz", 2)], writes=[("sq", t)])
                            S.op("dve", lambda e: e.tensor_copy(out=zq2[t][:], in_=pz[:, 768:1408]),
                                 reads=[("pz", 1), ("pz", 2)], writes=[("zq", t)])
                            S.op("dve", lambda e: e.tensor_copy(out=v_all[:, j, :, 0:64],
                                                                in_=pz[:, 1408:1536].rearrange("p (k d) -> p k d", d=64)),
                                 reads=[("pz", 2)], writes=[("v", j)])

                        front_B(0)
                        for j0 in range(0, NTL, 2):
                            pair = [(j0, 0), (j0 + 1, 1)]
                            for (j, t) in pair:
                                mm_and_evac(j, t)
                            for h in range(4):
                                for (j, t) in pair:
                                    S.op("dve", lambda e: e.bn_stats(stv2[t][:, h, :], gv2[t][:, h * 64:(h + 1) * 64]),
                                         reads=[("gv", t)], writes=[("stv", t, h)])
                                for (j, t) in pair:
                                    S.op("dve", lambda e: e.bn_aggr(mvv2[t][:, h, :], stv2[t][:, h, :]),
                                         reads=[("stv", t, h)], writes=[("mvv", t)])
                            for (j, t) in pair:
                                S.op("act", lambda e: e.activation(out=rsv2[t][:], in_=mvv2[t][:, :, 1], func=AF.Sqrt, bias=epsc[:], scale=1.0),
                                     reads=[("mvv", t), "epsc"], writes=[("rsv", t)])
                            for (j, t) in pair:
                                S.op("dve", lambda e: e.tensor_reduce(out=ss2[t][:], in_=sq2[t][:].rearrange("p (h d) -> p h d", d=64),
                                                                      axis=AX.X, op=ALU.add),
                                     reads=[("sq", t)], writes=[("ss", t)])
                            for (j, t) in pair:
                                S.op("act", lambda e: e.activation(out=ss2[t][:], in_=ss2[t][:], func=AF.Sqrt, bias=epsc[:], scale=1.0 / 64),
                                     reads=[("ss", t), "epsc"], writes=[("ss", t)])
                            for (j, t) in pair:
                                S.op("dve", lambda e: e.reciprocal(rsv2[t][:], rsv2[t][:]), reads=[("rsv", t)], writes=[("rsv", t)])
                            for h in range(4):
                                for (j, t) in pair:
                                    S.op("dve", lambda e: e.tensor_scalar(out=vn2[t][:, h * 64:(h + 1) * 64], in0=gv2[t][:, h * 64:(h + 1) * 64],
                                                                          scalar1=mvv2[t][:, h, 0:1], scalar2=rsv2[t][:, h:h + 1],
                                                                          op0=ALU.subtract, op1=ALU.mult),
                                         reads=[("gv", t), ("mvv", t), ("rsv", t)], writes=[("vn", t)])
                            for (j, t) in pair:
                                S.op("dve", lambda e: e.tensor_tensor(out=vh2[t][:], in0=vn2[t][:], in1=sgug[:], op=ALU.mult),
                                     reads=[("vn", t), "sgug"], writes=[("vh", t)])
                            for (j, t) in pair:
                                for h in range(4):
                                    S.op("pe", lambda e: e.matmul(psg2[t][:, h * 64:(h + 1) * 64], lhsT=wsT[:, h, :],
                                                                  rhs=vh2[t][:, h * 64:(h + 1) * 64], start=True, stop=True),
                                         reads=["wsT", ("vh", t)], writes=[("psg", t)])
                            for (j, t) in pair:
                                S.op("dve", lambda e: e.reciprocal(ss2[t][:], ss2[t][:]), reads=[("ss", t)], writes=[("ss", t)])
                            for (j, t) in pair:
                                S.op("dve", lambda e: e.tensor_tensor(out=qn2[t][:], in0=zq2[t][:].rearrange("p (h d) -> p h d", d=64),
                                                                      in1=ss2[t][:].unsqueeze(2).broadcast_to([128, 10, 64]), op=ALU.mult),
                                     reads=[("zq", t), ("ss", t)], writes=[("qn", t)])
                            for (j, t) in pair:
                                S.op("dve", lambda e: e.tensor_tensor(out=qn2[t][:], in0=qn2[t][:],
                                                                      in1=qkg[:].rearrange("p (h d) -> p h d", d=64), op=ALU.mult),
                                     reads=[("qn", t), "qkg"], writes=[("qn", t)])
                            for h in range(4):
                                for (j, t) in pair:
                                    S.op("dve", lambda e: e.scalar_tensor_tensor(out=sgo2[t][:, h * 64:(h + 1) * 64],
                                                                                 in0=psg2[t][:, h * 64:(h + 1) * 64],
                                                                                 scalar=sbT[:, h:h + 1],
                                                                                 in1=gu2[t][:, h * 64:(h + 1) * 64],
                                                                                 op0=ALU.add, op1=ALU.mult),
                                         reads=[("psg", t), "sbT", ("gu", t)], writes=[("sgo", t)])
                            for (j, t) in pair:
                                for c in range(2):
                                    S.op("pe", lambda e: e.transpose(ptr2[t][:, c, :], sgo2[t][:, c * 128:(c + 1) * 128], identb[:]),
                                         reads=[("sgo", t), "identb"], writes=[("ptr", t)])
                            for (j, t) in pair:
                                tok = slice(j * 128, (j + 1) * 128)
                                S.op("act", lambda e: e.activation(out=mixps[:, 2:4, tok], in_=ptr2[t][:, 0:2, :], func=AF.Copy),
                                     reads=[("ptr", t)], writes=[("mixps_s", j)])
                            def dsts(t):
                                qdst = qr2[t][:, 0:512].rearrange("p (g k d) -> p k g d", g=4, k=2, d=64)
                                kdst = qr2[t][:, 512:640].rearrange("p (k d) -> p k d", d=64)
                                return qdst, kdst
                            if j0 < NTL:
                                for half_, op_ in ((0, ALU.subtract), (1, ALU.add)):
                                    for (j, t) in pair:
                                        cosb = cs[:, j:j + 1, 0:32].broadcast_to([128, 10, 32])
                                        xa_ = qn2[t][:, :, 0:32] if half_ == 0 else qn2[t][:, :, 32:64]
                                        S.op("dve", lambda e: e.tensor_tensor(out=ta2[t][:], in0=xa_, in1=cosb, op=ALU.mult),
                                             reads=[("qn", t), "cs"], writes=[("ta", t)])
                                    for (j, t) in pair:
                                        sinb = cs[:, j:j + 1, 32:64].broadcast_to([128, 10, 32])
                                        xb_ = qn2[t][:, :, 32:64] if half_ == 0 else qn2[t][:, :, 0:32]
                                        S.op("dve", lambda e: e.tensor_tensor(out=tb2[t][:], in0=xb_, in1=sinb, op=ALU.mult),
                                             reads=[("qn", t), "cs"], writes=[("tb", t)])
                                    for (j, t) in pair:
                                        qdst, kdst = dsts(t)
                                        dsl = slice(0, 32) if half_ == 0 else slice(32, 64)
                                        S.op("dve", lambda e: e.tensor_tensor(out=qdst[:, :, :, dsl],
                                                                              in0=ta2[t][:, 0:8, :].rearrange("p (k g) d -> p k g d", k=2),
                                                                              in1=tb2[t][:, 0:8, :].rearrange("p (k g) d -> p k g d", k=2),
                                                                              op=op_),
                                             reads=[("ta", t), ("tb", t)], writes=[("qr", t, half_, 0)])
                                        S.op("dve", lambda e: e.tensor_tensor(out=kdst[:, :, dsl], in0=ta2[t][:, 8:10, :], in1=tb2[t][:, 8:10, :],
                                                                              op=op_),
                                             reads=[("ta", t), ("tb", t)], writes=[("qr", t, half_, 1)])
                            else:
                                for (j, t) in pair:
                                    qdst, kdst = dsts(t)
                                    S.op("dve", lambda e: e.tensor_copy(out=qdst, in_=qn2[t][:, 0:8, :].rearrange("p (k g) d -> p k g d", k=2)),
                                         reads=[("qn", t)], writes=[("qr", t, 0, 0), ("qr", t, 1, 0)])
                                    S.op("dve", lambda e: e.tensor_copy(out=kdst, in_=qn2[t][:, 8:10, :]),
                                         reads=[("qn", t)], writes=[("qr", t, 0, 1), ("qr", t, 1, 1)])
                            for (j, t) in pair:
                                for c in range(5):
                                    S.op("pe", lambda e: e.transpose(ptr2[t][:, 2 + c, :], qr2[t][:, c * 128:(c + 1) * 128], identb[:]),
                                         reads=[("qr", t, 0, 0), ("qr", t, 0, 1), ("qr", t, 1, 0), ("qr", t, 1, 1), "identb"],
                                         writes=[("ptr", t)])
                            for (j, t) in pair:
                                tok = slice(j * 128, (j + 1) * 128)
                                S.op("act", lambda e: e.activation(out=qT_all[:, :, tok], in_=ptr2[t][:, 2:6, :], func=AF.Copy),
                                     reads=[("ptr", t)], writes=[("qT", j)])
                                for kv_ in range(2):
                                    S.op("act", lambda e: e.activation(out=kT_all[kv_ * 64:(kv_ + 1) * 64, kv_, tok],
                                                                       in_=ptr2[t][kv_ * 64:(kv_ + 1) * 64, 6, :], func=AF.Copy),
                                         reads=[("ptr", t), "kT_zero"], writes=[("kT", j, kv_)])
                        S.barrier()
                        for j in range(NTL, NT):
                            s = 0 if j < NTL else 1
                            tok = slice(j * 128, (j + 1) * 128)
                            for cblk in range(3):
                                for k in range(8):
                                    S.op("pe", lambda e: e.matmul(pz[:, cblk * 512:(cblk + 1) * 512], lhsT=hT[:, k, :],
                                                                  rhs=winb[:, k, cblk * 512:(cblk + 1) * 512],
                                                                  start=(k == 0), stop=(k == 7)),
                                         reads=[("hT", k), "winb"], writes=[("pz", cblk)])
                            if j + 1 < NT:
                                front_B(j + 1)
                            S.op("act", lambda e: e.activation(out=p_all[:, j, :], in_=pz[:, 0:256], func=AF.Copy),
                                 reads=[("pz", 0)], writes=[("p_all", j)])
                            S.op("act", lambda e: e.activation(out=gu[:], in_=pz[:, 256:512], func=AF.Gelu_apprx_tanh),
                                 reads=[("pz", 0)], writes=["gu"])
                            S.op("act", lambda e: e.activation(out=gv[:], in_=pz[:, 512:768], func=AF.Gelu_apprx_tanh),
                                 reads=[("pz", 1)], writes=["gv"])
                            for h in range(4):
                                S.op("dve", lambda e: e.bn_stats(stv[:, h, :], gv[:, h * 64:(h + 1) * 64]),
                                     reads=["gv"], writes=[("stv", h)])
                                S.op("dve", lambda e: e.bn_aggr(mvv[:, h, :], stv[:, h, :]),
                                     reads=[("stv", h)], writes=["mvv"])
                            S.op("act", lambda e: e.activation(out=rsv[:], in_=mvv[:, :, 1], func=AF.Sqrt, bias=epsc[:], scale=1.0),
                                 reads=["mvv", "epsc"], writes=["rsv"])
                            S.op("dve", lambda e: e.reciprocal(rsv[:], rsv[:]), reads=["rsv"], writes=["rsv"])
                            for h in range(4):
                                S.op("dve", lambda e: e.tensor_scalar(out=vn[:, h * 64:(h + 1) * 64], in0=gv[:, h * 64:(h + 1) * 64],
                                                                      scalar1=mvv[:, h, 0:1], scalar2=rsv[:, h:h + 1],
                                                                      op0=ALU.subtract, op1=ALU.mult),
                                     reads=["gv", "mvv", "rsv"], writes=["vn"])
                            S.op("dve", lambda e: e.tensor_tensor(out=vh[:], in0=vn[:], in1=sgug[:], op=ALU.mult),
                                 reads=["vn", "sgug"], writes=["vh"])
                            for h in range(4):
                                S.op("pe", lambda e: e.matmul(psg[:, h * 64:(h + 1) * 64], lhsT=wsT[:, h, :],
                                                              rhs=vh[:, h * 64:(h + 1) * 64], start=True, stop=True),
                                     reads=["wsT", "vh"], writes=["psg"])
                            for h in range(4):
                                S.op("dve", lambda e: e.scalar_tensor_tensor(out=sgo[:, h * 64:(h + 1) * 64],
                                                                             in0=psg[:, h * 64:(h + 1) * 64],
                                                                             scalar=sbT[:, h:h + 1],
                                                                             in1=gu[:, h * 64:(h + 1) * 64],
                                                                             op0=ALU.add, op1=ALU.mult),
                                     reads=["psg", "sbT", "gu"], writes=["sgo"])
                            for c in range(2):
                                S.op("pe", lambda e: e.transpose(ptr[:, c, :], sgo[:, c * 128:(c + 1) * 128], identb[:]),
                                     reads=["sgo", "identb"], writes=[("ptr", 0)])
                            S.op("act", lambda e: e.activation(out=mixps[:, 2:4, tok], in_=ptr[:, 0:2, :], func=AF.Copy),
                                 reads=[("ptr", 0)], writes=[("mixps_s", j)])
                            S.op("act", lambda e: e.activation(out=sq[:], in_=pz[:, 768:1408], func=AF.Square),
                                 reads=[("pz", 1), ("pz", 2)], writes=["sq"])
                            S.op("dve", lambda e: e.tensor_reduce(out=ss[:], in_=sq[:].rearrange("p (h d) -> p h d", d=64),
                                                                  axis=AX.X, op=ALU.add),
                                 reads=["sq"], writes=["ss"])
                            S.op("act", lambda e: e.activation(out=ss[:], in_=ss[:], func=AF.Sqrt, bias=epsc[:], scale=1.0 / 64),
                                 reads=["ss", "epsc"], writes=["ss"])
                            S.op("dve", lambda e: e.reciprocal(ss[:], ss[:]), reads=["ss"], writes=["ss"])
                            S.op("dve", lambda e: e.tensor_tensor(out=qn[:], in0=pz[:, 768:1408].rearrange("p (h d) -> p h d", d=64),
                                                                  in1=ss[:].unsqueeze(2).broadcast_to([128, 10, 64]), op=ALU.mult),
                                 reads=[("pz", 1), ("pz", 2), "ss"], writes=["qn"])
                            S.op("dve", lambda e: e.tensor_tensor(out=qn[:], in0=qn[:],
                                                                  in1=qkg[:].rearrange("p (h d) -> p h d", d=64), op=ALU.mult),
                                 reads=["qn", "qkg"], writes=["qn"])
                            qdst = qr[:, 0:512].rearrange("p (g k d) -> p k g d", g=4, k=2, d=64)
                            kdst = qr[:, 512:640].rearrange("p (k d) -> p k d", d=64)
                            qsrc = qn[:, 0:8, :].rearrange("p (k g) d -> p k g d", k=2)
                            ksrc = qn[:, 8:10, :]
                            if j < NTL:
                                cosb = cs[:, j:j + 1, 0:32].broadcast_to([128, 10, 32])
                                sinb = cs[:, j:j + 1, 32:64].broadcast_to([128, 10, 32])
                                x1 = qn[:, :, 0:32]
                                x2 = qn[:, :, 32:64]
                                S.op("dve", lambda e: e.tensor_tensor(out=ta[:], in0=x1, in1=cosb, op=ALU.mult),
                                     reads=["qn", "cs"], writes=["ta"])
                                S.op("dve", lambda e: e.tensor_tensor(out=tb[:], in0=x2, in1=sinb, op=ALU.mult),
                                     reads=["qn", "cs"], writes=["tb"])
                                S.op("dve", lambda e: e.tensor_tensor(out=qdst[:, :, :, 0:32],
                                                                      in0=ta[:, 0:8, :].rearrange("p (k g) d -> p k g d", k=2),
                                                                      in1=tb[:, 0:8, :].rearrange("p (k g) d -> p k g d", k=2),
                                                                      op=ALU.subtract),
                                     reads=["ta", "tb"], writes=["qr_a"])
                                S.op("dve", lambda e: e.tensor_tensor(out=kdst[:, :, 0:32], in0=ta[:, 8:10, :], in1=tb[:, 8:10, :],
                                                                      op=ALU.subtract),
                                     reads=["ta", "tb"], writes=["qr_b"])
                                S.op("dve", lambda e: e.tensor_tensor(out=ta[:], in0=x2, in1=cosb, op=ALU.mult),
                                     reads=["qn", "cs"], writes=["ta"])
                                S.op("dve", lambda e: e.tensor_tensor(out=tb[:], in0=x1, in1=sinb, op=ALU.mult),
                                     reads=["qn", "cs"], writes=["tb"])
                                S.op("dve", lambda e: e.tensor_tensor(out=qdst[:, :, :, 32:64],
                                                                      in0=ta[:, 0:8, :].rearrange("p (k g) d -> p k g d", k=2),
                                                                      in1=tb[:, 0:8, :].rearrange("p (k g) d -> p k g d", k=2),
                                                                      op=ALU.add),
                                     reads=["ta", "tb"], writes=["qr_c"])
                                S.op("dve", lambda e: e.tensor_tensor(out=kdst[:, :, 32:64], in0=ta[:, 8:10, :], in1=tb[:, 8:10, :],
                                                                      op=ALU.add),
                                     reads=["ta", "tb"], writes=["qr_d"])
                            else:
                                S.op("dve", lambda e: e.tensor_copy(out=qdst, in_=qsrc), reads=["qn"], writes=["qr_a", "qr_c"])
                                S.op("dve", lambda e: e.tensor_copy(out=kdst, in_=ksrc), reads=["qn"], writes=["qr_b", "qr_d"])
                            for c in range(5):
                                S.op("pe", lambda e: e.transpose(ptr[:, 2 + c, :], qr[:, c * 128:(c + 1) * 128], identb[:]),
                                     reads=["qr_a", "qr_b", "qr_c", "qr_d", "identb"], writes=[("ptr", 1)])
                            S.op("act", lambda e: e.activation(out=qT_all[:, :, tok], in_=ptr[:, 2:6, :], func=AF.Copy),
                                 reads=[("ptr", 1)], writes=[("qT", j)])
                            for kv_ in range(2):
                                S.op("act", lambda e: e.activation(out=kT_all[kv_ * 64:(kv_ + 1) * 64, kv_, tok],
                                                                   in_=ptr[kv_ * 64:(kv_ + 1) * 64, 6, :], func=AF.Copy),
                                     reads=[("ptr", 1), "kT_zero"], writes=[("kT", j, kv_)])
                            S.op("dve", lambda e: e.tensor_copy(out=v_all[:, j, :, 0:64],
                                                                in_=pz[:, 1408:1536].rearrange("p (k d) -> p k d", d=64)),
                                 reads=[("pz", 2)], writes=[("v", j)])
                    S.barrier()
                    esC = ExitStack()
                    with esC:
                        bandf = SB(esC, "bandf", [128, 4, 5, 128], F32)
                        band = SB(esC, "band", [128, 4, 5, 128], BF16)
                        wbd = SB(esC, "wbd", [128, 2, 128], BF16)
                        pscT = SB(esC, "pscT", [128, 2], F32)
                        pooledT = SB(esC, "pooledT", [128, 2, 128], BF16)
                        ppool = PS(esC, "ppool", [128, 4, 128], F32)
                        pp2 = PS(esC, "pp2", [128, 2, 128], F32)
                        S.dma("sp", lambda e: e.dma_start(out=bandf[:], in_=band_in), writes=["bandf"])
                        S.op("dve", lambda e: e.tensor_copy(out=band[:], in_=bandf[:]), reads=["bandf"], writes=["band"])
                        S.dma("pool", lambda e: e.dma_start(out=wbd[:], in_=wbd_in[ll].rearrange("c p q -> p c q")), writes=["wbd"])
                        S.dma("sp", lambda e: e.dma_start(out=pscT[:], in_=pscT_in[ll]), writes=["pscT"])
                        for j in range(NT):
                            tok = slice(j * 128, (j + 1) * 128)
                            lo_t, hi_t = (0, NTL - 1) if j < NTL else (NTL, NT - 1)
                            rels = [r for r in (-1, 0, 1) if lo_t <= j + r <= hi_t]
                            for c in range(2):
                                for bi in range(2):
                                    g = 2 * c + bi
                                    for ri, r in enumerate(rels):
                                        if r == -1:
                                            v = 0
                                        elif r == 1:
                                            v = 4
                                        else:
                                            v = 2 if j == lo_t else (3 if j == hi_t else 1)
                                        S.op("pe", lambda e: e.matmul(ppool[:, c * 2 + bi, :], lhsT=p_all[:, j + r, c * 128:(c + 1) * 128],
                                                                      rhs=band[:, g, v, :], start=(ri == 0), stop=(ri == len(rels) - 1)),
                                             reads=["band"], writes=["ppool"])
                                S.op("act", lambda e: e.activation(out=pooledT[0:64, c, :], in_=ppool[0:64, c * 2, :], func=AF.Copy),
                                     reads=["ppool"], writes=[("pooledT", c, 0)])
                                S.op("act", lambda e: e.activation(out=pooledT[64:128, c, :], in_=ppool[64:128, c * 2 + 1, :], func=AF.Copy),
                                     reads=["ppool"], writes=[("pooledT", c, 1)])
                            for c in range(2):
                                S.op("pe", lambda e: e.matmul(pp2[:, c, :], lhsT=wbd[:, c, :], rhs=pooledT[:, c, :], start=True, stop=True),
                                     reads=["wbd", ("pooledT", c, 0), ("pooledT", c, 1)], writes=["pp2"])
                            for c in range(2):
                                S.op("act", lambda e: e.activation(out=mixps[:, c, tok], in_=pp2[:, c, :], func=AF.Copy,
                                                                   scale=pscT[:, c:c + 1]),
                                     reads=["pp2", "pscT"], writes=[("mixps_p", j)])
                    S.barrier()
                if debug and ll == 0:
                    S.dma("sp", lambda e: e.dma_start(out=dbg_mix, in_=mixps[:]), writes=["dbgmix"])
                esD = ExitStack()
                with esD:
                    woutb = SB(esD, "woutb", [128, 8, D], BF16)
                    lnr = SB(esD, "lnr", [128, 2, D], F32)
                    wr = SB(esD, "wr", [128, 8, NE], F32)
                    PT = [SB(esD, "PT%d" % i, [128, 512], BF16) for i in range(3)]
                    attn_tok = [SB(esD, "attn_tok%d" % i, [128, 4, 512], BF16) for i in range(2)]
                    rden = SB(esD, "rden", [128, 4], F32)
                    mixA = SB(esD, "mixA", [128, 4, 128], BF16)
                    bufA = SB(esD, "bufA", [128, D], F32)
                    bufB = SB(esD, "bufB", [128, D], F32)
                    xhb = SB(esD, "xhb", [128, D], BF16)
                    h2T = SB(esD, "h2T", [128, 8, 128], F32)
                    st = SB(esD, "stE", [128, 2, 6], F32)
                    mv = SB(esD, "mvE", [128, 2], F32)
                    rstd = SB(esD, "rstdE", [128, 1], F32)
                    mx = SB(esD, "mx", [128, 1], F32)
                    sm = SB(esD, "sm", [128, 1], F32)
                    ex = SB(esD, "ex", [128, NE], F32)
                    g1rows = build_gate_rows(esD, 16, "g1r")
                    pS = [PS(esD, "pS%d" % i, [128, 512], F32) for i in range(3)]
                    pO = [PS(esD, "pO%d" % i, [128, 4, 128], F32) for i in range(2)]
                    ptr = PS(esD, "ptrD", [128, 4, 128], BF16)
                    pbig = PS(esD, "pbig", [128, D], F32)
                    pr = pbig[:, 0:NE]
                    S.dma("pool", lambda e: e.dma_start(out=woutb[:], in_=w_out[ll].rearrange("(k p) c -> p k c", p=128)),
                          writes=["woutb"])
                    S.dma("sp", lambda e: e.dma_start(out=lnr[:], in_=ln_in[ll:ll + 1, 0:2, :].broadcast_to([128, 2, D])),
                          writes=["lnr"])
                    S.dma("sp", lambda e: e.dma_start(out=wr[:], in_=wr_in[ll]), writes=["wr"])

                    def rstd_act(tag):
                        S.op("act", lambda e: e.activation(out=rstd[:], in_=mv[:, 1:2], func=AF.Ln, bias=epsc[:], scale=1.0),
                             reads=[tag + "mv", "epsc"], writes=[tag + "rstd"])
                        S.op("act", lambda e: e.activation(out=rstd[:], in_=rstd[:], func=AF.Exp, scale=-0.5),
                             reads=[tag + "rstd"], writes=[tag + "rstd"])

                    def stats_dve(src, key_src, tag):
                        S.op("dve", lambda e: e.bn_stats(st[:, 0, :], src[:, 0:512]), reads=[key_src], writes=[tag + "st0"])
                        S.op("dve", lambda e: e.bn_stats(st[:, 1, :], src[:, 512:1024]), reads=[key_src], writes=[tag + "st1"])
                        S.op("dve", lambda e: e.bn_aggr(mv[:], st[:]), reads=[tag + "st0", tag + "st1"], writes=[tag + "mv"])

                    def make_ef_stages(j, qs, par):
                        s = 0 if j < NTL else 1
                        tok = slice(j * 128, (j + 1) * 128)
                        at = attn_tok[par]

                        def st0():
                            for cc in range(4):
                                S.op("pe", lambda e: e.transpose(ptr[:, cc, :], at[:, qs, cc * 128:(cc + 1) * 128], identb[:]),
                                     reads=[("attn_tok", par, qs), "identb"], writes=["ptrD"])
                            S.dma("sp", lambda e: e.dma_start(out=bufA[:], in_=src_tile(ll, j)), reads=[src_key(ll, j)], writes=["bufA"])

                        def st1():
                            S.op("act", lambda e: e.activation(out=mixA[:], in_=ptr[:], func=AF.Copy), reads=["ptrD"], writes=["mixA"])

                        def st2():
                            for half in range(2):
                                for c8 in range(8):
                                    lhs = mixps[:, c8, tok] if c8 < 4 else mixA[:, c8 - 4, :]
                                    S.op("pe", lambda e: e.matmul(pbig[:, half * 512:(half + 1) * 512], lhsT=lhs,
                                                                  rhs=woutb[:, c8, half * 512:(half + 1) * 512],
                                                                  start=(c8 == 0), stop=(c8 == 7)),
                                         reads=["mixA", "woutb"], writes=["pbig"])

                        def st3():
                            S.op("dve", lambda e: e.tensor_tensor(out=bufB[:], in0=pbig[:], in1=g1rows[s][:], op=ALU.mult),
                                 reads=["pbig", "g1r_%d" % s], writes=["bufB"])
                            S.op("dve", lambda e: e.scalar_tensor_tensor(out=bufA[:], in0=bufA[:], scalar=ALPHA, in1=bufB[:],
                                                                         op0=ALU.mult, op1=ALU.add),
                                 reads=["bufA", "bufB"], writes=["bufA"])
                            stats_dve(bufA, "bufA", "E")

                        def st4():
                            rstd_act("E")

                        def st5():
                            S.op("dve", lambda e: e.tensor_scalar(out=bufB[:], in0=bufA[:], scalar1=mv[:, 0:1], scalar2=rstd[:],
                                                                  op0=ALU.subtract, op1=ALU.mult),
                                 reads=["bufA", "Emv", "Erstd"], writes=["bufB"])
                            S.op("dve", lambda e: e.tensor_tensor(out=bufB[:], in0=bufB[:], in1=lnr[:, 0, :], op=ALU.mult),
                                 reads=["bufB", "lnr"], writes=["bufB"])
                            S.op("dve", lambda e: e.tensor_tensor(out=bufB[:], in0=bufB[:], in1=lnr[:, 1, :], op=ALU.add),
                                 reads=["bufB", "lnr"], writes=["bufB"])
                            S.op("dve", lambda e: e.tensor_scalar(out=bufA[:], in0=bufB[:], scalar1=ALPHA, scalar2=None, op0=ALU.mult),
                                 reads=["bufB"], writes=["bufA"])
                            S.dma("sp", lambda e: e.dma_start(out=macc[tok, :], in_=bufA[:]), reads=["bufA"], writes=[("macc", j)])
                            if debug and ll == 0:
                                S.dma("sp", lambda e: e.dma_start(out=dbg_x1[tok, :], in_=bufA[:]), reads=["bufA"], writes=[("dbgx1", j)])
                            stats_dve(bufB, "bufB", "E")

                        def st6():
                            rstd_act("E")

                        def st7():
                            S.op("dve", lambda e: e.tensor_scalar(out=bufB[:], in0=bufB[:], scalar1=mv[:, 0:1], scalar2=rstd[:],
                                                                  op0=ALU.subtract, op1=ALU.mult),
                                 reads=["bufB", "Emv", "Erstd"], writes=["bufB"])
                            S.op("dve", lambda e: e.tensor_copy(out=xhb[:], in_=bufB[:]), reads=["bufB"], writes=["xhb"])
                            S.dma("sp", lambda e: e.dma_start(out=xh2[tok, :], in_=xhb[:]), reads=["xhb"], writes=[("xh2", j)])

                        def st8():
                            for k in range(8):
                                S.op("pe", lambda e: e.transpose(pbig[:, k * 128:(k + 1) * 128], bufB[:, k * 128:(k + 1) * 128], identf[:]),
                                     reads=["bufB", "identf"], writes=["pbig"])

                        def st9():
                            for k in range(8):
                                S.op("dve", lambda e: e.tensor_scalar(out=h2T[:, k, :], in0=pbig[:, k * 128:(k + 1) * 128],
                                                                      scalar1=modT[:, 32 + k, s:s + 1], scalar2=modT[:, 24 + k, s:s + 1],
                                                                      op0=ALU.mult, op1=ALU.add),
                                     reads=["pbig", "modT"], writes=[("h2T", k)])

                        def st10():
                            for k in range(8):
                                S.op("pe", lambda e: e.matmul(pr, lhsT=h2T[:, k, :], rhs=wr[:, k, :], start=(k == 0), stop=(k == 7)),
                                     reads=[("h2T", k), "wr"], writes=["pbig"])

                        def st11():
                            S.op("dve", lambda e: e.reduce_max(out=mx[:], in_=pr, axis=AX.X), reads=["pbig"], writes=["mx"])
                            S.op("dve", lambda e: e.tensor_scalar(out=mx[:], in0=mx[:], scalar1=-1.0, scalar2=None, op0=ALU.mult),
                                 reads=["mx"], writes=["mx"])

                        def st12():
                            S.op("act", lambda e: e.activation(out=ex[:], in_=pr, func=AF.Exp, bias=mx[:], scale=1.0, accum_out=sm[:]),
                                 reads=["pbig", "mx"], writes=["ex", "sm"])

                        def st13():
                            S.op("dve", lambda e: e.reciprocal(sm[:], sm[:]), reads=["sm"], writes=["sm"])
                            S.op("dve", lambda e: e.tensor_scalar(out=aff_all[:, j, :], in0=ex[:], scalar1=sm[:], scalar2=None, op0=ALU.mult),
                                 reads=["ex", "sm"], writes=[("aff", j)])

                        return [st0, st1, st2, st3, st4, st5, st6, st7, st8, st9, st10, st11, st12, st13]

                    blocks = [(NTL, 2)] + [(qb * 4, 4) for qb in range(8)]
                    nS = 0
                    nO = 0
                    pending = []
                    for bi, (t0, ntile) in enumerate(blocks):
                        par = bi % 2
                        N = ntile * 128
                        qtok = slice(t0 * 128, t0 * 128 + N)
                        kts = list(range(NT)) if t0 < NTL else [NTL, NTL + 1]
                        steps = [(c, kv, ki, kt) for c in range(4) for kv in range(2) for ki, kt in enumerate(kts)]
                        spacing = max(1, (len(steps) - 4) // (len(pending) + 1))

                        def emit_S(i):
                            c, kv, ki, kt = steps[i]
                            pSb = pS[(nS + i) % 3]
                            pSk = "pS%d" % ((nS + i) % 3)
                            PTb = PT[(nS + i) % 3]
                            PTk = "PT%d" % ((nS + i) % 3)
                            S.op("pe", lambda e: e.matmul(pSb[:, 0:N], lhsT=kT_all[:, kv, kt * 128:(kt + 1) * 128],
                                                          rhs=qT_all[:, c, qtok], start=True, stop=True),
                                 writes=[pSk])
                            S.op("act", lambda e: e.activation(out=PTb[:, 0:N], in_=pSb[:, 0:N], func=AF.Exp, scale=0.125),
                                 reads=[pSk], writes=[PTk])

                        emit_S(0)
                        emit_S(1)
                        for i, (c, kv, ki, kt) in enumerate(steps):
                            h = kv * 4 + c
                            if i + 2 < len(steps):
                                emit_S(i + 2)
                            if pending and i % spacing == spacing - 1:
                                pending.pop(0)()
                            if ki == 0:
                                nO += 1
                                if nO == 1:
                                    S.op("dve", lambda e: e.memset(pO[nO % 2][:], 0.0), writes=["pO%d" % (nO % 2)])
                            if ki == 1:
                                S.op("dve", lambda e: e.memset(pO[(nO + 1) % 2][:], 0.0), writes=["pO%d" % ((nO + 1) % 2)])
                            pOb = pO[nO % 2]
                            pOk = "pO%d" % (nO % 2)
                            PTb = PT[(nS + i) % 3]
                            PTk = "PT%d" % ((nS + i) % 3)
                            for qs in range(ntile):
                                S.op("pe", lambda e: e.matmul(pOb[:, qs, 0:65], lhsT=PTb[:, qs * 128:(qs + 1) * 128],
                                                              rhs=v_all[:, kt, kv, :], start=False, stop=(ki == len(kts) - 1)),
                                     reads=[PTk], writes=[pOk])
                            if ki == len(kts) - 1:
                                S.op("dve", lambda e: e.reciprocal(rden[:, 0:ntile], pOb[:, 0:ntile, 64]),
                                     reads=[pOk], writes=["rden"])
                                for qs in range(ntile):
                                    S.op("dve", lambda e: e.tensor_scalar(out=attn_tok[par][:, qs, h * 64:(h + 1) * 64], in0=pOb[:, qs, 0:64],
                                                                          scalar1=rden[:, qs:qs + 1], scalar2=None, op0=ALU.mult),
                                         reads=[pOk, "rden"], writes=[("attn_tok", par, qs)])
                        nS += len(steps)
                        while pending:
                            pending.pop(0)()
                        for qs in range(ntile):
                            pending.extend(make_ef_stages(t0 + qs, qs, par))
                    while pending:
                        pending.pop(0)()
                S.barrier()
            if debug and ll == 0:
                S.dma("sp", lambda e: e.dma_start(out=dbg_aff, in_=aff_all[:]), writes=["dbgaff"])
            esGH = ExitStack()
            with esGH:
                idx_i = SB(esGH, "idx_i", [128, NE, 5], I32)
                gate_s = SB(esGH, "gate_s", [128, NE, 5], F32)
                esG = ExitStack()
                with esG:
                    lo = SB(esG, "lo", [128, 2, NE], F32)
                    hi = SB(esG, "hi", [128, 2, NE], F32)
                    mid = SB(esG, "mid", [128, 2, NE], F32)
                    capv = SB(esG, "capv", [128, 2, NE], F32)
                    cmp_ = SB(esG, "cmp", [128, NT, NE], F32)
                    cnt = SB(esG, "cnt", [128, 2, NE], F32)
                    ge = SB(esG, "ge", [128, 2, NE], F32)
                    d1 = SB(esG, "d1", [128, 2, NE], F32)
                    mask = SB(esG, "mask", [128, NT, NE], F32)
                    tg = SB(esG, "tg", [128, NT, NE, 5], BF16)
                    gr1 = SB(esG, "gr1", [128, NT, NE], F32)
                    gr2 = SB(esG, "gr2", [128, NT, NE], F32)
                    lst = SB(esG, "lst", [128, 5, 5], F32)
                    pos = SB(esG, "pos", [128, NT, NE], F32)
                    off = SB(esG, "off", [128, NT, NE], F32)
                    tot = SB(esG, "tot", [128, NT, NE], F32)
                    oh = [SB(esG, "oh%d" % i, [128, 512], BF16) for i in range(4)]
                    idx_f = SB(esG, "idx_f", [128, NE, 5], F32)
                    ptot = PS(esG, "ptot", [128, 2, NE], F32)
                    pwl = PS(esG, "pwl", [128, 512], F32)
                    pwc = PS(esG, "pwc", [128, 32], F32)
                    ptl = PS(esG, "ptl", [128, 512], F32)
                    ptc = PS(esG, "ptc", [128, 32], F32)
                    plist = [PS(esG, "plist0", [128, 5, 128], F32)]
                    S.op("pool", lambda e: e.memset(lo[:], 0.0), writes=["lo"])
                    S.op("pool", lambda e: e.memset(hi[:], 1.0), writes=["hi"])
                    S.op("pool", lambda e: e.memset(capv[:, 0, :], float(CAP_L)), writes=["capv0"])
                    S.op("pool", lambda e: e.memset(capv[:, 1, :], float(CAP_C)), writes=["capv1"])
                    aff_l = aff_all[:, 0:NTL, :]
                    aff_c = aff_all[:, NTL:NT, :]
                    for it in range(30):
                        S.op("dve", lambda e: e.tensor_tensor(out=mid[:], in0=lo[:], in1=hi[:], op=ALU.add),
                             reads=["lo", "hi"], writes=["mid"])
                        S.op("dve", lambda e: e.tensor_scalar(out=mid[:], in0=mid[:], scalar1=0.5, scalar2=None, op0=ALU.mult),
                             reads=["mid"], writes=["mid"])
                        S.op("dve", lambda e: e.tensor_tensor(out=cmp_[:, 0:NTL, :], in0=aff_l,
                                                              in1=mid[:, 0:1, :].broadcast_to([128, NTL, NE]), op=ALU.is_ge),
                             reads=["mid"], writes=["cmp_l"])
                        S.op("dve", lambda e: e.tensor_tensor(out=cmp_[:, NTL:NT, :], in0=aff_c,
                                                              in1=mid[:, 1:2, :].broadcast_to([128, 2, NE]), op=ALU.is_ge),
                             reads=["mid"], writes=["cmp_c"])
                        S.op("dve", lambda e: e.tensor_reduce(out=cnt[:, 0, :], in_=cmp_[:, 0:NTL, :].rearrange("p j e -> p e j"),
                                                              axis=AX.X, op=ALU.add),
                             reads=["cmp_l"], writes=["cnt0"])
                        S.op("dve", lambda e: e.tensor_reduce(out=cnt[:, 1, :], in_=cmp_[:, NTL:NT, :].rearrange("p j e -> p e j"),
                                                              axis=AX.X, op=ALU.add),
                             reads=["cmp_c"], writes=["cnt1"])
                        S.op("pe", lambda e: e.matmul(ptot[:], lhsT=ones_f[:], rhs=cnt[:], start=True, stop=True),
                             reads=["cnt0", "cnt1", "ones_f"], writes=["ptot"])
                        S.op("dve", lambda e: e.tensor_tensor(out=ge[:], in0=ptot[:], in1=capv[:], op=ALU.is_ge),
                             reads=["ptot", "capv0", "capv1"], writes=["ge"])
                        S.op("dve", lambda e: e.tensor_tensor(out=d1[:], in0=mid[:], in1=lo[:], op=ALU.subtract),
                             reads=["mid", "lo"], writes=["d1"])
                        S.op("dve", lambda e: e.tensor_tensor(out=d1[:], in0=d1[:], in1=ge[:], op=ALU.mult),
                             reads=["d1", "ge"], writes=["d1"])
                        S.op("dve", lambda e: e.tensor_tensor(out=lo[:], in0=lo[:], in1=d1[:], op=ALU.add),
                             reads=["d1", "lo"], writes=["lo"])
                        S.op("dve", lambda e: e.tensor_tensor(out=d1[:], in0=hi[:], in1=mid[:], op=ALU.subtract),
                             reads=["mid", "hi"], writes=["d1"])
                        S.op("dve", lambda e: e.tensor_tensor(out=d1[:], in0=d1[:], in1=ge[:], op=ALU.mult),
                             reads=["d1", "ge"], writes=["d1"])
                        S.op("dve", lambda e: e.tensor_tensor(out=hi[:], in0=mid[:], in1=d1[:], op=ALU.add),
                             reads=["d1", "mid"], writes=["hi"])
                    S.op("dve", lambda e: e.tensor_tensor(out=mask[:, 0:NTL, :], in0=aff_l,
                                                          in1=lo[:, 0:1, :].broadcast_to([128, NTL, NE]), op=ALU.is_ge),
                         reads=["lo"], writes=["mask_l"])
                    S.op("dve", lambda e: e.tensor_tensor(out=mask[:, NTL:NT, :], in0=aff_c,
                                                          in1=lo[:, 1:2, :].broadcast_to([128, 2, NE]), op=ALU.is_ge),
                         reads=["lo"], writes=["mask_c"])
                    S.op("dve", lambda e: e.tensor_copy(out=tg[:, :, :, 0], in_=tokA[:].unsqueeze(2).broadcast_to([128, NT, NE])),
                         reads=["tokA"], writes=["tg0"])
                    S.op("dve", lambda e: e.tensor_copy(out=tg[:, :, :, 1], in_=tokB[:].unsqueeze(2).broadcast_to([128, NT, NE])),
                         reads=["tokB"], writes=["tg0"])
                    S.op("dve", lambda e: e.tensor_copy(out=tg[:, :, :, 2], in_=aff_all[:]), writes=["tg1"])
                    S.op("dve", lambda e: e.tensor_tensor(out=gr1[:], in0=aff_all[:], in1=tg[:, :, :, 2], op=ALU.subtract),
                         reads=["tg1"], writes=["gr1"])
                    S.op("dve", lambda e: e.tensor_copy(out=tg[:, :, :, 3], in_=gr1[:]), reads=["gr1"], writes=["tg1"])
                    S.op("dve", lambda e: e.tensor_tensor(out=gr2[:], in0=gr1[:], in1=tg[:, :, :, 3], op=ALU.subtract),
                         reads=["gr1", "tg1"], writes=["gr2"])
                    S.op("dve", lambda e: e.tensor_copy(out=tg[:, :, :, 4], in_=gr2[:]), reads=["gr2"], writes=["tg1"])
                    mk2 = mask[:].rearrange("p j e -> p (j e)")
                    S.op("pe", lambda e: e.matmul(pwl[:], lhsT=lmat[:], rhs=mk2[:, 0:512], start=True, stop=True),
                         reads=["mask_l", "lmat"], writes=["pwl"])
                    S.op("pe", lambda e: e.matmul(pwc[:], lhsT=lmat[:], rhs=mk2[:, 512:544], start=True, stop=True),
                         reads=["mask_c", "lmat"], writes=["pwc"])
                    S.op("pe", lambda e: e.matmul(ptl[:], lhsT=ones_f[:], rhs=mk2[:, 0:512], start=True, stop=True),
                         reads=["mask_l", "ones_f"], writes=["ptl"])
                    S.op("pe", lambda e: e.matmul(ptc[:], lhsT=ones_f[:], rhs=mk2[:, 512:544], start=True, stop=True),
                         reads=["mask_c", "ones_f"], writes=["ptc"])
                    pos2 = pos[:].rearrange("p j e -> p (j e)")
                    tot2 = tot[:].rearrange("p j e -> p (j e)")
                    S.op("act", lambda e: e.activation(out=pos2[:, 0:512], in_=pwl[:], func=AF.Copy), reads=["pwl"], writes=["pos_l"])
                    S.op("act", lambda e: e.activation(out=pos2[:, 512:544], in_=pwc[:], func=AF.Copy), reads=["pwc"], writes=["pos_c"])
                    S.op("act", lambda e: e.activation(out=tot2[:, 0:512], in_=ptl[:], func=AF.Copy), reads=["ptl"], writes=["tot"])
                    S.op("act", lambda e: e.activation(out=tot2[:, 512:544], in_=ptc[:], func=AF.Copy), reads=["ptc"], writes=["tot"])
                    S.op("pool", lambda e: e.memset(off[:], 0.0), writes=["off"])
                    for j in range(1, NTL):
                        S.op("dve", lambda e: e.tensor_tensor(out=off[:, j, :], in0=off[:, j - 1, :], in1=tot[:, j - 1, :], op=ALU.add),
                             reads=["off", "tot"], writes=["off"])
                    S.op("dve", lambda e: e.tensor_copy(out=off[:, NTL + 1, :], in_=tot[:, NTL, :]), reads=["off", "tot"], writes=["off"])
                    S.op("dve", lambda e: e.tensor_tensor(out=pos[:], in0=pos[:], in1=off[:], op=ALU.add),
                         reads=["pos_l", "pos_c", "off"], writes=["pos_l", "pos_c"])
                    S.op("dve", lambda e: e.scalar_tensor_tensor(out=pos[:], in0=pos[:], scalar=-BIG, in1=mask[:], op0=ALU.add, op1=ALU.mult),
                         reads=["pos_l", "pos_c", "mask_l", "mask_c"], writes=["pos_l", "pos_c"])
                    S.op("dve", lambda e: e.tensor_scalar(out=pos[:], in0=pos[:], scalar1=BIG, scalar2=None, op0=ALU.add),
                         reads=["pos_l", "pos_c"], writes=["pos"])
                    n_oh = 0
                    for ex_ in range(NE):
                        pl = plist[0]
                        plk = "plist0"
                        S.op("dve", lambda e: e.memset(pl[:], 0.0), writes=[plk])
                        for j in range(NT):
                            ohb = oh[n_oh % 4]
                            ohk = "oh%d" % (n_oh % 4)
                            eng_ = "dve"
                            n_oh += 1
                            if j < NTL:
                                S.op(eng_, lambda e: e.tensor_scalar(out=ohb[:], in0=iota_row[:], scalar1=pos[:, j, ex_:ex_ + 1],
                                                                     scalar2=None, op0=ALU.is_equal),
                                     reads=["iota_row", "pos"], writes=[ohk])
                                for st_ in range(4):
                                    S.op("pe", lambda e: e.matmul(pl[:, st_, 0:5], lhsT=ohb[:, st_ * 128:(st_ + 1) * 128],
                                                                  rhs=tg[:, j, ex_, :], start=False, stop=(j == NTL - 1)),
                                         reads=[ohk, "tg0", "tg1"], writes=[plk])
                            else:
                                S.op(eng_, lambda e: e.tensor_scalar(out=ohb[:, 0:128], in0=iota_row[:, 0:128], scalar1=pos[:, j, ex_:ex_ + 1],
                                                                     scalar2=None, op0=ALU.is_equal),
                                     reads=["iota_row", "pos"], writes=[ohk])
                                S.op("pe", lambda e: e.matmul(pl[:, 4, 0:5], lhsT=ohb[:, 0:128], rhs=tg[:, j, ex_, :],
                                                              start=False, stop=(j == NT - 1)),
                                     reads=[ohk, "tg0", "tg1"], writes=[plk])
                        S.op("act", lambda e: e.activation(out=lst[:], in_=pl[:, :, 0:5], func=AF.Copy),
                             reads=[plk], writes=["lst"])
                        S.op("dve", lambda e: e.tensor_tensor(out=idx_f[:, ex_, :], in0=lst[:, :, 0], in1=lst[:, :, 1], op=ALU.add),
                             reads=["lst"], writes=["idx_f"])
                        S.op("dve", lambda e: e.tensor_tensor(out=gate_s[:, ex_, :], in0=lst[:, :, 2], in1=lst[:, :, 3], op=ALU.add),
                             reads=["lst"], writes=["gate_s"])
                        S.op("dve", lambda e: e.tensor_tensor(out=gate_s[:, ex_, :], in0=gate_s[:, ex_, :], in1=lst[:, :, 4], op=ALU.add),
                             reads=["lst", "gate_s"], writes=["gate_s"])
                    S.op("dve", lambda e: e.tensor_copy(out=idx_i[:], in_=idx_f[:]), reads=["idx_f"], writes=["idx_i"])
                    if debug and ll == 0:
                        S.dma("sp", lambda e: e.dma_start(out=dbg_idx, in_=idx_i[:]), reads=["idx_i"], writes=["dbgidx"])
                        S.dma("sp", lambda e: e.dma_start(out=dbg_gate, in_=gate_s[:]), reads=["gate_s"], writes=["dbggate"])
                        S.dma("sp", lambda e: e.dma_start(out=dbg_pos, in_=pos[:]), reads=["pos"], writes=["dbgpos"])
                S.barrier()
                esH = ExitStack()
                with esH:
                    g2rows = build_gate_rows(esH, 40, "g2r")
                    wb = [[SB(esH, "wb%d_%d" % (m, i), [128, 8, D], BF16) for m in range(3)] for i in range(2)]
                    xg = [[SB(esH, "xg%d_%d" % (st_, i), [128, D], BF16) for st_ in range(5)] for i in range(2)]
                    xgT = [SB(esH, "xgT%d" % i, [128, 8, 544], BF16) for i in range(2)]
                    hidT = SB(esH, "hidT", [128, 8, 544], BF16)
                    s1 = SB(esH, "s1", [128, 544], F32)
                    ysc = [SB(esH, "ysc%d" % i, [128, D], F32) for i in range(4)]
                    pxT = [PS(esH, "pxT%d" % i, [128, 8, 128], BF16) for i in range(1)]
                    ph1 = [PS(esH, "ph1_%d" % i, [128, 512], F32) for i in range(2)]
                    ph3 = [PS(esH, "ph3_%d" % i, [128, 512], F32) for i in range(2)]
                    phc = PS(esH, "phc", [128, 2, 32], F32)
                    py = [PS(esH, "pyH%d" % i, [128, 512], F32) for i in range(2)]
                    wsrc = (w1, w3, w2)

                    def issue_loads(ex_):
                        i = ex_ % 2
                        for st_ in range(5):
                            npart = 128 if st_ < 4 else CAP_C
                            S.dma("pool", lambda e: e.indirect_dma_start(
                                out=xg[i][st_][0:npart, :], out_offset=None, in_=xh2[:, :],
                                in_offset=bass.IndirectOffsetOnAxis(ap=idx_i[0:npart, ex_, st_:st_ + 1], axis=0),
                                bounds_check=bcreg, oob_is_err=False),
                                reads=["idx_i"] + [("xh2", j) for j in range(NT)], writes=["xg%d_%d" % (st_, i)])
                        for m in range(3):
                            S.dma("pool", lambda e: e.dma_start(out=wb[i][m][:], in_=wsrc[m][ll, ex_].rearrange("(k p) c -> p k c", p=128)),
                                  writes=["wb%d_%d" % (m, i)])

                    def prep_tile(ex_, st_):
                        i = ex_ % 2
                        npart = 128 if st_ < 4 else CAP_C
                        s = 0 if st_ < 4 else 1
                        c0 = st_ * 128
                        npx[0] += 1
                        pb = pxT[0]
                        pk = "pxT0"
                        for k in range(8):
                            S.op("pe", lambda e: e.transpose(pb[:, k, 0:npart], xg[i][st_][0:npart, k * 128:(k + 1) * 128],
                                                             identb[0:npart, 0:npart]),
                                 reads=["xg%d_%d" % (st_, i), "identb"], writes=[pk])
                        for k in range(8):
                            if k % 2 == 0:
                                S.op("act", lambda e: e.activation(out=xgT[i][:, k, c0:c0 + npart], in_=pb[:, k, 0:npart], func=AF.Identity,
                                                                   bias=modT[:, 24 + k, s:s + 1], scale=modT[:, 32 + k, s:s + 1]),
                                     reads=[pk, "modT"], writes=["xgT%d" % i])
                            else:
                                S.op("dve", lambda e: e.tensor_scalar(out=xgT[i][:, k, c0:c0 + npart], in0=pb[:, k, 0:npart],
                                                                      scalar1=modT[:, 32 + k, s:s + 1], scalar2=modT[:, 24 + k, s:s + 1],
                                                                      op0=ALU.mult, op1=ALU.add),
                                     reads=[pk, "modT"], writes=["xgT%d" % i])

                    npx = [0]
                    issue_loads(0)
                    for st_ in range(5):
                        prep_tile(0, st_)
                    nys = 0
                    npy = 0
                    for ex_ in range(NE):
                        i = ex_ % 2
                        xgTi = xgT[i]
                        xk_ = "xgT%d" % i
                        if ex_ + 1 < NE:
                            issue_loads(ex_ + 1)
                        for fc in range(8):
                            fb = fc % 2
                            for m, ph in ((0, ph1[fb]), (1, ph3[fb])):
                                for k in range(8):
                                    S.op("pe", lambda e: e.matmul(ph[:], lhsT=wb[i][m][:, k, fc * 128:(fc + 1) * 128], rhs=xgTi[:, k, 0:512],
                                                                  start=(k == 0), stop=(k == 7)),
                                         reads=[xk_, "wb%d_%d" % (m, i)], writes=["ph%d_%d" % (m, fb)])
                                for k in range(8):
                                    S.op("pe", lambda e: e.matmul(phc[:, m, :], lhsT=wb[i][m][:, k, fc * 128:(fc + 1) * 128], rhs=xgTi[:, k, 512:544],
                                                                  start=(k == 0), stop=(k == 7)),
                                         reads=[xk_, "wb%d_%d" % (m, i)], writes=["phc"])
                            S.op("act", lambda e: e.activation(out=s1[:, 512:544], in_=phc[:, 0, :], func=AF.Silu), reads=["phc"], writes=["s1b"])
                            S.op("dve", lambda e: e.tensor_tensor(out=hidT[:, fc, 512:544], in0=s1[:, 512:544], in1=phc[:, 1, :], op=ALU.mult),
                                 reads=["s1b", "phc"], writes=[("hidT", 1)])
                            S.op("act", lambda e: e.activation(out=s1[:, 0:512], in_=ph1[fb][:], func=AF.Silu), reads=["ph0_%d" % fb], writes=["s1a"])
                            S.op("dve", lambda e: e.tensor_tensor(out=hidT[:, fc, 0:512], in0=s1[:, 0:512], in1=ph3[fb][:], op=ALU.mult),
                                 reads=["s1a", "ph1_%d" % fb], writes=[("hidT", 0)])
                            if ex_ + 1 < NE and fc < 5:
                                prep_tile(ex_ + 1, fc)
                        for st_ in range(5):
                            npart = 128 if st_ < 4 else CAP_C
                            s = 0 if st_ < 4 else 1
                            c0 = st_ * 128
                            yb = ysc[nys % 4]
                            yk = "ysc%d" % (nys % 4)
                            nys += 1
                            for half in range(2):
                                pyb = py[npy % 2]
                                pyk = "pyH%d" % (npy % 2)
                                npy += 1
                                for fc in range(8):
                                    S.op("pe", lambda e: e.matmul(pyb[0:npart, :], lhsT=hidT[:, fc, c0:c0 + npart],
                                                                  rhs=wb[i][2][:, fc, half * 512:(half + 1) * 512],
                                                                  start=(fc == 0), stop=(fc == 7)),
                                         reads=[("hidT", 0), ("hidT", 1), "wb2_%d" % i], writes=[pyk])
                                S.op("dve", lambda e: e.scalar_tensor_tensor(out=yb[0:npart, half * 512:(half + 1) * 512], in0=pyb[0:npart, :],
                                                                             scalar=gate_s[0:npart, ex_, st_:st_ + 1],
                                                                             in1=g2rows[s][0:npart, half * 512:(half + 1) * 512],
                                                                             op0=ALU.mult, op1=ALU.mult),
                                     reads=[pyk, "gate_s", "g2r_%d" % s], writes=[(yk, half)])
                            S.dma("pool", lambda e: e.indirect_dma_start(
                                out=macc[:, :], out_offset=bass.IndirectOffsetOnAxis(ap=idx_i[0:npart, ex_, st_:st_ + 1], axis=0),
                                in_=yb[0:npart, :], in_offset=None, bounds_check=bcreg, oob_is_err=False,
                                compute_op=ALU.add),
                                reads=[(yk, 0), (yk, 1), "idx_i"] + [("msc", ex_ - 1, k) for k in range(5)],
                                writes=[("msc", ex_, st_)])
                S.barrier()
            if debug and ll == 0:
                S.dma("sp", lambda e: e.dma_start(out=dbg_macc, in_=macc), writes=["dbgmacc"])
                S.barrier()
            esI = ExitStack()
            with esI:
                finish_next_A = None
                if ll + 1 < L:
                    finish_next_A = emit_phase_A(ll + 1, esI, "pool")
                lnr2 = SB(esI, "lnr2", [128, 2, D], F32)
                mt = [SB(esI, "mt%d" % i, [128, D], F32) for i in range(4)]
                xo = [SB(esI, "xo%d" % i, [128, D], F32) for i in range(4)]
                st = SB(esI, "stI", [128, 2, 6], F32)
                mv = SB(esI, "mvI", [128, 2], F32)
                rstd = SB(esI, "rstdI", [128, 1], F32)
                nmr = SB(esI, "nmr", [128, 1], F32)
                S.dma("sp", lambda e: e.dma_start(out=lnr2[:], in_=ln_in[ll:ll + 1, 2:4, :].broadcast_to([128, 2, D])),
                      writes=["lnr2"])
                stI = [st, SB(esI, "stI_b", [128, 2, 6], F32)]
                mvI = [mv, SB(esI, "mvI_b", [128, 2], F32)]
                rsI = [rstd, SB(esI, "rstdI_b", [128, 1], F32)]
                nmI = [nmr, SB(esI, "nmr_b", [128, 1], F32)]
                for j0 in range(0, NT, 2):
                    pair = [(j0, 0), (j0 + 1, 1)]
                    bo = (j0 // 2 % 2) * 2
                    for (j, t) in pair:
                        tok = slice(j * 128, (j + 1) * 128)
                        S.dma("sp", lambda e: e.dma_start(out=mt[bo + t][:], in_=macc[tok, :]), reads=[("macc", j)], writes=[("mt", bo + t)])
                    for c_ in range(2):
                        for (j, t) in pair:
                            S.op("dve", lambda e: e.bn_stats(stI[t][:, c_, :], mt[bo + t][:, c_ * 512:(c_ + 1) * 512]),
                                 reads=[("mt", bo + t)], writes=[("Ist", t, c_)])
                    for (j, t) in pair:
                        S.op("dve", lambda e: e.bn_aggr(mvI[t][:], stI[t][:]), reads=[("Ist", t, 0), ("Ist", t, 1)], writes=[("Imv", t)])
                    for (j, t) in pair:
                        S.op("act", lambda e: e.activation(out=rsI[t][:], in_=mvI[t][:, 1:2], func=AF.Sqrt, bias=epsc[:], scale=1.0),
                             reads=[("Imv", t), "epsc"], writes=[("Irs", t)])
                    for (j, t) in pair:
                        S.op("dve", lambda e: e.reciprocal(rsI[t][:], rsI[t][:]), reads=[("Irs", t)], writes=[("Irs", t)])
                    for (j, t) in pair:
                        S.op("dve", lambda e: e.scalar_tensor_tensor(out=nmI[t][:], in0=mvI[t][:, 0:1], scalar=-1.0, in1=rsI[t][:],
                                                                     op0=ALU.mult, op1=ALU.mult),
                             reads=[("Imv", t), ("Irs", t)], writes=[("Inm", t)])
                    for (j, t) in pair:
                        S.op("act", lambda e: e.activation(out=xo[bo + t][:], in_=mt[bo + t][:], func=AF.Identity, bias=nmI[t][:], scale=rsI[t][:]),
                             reads=[("mt", bo + t), ("Inm", t), ("Irs", t)], writes=[("xo", bo + t)])
                    for (j, t) in pair:
                        S.op("dve", lambda e: e.tensor_tensor(out=xo[bo + t][:], in0=xo[bo + t][:], in1=lnr2[:, 0, :], op=ALU.mult),
                             reads=[("xo", bo + t), "lnr2"], writes=[("xo", bo + t)])
                    for (j, t) in pair:
                        S.op("dve", lambda e: e.tensor_tensor(out=xo[bo + t][:], in0=xo[bo + t][:], in1=lnr2[:, 1, :], op=ALU.add),
                             reads=[("xo", bo + t), "lnr2"], writes=[("xo", bo + t)])
                    for (j, t) in pair:
                        S.dma("sp", lambda e: e.dma_start(out=dst_tile(ll, j), in_=xo[bo + t][:]), reads=[("xo", bo + t)], writes=[dst_key(ll, j)])
                if finish_next_A is not None:
                    finish_next_A()
            S.barrier()
        print("instructions", S.n_inst, "waits", S.n_wait, flush=True)
    return nc


def _band_tables():
    wins = (2, 4, 8, 16)
    Ls = 384
    band = np.zeros((128, 4, 5, 128), np.float32)
    for g, w in enumerate(wins):
        A = np.zeros((Ls, Ls), np.float64)
        for t in range(Ls):
            lo = min(max(t - w // 2, 0), Ls)
            hi = min(max(t + w // 2, 0), Ls)
            A[t, lo:hi] = 1.0 / (hi - lo)
            A[t, t] -= 1.0
        AT = A.T
        band[:, g, 0, :] = AT[0:128, 128:256]
        band[:, g, 1, :] = AT[128:256, 128:256]
        band[:, g, 2, :] = AT[0:128, 0:128]
        band[:, g, 3, :] = AT[256:384, 256:384]
        band[:, g, 4, :] = AT[256:384, 128:256]
    return band


def _rope_tables():
    t = np.arange(SEQ)
    r = (t // 64).astype(np.float32)
    col = (t % 64).astype(np.float32)
    inv = (np.float32(10000.0) ** (-np.arange(16, dtype=np.float32) / np.float32(16))).astype(np.float32)
    ang = np.concatenate([r[:, None] * inv, col[:, None] * inv], axis=-1).astype(np.float32)
    cs = np.concatenate([np.cos(ang), np.sin(ang)], axis=-1).astype(np.float32)
    return np.ascontiguousarray(cs.reshape(NTL, 128, 64).transpose(1, 0, 2))


def _prep_common(inp, layers):
    L = len(layers)
    sl = lambda a: np.ascontiguousarray(np.asarray(a)[layers])
    pool_w = sl(inp["pool_w"])
    wbd = np.zeros((L, 2, 128, 128), np.float32)
    for c in range(2):
        for gi in range(2):
            wbd[:, c, gi * 64:(gi + 1) * 64, gi * 64:(gi + 1) * 64] = pool_w[:, 2 * c + gi]
    com = {
        "w_mod": sl(inp["w_mod"]),
        "b_modT": np.ascontiguousarray(sl(inp["b_mod"]).reshape(L, 48, 128).transpose(0, 2, 1)),
        "w_in": sl(inp["w_in"]),
        "wbd": wbd,
        "pscT": np.ascontiguousarray(sl(inp["pool_scale"]).reshape(L, 2, 128).transpose(0, 2, 1)),
        "band": _band_tables(),
        "sgu_g": np.ascontiguousarray(sl(inp["sgu_g"]).reshape(L, 256)),
        "sgu_wT": np.ascontiguousarray(sl(inp["sgu_w"]).transpose(0, 3, 1, 2)),
        "sgu_bT": np.ascontiguousarray(sl(inp["sgu_b"]).transpose(0, 2, 1)),
        "qkg": np.ascontiguousarray(np.concatenate([np.tile(sl(inp["q_g"]), (1, 8)), np.tile(sl(inp["k_g"]), (1, 2))], axis=1)),
        "w_out": sl(inp["w_out"]),
        "ln": np.ascontiguousarray(np.stack([sl(inp["ln1_g"]), sl(inp["ln1_b"]), sl(inp["ln2_g"]), sl(inp["ln2_b"])], axis=1)),
        "w_router": np.ascontiguousarray(sl(inp["w_router"]).reshape(L, 8, 128, NE).transpose(0, 2, 1, 3)),
        "w1": sl(inp["w1"]), "w3": sl(inp["w3"]), "w2": sl(inp["w2"]),
        "cs": _rope_tables(),
    }
    return com


_NC_CACHE = {}


def _run(inp, x, ctx, layers, n_cores=8):
    L = len(layers)
    if L not in _NC_CACHE:
        _NC_CACHE[L] = build(L)
    nc = _NC_CACHE[L]
    com = _prep_common(inp, layers)
    c = np.asarray(inp["c"], np.float32)
    c_ctx = np.asarray(inp["c_ctx"], np.float32)
    in_maps = []
    for b in range(n_cores):
        cond = np.stack([c[b].reshape(8, 128).T, c_ctx.reshape(8, 128).T], axis=-1)
        m = dict(com)
        m["x"] = np.ascontiguousarray(x[b])
        m["ctx"] = np.ascontiguousarray(ctx[b])
        m["cond"] = np.ascontiguousarray(cond.astype(np.float32))
        in_maps.append(m)
    res = run_bass_kernel_spmd(nc, in_maps, core_ids=list(range(n_cores)))
    xo = np.stack([np.asarray(r["out"]) for r in res.results], 0)
    co = np.stack([np.asarray(r["ctx_out"]) for r in res.results], 0)
    return xo, co


LAYERS_PER_LAUNCH = 4


def kernel(**inputs):
    inp = {k: np.asarray(v) for k, v in inputs.items()}
    x = np.asarray(inp["x"], np.float32)
    ctx = np.asarray(inp["ctx"], np.float32)
    for l0 in range(0, DEPTH, LAYERS_PER_LAUNCH):
        x, ctx = _run(inp, x, ctx, list(range(l0, l0 + LAYERS_PER_LAUNCH)))
    return x.astype(np.float32)
```

```python
import numpy as np
from contextlib import ExitStack
import concourse.bass as bass
import concourse.mybir as mybir
from concourse.bass_utils import run_bass_kernel_spmd

F32 = mybir.dt.float32
BF16 = mybir.dt.bfloat16
I32 = mybir.dt.int32
AF = mybir.ActivationFunctionType
ALU = mybir.AluOpType
AX = mybir.AxisListType

DEPTH = 4
D = 1024
SEQ = 4096
CTX = 256
NT = 34
NTL = 32
NTOK = SEQ + CTX
NE = 16
CAP_L = 512
CAP_C = 32
ALPHA = float((2 * DEPTH) ** 0.25)
EPS = 1e-6
BIG = 100000.0


class Sched:
    def __init__(self, nc, es, n_dma_sems=48):
        self.nc = nc
        self.eng = {"pe": nc.tensor, "act": nc.scalar, "dve": nc.vector, "pool": nc.gpsimd, "sp": nc.sync}
        self.sem = {k: es.enter_context(nc.semaphore("prog_" + k)) for k in self.eng}
        self.cnt = {k: 0 for k in self.eng}
        self.dsem = [es.enter_context(nc.semaphore("dma%d" % i)) for i in range(n_dma_sems)]
        self.dtot = [0] * n_dma_sems
        self.dnext = 0
        self.dnext_sw = 0
        self.waited = {k: {} for k in self.eng}
        self.res = {}
        self.n_inst = 0
        self.n_wait = 0

    def _semobj(self, key):
        return self.sem[key] if isinstance(key, str) else self.dsem[key]

    def _collect(self, reads, writes):
        deps = {}

        def add(d):
            if d is None:
                return
            k, v = d
            if deps.get(k, 0) < v:
                deps[k] = v
        for r in reads:
            e = self.res.get(r)
            if e is not None:
                add(e["w"])
        for w in writes:
            e = self.res.get(w)
            if e is not None:
                add(e["w"])
                for k, v in e["r"].items():
                    add((k, v))
        return deps

    def _wait(self, F, deps, skip_self=False):
        for k, v in deps.items():
            if skip_self and k == F:
                continue
            if self.waited[F].get(k, 0) < v:
                self.eng[F].wait_ge(self._semobj(k), v)
                self.waited[F][k] = v
                self.n_wait += 1

    def _update(self, dep, reads, writes):
        k, v = dep
        for r in reads:
            e = self.res.setdefault(r, {"w": None, "r": {}})
            if e["r"].get(k, 0) < v:
                e["r"][k] = v
        for w in writes:
            self.res[w] = {"w": dep, "r": {}}

    def op(self, F, fn, reads=(), writes=(), skip_self=None):
        if skip_self is None:
            skip_self = (F == "pe")
        deps = self._collect(reads, writes)
        self._wait(F, deps, skip_self=skip_self)
        inst = fn(self.eng[F])
        self.cnt[F] += 1
        inst.then_inc(self.sem[F], 1)
        self._update((F, self.cnt[F]), reads, writes)
        self.n_inst += 1
        return inst

    def dma(self, Q, fn, reads=(), writes=()):
        deps = self._collect(reads, writes)
        self._wait(Q, deps)
        half = len(self.dsem) // 2
        if Q == "pool":
            i = half + self.dnext_sw
            self.dnext_sw = (self.dnext_sw + 1) % (len(self.dsem) - half)
        else:
            i = self.dnext
            self.dnext = (self.dnext + 1) % half
        if self.dtot[i] > 0 and self.waited[Q].get(i, 0) < self.dtot[i]:
            self.eng[Q].wait_ge(self.dsem[i], self.dtot[i])
            self.waited[Q][i] = self.dtot[i]
            self.n_wait += 1
        inst = fn(self.eng[Q])
        self.dtot[i] += 16
        inst.then_inc(self.dsem[i], 16)
        self._update((i, self.dtot[i]), reads, writes)
        self.n_inst += 1
        return inst

    def barrier(self):
        for F in self.eng:
            for i, t in enumerate(self.dtot):
                if t > 0 and self.waited[F].get(i, 0) < t:
                    self.eng[F].wait_ge(self.dsem[i], t)
                    self.waited[F][i] = t
            for k in self.eng:
                if k != F and self.cnt[k] > 0 and self.waited[F].get(k, 0) < self.cnt[k]:
                    self.eng[F].wait_ge(self.sem[k], self.cnt[k])
                    self.waited[F][k] = self.cnt[k]
        self.res = {}


def build(L, debug=False):
    nc = bass.Bass("TRN2", target_bir_lowering=False)

    def DT(name, shape, dt=F32, kind="ExternalInput"):
        return nc.dram_tensor(name, shape, dt, kind=kind).ap()

    x_in = DT("x", [SEQ, D])
    ctx_in = DT("ctx", [CTX, D])
    cond_in = DT("cond", [128, 8, 2])
    w_mod = DT("w_mod", [L, D, 6 * D])
    b_modT = DT("b_modT", [L, 128, 48])
    w_in = DT("w_in", [L, D, 1536])
    wbd_in = DT("wbd", [L, 2, 128, 128])
    pscT_in = DT("pscT", [L, 128, 2])
    band_in = DT("band", [128, 4, 5, 128])
    sgug_in = DT("sgu_g", [L, 256])
    sguwT_in = DT("sgu_wT", [L, 128, 4, 128])
    sgubT_in = DT("sgu_bT", [L, 128, 4])
    qkg_in = DT("qkg", [L, 640])
    w_out = DT("w_out", [L, D, D])
    ln_in = DT("ln", [L, 4, D])
    wr_in = DT("w_router", [L, 128, 8, NE])
    w1 = DT("w1", [L, NE, D, D])
    w3 = DT("w3", [L, NE, D, D])
    w2 = DT("w2", [L, NE, D, D])
    cs_in = DT("cs", [128, NTL, 64])
    out = DT("out", [SEQ, D], kind="ExternalOutput")
    ctx_out = DT("ctx_out", [CTX, D], kind="ExternalOutput")
    if debug:
        dbg_x1 = DT("dbg_x1", [NTOK, D], kind="ExternalOutput")
        dbg_aff = DT("dbg_aff", [128, NT, NE], kind="ExternalOutput")
        dbg_mix = DT("dbg_mix", [128, 4, NTOK], BF16, kind="ExternalOutput")
        dbg_idx = DT("dbg_idx", [128, NE, 5], I32, kind="ExternalOutput")
        dbg_gate = DT("dbg_gate", [128, NE, 5], kind="ExternalOutput")
        dbg_macc = DT("dbg_macc", [NTOK, D], kind="ExternalOutput")
        dbg_pos = DT("dbg_pos", [128, NT, NE], kind="ExternalOutput")
    xs = DT("xs", [NTOK, D], kind="Internal")
    macc = DT("macc", [NTOK, D], kind="Internal")
    xh2 = DT("xh2", [NTOK, D], BF16, kind="Internal")

    def src_tile(ll, j):
        if ll == 0:
            return x_in[j * 128:(j + 1) * 128, :] if j < NTL else ctx_in[(j - NTL) * 128:(j - NTL + 1) * 128, :]
        return xs[j * 128:(j + 1) * 128, :]

    def dst_tile(ll, j):
        if ll == L - 1:
            return out[j * 128:(j + 1) * 128, :] if j < NTL else ctx_out[(j - NTL) * 128:(j - NTL + 1) * 128, :]
        return xs[j * 128:(j + 1) * 128, :]

    def src_key(ll, j):
        return ("xin", j) if ll == 0 else ("xs", j)

    def dst_key(ll, j):
        return ("xout", j) if ll == L - 1 else ("xs", j)

    es0 = ExitStack()
    with es0:
        S = Sched(nc, es0)
        bcreg = es0.enter_context(nc.gpsimd.register("bcreg"))
        nc.gpsimd.reg_mov(bcreg, NTOK - 1)

        uid = [0]

        def SB(es, name, shape, dt):
            uid[0] += 1
            return es.enter_context(nc.sbuf_tensor("s%d_%s" % (uid[0], name), shape, dt))

        def PS(es, name, shape, dt):
            uid[0] += 1
            return es.enter_context(nc.psum_tensor("p%d_%s" % (uid[0], name), shape, dt))

        identf = SB(es0, "identf", [128, 128], F32)
        identb = SB(es0, "identb", [128, 128], BF16)
        ones_f = SB(es0, "ones_f", [128, 128], F32)
        lmat = SB(es0, "lmat", [128, 128], F32)
        iota_row = SB(es0, "iota_row", [128, 512], mybir.dt.float16)
        tokid = SB(es0, "tokid", [128, NT], F32)
        jmp = SB(es0, "jmp", [128, 128], F32)
        modT = SB(es0, "modT", [128, 48, 2], F32)
        aff_all = SB(es0, "aff_all", [128, NT, NE], F32)
        epsc = SB(es0, "epsc", [128, 1], F32)
        S.op("pool", lambda e: e.memset(epsc[:], EPS), writes=["epsc"])
        S.op("pool", lambda e: e.iota(jmp[:], [[1, 128]], base=0, channel_multiplier=-1,
                                      allow_small_or_imprecise_dtypes=True), writes=["jmp"])
        S.op("pool", lambda e: e.tensor_single_scalar(out=identf[:], in_=jmp[:], scalar=0.0, op=ALU.is_equal),
             reads=["jmp"], writes=["identf"])
        S.op("pool", lambda e: e.tensor_single_scalar(out=identb[:], in_=jmp[:], scalar=0.0, op=ALU.is_equal),
             reads=["jmp"], writes=["identb"])
        S.op("pool", lambda e: e.tensor_single_scalar(out=lmat[:], in_=jmp[:], scalar=0.0, op=ALU.is_gt),
             reads=["jmp"], writes=["lmat"])
        S.op("pool", lambda e: e.memset(ones_f[:], 1.0), writes=["ones_f"])
        S.op("pool", lambda e: e.iota(iota_row[:], [[1, 512]], base=0, channel_multiplier=0,
                                      allow_small_or_imprecise_dtypes=True), writes=["iota_row"])
        S.op("pool", lambda e: e.iota(tokid[:], [[128, NT]], base=0, channel_multiplier=1,
                                      allow_small_or_imprecise_dtypes=True), writes=["tokid"])
        tokA = SB(es0, "tokA", [128, NT], F32)
        tokB = SB(es0, "tokB", [128, 1], F32)
        S.op("pool", lambda e: e.iota(tokA[:], [[128, NT]], base=0, channel_multiplier=0,
                                      allow_small_or_imprecise_dtypes=True), writes=["tokA"])
        S.op("pool", lambda e: e.iota(tokB[:], [[0, 1]], base=0, channel_multiplier=1,
                                      allow_small_or_imprecise_dtypes=True), writes=["tokB"])

        def ln_stats(es_tiles, src, key_src, tag):
            st, mv, rstd = es_tiles
            S.op("dve", lambda e: e.bn_stats(st[:, 0, :], src[:, 0:512]), reads=[key_src], writes=[tag + "st0"])
            S.op("dve", lambda e: e.bn_stats(st[:, 1, :], src[:, 512:1024]), reads=[key_src], writes=[tag + "st1"])
            S.op("dve", lambda e: e.bn_aggr(mv[:], st[:]), reads=[tag + "st0", tag + "st1"], writes=[tag + "mv"])
            S.op("act", lambda e: e.activation(out=rstd[:], in_=mv[:, 1:2], func=AF.Sqrt, bias=epsc[:], scale=1.0),
                 reads=[tag + "mv", "epsc"], writes=[tag + "rstd"])
            S.op("dve", lambda e: e.reciprocal(rstd[:], rstd[:]), reads=[tag + "rstd"], writes=[tag + "rstd"])

        for ll in range(L):
            def emit_phase_A(la, esA_, qa):
                condT = SB(esA_, "condT", [128, 8, 2], F32)
                bmT = SB(esA_, "bmT", [128, 48], F32)
                wblk = [SB(esA_, "wblk%d" % i, [128, 8, 512], F32) for i in range(2)]
                pmod = PS(esA_, "pmod", [128, 48, 2], F32)
                S.dma("sp", lambda e: e.dma_start(out=condT[:], in_=cond_in), writes=["condT"])
                S.dma("sp", lambda e: e.dma_start(out=bmT[:], in_=b_modT[la]), writes=["bmT"])
                S.op("act", lambda e: e.activation(out=condT[:], in_=condT[:], func=AF.Silu),
                     reads=["condT"], writes=["condT"])
                wm = w_mod[la].rearrange("(k p) c -> p k c", p=128)
                for cb in range(12):
                    wb_ = wblk[cb % 2]
                    wk = "wblk%d" % (cb % 2)
                    S.dma(qa, lambda e: e.dma_start(out=wb_[:], in_=wm[:, :, cb * 512:(cb + 1) * 512]), writes=[wk])
                    for sub in range(4):
                        j = cb * 4 + sub
                        for k in range(8):
                            S.op("pe", lambda e: e.matmul(pmod[:, j, :], lhsT=wb_[:, k, sub * 128:(sub + 1) * 128],
                                                          rhs=condT[:, k, :], start=(k == 0), stop=(k == 7)),
                                 reads=[wk, "condT"], writes=["pmod"])
                def finish_A():
                    S.op("dve", lambda e: e.tensor_tensor(out=modT[:], in0=pmod[:],
                                                          in1=bmT[:].unsqueeze(2).broadcast_to([128, 48, 2]), op=ALU.add),
                         reads=["pmod", "bmT"], writes=["modT"])
                    for base in (8, 32):
                        S.op("dve", lambda e: e.tensor_scalar_add(modT[:, base:base + 8, :], modT[:, base:base + 8, :], 1.0),
                             reads=["modT"], writes=["modT"])
                return finish_A
            if ll == 0:
                esA0 = ExitStack()
                with esA0:
                    emit_phase_A(0, esA0, "sp")()
                    S.barrier()

            def build_gate_rows(es, vbase, tagname):
                rows = [SB(es, "%s_%d" % (tagname, s), [128, D], F32) for s in range(2)]
                est = ExitStack()
                with est:
                  diag = [SB(est, "%s_dg%d" % (tagname, i), [128, 128], F32) for i in range(2)]
                  pg = PS(est, tagname + "_pg", [128, D], F32)
                  n = 0
                  for s in range(2):
                    for dc in range(8):
                        dg = diag[n % 2]
                        dk = "%s_dg%d" % (tagname, n % 2)
                        n += 1
                        S.op("dve", lambda e: e.tensor_scalar(out=dg[:], in0=identf[:], scalar1=modT[:, vbase + dc, s:s + 1],
                                                              scalar2=None, op0=ALU.mult),
                             reads=["identf", "modT"], writes=[dk])
                        S.op("pe", lambda e: e.matmul(pg[:, dc * 128:(dc + 1) * 128], lhsT=ones_f[:], rhs=dg[:],
                                                      start=True, stop=True),
                             reads=[dk, "ones_f"], writes=[tagname + "_pg"])
                    S.op("act", lambda e: e.activation(out=rows[s][:], in_=pg[:], func=AF.Copy),
                         reads=[tagname + "_pg"], writes=["%s_%d" % (tagname, s)])
                  S.barrier()
                return rows

            esBE = ExitStack()
            with esBE:
                mixps = SB(esBE, "mixps", [128, 4, NTOK], BF16)
                qT_all = SB(esBE, "qT_all", [128, 4, NTOK], BF16)
                kT_all = SB(esBE, "kT_all", [128, 2, NTOK], BF16)
                S.op("pool", lambda e: e.memset(kT_all[:], 0.0), writes=["kT_zero"])
                v_all = SB(esBE, "v_all", [128, NT, 2, 65], BF16)
                S.op("pool", lambda e: e.memset(v_all[:, :, :, 64:65], 1.0), writes=["v_ones"])
                esBC = ExitStack()
                with esBC:
                    p_all = SB(esBC, "p_all", [128, NT, 256], BF16)
                    esB = ExitStack()
                    with esB:
                        winb = SB(esB, "winb", [128, 8, 1536], BF16)
                        cs = SB(esB, "cs", [128, NTL, 64], F32)
                        sgug = SB(esB, "sgug", [128, 256], F32)
                        wsT = SB(esB, "wsT", [128, 4, 128], BF16)
                        sbT = SB(esB, "sbT", [128, 4], F32)
                        qkg = SB(esB, "qkg", [128, 640], F32)
                        xt = [SB(esB, "xt%d" % i, [128, D], F32) for i in range(2)]
                        st = SB(esB, "st", [128, 2, 6], F32)
                        mv = SB(esB, "mv", [128, 2], F32)
                        rstd = SB(esB, "rstd", [128, 1], F32)
                        xb = SB(esB, "xb", [128, D], BF16)
                        hT = SB(esB, "hT", [128, 8, 128], BF16)
                        gu = SB(esB, "gu", [128, 256], BF16)
                        gv = SB(esB, "gv", [128, 256], F32)
                        stv = SB(esB, "stv", [128, 4, 6], F32)
                        mvv = SB(esB, "mvv", [128, 4, 2], F32)
                        rsv = SB(esB, "rsv", [128, 4], F32)
                        vn = SB(esB, "vn", [128, 256], F32)
                        vh = SB(esB, "vh", [128, 256], BF16)
                        sgo = SB(esB, "sgo", [128, 256], BF16)
                        sq = SB(esB, "sq", [128, 640], F32)
                        ss = SB(esB, "ss", [128, 10], F32)
                        qn = SB(esB, "qn", [128, 10, 64], F32)
                        ta = SB(esB, "ta", [128, 10, 32], F32)
                        tb = SB(esB, "tb", [128, 10, 32], F32)
                        qr = SB(esB, "qr", [128, 640], BF16)
                        pT = PS(esB, "pT", [128, 8, 128], BF16)
                        pz = PS(esB, "pz", [128, 1536], F32)
                        psg = PS(esB, "psg", [128, 256], F32)
                        ptr = PS(esB, "ptr", [128, 7, 128], BF16)
                        S.dma("pool", lambda e: e.dma_start(out=winb[:], in_=w_in[ll].rearrange("(k p) c -> p k c", p=128)),
                              writes=["winb"])
                        S.dma("pool", lambda e: e.dma_start(out=wsT[:], in_=sguwT_in[ll]), writes=["wsT"])
                        S.dma("sp", lambda e: e.dma_start(out=cs[:], in_=cs_in), writes=["cs"])
                        S.dma("sp", lambda e: e.dma_start(out=sgug[:], in_=sgug_in[ll:ll + 1, :].broadcast_to([128, 256])),
                              writes=["sgug"])
                        S.dma("sp", lambda e: e.dma_start(out=qkg[:], in_=qkg_in[ll:ll + 1, :].broadcast_to([128, 640])),
                              writes=["qkg"])
                        S.dma("sp", lambda e: e.dma_start(out=sbT[:], in_=sgubT_in[ll]), writes=["sbT"])
                        def front_B(j):
                            s = 0 if j < NTL else 1
                            tok = slice(j * 128, (j + 1) * 128)
                            xtj = xt[j % 2]
                            xk = "xt%d" % (j % 2)
                            S.dma("sp", lambda e: e.dma_start(out=xtj[:], in_=src_tile(ll, j)),
                                  reads=[src_key(ll, j)], writes=[xk])
                            ln_stats((st, mv, rstd), xtj, xk, "B")
                            S.op("dve", lambda e: e.tensor_scalar(out=xb[:], in0=xtj[:], scalar1=mv[:, 0:1], scalar2=rstd[:],
                                                                  op0=ALU.subtract, op1=ALU.mult),
                                 reads=[xk, "Bmv", "Brstd"], writes=["xb"])
                            for k in range(8):
                                S.op("pe", lambda e: e.transpose(pT[:, k, :], xb[:, k * 128:(k + 1) * 128], identb[:]),
                                     reads=["xb", "identb"], writes=["pT"])
                            for k in range(8):
                                S.op("act", lambda e: e.activation(out=hT[:, k, :], in_=pT[:, k, :], func=AF.Identity,
                                                                   bias=modT[:, 0 + k, s:s + 1], scale=modT[:, 8 + k, s:s + 1]),
                                     reads=["pT", "modT"], writes=[("hT", k)])
                        gu2 = [gu, SB(esB, "gu_b", [128, 256], BF16)]
                        gv2 = [gv, SB(esB, "gv_b", [128, 256], F32)]
                        stv2 = [stv, SB(esB, "stv_b", [128, 4, 6], F32)]
                        mvv2 = [mvv, SB(esB, "mvv_b", [128, 4, 2], F32)]
                        rsv2 = [rsv, SB(esB, "rsv_b", [128, 4], F32)]
                        vn2 = [vn, SB(esB, "vn_b", [128, 256], F32)]
                        vh2 = [vh, SB(esB, "vh_b", [128, 256], BF16)]
                        sgo2 = [sgo, SB(esB, "sgo_b", [128, 256], BF16)]
                        sq2 = [sq, SB(esB, "sq_b", [128, 640], F32)]
                        zq2 = [SB(esB, "zq_a", [128, 640], F32), SB(esB, "zq_b", [128, 640], F32)]
                        ss2 = [ss, SB(esB, "ss_b", [128, 10], F32)]
                        qn2 = [qn, SB(esB, "qn_b", [128, 10, 64], F32)]
                        ta2 = [ta, SB(esB, "ta_b", [128, 10, 32], F32)]
                        tb2 = [tb, SB(esB, "tb_b", [128, 10, 32], F32)]
                        qr2 = [qr, SB(esB, "qr_b", [128, 640], BF16)]
                        psg2 = [psg, PS(esB, "psg_b", [128, 256], F32)]
                        ptr2 = [ptr, PS(esB, "ptr_b", [128, 7, 128], BF16)]

                        def mm_and_evac(j, t):
                            for cblk in range(3):
                                for k in range(8):
                                    S.op("pe", lambda e: e.matmul(pz[:, cblk * 512:(cblk + 1) * 512], lhsT=hT[:, k, :],
                                                                  rhs=winb[:, k, cblk * 512:(cblk + 1) * 512],
                                                                  start=(k == 0), stop=(k == 7)),
                                         reads=[("hT", k), "winb"], writes=[("pz", cblk)])
                            if j + 1 < NT:
                                front_B(j + 1)
                            S.op("act", lambda e: e.activation(out=p_all[:, j, :], in_=pz[:, 0:256], func=AF.Copy),
                                 reads=[("pz", 0)], writes=[("p_all", j)])
                            S.op("act", lambda e: e.activation(out=gu2[t][:], in_=pz[:, 256:512], func=AF.Gelu_apprx_tanh),
                                 reads=[("pz", 0)], writes=[("gu", t)])
                            S.op("act", lambda e: e.activation(out=gv2[t][:], in_=pz[:, 512:768], func=AF.Gelu_apprx_tanh),
                                 reads=[("pz", 1)], writes=[("gv", t)])
                            S.op("act", lambda e: e.activation(out=sq2[t][:], in_=pz[:, 768:1408], func=AF.Square),
                                 reads=[("pz", 1), ("pz", 2)], writes=[("sq", t)])
                            S.op("dve", lambda e: e.tensor_copy(out=zq2[t][:], in_=pz[:, 768:1408]),
                                 reads=[("pz", 1), ("pz", 2)], writes=[("zq", t)])
                            S.op("dve", lambda e: e.tensor_copy(out=v_all[:, j, :, 0:64],
                                                                in_=pz[:, 1408:1536].rearrange("p (k d) -> p k d", d=64)),
                                 reads=[("pz", 2)], writes=[("v", j)])

                        front_B(0)
                        for j0 in range(0, NTL, 2):
                            pair = [(j0, 0), (j0 + 1, 1)]
                            for (j, t) in pair:
                                mm_and_evac(j, t)
                            for h in range(4):
                                for (j, t) in pair:
                                    S.op("dve", lambda e: e.bn_stats(stv2[t][:, h, :], gv2[t][:, h * 64:(h + 1) * 64]),
                                         reads=[("gv", t)], writes=[("stv", t, h)])
                                for (j, t) in pair:
                                    S.op("dve", lambda e: e.bn_aggr(mvv2[t][:, h, :], stv2[t][:, h, :]),
                                         reads=[("stv", t, h)], writes=[("mvv", t)])
                            for (j, t) in pair:
                                S.op("act", lambda e: e.activation(out=rsv2[t][:], in_=mvv2[t][:, :, 1], func=AF.Sqrt, bias=epsc[:], scale=1.0),
                                     reads=[("mvv", t), "epsc"], writes=[("rsv", t)])
                            for (j, t) in pair:
                                S.op("dve", lambda e: e.tensor_reduce(out=ss2[t][:], in_=sq2[t][:].rearrange("p (h d) -> p h d", d=64),
                                                                      axis=AX.X, op=ALU.add),
                                     reads=[("sq", t)], writes=[("ss", t)])
                            for (j, t) in pair:
                                S.op("act", lambda e: e.activation(out=ss2[t][:], in_=ss2[t][:], func=AF.Sqrt, bias=epsc[:], scale=1.0 / 64),
                                     reads=[("ss", t), "epsc"], writes=[("ss", t)])
                            for (j, t) in pair:
                                S.op("dve", lambda e: e.reciprocal(rsv2[t][:], rsv2[t][:]), reads=[("rsv", t)], writes=[("rsv", t)])
                            for h in range(4):
                                for (j, t) in pair:
                                    S.op("dve", lambda e: e.tensor_scalar(out=vn2[t][:, h * 64:(h + 1) * 64], in0=gv2[t][:, h * 64:(h + 1) * 64],
                                                                          scalar1=mvv2[t][:, h, 0:1], scalar2=rsv2[t][:, h:h + 1],
                                                                          op0=ALU.subtract, op1=ALU.mult),
                                         reads=[("gv", t), ("mvv", t), ("rsv", t)], writes=[("vn", t)])
                            for (j, t) in pair:
                                S.op("dve", lambda e: e.tensor_tensor(out=vh2[t][:], in0=vn2[t][:], in1=sgug[:], op=ALU.mult),
                                     reads=[("vn", t), "sgug"], writes=[("vh", t)])
                            for (j, t) in pair:
                                for h in range(4):
                                    S.op("pe", lambda e: e.matmul(psg2[t][:, h * 64:(h + 1) * 64], lhsT=wsT[:, h, :],
                                                                  rhs=vh2[t][:, h * 64:(h + 1) * 64], start=True, stop=True),
                                         reads=["wsT", ("vh", t)], writes=[("psg", t)])
                            for (j, t) in pair:
                                S.op("dve", lambda e: e.reciprocal(ss2[t][:], ss2[t][:]), reads=[("ss", t)], writes=[("ss", t)])
                            for (j, t) in pair:
                                S.op("dve", lambda e: e.tensor_tensor(out=qn2[t][:], in0=zq2[t][:].rearrange("p (h d) -> p h d", d=64),
                                                                      in1=ss2[t][:].unsqueeze(2).broadcast_to([128, 10, 64]), op=ALU.mult),
                                     reads=[("zq", t), ("ss", t)], writes=[("qn", t)])
                            for (j, t) in pair:
                                S.op("dve", lambda e: e.tensor_tensor(out=qn2[t][:], in0=qn2[t][:],
                                                                      in1=qkg[:].rearrange("p (h d) -> p h d", d=64), op=ALU.mult),
                                     reads=[("qn", t), "qkg"], writes=[("qn", t)])
                            for h in range(4):
                                for (j, t) in pair:
                                    S.op("dve", lambda e: e.scalar_tensor_tensor(out=sgo2[t][:, h * 64:(h + 1) * 64],
                                                                                 in0=psg2[t][:, h * 64:(h + 1) * 64],
                                                                                 scalar=sbT[:, h:h + 1],
                                                                                 in1=gu2[t][:, h * 64:(h + 1) * 64],
                                                                                 op0=ALU.add, op1=ALU.mult),
                                         reads=[("psg", t), "sbT", ("gu", t)], writes=[("sgo", t)])
                            for (j, t) in pair:
                                for c in range(2):
                                    S.op("pe", lambda e: e.transpose(ptr2[t][:, c, :], sgo2[t][:, c * 128:(c + 1) * 128], identb[:]),
                                         reads=[("sgo", t), "identb"], writes=[("ptr", t)])
                            for (j, t) in pair:
                                tok = slice(j * 128, (j + 1) * 128)
                                S.op("act", lambda e: e.activation(out=mixps[:, 2:4, tok], in_=ptr2[t][:, 0:2, :], func=AF.Copy),
                                     reads=[("ptr", t)], writes=[("mixps_s", j)])
                            def dsts(t):
                                qdst = qr2[t][:, 0:512].rearrange("p (g k d) -> p k g d", g=4, k=2, d=64)
                                kdst = qr2[t][:, 512:640].rearrange("p (k d) -> p k d", d=64)
                                return qdst, kdst
                            if j0 < NTL:
                                for half_, op_ in ((0, ALU.subtract), (1, ALU.add)):
                                    for (j, t) in pair:
                                        cosb = cs[:, j:j + 1, 0:32].broadcast_to([128, 10, 32])
                                        xa_ = qn2[t][:, :, 0:32] if half_ == 0 else qn2[t][:, :, 32:64]
                                        S.op("dve", lambda e: e.tensor_tensor(out=ta2[t][:], in0=xa_, in1=cosb, op=ALU.mult),
                                             reads=[("qn", t), "cs"], writes=[("ta", t)])
                                    for (j, t) in pair:
                                        sinb = cs[:, j:j + 1, 32:64].broadcast_to([128, 10, 32])
                                        xb_ = qn2[t][:, :, 32:64] if half_ == 0 else qn2[t][:, :, 0:32]
                                        S.op("dve", lambda e: e.tensor_tensor(out=tb2[t][:], in0=xb_, in1=sinb, op=ALU.mult),
                                             reads=[("qn", t), "cs"], writes=[("tb", t)])
                                    for (j, t) in pair:
                                        qdst, kdst = dsts(t)
                                        dsl = slice(0, 32) if half_ == 0 else slice(32, 64)
                                        S.op("dve", lambda e: e.tensor_tensor(out=qdst[:, :, :, dsl],
                                                                              in0=ta2[t][:, 0:8, :].rearrange("p (k g) d -> p k g d", k=2),
                                                                              in1=tb2[t][:, 0:8, :].rearrange("p (k g) d -> p k g d", k=2),
                                                                              op=op_),
                                             reads=[("ta", t), ("tb", t)], writes=[("qr", t, half_, 0)])
                                        S.op("dve", lambda e: e.tensor_tensor(out=kdst[:, :, dsl], in0=ta2[t][:, 8:10, :], in1=tb2[t][:, 8:10, :],
                                                                              op=op_),
                                             reads=[("ta", t), ("tb", t)], writes=[("qr", t, half_, 1)])
                            else:
                                for (j, t) in pair:
                                    qdst, kdst = dsts(t)
                                    S.op("dve", lambda e: e.tensor_copy(out=qdst, in_=qn2[t][:, 0:8, :].rearrange("p (k g) d -> p k g d", k=2)),
                                         reads=[("qn", t)], writes=[("qr", t, 0, 0), ("qr", t, 1, 0)])
                                    S.op("dve", lambda e: e.tensor_copy(out=kdst, in_=qn2[t][:, 8:10, :]),
                                         reads=[("qn", t)], writes=[("qr", t, 0, 1), ("qr", t, 1, 1)])
                            for (j, t) in pair:
                                for c in range(5):
                                    S.op("pe", lambda e: e.transpose(ptr2[t][:, 2 + c, :], qr2[t][:, c * 128:(c + 1) * 128], identb[:]),
                                         reads=[("qr", t, 0, 0), ("qr", t, 0, 1), ("qr", t, 1, 0), ("qr", t, 1, 1), "identb"],
                                         writes=[("ptr", t)])
                            for (j, t) in pair:
                                tok = slice(j * 128, (j + 1) * 128)
                                S.op("act", lambda e: e.activation(out=qT_all[:, :, tok], in_=ptr2[t][:, 2:6, :], func=AF.Copy),
                                     reads=[("ptr", t)], writes=[("qT", j)])
                                for kv_ in range(2):
                                    S.op("act", lambda e: e.activation(out=kT_all[kv_ * 64:(kv_ + 1) * 64, kv_, tok],
                                                                       in_=ptr2[t][kv_ * 64:(kv_ + 1) * 64, 6, :], func=AF.Copy),
                                         reads=[("ptr", t), "kT_zero"], writes=[("kT", j, kv_)])
                        S.barrier()
                        for j in range(NTL, NT):
                            s = 0 if j < NTL else 1
                            tok = slice(j * 128, (j + 1) * 128)
                            for cblk in range(3):
                                for k in range(8):
                                    S.op("pe", lambda e: e.matmul(pz[:, cblk * 512:(cblk + 1) * 512], lhsT=hT[:, k, :],
                                                                  rhs=winb[:, k, cblk * 512:(cblk + 1) * 512],
                                                                  start=(k == 0), stop=(k == 7)),
                                         reads=[("hT", k), "winb"], writes=[("pz", cblk)])
                            if j + 1 < NT:
                                front_B(j + 1)
                            S.op("act", lambda e: e.activation(out=p_all[:, j, :], in_=pz[:, 0:256], func=AF.Copy),
                                 reads=[("pz", 0)], writes=[("p_all", j)])
                            S.op("act", lambda e: e.activation(out=gu[:], in_=pz[:, 256:512], func=AF.Gelu_apprx_tanh),
                                 reads=[("pz", 0)], writes=["gu"])
                            S.op("act", lambda e: e.activation(out=gv[:], in_=pz[:, 512:768], func=AF.Gelu_apprx_tanh),
                                 reads=[("pz", 1)], writes=["gv"])
                            for h in range(4):
                                S.op("dve", lambda e: e.bn_stats(stv[:, h, :], gv[:, h * 64:(h + 1) * 64]),
                                     reads=["gv"], writes=[("stv", h)])
                                S.op("dve", lambda e: e.bn_aggr(mvv[:, h, :], stv[:, h, :]),
                                     reads=[("stv", h)], writes=["mvv"])
                            S.op("act", lambda e: e.activation(out=rsv[:], in_=mvv[:, :, 1], func=AF.Sqrt, bias=epsc[:], scale=1.0),
                                 reads=["mvv", "epsc"], writes=["rsv"])
                            S.op("dve", lambda e: e.reciprocal(rsv[:], rsv[:]), reads=["rsv"], writes=["rsv"])
                            for h in range(4):
                                S.op("dve", lambda e: e.tensor_scalar(out=vn[:, h * 64:(h + 1) * 64], in0=gv[:, h * 64:(h + 1) * 64],
                                                                      scalar1=mvv[:, h, 0:1], scalar2=rsv[:, h:h + 1],
                                                                      op0=ALU.subtract, op1=ALU.mult),
                                     reads=["gv", "mvv", "rsv"], writes=["vn"])
                            S.op("dve", lambda e: e.tensor_tensor(out=vh[:], in0=vn[:], in1=sgug[:], op=ALU.mult),
                                 reads=["vn", "sgug"], writes=["vh"])
                            for h in range(4):
                                S.op("pe", lambda e: e.matmul(psg[:, h * 64:(h + 1) * 64], lhsT=wsT[:, h, :],
                                                              rhs=vh[:, h * 64:(h + 1) * 64], start=True, stop=True),
                                     reads=["wsT", "vh"], writes=["psg"])
                            for h in range(4):
                                S.op("dve", lambda e: e.scalar_tensor_tensor(out=sgo[:, h * 64:(h + 1) * 64],
                                                                             in0=psg[:, h * 64:(h + 1) * 64],
                                                                             scalar=sbT[:, h:h + 1],
                                                                             in1=gu[:, h * 64:(h + 1) * 64],
                                                                             op0=ALU.add, op1=ALU.mult),
                                     reads=["psg", "sbT", "gu"], writes=["sgo"])
                            for c in range(2):
                                S.op("pe", lambda e: e.transpose(ptr[:, c, :], sgo[:, c * 128:(c + 1) * 128], identb[:]),
                                     reads=["sgo", "identb"], writes=[("ptr", 0)])
                            S.op("act", lambda e: e.activation(out=mixps[:, 2:4, tok], in_=ptr[:, 0:2, :], func=AF.Copy),
                                 reads=[("ptr", 0)], writes=[("mixps_s", j)])
                            S.op("act", lambda e: e.activation(out=sq[:], in_=pz[:, 768:1408], func=AF.Square),
                                 reads=[("pz", 1), ("pz", 2)], writes=["sq"])
                            S.op("dve", lambda e: e.tensor_reduce(out=ss[:], in_=sq[:].rearrange("p (h d) -> p h d", d=64),
                                                                  axis=AX.X, op=ALU.add),
                                 reads=["sq"], writes=["ss"])
                            S.op("act", lambda e: e.activation(out=ss[:], in_=ss[:], func=AF.Sqrt, bias=epsc[:], scale=1.0 / 64),
                                 reads=["ss", "epsc"], writes=["ss"])
                            S.op("dve", lambda e: e.reciprocal(ss[:], ss[:]), reads=["ss"], writes=["ss"])
                            S.op("dve", lambda e: e.tensor_tensor(out=qn[:], in0=pz[:, 768:1408].rearrange("p (h d) -> p h d", d=64),
                                                                  in1=ss[:].unsqueeze(2).broadcast_to([128, 10, 64]), op=ALU.mult),
                                 reads=[("pz", 1), ("pz", 2), "ss"], writes=["qn"])
                            S.op("dve", lambda e: e.tensor_tensor(out=qn[:], in0=qn[:],
                                                                  in1=qkg[:].rearrange("p (h d) -> p h d", d=64), op=ALU.mult),
                                 reads=["qn", "qkg"], writes=["qn"])
                            qdst = qr[:, 0:512].rearrange("p (g k d) -> p k g d", g=4, k=2, d=64)
                            kdst = qr[:, 512:640].rearrange("p (k d) -> p k d", d=64)
                            qsrc = qn[:, 0:8, :].rearrange("p (k g) d -> p k g d", k=2)
                            ksrc = qn[:, 8:10, :]
                            if j < NTL:
                                cosb = cs[:, j:j + 1, 0:32].broadcast_to([128, 10, 32])
                                sinb = cs[:, j:j + 1, 32:64].broadcast_to([128, 10, 32])
                                x1 = qn[:, :, 0:32]
                                x2 = qn[:, :, 32:64]
                                S.op("dve", lambda e: e.tensor_tensor(out=ta[:], in0=x1, in1=cosb, op=ALU.mult),
                                     reads=["qn", "cs"], writes=["ta"])
                                S.op("dve", lambda e: e.tensor_tensor(out=tb[:], in0=x2, in1=sinb, op=ALU.mult),
                                     reads=["qn", "cs"], writes=["tb"])
                                S.op("dve", lambda e: e.tensor_tensor(out=qdst[:, :, :, 0:32],
                                                                      in0=ta[:, 0:8, :].rearrange("p (k g) d -> p k g d", k=2),
                                                                      in1=tb[:, 0:8, :].rearrange("p (k g) d -> p k g d", k=2),
                                                                      op=ALU.subtract),
                                     reads=["ta", "tb"], writes=["qr_a"])
                                S.op("dve", lambda e: e.tensor_tensor(out=kdst[:, :, 0:32], in0=ta[:, 8:10, :], in1=tb[:, 8:10, :],
                                                                      op=ALU.subtract),
                                     reads=["ta", "tb"], writes=["qr_b"])
                                S.op("dve", lambda e: e.tensor_tensor(out=ta[:], in0=x2, in1=cosb, op=ALU.mult),
                                     reads=["qn", "cs"], writes=["ta"])
                                S.op("dve", lambda e: e.tensor_tensor(out=tb[:], in0=x1, in1=sinb, op=ALU.mult),
                                     reads=["qn", "cs"], writes=["tb"])
                                S.op("dve", lambda e: e.tensor_tensor(out=qdst[:, :, :, 32:64],
                                                                      in0=ta[:, 0:8, :].rearrange("p (k g) d -> p k g d", k=2),
                                                                      in1=tb[:, 0:8, :].rearrange("p (k g) d -> p k g d", k=2),
                                                                      op=ALU.add),
                                     reads=["ta", "tb"], writes=["qr_c"])
                                S.op("dve", lambda e: e.tensor_tensor(out=kdst[:, :, 32:64], in0=ta[:, 8:10, :], in1=tb[:, 8:10, :],
                                                                      op=ALU.add),
                                     reads=["ta", "tb"], writes=["qr_d"])
                            else:
                                S.op("dve", lambda e: e.tensor_copy(out=qdst, in_=qsrc), reads=["qn"], writes=["qr_a", "qr_c"])
                                S.op("dve", lambda e: e.tensor_copy(out=kdst, in_=ksrc), reads=["qn"], writes=["qr_b", "qr_d"])
                            for c in range(5):
                                S.op("pe", lambda e: e.transpose(ptr[:, 2 + c, :], qr[:, c * 128:(c + 1) * 128], identb[:]),
                                     reads=["qr_a", "qr_b", "qr_c", "qr_d", "identb"], writes=[("ptr", 1)])
                            S.op("act", lambda e: e.activation(out=qT_all[:, :, tok], in_=ptr[:, 2:6, :], func=AF.Copy),
                                 reads=[("ptr", 1)], writes=[("qT", j)])
                            for kv_ in range(2):
                                S.op("act", lambda e: e.activation(out=kT_all[kv_ * 64:(kv_ + 1) * 64, kv_, tok],
                                                                   in_=ptr[kv_ * 64:(kv_ + 1) * 64, 6, :], func=AF.Copy),
                                     reads=[("ptr", 1), "kT_zero"], writes=[("kT", j, kv_)])
                            S.op("dve", lambda e: e.tensor_copy(out=v_all[:, j, :, 0:64],
                                                                in_=pz[:, 1408:1536].rearrange("p (k d) -> p k d", d=64)),
                                 reads=[("pz", 2)], writes=[("v", j)])
                    S.barrier()
                    esC = ExitStack()
                    with esC:
                        bandf = SB(esC, "bandf", [128, 4, 5, 128], F32)
                        band = SB(esC, "band", [128, 4, 5, 128], BF16)
                        wbd = SB(esC, "wbd", [128, 2, 128], BF16)
                        pscT = SB(esC, "pscT", [128, 2], F32)
                        pooledT = SB(esC, "pooledT", [128, 2, 128], BF16)
                        ppool = PS(esC, "ppool", [128, 4, 128], F32)
                        pp2 = PS(esC, "pp2", [128, 2, 128], F32)
                        S.dma("sp", lambda e: e.dma_start(out=bandf[:], in_=band_in), writes=["bandf"])
                        S.op("dve", lambda e: e.tensor_copy(out=band[:], in_=bandf[:]), reads=["bandf"], writes=["band"])
                        S.dma("pool", lambda e: e.dma_start(out=wbd[:], in_=wbd_in[ll].rearrange("c p q -> p c q")), writes=["wbd"])
                        S.dma("sp", lambda e: e.dma_start(out=pscT[:], in_=pscT_in[ll]), writes=["pscT"])
                        for j in range(NT):
                            tok = slice(j * 128, (j + 1) * 128)
                            lo_t, hi_t = (0, NTL - 1) if j < NTL else (NTL, NT - 1)
                            rels = [r for r in (-1, 0, 1) if lo_t <= j + r <= hi_t]
                            for c in range(2):
                                for bi in range(2):
                                    g = 2 * c + bi
                                    for ri, r in enumerate(rels):
                                        if r == -1:
                                            v = 0
                                        elif r == 1:
                                            v = 4
                                        else:
                                            v = 2 if j == lo_t else (3 if j == hi_t else 1)
                                        S.op("pe", lambda e: e.matmul(ppool[:, c * 2 + bi, :], lhsT=p_all[:, j + r, c * 128:(c + 1) * 128],
                                                                      rhs=band[:, g, v, :], start=(ri == 0), stop=(ri == len(rels) - 1)),
                                             reads=["band"], writes=["ppool"])
                                S.op("act", lambda e: e.activation(out=pooledT[0:64, c, :], in_=ppool[0:64, c * 2, :], func=AF.Copy),
                                     reads=["ppool"], writes=[("pooledT", c, 0)])
                                S.op("act", lambda e: e.activation(out=pooledT[64:128, c, :], in_=ppool[64:128, c * 2 + 1, :], func=AF.Copy),
                                     reads=["ppool"], writes=[("pooledT", c, 1)])
                            for c in range(2):
                                S.op("pe", lambda e: e.matmul(pp2[:, c, :], lhsT=wbd[:, c, :], rhs=pooledT[:, c, :], start=True, stop=True),
                                     reads=["wbd", ("pooledT", c, 0), ("pooledT", c, 1)], writes=["pp2"])
                            for c in range(2):
                                S.op("act", lambda e: e.activation(out=mixps[:, c, tok], in_=pp2[:, c, :], func=AF.Copy,
                                                                   scale=pscT[:, c:c + 1]),
                                     reads=["pp2", "pscT"], writes=[("mixps_p", j)])
                    S.barrier()
                if debug and ll == 0:
                    S.dma("sp", lambda e: e.dma_start(out=dbg_mix, in_=mixps[:]), writes=["dbgmix"])
                esD = ExitStack()
                with esD:
                    woutb = SB(esD, "woutb", [128, 8, D], BF16)
                    lnr = SB(esD, "lnr", [128, 2, D], F32)
                    wr = SB(esD, "wr", [128, 8, NE], F32)
                    PT = [SB(esD, "PT%d" % i, [128, 512], BF16) for i in range(3)]
                    attn_tok = [SB(esD, "attn_tok%d" % i, [128, 4, 512], BF16) for i in range(2)]
                    rden = SB(esD, "rden", [128, 4], F32)
                    mixA = SB(esD, "mixA", [128, 4, 128], BF16)
                    bufA = SB(esD, "bufA", [128, D], F32)
                    bufB = SB(esD, "bufB", [128, D], F32)
                    xhb = SB(esD, "xhb", [128, D], BF16)
                    h2T = SB(esD, "h2T", [128, 8, 128], F32)
                    st = SB(esD, "stE", [128, 2, 6], F32)
                    mv = SB(esD, "mvE", [128, 2], F32)
                    rstd = SB(esD, "rstdE", [128, 1], F32)
                    mx = SB(esD, "mx", [128, 1], F32)
                    sm = SB(esD, "sm", [128, 1], F32)
                    ex = SB(esD, "ex", [128, NE], F32)
                    g1rows = build_gate_rows(esD, 16, "g1r")
                    pS = [PS(esD, "pS%d" % i, [128, 512], F32) for i in range(3)]
                    pO = [PS(esD, "pO%d" % i, [128, 4, 128], F32) for i in range(2)]
                    ptr = PS(esD, "ptrD", [128, 4, 128], BF16)
                    pbig = PS(esD, "pbig", [128, D], F32)
                    pr = pbig[:, 0:NE]
                    S.dma("pool", lambda e: e.dma_start(out=woutb[:], in_=w_out[ll].rearrange("(k p) c -> p k c", p=128)),
                          writes=["woutb"])
                    S.dma("sp", lambda e: e.dma_start(out=lnr[:], in_=ln_in[ll:ll + 1, 0:2, :].broadcast_to([128, 2, D])),
                          writes=["lnr"])
                    S.dma("sp", lambda e: e.dma_start(out=wr[:], in_=wr_in[ll]), writes=["wr"])

                    def rstd_act(tag):
                        S.op("act", lambda e: e.activation(out=rstd[:], in_=mv[:, 1:2], func=AF.Ln, bias=epsc[:], scale=1.0),
                             reads=[tag + "mv", "epsc"], writes=[tag + "rstd"])
                        S.op("act", lambda e: e.activation(out=rstd[:], in_=rstd[:], func=AF.Exp, scale=-0.5),
                             reads=[tag + "rstd"], writes=[tag + "rstd"])

                    def stats_dve(src, key_src, tag):
                        S.op("dve", lambda e: e.bn_stats(st[:, 0, :], src[:, 0:512]), reads=[key_src], writes=[tag + "st0"])
                        S.op("dve", lambda e: e.bn_stats(st[:, 1, :], src[:, 512:1024]), reads=[key_src], writes=[tag + "st1"])
                        S.op("dve", lambda e: e.bn_aggr(mv[:], st[:]), reads=[tag + "st0", tag + "st1"], writes=[tag + "mv"])

                    def make_ef_stages(j, qs, par):
                        s = 0 if j < NTL else 1
                        tok = slice(j * 128, (j + 1) * 128)
                        at = attn_tok[par]

                        def st0():
                            for cc in range(4):
                                S.op("pe", lambda e: e.transpose(ptr[:, cc, :], at[:, qs, cc * 128:(cc + 1) * 128], identb[:]),
                                     reads=[("attn_tok", par, qs), "identb"], writes=["ptrD"])
                            S.dma("sp", lambda e: e.dma_start(out=bufA[:], in_=src_tile(ll, j)), reads=[src_key(ll, j)], writes=["bufA"])

                        def st1():
                            S.op("act", lambda e: e.activation(out=mixA[:], in_=ptr[:], func=AF.Copy), reads=["ptrD"], writes=["mixA"])

                        def st2():
                            for half in range(2):
                                for c8 in range(8):
                                    lhs = mixps[:, c8, tok] if c8 < 4 else mixA[:, c8 - 4, :]
                                    S.op("pe", lambda e: e.matmul(pbig[:, half * 512:(half + 1) * 512], lhsT=lhs,
                                                                  rhs=woutb[:, c8, half * 512:(half + 1) * 512],
                                                                  start=(c8 == 0), stop=(c8 == 7)),
                                         reads=["mixA", "woutb"], writes=["pbig"])

                        def st3():
                            S.op("dve", lambda e: e.tensor_tensor(out=bufB[:], in0=pbig[:], in1=g1rows[s][:], op=ALU.mult),
                                 reads=["pbig", "g1r_%d" % s], writes=["bufB"])
                            S.op("dve", lambda e: e.scalar_tensor_tensor(out=bufA[:], in0=bufA[:], scalar=ALPHA, in1=bufB[:],
                                                                         op0=ALU.mult, op1=ALU.add),
                                 reads=["bufA", "bufB"], writes=["bufA"])
                            stats_dve(bufA, "bufA", "E")

                        def st4():
                            rstd_act("E")

                        def st5():
                            S.op("dve", lambda e: e.tensor_scalar(out=bufB[:], in0=bufA[:], scalar1=mv[:, 0:1], scalar2=rstd[:],
                                                                  op0=ALU.subtract, op1=ALU.mult),
                                 reads=["bufA", "Emv", "Erstd"], writes=["bufB"])
                            S.op("dve", lambda e: e.tensor_tensor(out=bufB[:], in0=bufB[:], in1=lnr[:, 0, :], op=ALU.mult),
                                 reads=["bufB", "lnr"], writes=["bufB"])
                            S.op("dve", lambda e: e.tensor_tensor(out=bufB[:], in0=bufB[:], in1=lnr[:, 1, :], op=ALU.add),
                                 reads=["bufB", "lnr"], writes=["bufB"])
                            S.op("dve", lambda e: e.tensor_scalar(out=bufA[:], in0=bufB[:], scalar1=ALPHA, scalar2=None, op0=ALU.mult),
                                 reads=["bufB"], writes=["bufA"])
                            S.dma("sp", lambda e: e.dma_start(out=macc[tok, :], in_=bufA[:]), reads=["bufA"], writes=[("macc", j)])
                            if debug and ll == 0:
                                S.dma("sp", lambda e: e.dma_start(out=dbg_x1[tok, :], in_=bufA[:]), reads=["bufA"], writes=[("dbgx1", j)])
                            stats_dve(bufB, "bufB", "E")

                        def st6():
                            rstd_act("E")

                        def st7():
                            S.op("dve", lambda e: e.tensor_scalar(out=bufB[:], in0=bufB[:], scalar1=mv[:, 0:1], scalar2=rstd[:],
                                                                  op0=ALU.subtract, op1=ALU.mult),
                                 reads=["bufB", "Emv", "Erstd"], writes=["bufB"])
                            S.op("dve", lambda e: e.tensor_copy(out=xhb[:], in_=bufB[:]), reads=["bufB"], writes=["xhb"])
                            S.dma("sp", lambda e: e.dma_start(out=xh2[tok, :], in_=xhb[:]), reads=["xhb"], writes=[("xh2", j)])

                        def st8():
                            for k in range(8):
                                S.op("pe", lambda e: e.transpose(pbig[:, k * 128:(k + 1) * 128], bufB[:, k * 128:(k + 1) * 128], identf[:]),
                                     reads=["bufB", "identf"], writes=["pbig"])

                        def st9():
                            for k in range(8):
                                S.op("dve", lambda e: e.tensor_scalar(out=h2T[:, k, :], in0=pbig[:, k * 128:(k + 1) * 128],
                                                                      scalar1=modT[:, 32 + k, s:s + 1], scalar2=modT[:, 24 + k, s:s + 1],
                                                                      op0=ALU.mult, op1=ALU.add),
                                     reads=["pbig", "modT"], writes=[("h2T", k)])

                        def st10():
                            for k in range(8):
                                S.op("pe", lambda e: e.matmul(pr, lhsT=h2T[:, k, :], rhs=wr[:, k, :], start=(k == 0), stop=(k == 7)),
                                     reads=[("h2T", k), "wr"], writes=["pbig"])

                        def st11():
                            S.op("dve", lambda e: e.reduce_max(out=mx[:], in_=pr, axis=AX.X), reads=["pbig"], writes=["mx"])
                            S.op("dve", lambda e: e.tensor_scalar(out=mx[:], in0=mx[:], scalar1=-1.0, scalar2=None, op0=ALU.mult),
                                 reads=["mx"], writes=["mx"])

                        def st12():
                            S.op("act", lambda e: e.activation(out=ex[:], in_=pr, func=AF.Exp, bias=mx[:], scale=1.0, accum_out=sm[:]),
                                 reads=["pbig", "mx"], writes=["ex", "sm"])

                        def st13():
                            S.op("dve", lambda e: e.reciprocal(sm[:], sm[:]), reads=["sm"], writes=["sm"])
                            S.op("dve", lambda e: e.tensor_scalar(out=aff_all[:, j, :], in0=ex[:], scalar1=sm[:], scalar2=None, op0=ALU.mult),
                                 reads=["ex", "sm"], writes=[("aff", j)])

                        costs = [1, 1, 4, 5, 1, 8, 1, 3, 2, 2, 2, 1, 1, 1]
                        return list(zip([st0, st1, st2, st3, st4, st5, st6, st7, st8, st9, st10, st11, st12, st13], costs))

                    blocks = [(NTL, 2)] + [(qb * 4, 4) for qb in range(8)]
                    nS = 0
                    nO = 0
                    pending = []
                    for bi, (t0, ntile) in enumerate(blocks):
                        par = bi % 2
                        N = ntile * 128
                        qtok = slice(t0 * 128, t0 * 128 + N)
                        kts = list(range(NT)) if t0 < NTL else [NTL, NTL + 1]
                        steps = [(c, kv, ki, kt) for c in range(4) for kv in range(2) for ki, kt in enumerate(kts)]
                        tot_cost = sum(c_ for (_, c_) in pending)
                        unit = (len(steps) - 6) / float(tot_cost) if tot_cost else 0.0
                        wait_ = 2.0

                        def emit_S(i):
                            c, kv, ki, kt = steps[i]
                            pSb = pS[(nS + i) % 3]
                            pSk = "pS%d" % ((nS + i) % 3)
                            PTb = PT[(nS + i) % 3]
                            PTk = "PT%d" % ((nS + i) % 3)
                            S.op("pe", lambda e: e.matmul(pSb[:, 0:N], lhsT=kT_all[:, kv, kt * 128:(kt + 1) * 128],
                                                          rhs=qT_all[:, c, qtok], start=True, stop=True),
                                 writes=[pSk])
                            S.op("act", lambda e: e.activation(out=PTb[:, 0:N], in_=pSb[:, 0:N], func=AF.Exp, scale=0.125),
                                 reads=[pSk], writes=[PTk])

                        emit_S(0)
                        emit_S(1)
                        for i, (c, kv, ki, kt) in enumerate(steps):
                            h = kv * 4 + c
                            if i + 2 < len(steps):
                                emit_S(i + 2)
                            wait_ -= 1.0
                            if pending and wait_ <= 0.0:
                                fn_, c_ = pending.pop(0)
                                fn_()
                                wait_ += c_ * unit
                            if ki == 0:
                                nO += 1
                                if nO == 1:
                                    S.op("dve", lambda e: e.memset(pO[nO % 2][:], 0.0), writes=["pO%d" % (nO % 2)])
                            if ki == 1:
                                S.op("dve", lambda e: e.memset(pO[(nO + 1) % 2][:], 0.0), writes=["pO%d" % ((nO + 1) % 2)])
                            pOb = pO[nO % 2]
                            pOk = "pO%d" % (nO % 2)
                            PTb = PT[(nS + i) % 3]
                            PTk = "PT%d" % ((nS + i) % 3)
                            for qs in range(ntile):
                                S.op("pe", lambda e: e.matmul(pOb[:, qs, 0:65], lhsT=PTb[:, qs * 128:(qs + 1) * 128],
                                                              rhs=v_all[:, kt, kv, :], start=False, stop=(ki == len(kts) - 1)),
                                     reads=[PTk], writes=[pOk])
                            if ki == len(kts) - 1:
                                S.op("dve", lambda e: e.reciprocal(rden[:, 0:ntile], pOb[:, 0:ntile, 64]),
                                     reads=[pOk], writes=["rden"])
                                for qs in range(ntile):
                                    S.op("dve", lambda e: e.tensor_scalar(out=attn_tok[par][:, qs, h * 64:(h + 1) * 64], in0=pOb[:, qs, 0:64],
                                                                          scalar1=rden[:, qs:qs + 1], scalar2=None, op0=ALU.mult),
                                         reads=[pOk, "rden"], writes=[("attn_tok", par, qs)])
                        nS += len(steps)
                        while pending:
                            pending.pop(0)[0]()
                        for qs in range(ntile):
                            pending.extend(make_ef_stages(t0 + qs, qs, par))
                    while pending:
                        pending.pop(0)[0]()
                S.barrier()
            if debug and ll == 0:
                S.dma("sp", lambda e: e.dma_start(out=dbg_aff, in_=aff_all[:]), writes=["dbgaff"])
            esGH = ExitStack()
            with esGH:
                idx_i = SB(esGH, "idx_i", [128, NE, 5], I32)
                gate_s = SB(esGH, "gate_s", [128, NE, 5], F32)
                esG = ExitStack()
                with esG:
                    lo = SB(esG, "lo", [128, 2, NE], F32)
                    hi = SB(esG, "hi", [128, 2, NE], F32)
                    mid = SB(esG, "mid", [128, 2, NE], F32)
                    capv = SB(esG, "capv", [128, 2, NE], F32)
                    cmp_ = SB(esG, "cmp", [128, NT, NE], F32)
                    cnt = SB(esG, "cnt", [128, 2, NE], F32)
                    ge = SB(esG, "ge", [128, 2, NE], F32)
                    d1 = SB(esG, "d1", [128, 2, NE], F32)
                    mask = SB(esG, "mask", [128, NT, NE], F32)
                    tg = SB(esG, "tg", [128, NT, NE, 5], BF16)
                    gr1 = SB(esG, "gr1", [128, NT, NE], F32)
                    gr2 = SB(esG, "gr2", [128, NT, NE], F32)
                    lst = SB(esG, "lst", [128, 5, 5], F32)
                    pos = SB(esG, "pos", [128, NT, NE], F32)
                    off = SB(esG, "off", [128, NT, NE], F32)
                    tot = SB(esG, "tot", [128, NT, NE], F32)
                    oh = [SB(esG, "oh%d" % i, [128, 512], BF16) for i in range(4)]
                    idx_f = SB(esG, "idx_f", [128, NE, 5], F32)
                    ptot = PS(esG, "ptot", [128, 2, NE], F32)
                    pwl = PS(esG, "pwl", [128, 512], F32)
                    pwc = PS(esG, "pwc", [128, 32], F32)
                    ptl = PS(esG, "ptl", [128, 512], F32)
                    ptc = PS(esG, "ptc", [128, 32], F32)
                    plist = [PS(esG, "plist0", [128, 5, 128], F32)]
                    S.op("pool", lambda e: e.memset(lo[:], 0.0), writes=["lo"])
                    S.op("pool", lambda e: e.memset(hi[:], 1.0), writes=["hi"])
                    S.op("pool", lambda e: e.memset(capv[:, 0, :], float(CAP_L)), writes=["capv0"])
                    S.op("pool", lambda e: e.memset(capv[:, 1, :], float(CAP_C)), writes=["capv1"])
                    aff_l = aff_all[:, 0:NTL, :]
                    aff_c = aff_all[:, NTL:NT, :]
                    for it in range(30):
                        S.op("dve", lambda e: e.tensor_tensor(out=mid[:], in0=lo[:], in1=hi[:], op=ALU.add),
                             reads=["lo", "hi"], writes=["mid"])
                        S.op("dve", lambda e: e.tensor_scalar(out=mid[:], in0=mid[:], scalar1=0.5, scalar2=None, op0=ALU.mult),
                             reads=["mid"], writes=["mid"])
                        S.op("dve", lambda e: e.tensor_tensor(out=cmp_[:, 0:NTL, :], in0=aff_l,
                                                              in1=mid[:, 0:1, :].broadcast_to([128, NTL, NE]), op=ALU.is_ge),
                             reads=["mid"], writes=["cmp_l"])
                        S.op("dve", lambda e: e.tensor_tensor(out=cmp_[:, NTL:NT, :], in0=aff_c,
                                                              in1=mid[:, 1:2, :].broadcast_to([128, 2, NE]), op=ALU.is_ge),
                             reads=["mid"], writes=["cmp_c"])
                        S.op("dve", lambda e: e.tensor_reduce(out=cnt[:, 0, :], in_=cmp_[:, 0:NTL, :].rearrange("p j e -> p e j"),
                                                              axis=AX.X, op=ALU.add),
                             reads=["cmp_l"], writes=["cnt0"])
                        S.op("dve", lambda e: e.tensor_reduce(out=cnt[:, 1, :], in_=cmp_[:, NTL:NT, :].rearrange("p j e -> p e j"),
                                                              axis=AX.X, op=ALU.add),
                             reads=["cmp_c"], writes=["cnt1"])
                        S.op("pe", lambda e: e.matmul(ptot[:], lhsT=ones_f[:], rhs=cnt[:], start=True, stop=True),
                             reads=["cnt0", "cnt1", "ones_f"], writes=["ptot"])
                        S.op("dve", lambda e: e.tensor_tensor(out=ge[:], in0=ptot[:], in1=capv[:], op=ALU.is_ge),
                             reads=["ptot", "capv0", "capv1"], writes=["ge"])
                        S.op("dve", lambda e: e.tensor_tensor(out=d1[:], in0=mid[:], in1=lo[:], op=ALU.subtract),
                             reads=["mid", "lo"], writes=["d1"])
                        S.op("dve", lambda e: e.tensor_tensor(out=d1[:], in0=d1[:], in1=ge[:], op=ALU.mult),
                             reads=["d1", "ge"], writes=["d1"])
                        S.op("dve", lambda e: e.tensor_tensor(out=lo[:], in0=lo[:], in1=d1[:], op=ALU.add),
                             reads=["d1", "lo"], writes=["lo"])
                        S.op("dve", lambda e: e.tensor_tensor(out=d1[:], in0=hi[:], in1=mid[:], op=ALU.subtract),
                             reads=["mid", "hi"], writes=["d1"])
                        S.op("dve", lambda e: e.tensor_tensor(out=d1[:], in0=d1[:], in1=ge[:], op=ALU.mult),
                             reads=["d1", "ge"], writes=["d1"])
                        S.op("dve", lambda e: e.tensor_tensor(out=hi[:], in0=mid[:], in1=d1[:], op=ALU.add),
                             reads=["d1", "mid"], writes=["hi"])
                    S.op("dve", lambda e: e.tensor_tensor(out=mask[:, 0:NTL, :], in0=aff_l,
                                                          in1=lo[:, 0:1, :].broadcast_to([128, NTL, NE]), op=ALU.is_ge),
                         reads=["lo"], writes=["mask_l"])
                    S.op("dve", lambda e: e.tensor_tensor(out=mask[:, NTL:NT, :], in0=aff_c,
                                                          in1=lo[:, 1:2, :].broadcast_to([128, 2, NE]), op=ALU.is_ge),
                         reads=["lo"], writes=["mask_c"])
                    S.op("dve", lambda e: e.tensor_copy(out=tg[:, :, :, 0], in_=tokA[:].unsqueeze(2).broadcast_to([128, NT, NE])),
                         reads=["tokA"], writes=["tg0"])
                    S.op("dve", lambda e: e.tensor_copy(out=tg[:, :, :, 1], in_=tokB[:].unsqueeze(2).broadcast_to([128, NT, NE])),
                         reads=["tokB"], writes=["tg0"])
                    S.op("dve", lambda e: e.tensor_copy(out=tg[:, :, :, 2], in_=aff_all[:]), writes=["tg1"])
                    S.op("dve", lambda e: e.tensor_tensor(out=gr1[:], in0=aff_all[:], in1=tg[:, :, :, 2], op=ALU.subtract),
                         reads=["tg1"], writes=["gr1"])
                    S.op("dve", lambda e: e.tensor_copy(out=tg[:, :, :, 3], in_=gr1[:]), reads=["gr1"], writes=["tg1"])
                    S.op("dve", lambda e: e.tensor_tensor(out=gr2[:], in0=gr1[:], in1=tg[:, :, :, 3], op=ALU.subtract),
                         reads=["gr1", "tg1"], writes=["gr2"])
                    S.op("dve", lambda e: e.tensor_copy(out=tg[:, :, :, 4], in_=gr2[:]), reads=["gr2"], writes=["tg1"])
                    mk2 = mask[:].rearrange("p j e -> p (j e)")
                    S.op("pe", lambda e: e.matmul(pwl[:], lhsT=lmat[:], rhs=mk2[:, 0:512], start=True, stop=True),
                         reads=["mask_l", "lmat"], writes=["pwl"])
                    S.op("pe", lambda e: e.matmul(pwc[:], lhsT=lmat[:], rhs=mk2[:, 512:544], start=True, stop=True),
                         reads=["mask_c", "lmat"], writes=["pwc"])
                    S.op("pe", lambda e: e.matmul(ptl[:], lhsT=ones_f[:], rhs=mk2[:, 0:512], start=True, stop=True),
                         reads=["mask_l", "ones_f"], writes=["ptl"])
                    S.op("pe", lambda e: e.matmul(ptc[:], lhsT=ones_f[:], rhs=mk2[:, 512:544], start=True, stop=True),
                         reads=["mask_c", "ones_f"], writes=["ptc"])
                    pos2 = pos[:].rearrange("p j e -> p (j e)")
                    tot2 = tot[:].rearrange("p j e -> p (j e)")
                    S.op("act", lambda e: e.activation(out=pos2[:, 0:512], in_=pwl[:], func=AF.Copy), reads=["pwl"], writes=["pos_l"])
                    S.op("act", lambda e: e.activation(out=pos2[:, 512:544], in_=pwc[:], func=AF.Copy), reads=["pwc"], writes=["pos_c"])
                    S.op("act", lambda e: e.activation(out=tot2[:, 0:512], in_=ptl[:], func=AF.Copy), reads=["ptl"], writes=["tot"])
                    S.op("act", lambda e: e.activation(out=tot2[:, 512:544], in_=ptc[:], func=AF.Copy), reads=["ptc"], writes=["tot"])
                    S.op("pool", lambda e: e.memset(off[:], 0.0), writes=["off"])
                    for j in range(1, NTL):
                        S.op("dve", lambda e: e.tensor_tensor(out=off[:, j, :], in0=off[:, j - 1, :], in1=tot[:, j - 1, :], op=ALU.add),
                             reads=["off", "tot"], writes=["off"])
                    S.op("dve", lambda e: e.tensor_copy(out=off[:, NTL + 1, :], in_=tot[:, NTL, :]), reads=["off", "tot"], writes=["off"])
                    S.op("dve", lambda e: e.tensor_tensor(out=pos[:], in0=pos[:], in1=off[:], op=ALU.add),
                         reads=["pos_l", "pos_c", "off"], writes=["pos_l", "pos_c"])
                    S.op("dve", lambda e: e.scalar_tensor_tensor(out=pos[:], in0=pos[:], scalar=-BIG, in1=mask[:], op0=ALU.add, op1=ALU.mult),
                         reads=["pos_l", "pos_c", "mask_l", "mask_c"], writes=["pos_l", "pos_c"])
                    S.op("dve", lambda e: e.tensor_scalar(out=pos[:], in0=pos[:], scalar1=BIG, scalar2=None, op0=ALU.add),
                         reads=["pos_l", "pos_c"], writes=["pos"])
                    n_oh = 0
                    for ex_ in range(NE):
                        pl = plist[0]
                        plk = "plist0"
                        S.op("dve", lambda e: e.memset(pl[:], 0.0), writes=[plk])
                        for j in range(NT):
                            ohb = oh[n_oh % 4]
                            ohk = "oh%d" % (n_oh % 4)
                            eng_ = "dve"
                            n_oh += 1
                            if j < NTL:
                                S.op(eng_, lambda e: e.tensor_scalar(out=ohb[:], in0=iota_row[:], scalar1=pos[:, j, ex_:ex_ + 1],
                                                                     scalar2=None, op0=ALU.is_equal),
                                     reads=["iota_row", "pos"], writes=[ohk])
                                for st_ in range(4):
                                    S.op("pe", lambda e: e.matmul(pl[:, st_, 0:5], lhsT=ohb[:, st_ * 128:(st_ + 1) * 128],
                                                                  rhs=tg[:, j, ex_, :], start=False, stop=(j == NTL - 1)),
                                         reads=[ohk, "tg0", "tg1"], writes=[plk])
                            else:
                                S.op(eng_, lambda e: e.tensor_scalar(out=ohb[:, 0:128], in0=iota_row[:, 0:128], scalar1=pos[:, j, ex_:ex_ + 1],
                                                                     scalar2=None, op0=ALU.is_equal),
                                     reads=["iota_row", "pos"], writes=[ohk])
                                S.op("pe", lambda e: e.matmul(pl[:, 4, 0:5], lhsT=ohb[:, 0:128], rhs=tg[:, j, ex_, :],
                                                              start=False, stop=(j == NT - 1)),
                                     reads=[ohk, "tg0", "tg1"], writes=[plk])
                        S.op("act", lambda e: e.activation(out=lst[:], in_=pl[:, :, 0:5], func=AF.Copy),
                             reads=[plk], writes=["lst"])
                        S.op("dve", lambda e: e.tensor_tensor(out=idx_f[:, ex_, :], in0=lst[:, :, 0], in1=lst[:, :, 1], op=ALU.add),
                             reads=["lst"], writes=["idx_f"])
                        S.op("dve", lambda e: e.tensor_tensor(out=gate_s[:, ex_, :], in0=lst[:, :, 2], in1=lst[:, :, 3], op=ALU.add),
                             reads=["lst"], writes=["gate_s"])
                        S.op("dve", lambda e: e.tensor_tensor(out=gate_s[:, ex_, :], in0=gate_s[:, ex_, :], in1=lst[:, :, 4], op=ALU.add),
                             reads=["lst", "gate_s"], writes=["gate_s"])
                    S.op("dve", lambda e: e.tensor_copy(out=idx_i[:], in_=idx_f[:]), reads=["idx_f"], writes=["idx_i"])
                    if debug and ll == 0:
                        S.dma("sp", lambda e: e.dma_start(out=dbg_idx, in_=idx_i[:]), reads=["idx_i"], writes=["dbgidx"])
                        S.dma("sp", lambda e: e.dma_start(out=dbg_gate, in_=gate_s[:]), reads=["gate_s"], writes=["dbggate"])
                        S.dma("sp", lambda e: e.dma_start(out=dbg_pos, in_=pos[:]), reads=["pos"], writes=["dbgpos"])
                S.barrier()
                esH = ExitStack()
                with esH:
                    g2rows = build_gate_rows(esH, 40, "g2r")
                    wb = [[SB(esH, "wb%d_%d" % (m, i), [128, 8, D], BF16) for m in range(3)] for i in range(2)]
                    xg = [[SB(esH, "xg%d_%d" % (st_, i), [128, D], BF16) for st_ in range(5)] for i in range(2)]
                    xgT = [SB(esH, "xgT%d" % i, [128, 8, 544], BF16) for i in range(2)]
                    hidT = SB(esH, "hidT", [128, 8, 544], BF16)
                    s1 = SB(esH, "s1", [128, 544], F32)
                    ysc = [SB(esH, "ysc%d" % i, [128, D], F32) for i in range(4)]
                    pxT = [PS(esH, "pxT%d" % i, [128, 8, 128], BF16) for i in range(1)]
                    ph1 = [PS(esH, "ph1_%d" % i, [128, 512], F32) for i in range(2)]
                    ph3 = [PS(esH, "ph3_%d" % i, [128, 512], F32) for i in range(2)]
                    phc = PS(esH, "phc", [128, 2, 32], F32)
                    py = [PS(esH, "pyH%d" % i, [128, 512], F32) for i in range(2)]
                    wsrc = (w1, w3, w2)

                    def issue_loads(ex_):
                        i = ex_ % 2
                        for st_ in range(5):
                            npart = 128 if st_ < 4 else CAP_C
                            S.dma("pool", lambda e: e.indirect_dma_start(
                                out=xg[i][st_][0:npart, :], out_offset=None, in_=xh2[:, :],
                                in_offset=bass.IndirectOffsetOnAxis(ap=idx_i[0:npart, ex_, st_:st_ + 1], axis=0),
                                bounds_check=bcreg, oob_is_err=False),
                                reads=["idx_i"] + [("xh2", j) for j in range(NT)], writes=["xg%d_%d" % (st_, i)])
                        for m in range(3):
                            S.dma("pool", lambda e: e.dma_start(out=wb[i][m][:], in_=wsrc[m][ll, ex_].rearrange("(k p) c -> p k c", p=128)),
                                  writes=["wb%d_%d" % (m, i)])

                    def prep_tile(ex_, st_):
                        i = ex_ % 2
                        npart = 128 if st_ < 4 else CAP_C
                        s = 0 if st_ < 4 else 1
                        c0 = st_ * 128
                        npx[0] += 1
                        pb = pxT[0]
                        pk = "pxT0"
                        for k in range(8):
                            S.op("pe", lambda e: e.transpose(pb[:, k, 0:npart], xg[i][st_][0:npart, k * 128:(k + 1) * 128],
                                                             identb[0:npart, 0:npart]),
                                 reads=["xg%d_%d" % (st_, i), "identb"], writes=[pk])
                        for k in range(8):
                            if k % 2 == 0:
                                S.op("act", lambda e: e.activation(out=xgT[i][:, k, c0:c0 + npart], in_=pb[:, k, 0:npart], func=AF.Identity,
                                                                   bias=modT[:, 24 + k, s:s + 1], scale=modT[:, 32 + k, s:s + 1]),
                                     reads=[pk, "modT"], writes=["xgT%d" % i])
                            else:
                                S.op("dve", lambda e: e.tensor_scalar(out=xgT[i][:, k, c0:c0 + npart], in0=pb[:, k, 0:npart],
                                                                      scalar1=modT[:, 32 + k, s:s + 1], scalar2=modT[:, 24 + k, s:s + 1],
                                                                      op0=ALU.mult, op1=ALU.add),
                                     reads=[pk, "modT"], writes=["xgT%d" % i])

                    npx = [0]
                    issue_loads(0)
                    for st_ in range(5):
                        prep_tile(0, st_)
                    nys = 0
                    npy = 0
                    for ex_ in range(NE):
                        i = ex_ % 2
                        xgTi = xgT[i]
                        xk_ = "xgT%d" % i
                        if ex_ + 1 < NE:
                            issue_loads(ex_ + 1)
                        for fc in range(8):
                            fb = fc % 2
                            for m, ph in ((0, ph1[fb]), (1, ph3[fb])):
                                for k in range(8):
                                    S.op("pe", lambda e: e.matmul(ph[:], lhsT=wb[i][m][:, k, fc * 128:(fc + 1) * 128], rhs=xgTi[:, k, 0:512],
                                                                  start=(k == 0), stop=(k == 7)),
                                         reads=[xk_, "wb%d_%d" % (m, i)], writes=["ph%d_%d" % (m, fb)])
                                for k in range(8):
                                    S.op("pe", lambda e: e.matmul(phc[:, m, :], lhsT=wb[i][m][:, k, fc * 128:(fc + 1) * 128], rhs=xgTi[:, k, 512:544],
                                                                  start=(k == 0), stop=(k == 7)),
                                         reads=[xk_, "wb%d_%d" % (m, i)], writes=["phc"])
                            S.op("act", lambda e: e.activation(out=s1[:, 512:544], in_=phc[:, 0, :], func=AF.Silu), reads=["phc"], writes=["s1b"])
                            S.op("dve", lambda e: e.tensor_tensor(out=hidT[:, fc, 512:544], in0=s1[:, 512:544], in1=phc[:, 1, :], op=ALU.mult),
                                 reads=["s1b", "phc"], writes=[("hidT", 1)])
                            S.op("act", lambda e: e.activation(out=s1[:, 0:512], in_=ph1[fb][:], func=AF.Silu), reads=["ph0_%d" % fb], writes=["s1a"])
                            S.op("dve", lambda e: e.tensor_tensor(out=hidT[:, fc, 0:512], in0=s1[:, 0:512], in1=ph3[fb][:], op=ALU.mult),
                                 reads=["s1a", "ph1_%d" % fb], writes=[("hidT", 0)])
                            if ex_ + 1 < NE and fc < 5:
                                prep_tile(ex_ + 1, fc)
                        for st_ in range(5):
                            npart = 128 if st_ < 4 else CAP_C
                            s = 0 if st_ < 4 else 1
                            c0 = st_ * 128
                            yb = ysc[nys % 4]
                            yk = "ysc%d" % (nys % 4)
                            nys += 1
                            for half in range(2):
                                pyb = py[npy % 2]
                                pyk = "pyH%d" % (npy % 2)
                                npy += 1
                                for fc in range(8):
                                    S.op("pe", lambda e: e.matmul(pyb[0:npart, :], lhsT=hidT[:, fc, c0:c0 + npart],
                                                                  rhs=wb[i][2][:, fc, half * 512:(half + 1) * 512],
                                                                  start=(fc == 0), stop=(fc == 7)),
                                         reads=[("hidT", 0), ("hidT", 1), "wb2_%d" % i], writes=[pyk])
                                S.op("dve", lambda e: e.scalar_tensor_tensor(out=yb[0:npart, half * 512:(half + 1) * 512], in0=pyb[0:npart, :],
                                                                             scalar=gate_s[0:npart, ex_, st_:st_ + 1],
                                                                             in1=g2rows[s][0:npart, half * 512:(half + 1) * 512],
                                                                             op0=ALU.mult, op1=ALU.mult),
                                     reads=[pyk, "gate_s", "g2r_%d" % s], writes=[(yk, half)])
                            S.dma("pool", lambda e: e.indirect_dma_start(
                                out=macc[:, :], out_offset=bass.IndirectOffsetOnAxis(ap=idx_i[0:npart, ex_, st_:st_ + 1], axis=0),
                                in_=yb[0:npart, :], in_offset=None, bounds_check=bcreg, oob_is_err=False,
                                compute_op=ALU.add),
                                reads=[(yk, 0), (yk, 1), "idx_i"] + [("msc", ex_ - 1, k) for k in range(5)],
                                writes=[("msc", ex_, st_)])
                S.barrier()
            if debug and ll == 0:
                S.dma("sp", lambda e: e.dma_start(out=dbg_macc, in_=macc), writes=["dbgmacc"])
                S.barrier()
            esI = ExitStack()
            with esI:
                finish_next_A = None
                if ll + 1 < L:
                    finish_next_A = emit_phase_A(ll + 1, esI, "pool")
                lnr2 = SB(esI, "lnr2", [128, 2, D], F32)
                mt = [SB(esI, "mt%d" % i, [128, D], F32) for i in range(4)]
                xo = [SB(esI, "xo%d" % i, [128, D], F32) for i in range(4)]
                st = SB(esI, "stI", [128, 2, 6], F32)
                mv = SB(esI, "mvI", [128, 2], F32)
                rstd = SB(esI, "rstdI", [128, 1], F32)
                nmr = SB(esI, "nmr", [128, 1], F32)
                S.dma("sp", lambda e: e.dma_start(out=lnr2[:], in_=ln_in[ll:ll + 1, 2:4, :].broadcast_to([128, 2, D])),
                      writes=["lnr2"])
                stI = [st, SB(esI, "stI_b", [128, 2, 6], F32)]
                mvI = [mv, SB(esI, "mvI_b", [128, 2], F32)]
                rsI = [rstd, SB(esI, "rstdI_b", [128, 1], F32)]
                nmI = [nmr, SB(esI, "nmr_b", [128, 1], F32)]
                for j0 in range(0, NT, 2):
                    pair = [(j0, 0), (j0 + 1, 1)]
                    bo = (j0 // 2 % 2) * 2
                    for (j, t) in pair:
                        tok = slice(j * 128, (j + 1) * 128)
                        S.dma("sp", lambda e: e.dma_start(out=mt[bo + t][:], in_=macc[tok, :]), reads=[("macc", j)], writes=[("mt", bo + t)])
                    for c_ in range(2):
                        for (j, t) in pair:
                            S.op("dve", lambda e: e.bn_stats(stI[t][:, c_, :], mt[bo + t][:, c_ * 512:(c_ + 1) * 512]),
                                 reads=[("mt", bo + t)], writes=[("Ist", t, c_)])
                    for (j, t) in pair:
                        S.op("dve", lambda e: e.bn_aggr(mvI[t][:], stI[t][:]), reads=[("Ist", t, 0), ("Ist", t, 1)], writes=[("Imv", t)])
                    for (j, t) in pair:
                        S.op("act", lambda e: e.activation(out=rsI[t][:], in_=mvI[t][:, 1:2], func=AF.Sqrt, bias=epsc[:], scale=1.0),
                             reads=[("Imv", t), "epsc"], writes=[("Irs", t)])
                    for (j, t) in pair:
                        S.op("dve", lambda e: e.reciprocal(rsI[t][:], rsI[t][:]), reads=[("Irs", t)], writes=[("Irs", t)])
                    for (j, t) in pair:
                        S.op("dve", lambda e: e.scalar_tensor_tensor(out=nmI[t][:], in0=mvI[t][:, 0:1], scalar=-1.0, in1=rsI[t][:],
                                                                     op0=ALU.mult, op1=ALU.mult),
                             reads=[("Imv", t), ("Irs", t)], writes=[("Inm", t)])
                    for (j, t) in pair:
                        S.op("act", lambda e: e.activation(out=xo[bo + t][:], in_=mt[bo + t][:], func=AF.Identity, bias=nmI[t][:], scale=rsI[t][:]),
                             reads=[("mt", bo + t), ("Inm", t), ("Irs", t)], writes=[("xo", bo + t)])
                    for (j, t) in pair:
                        S.op("dve", lambda e: e.tensor_tensor(out=xo[bo + t][:], in0=xo[bo + t][:], in1=lnr2[:, 0, :], op=ALU.mult),
                             reads=[("xo", bo + t), "lnr2"], writes=[("xo", bo + t)])
                    for (j, t) in pair:
                        S.op("dve", lambda e: e.tensor_tensor(out=xo[bo + t][:], in0=xo[bo + t][:], in1=lnr2[:, 1, :], op=ALU.add),
                             reads=[("xo", bo + t), "lnr2"], writes=[("xo", bo + t)])
                    for (j, t) in pair:
                        S.dma("sp", lambda e: e.dma_start(out=dst_tile(ll, j), in_=xo[bo + t][:]), reads=[("xo", bo + t)], writes=[dst_key(ll, j)])
                if finish_next_A is not None:
                    finish_next_A()
            S.barrier()
        print("instructions", S.n_inst, "waits", S.n_wait, flush=True)
    return nc


def _band_tables():
    wins = (2, 4, 8, 16)
    Ls = 384
    band = np.zeros((128, 4, 5, 128), np.float32)
    for g, w in enumerate(wins):
        A = np.zeros((Ls, Ls), np.float64)
        for t in range(Ls):
            lo = min(max(t - w // 2, 0), Ls)
            hi = min(max(t + w // 2, 0), Ls)
            A[t, lo:hi] = 1.0 / (hi - lo)
            A[t, t] -= 1.0
        AT = A.T
        band[:, g, 0, :] = AT[0:128, 128:256]
        band[:, g, 1, :] = AT[128:256, 128:256]
        band[:, g, 2, :] = AT[0:128, 0:128]
        band[:, g, 3, :] = AT[256:384, 256:384]
        band[:, g, 4, :] = AT[256:384, 128:256]
    return band


def _rope_tables():
    t = np.arange(SEQ)
    r = (t // 64).astype(np.float32)
    col = (t % 64).astype(np.float32)
    inv = (np.float32(10000.0) ** (-np.arange(16, dtype=np.float32) / np.float32(16))).astype(np.float32)
    ang = np.concatenate([r[:, None] * inv, col[:, None] * inv], axis=-1).astype(np.float32)
    cs = np.concatenate([np.cos(ang), np.sin(ang)], axis=-1).astype(np.float32)
    return np.ascontiguousarray(cs.reshape(NTL, 128, 64).transpose(1, 0, 2))


def _prep_common(inp, layers):
    L = len(layers)
    sl = lambda a: np.ascontiguousarray(np.asarray(a)[layers])
    pool_w = sl(inp["pool_w"])
    wbd = np.zeros((L, 2, 128, 128), np.float32)
    for c in range(2):
        for gi in range(2):
            wbd[:, c, gi * 64:(gi + 1) * 64, gi * 64:(gi + 1) * 64] = pool_w[:, 2 * c + gi]
    com = {
        "w_mod": sl(inp["w_mod"]),
        "b_modT": np.ascontiguousarray(sl(inp["b_mod"]).reshape(L, 48, 128).transpose(0, 2, 1)),
        "w_in": sl(inp["w_in"]),
        "wbd": wbd,
        "pscT": np.ascontiguousarray(sl(inp["pool_scale"]).reshape(L, 2, 128).transpose(0, 2, 1)),
        "band": _band_tables(),
        "sgu_g": np.ascontiguousarray(sl(inp["sgu_g"]).reshape(L, 256)),
        "sgu_wT": np.ascontiguousarray(sl(inp["sgu_w"]).transpose(0, 3, 1, 2)),
        "sgu_bT": np.ascontiguousarray(sl(inp["sgu_b"]).transpose(0, 2, 1)),
        "qkg": np.ascontiguousarray(np.concatenate([np.tile(sl(inp["q_g"]), (1, 8)), np.tile(sl(inp["k_g"]), (1, 2))], axis=1)),
        "w_out": sl(inp["w_out"]),
        "ln": np.ascontiguousarray(np.stack([sl(inp["ln1_g"]), sl(inp["ln1_b"]), sl(inp["ln2_g"]), sl(inp["ln2_b"])], axis=1)),
        "w_router": np.ascontiguousarray(sl(inp["w_router"]).reshape(L, 8, 128, NE).transpose(0, 2, 1, 3)),
        "w1": sl(inp["w1"]), "w3": sl(inp["w3"]), "w2": sl(inp["w2"]),
        "cs": _rope_tables(),
    }
    return com


_NC_CACHE = {}


def _run(inp, x, ctx, layers, n_cores=8):
    L = len(layers)
    if L not in _NC_CACHE:
        _NC_CACHE[L] = build(L)
    nc = _NC_CACHE[L]
    com = _prep_common(inp, layers)
    c = np.asarray(inp["c"], np.float32)
    c_ctx = np.asarray(inp["c_ctx"], np.float32)
    in_maps = []
    for b in range(n_cores):
        cond = np.stack([c[b].reshape(8, 128).T, c_ctx.reshape(8, 128).T], axis=-1)
        m = dict(com)
        m["x"] = np.ascontiguousarray(x[b])
        m["ctx"] = np.ascontiguousarray(ctx[b])
        m["cond"] = np.ascontiguousarray(cond.astype(np.float32))
        in_maps.append(m)
    res = run_bass_kernel_spmd(nc, in_maps, core_ids=list(range(n_cores)))
    xo = np.stack([np.asarray(r["out"]) for r in res.results], 0)
    co = np.stack([np.asarray(r["ctx_out"]) for r in res.results], 0)
    return xo, co


LAYERS_PER_LAUNCH = 4


def kernel(**inputs):
    inp = {k: np.asarray(v) for k, v in inputs.items()}
    x = np.asarray(inp["x"], np.float32)
    ctx = np.asarray(inp["ctx"], np.float32)
    for l0 in range(0, DEPTH, LAYERS_PER_LAUNCH):
        x, ctx = _run(inp, x, ctx, list(range(l0, l0 + LAYERS_PER_LAUNCH)))
    return x.astype(np.float32)
```

```python
import numpy as np
from contextlib import ExitStack
import concourse.bass as bass
import concourse.mybir as mybir
from concourse.bass_utils import run_bass_kernel_spmd

F32 = mybir.dt.float32
BF16 = mybir.dt.bfloat16
I32 = mybir.dt.int32
AF = mybir.ActivationFunctionType
ALU = mybir.AluOpType
AX = mybir.AxisListType

DEPTH = 4
D = 1024
SEQ = 4096
CTX = 256
NT = 34
NTL = 32
NTOK = SEQ + CTX
NE = 16
CAP_L = 512
CAP_C = 32
ALPHA = float((2 * DEPTH) ** 0.25)
EPS = 1e-6
BIG = 100000.0


class Sched:
    def __init__(self, nc, es, n_dma_sems=48):
        self.nc = nc
        self.eng = {"pe": nc.tensor, "act": nc.scalar, "dve": nc.vector, "pool": nc.gpsimd, "sp": nc.sync}
        self.sem = {k: es.enter_context(nc.semaphore("prog_" + k)) for k in self.eng}
        self.cnt = {k: 0 for k in self.eng}
        self.dsem = [es.enter_context(nc.semaphore("dma%d" % i)) for i in range(n_dma_sems)]
        self.dtot = [0] * n_dma_sems
        self.dnext = 0
        self.dnext_sw = 0
        self.waited = {k: {} for k in self.eng}
        self.res = {}
        self.n_inst = 0
        self.n_wait = 0

    def _semobj(self, key):
        return self.sem[key] if isinstance(key, str) else self.dsem[key]

    def _collect(self, reads, writes):
        deps = {}

        def add(d):
            if d is None:
                return
            k, v = d
            if deps.get(k, 0) < v:
                deps[k] = v
        for r in reads:
            e = self.res.get(r)
            if e is not None:
                add(e["w"])
        for w in writes:
            e = self.res.get(w)
            if e is not None:
                add(e["w"])
                for k, v in e["r"].items():
                    add((k, v))
        return deps

    def _wait(self, F, deps, skip_self=False):
        for k, v in deps.items():
            if skip_self and k == F:
                continue
            if self.waited[F].get(k, 0) < v:
                self.eng[F].wait_ge(self._semobj(k), v)
                self.waited[F][k] = v
                self.n_wait += 1

    def _update(self, dep, reads, writes):
        k, v = dep
        for r in reads:
            e = self.res.setdefault(r, {"w": None, "r": {}})
            if e["r"].get(k, 0) < v:
                e["r"][k] = v
        for w in writes:
            self.res[w] = {"w": dep, "r": {}}

    def op(self, F, fn, reads=(), writes=(), skip_self=None):
        if skip_self is None:
            skip_self = (F == "pe")
        deps = self._collect(reads, writes)
        self._wait(F, deps, skip_self=skip_self)
        inst = fn(self.eng[F])
        self.cnt[F] += 1
        inst.then_inc(self.sem[F], 1)
        self._update((F, self.cnt[F]), reads, writes)
        self.n_inst += 1
        return inst

    def dma(self, Q, fn, reads=(), writes=()):
        deps = self._collect(reads, writes)
        self._wait(Q, deps)
        half = len(self.dsem) // 2
        if Q == "pool":
            i = half + self.dnext_sw
            self.dnext_sw = (self.dnext_sw + 1) % (len(self.dsem) - half)
        else:
            i = self.dnext
            self.dnext = (self.dnext + 1) % half
        if self.dtot[i] > 0 and self.waited[Q].get(i, 0) < self.dtot[i]:
            self.eng[Q].wait_ge(self.dsem[i], self.dtot[i])
            self.waited[Q][i] = self.dtot[i]
            self.n_wait += 1
        inst = fn(self.eng[Q])
        self.dtot[i] += 16
        inst.then_inc(self.dsem[i], 16)
        self._update((i, self.dtot[i]), reads, writes)
        self.n_inst += 1
        return inst

    def barrier(self):
        for F in self.eng:
            for i, t in enumerate(self.dtot):
                if t > 0 and self.waited[F].get(i, 0) < t:
                    self.eng[F].wait_ge(self.dsem[i], t)
                    self.waited[F][i] = t
            for k in self.eng:
                if k != F and self.cnt[k] > 0 and self.waited[F].get(k, 0) < self.cnt[k]:
                    self.eng[F].wait_ge(self.sem[k], self.cnt[k])
                    self.waited[F][k] = self.cnt[k]
        self.res = {}


def build(L, debug=False):
    nc = bass.Bass("TRN2", target_bir_lowering=False)

    def DT(name, shape, dt=F32, kind="ExternalInput"):
        return nc.dram_tensor(name, shape, dt, kind=kind).ap()

    x_in = DT("x", [SEQ, D])
    ctx_in = DT("ctx", [CTX, D])
    cond_in = DT("cond", [128, 8, 2])
    w_mod = DT("w_mod", [L, D, 6 * D])
    b_modT = DT("b_modT", [L, 128, 48])
    w_in = DT("w_in", [L, D, 1536])
    wbd_in = DT("wbd", [L, 2, 128, 128])
    pscT_in = DT("pscT", [L, 128, 2])
    band_in = DT("band", [128, 4, 5, 128])
    sgug_in = DT("sgu_g", [L, 256])
    sguwT_in = DT("sgu_wT", [L, 128, 4, 128])
    sgubT_in = DT("sgu_bT", [L, 128, 4])
    qkg_in = DT("qkg", [L, 640])
    w_out = DT("w_out", [L, D, D])
    ln_in = DT("ln", [L, 4, D])
    wr_in = DT("w_router", [L, 128, 8, NE])
    w1 = DT("w1", [L, NE, D, D])
    w3 = DT("w3", [L, NE, D, D])
    w2 = DT("w2", [L, NE, D, D])
    cs_in = DT("cs", [128, NTL, 64])
    out = DT("out", [SEQ, D], kind="ExternalOutput")
    ctx_out = DT("ctx_out", [CTX, D], kind="ExternalOutput")
    if debug:
        dbg_x1 = DT("dbg_x1", [NTOK, D], kind="ExternalOutput")
        dbg_aff = DT("dbg_aff", [128, NT, NE], kind="ExternalOutput")
        dbg_mix = DT("dbg_mix", [128, 4, NTOK], BF16, kind="ExternalOutput")
        dbg_idx = DT("dbg_idx", [128, NE, 5], I32, kind="ExternalOutput")
        dbg_gate = DT("dbg_gate", [128, NE, 5], kind="ExternalOutput")
        dbg_macc = DT("dbg_macc", [NTOK, D], kind="ExternalOutput")
        dbg_pos = DT("dbg_pos", [128, NT, NE], kind="ExternalOutput")
    xs = DT("xs", [NTOK, D], kind="Internal")
    macc = DT("macc", [NTOK, D], kind="Internal")
    xh2 = DT("xh2", [NTOK, D], BF16, kind="Internal")

    def src_tile(ll, j):
        if ll == 0:
            return x_in[j * 128:(j + 1) * 128, :] if j < NTL else ctx_in[(j - NTL) * 128:(j - NTL + 1) * 128, :]
        return xs[j * 128:(j + 1) * 128, :]

    def dst_tile(ll, j):
        if ll == L - 1:
            return out[j * 128:(j + 1) * 128, :] if j < NTL else ctx_out[(j - NTL) * 128:(j - NTL + 1) * 128, :]
        return xs[j * 128:(j + 1) * 128, :]

    def src_key(ll, j):
        return ("xin", j) if ll == 0 else ("xs", j)

    def dst_key(ll, j):
        return ("xout", j) if ll == L - 1 else ("xs", j)

    es0 = ExitStack()
    with es0:
        S = Sched(nc, es0)
        bcreg = es0.enter_context(nc.gpsimd.register("bcreg"))
        nc.gpsimd.reg_mov(bcreg, NTOK - 1)

        uid = [0]

        def SB(es, name, shape, dt):
            uid[0] += 1
            return es.enter_context(nc.sbuf_tensor("s%d_%s" % (uid[0], name), shape, dt))

        def PS(es, name, shape, dt):
            uid[0] += 1
            return es.enter_context(nc.psum_tensor("p%d_%s" % (uid[0], name), shape, dt))

        identf = SB(es0, "identf", [128, 128], F32)
        identb = SB(es0, "identb", [128, 128], BF16)
        ones_f = SB(es0, "ones_f", [128, 128], F32)
        lmat = SB(es0, "lmat", [128, 128], F32)
        iota_row = SB(es0, "iota_row", [128, 512], mybir.dt.float16)
        tokid = SB(es0, "tokid", [128, NT], F32)
        jmp = SB(es0, "jmp", [128, 128], F32)
        modT = SB(es0, "modT", [128, 48, 2], F32)
        aff_all = SB(es0, "aff_all", [128, NT, NE], F32)
        epsc = SB(es0, "epsc", [128, 1], F32)
        S.op("pool", lambda e: e.memset(epsc[:], EPS), writes=["epsc"])
        S.op("pool", lambda e: e.iota(jmp[:], [[1, 128]], base=0, channel_multiplier=-1,
                                      allow_small_or_imprecise_dtypes=True), writes=["jmp"])
        S.op("pool", lambda e: e.tensor_single_scalar(out=identf[:], in_=jmp[:], scalar=0.0, op=ALU.is_equal),
             reads=["jmp"], writes=["identf"])
        S.op("pool", lambda e: e.tensor_single_scalar(out=identb[:], in_=jmp[:], scalar=0.0, op=ALU.is_equal),
             reads=["jmp"], writes=["identb"])
        S.op("pool", lambda e: e.tensor_single_scalar(out=lmat[:], in_=jmp[:], scalar=0.0, op=ALU.is_gt),
             reads=["jmp"], writes=["lmat"])
        S.op("pool", lambda e: e.memset(ones_f[:], 1.0), writes=["ones_f"])
        S.op("pool", lambda e: e.iota(iota_row[:], [[1, 512]], base=0, channel_multiplier=0,
                                      allow_small_or_imprecise_dtypes=True), writes=["iota_row"])
        S.op("pool", lambda e: e.iota(tokid[:], [[128, NT]], base=0, channel_multiplier=1,
                                      allow_small_or_imprecise_dtypes=True), writes=["tokid"])
        tokA = SB(es0, "tokA", [128, NT], F32)
        tokB = SB(es0, "tokB", [128, 1], F32)
        S.op("pool", lambda e: e.iota(tokA[:], [[128, NT]], base=0, channel_multiplier=0,
                                      allow_small_or_imprecise_dtypes=True), writes=["tokA"])
        S.op("pool", lambda e: e.iota(tokB[:], [[0, 1]], base=0, channel_multiplier=1,
                                      allow_small_or_imprecise_dtypes=True), writes=["tokB"])

        def ln_stats(es_tiles, src, key_src, tag):
            st, mv, rstd = es_tiles
            S.op("dve", lambda e: e.bn_stats(st[:, 0, :], src[:, 0:512]), reads=[key_src], writes=[tag + "st0"])
            S.op("dve", lambda e: e.bn_stats(st[:, 1, :], src[:, 512:1024]), reads=[key_src], writes=[tag + "st1"])
            S.op("dve", lambda e: e.bn_aggr(mv[:], st[:]), reads=[tag + "st0", tag + "st1"], writes=[tag + "mv"])
            S.op("act", lambda e: e.activation(out=rstd[:], in_=mv[:, 1:2], func=AF.Sqrt, bias=epsc[:], scale=1.0),
                 reads=[tag + "mv", "epsc"], writes=[tag + "rstd"])
            S.op("dve", lambda e: e.reciprocal(rstd[:], rstd[:]), reads=[tag + "rstd"], writes=[tag + "rstd"])

        for ll in range(L):
            def emit_phase_A(la, esA_, qa):
                condT = SB(esA_, "condT", [128, 8, 2], F32)
                bmT = SB(esA_, "bmT", [128, 48], F32)
                wblk = [SB(esA_, "wblk%d" % i, [128, 8, 512], F32) for i in range(2)]
                pmod = PS(esA_, "pmod", [128, 48, 2], F32)
                S.dma("sp", lambda e: e.dma_start(out=condT[:], in_=cond_in), writes=["condT"])
                S.dma("sp", lambda e: e.dma_start(out=bmT[:], in_=b_modT[la]), writes=["bmT"])
                S.op("act", lambda e: e.activation(out=condT[:], in_=condT[:], func=AF.Silu),
                     reads=["condT"], writes=["condT"])
                wm = w_mod[la].rearrange("(k p) c -> p k c", p=128)
                for cb in range(12):
                    wb_ = wblk[cb % 2]
                    wk = "wblk%d" % (cb % 2)
                    S.dma(qa, lambda e: e.dma_start(out=wb_[:], in_=wm[:, :, cb * 512:(cb + 1) * 512]), writes=[wk])
                    for sub in range(4):
                        j = cb * 4 + sub
                        for k in range(8):
                            S.op("pe", lambda e: e.matmul(pmod[:, j, :], lhsT=wb_[:, k, sub * 128:(sub + 1) * 128],
                                                          rhs=condT[:, k, :], start=(k == 0), stop=(k == 7)),
                                 reads=[wk, "condT"], writes=["pmod"])
                def finish_A():
                    S.op("dve", lambda e: e.tensor_tensor(out=modT[:], in0=pmod[:],
                                                          in1=bmT[:].unsqueeze(2).broadcast_to([128, 48, 2]), op=ALU.add),
                         reads=["pmod", "bmT"], writes=["modT"])
                    for base in (8, 32):
                        S.op("dve", lambda e: e.tensor_scalar_add(modT[:, base:base + 8, :], modT[:, base:base + 8, :], 1.0),
                             reads=["modT"], writes=["modT"])
                return finish_A
            if ll == 0:
                esA0 = ExitStack()
                with esA0:
                    emit_phase_A(0, esA0, "sp")()
                    S.barrier()

            def build_gate_rows(es, vbase, tagname):
                rows = [SB(es, "%s_%d" % (tagname, s), [128, D], F32) for s in range(2)]
                est = ExitStack()
                with est:
                  diag = [SB(est, "%s_dg%d" % (tagname, i), [128, 128], F32) for i in range(2)]
                  pg = PS(est, tagname + "_pg", [128, D], F32)
                  n = 0
                  for s in range(2):
                    for dc in range(8):
                        dg = diag[n % 2]
                        dk = "%s_dg%d" % (tagname, n % 2)
                        n += 1
                        S.op("dve", lambda e: e.tensor_scalar(out=dg[:], in0=identf[:], scalar1=modT[:, vbase + dc, s:s + 1],
                                                              scalar2=None, op0=ALU.mult),
                             reads=["identf", "modT"], writes=[dk])
                        S.op("pe", lambda e: e.matmul(pg[:, dc * 128:(dc + 1) * 128], lhsT=ones_f[:], rhs=dg[:],
                                                      start=True, stop=True),
                             reads=[dk, "ones_f"], writes=[tagname + "_pg"])
                    S.op("act", lambda e: e.activation(out=rows[s][:], in_=pg[:], func=AF.Copy),
                         reads=[tagname + "_pg"], writes=["%s_%d" % (tagname, s)])
                  S.barrier()
                return rows

            esBE = ExitStack()
            with esBE:
                mixps = SB(esBE, "mixps", [128, 4, NTOK], BF16)
                qT_all = SB(esBE, "qT_all", [128, 4, NTOK], BF16)
                kT_all = SB(esBE, "kT_all", [128, 2, NTOK], BF16)
                S.op("pool", lambda e: e.memset(kT_all[:], 0.0), writes=["kT_zero"])
                v_all = SB(esBE, "v_all", [128, NT, 2, 65], BF16)
                S.op("pool", lambda e: e.memset(v_all[:, :, :, 64:65], 1.0), writes=["v_ones"])
                esBC = ExitStack()
                with esBC:
                    p_all = SB(esBC, "p_all", [128, NT, 256], BF16)
                    esB = ExitStack()
                    with esB:
                        winb = SB(esB, "winb", [128, 8, 1536], BF16)
                        cs = SB(esB, "cs", [128, NTL, 64], F32)
                        sgug = SB(esB, "sgug", [128, 256], F32)
                        wsT = SB(esB, "wsT", [128, 4, 128], BF16)
                        sbT = SB(esB, "sbT", [128, 4], F32)
                        qkg = SB(esB, "qkg", [128, 640], F32)
                        xt = [SB(esB, "xt%d" % i, [128, D], F32) for i in range(2)]
                        st = SB(esB, "st", [128, 2, 6], F32)
                        mv = SB(esB, "mv", [128, 2], F32)
                        rstd = SB(esB, "rstd", [128, 1], F32)
                        xb = SB(esB, "xb", [128, D], BF16)
                        hT = SB(esB, "hT", [128, 8, 128], BF16)
                        gu = SB(esB, "gu", [128, 256], BF16)
                        gv = SB(esB, "gv", [128, 256], F32)
                        stv = SB(esB, "stv", [128, 4, 6], F32)
                        mvv = SB(esB, "mvv", [128, 4, 2], F32)
                        rsv = SB(esB, "rsv", [128, 4], F32)
                        vn = SB(esB, "vn", [128, 256], F32)
                        vh = SB(esB, "vh", [128, 256], BF16)
                        sgo = SB(esB, "sgo", [128, 256], BF16)
                        sq = SB(esB, "sq", [128, 640], F32)
                        ss = SB(esB, "ss", [128, 10], F32)
                        qn = SB(esB, "qn", [128, 10, 64], F32)
                        ta = SB(esB, "ta", [128, 10, 32], F32)
                        tb = SB(esB, "tb", [128, 10, 32], F32)
                        qr = SB(esB, "qr", [128, 640], BF16)
                        pT = PS(esB, "pT", [128, 8, 128], BF16)
                        pz = PS(esB, "pz", [128, 1536], F32)
                        psg = PS(esB, "psg", [128, 256], F32)
                        ptr = PS(esB, "ptr", [128, 7, 128], BF16)
                        S.dma("pool", lambda e: e.dma_start(out=winb[:], in_=w_in[ll].rearrange("(k p) c -> p k c", p=128)),
                              writes=["winb"])
                        S.dma("pool", lambda e: e.dma_start(out=wsT[:], in_=sguwT_in[ll]), writes=["wsT"])
                        S.dma("sp", lambda e: e.dma_start(out=cs[:], in_=cs_in), writes=["cs"])
                        S.dma("sp", lambda e: e.dma_start(out=sgug[:], in_=sgug_in[ll:ll + 1, :].broadcast_to([128, 256])),
                              writes=["sgug"])
                        S.dma("sp", lambda e: e.dma_start(out=qkg[:], in_=qkg_in[ll:ll + 1, :].broadcast_to([128, 640])),
                              writes=["qkg"])
                        S.dma("sp", lambda e: e.dma_start(out=sbT[:], in_=sgubT_in[ll]), writes=["sbT"])
                        def front_B(j):
                            s = 0 if j < NTL else 1
                            tok = slice(j * 128, (j + 1) * 128)
                            xtj = xt[j % 2]
                            xk = "xt%d" % (j % 2)
                            S.dma("sp", lambda e: e.dma_start(out=xtj[:], in_=src_tile(ll, j)),
                                  reads=[src_key(ll, j)], writes=[xk])
                            ln_stats((st, mv, rstd), xtj, xk, "B")
                            S.op("dve", lambda e: e.tensor_scalar(out=xb[:], in0=xtj[:], scalar1=mv[:, 0:1], scalar2=rstd[:],
                                                                  op0=ALU.subtract, op1=ALU.mult),
                                 reads=[xk, "Bmv", "Brstd"], writes=["xb"])
                            for k in range(8):
                                S.op("pe", lambda e: e.transpose(pT[:, k, :], xb[:, k * 128:(k + 1) * 128], identb[:]),
                                     reads=["xb", "identb"], writes=["pT"])
                            for k in range(8):
                                S.op("act", lambda e: e.activation(out=hT[:, k, :], in_=pT[:, k, :], func=AF.Identity,
                                                                   bias=modT[:, 0 + k, s:s + 1], scale=modT[:, 8 + k, s:s + 1]),
                                     reads=["pT", "modT"], writes=[("hT", k)])
                        gu2 = [gu, SB(esB, "gu_b", [128, 256], BF16)]
                        gv2 = [gv, SB(esB, "gv_b", [128, 256], F32)]
                        stv2 = [stv, SB(esB, "stv_b", [128, 4, 6], F32)]
                        mvv2 = [mvv, SB(esB, "mvv_b", [128, 4, 2], F32)]
                        rsv2 = [rsv, SB(esB, "rsv_b", [128, 4], F32)]
                        vn2 = [vn, SB(esB, "vn_b", [128, 256], F32)]
                        vh2 = [vh, SB(esB, "vh_b", [128, 256], BF16)]
                        sgo2 = [sgo, SB(esB, "sgo_b", [128, 256], BF16)]
                        sq2 = [sq, SB(esB, "sq_b", [128, 640], F32)]
                        zq2 = [SB(esB, "zq_a", [128, 640], F32), SB(esB, "zq_b", [128, 640], F32)]
                        ss2 = [ss, SB(esB, "ss_b", [128, 10], F32)]
                        qn2 = [qn, SB(esB, "qn_b", [128, 10, 64], F32)]
                        ta2 = [ta, SB(esB, "ta_b", [128, 10, 32], F32)]
                        tb2 = [tb, SB(esB, "tb_b", [128, 10, 32], F32)]
                        qr2 = [qr, SB(esB, "qr_b", [128, 640], BF16)]
                        psg2 = [psg, PS(esB, "psg_b", [128, 256], F32)]
                        ptr2 = [ptr, PS(esB, "ptr_b", [128, 7, 128], BF16)]

                        def mm_and_evac(j, t):
                            for cblk in range(3):
                                for k in range(8):
                                    S.op("pe", lambda e: e.matmul(pz[:, cblk * 512:(cblk + 1) * 512], lhsT=hT[:, k, :],
                                                                  rhs=winb[:, k, cblk * 512:(cblk + 1) * 512],
                                                                  start=(k == 0), stop=(k == 7)),
                                         reads=[("hT", k), "winb"], writes=[("pz", cblk)])
                            if j + 1 < NT:
                                front_B(j + 1)
                            S.op("act", lambda e: e.activation(out=p_all[:, j, :], in_=pz[:, 0:256], func=AF.Copy),
                                 reads=[("pz", 0)], writes=[("p_all", j)])
                            S.op("act", lambda e: e.activation(out=gu2[t][:], in_=pz[:, 256:512], func=AF.Gelu_apprx_tanh),
                                 reads=[("pz", 0)], writes=[("gu", t)])
                            S.op("act", lambda e: e.activation(out=gv2[t][:], in_=pz[:, 512:768], func=AF.Gelu_apprx_tanh),
                                 reads=[("pz", 1)], writes=[("gv", t)])
                            S.op("act", lambda e: e.activation(out=sq2[t][:], in_=pz[:, 768:1408], func=AF.Square),
                                 reads=[("pz", 1), ("pz", 2)], writes=[("sq", t)])
                            S.op("dve", lambda e: e.tensor_copy(out=zq2[t][:], in_=pz[:, 768:1408]),
                                 reads=[("pz", 1), ("pz", 2)], writes=[("zq", t)])
                            S.op("dve", lambda e: e.tensor_copy(out=v_all[:, j, :, 0:64],
                                                                in_=pz[:, 1408:1536].rearrange("p (k d) -> p k d", d=64)),
                                 reads=[("pz", 2)], writes=[("v", j)])

                        front_B(0)
                        for j0 in range(0, NTL, 2):
                            pair = [(j0, 0), (j0 + 1, 1)]
                            for (j, t) in pair:
                                mm_and_evac(j, t)
                            for h in range(4):
                                for (j, t) in pair:
                                    S.op("dve", lambda e: e.bn_stats(stv2[t][:, h, :], gv2[t][:, h * 64:(h + 1) * 64]),
                                         reads=[("gv", t)], writes=[("stv", t, h)])
                                for (j, t) in pair:
                                    S.op("dve", lambda e: e.bn_aggr(mvv2[t][:, h, :], stv2[t][:, h, :]),
                                         reads=[("stv", t, h)], writes=[("mvv", t)])
                            for (j, t) in pair:
                                S.op("act", lambda e: e.activation(out=rsv2[t][:], in_=mvv2[t][:, :, 1], func=AF.Sqrt, bias=epsc[:], scale=1.0),
                                     reads=[("mvv", t), "epsc"], writes=[("rsv", t)])
                            for (j, t) in pair:
                                S.op("dve", lambda e: e.tensor_reduce(out=ss2[t][:], in_=sq2[t][:].rearrange("p (h d) -> p h d", d=64),
                                                                      axis=AX.X, op=ALU.add),
                                     reads=[("sq", t)], writes=[("ss", t)])
                            for (j, t) in pair:
                                S.op("act", lambda e: e.activation(out=ss2[t][:], in_=ss2[t][:], func=AF.Sqrt, bias=epsc[:], scale=1.0 / 64),
                                     reads=[("ss", t), "epsc"], writes=[("ss", t)])
                            for (j, t) in pair:
                                S.op("dve", lambda e: e.reciprocal(rsv2[t][:], rsv2[t][:]), reads=[("rsv", t)], writes=[("rsv", t)])
                            for h in range(4):
                                for (j, t) in pair:
                                    S.op("dve", lambda e: e.tensor_scalar(out=vn2[t][:, h * 64:(h + 1) * 64], in0=gv2[t][:, h * 64:(h + 1) * 64],
                                                                          scalar1=mvv2[t][:, h, 0:1], scalar2=rsv2[t][:, h:h + 1],
                                                                          op0=ALU.subtract, op1=ALU.mult),
                                         reads=[("gv", t), ("mvv", t), ("rsv", t)], writes=[("vn", t)])
                            for (j, t) in pair:
                                S.op("dve", lambda e: e.tensor_tensor(out=vh2[t][:], in0=vn2[t][:], in1=sgug[:], op=ALU.mult),
                                     reads=[("vn", t), "sgug"], writes=[("vh", t)])
                            for (j, t) in pair:
                                for h in range(4):
                                    S.op("pe", lambda e: e.matmul(psg2[t][:, h * 64:(h + 1) * 64], lhsT=wsT[:, h, :],
                                                                  rhs=vh2[t][:, h * 64:(h + 1) * 64], start=True, stop=True),
                                         reads=["wsT", ("vh", t)], writes=[("psg", t)])
                            for (j, t) in pair:
                                S.op("dve", lambda e: e.reciprocal(ss2[t][:], ss2[t][:]), reads=[("ss", t)], writes=[("ss", t)])
                            for (j, t) in pair:
                                S.op("dve", lambda e: e.tensor_tensor(out=qn2[t][:], in0=zq2[t][:].rearrange("p (h d) -> p h d", d=64),
                                                                      in1=ss2[t][:].unsqueeze(2).broadcast_to([128, 10, 64]), op=ALU.mult),
                                     reads=[("zq", t), ("ss", t)], writes=[("qn", t)])
                            for (j, t) in pair:
                                S.op("dve", lambda e: e.tensor_tensor(out=qn2[t][:], in0=qn2[t][:],
                                                                      in1=qkg[:].rearrange("p (h d) -> p h d", d=64), op=ALU.mult),
                                     reads=[("qn", t), "qkg"], writes=[("qn", t)])
                            for h in range(4):
                                for (j, t) in pair:
                                    S.op("dve", lambda e: e.scalar_tensor_tensor(out=sgo2[t][:, h * 64:(h + 1) * 64],
                                                                                 in0=psg2[t][:, h * 64:(h + 1) * 64],
                                                                                 scalar=sbT[:, h:h + 1],
                                                                                 in1=gu2[t][:, h * 64:(h + 1) * 64],
                                                                                 op0=ALU.add, op1=ALU.mult),
                                         reads=[("psg", t), "sbT", ("gu", t)], writes=[("sgo", t)])
                            for (j, t) in pair:
                                for c in range(2):
                                    S.op("pe", lambda e: e.transpose(ptr2[t][:, c, :], sgo2[t][:, c * 128:(c + 1) * 128], identb[:]),
                                         reads=[("sgo", t), "identb"], writes=[("ptr", t)])
                            for (j, t) in pair:
                                tok = slice(j * 128, (j + 1) * 128)
                                S.op("act", lambda e: e.activation(out=mixps[:, 2:4, tok], in_=ptr2[t][:, 0:2, :], func=AF.Copy),
                                     reads=[("ptr", t)], writes=[("mixps_s", j)])
                            def dsts(t):
                                qdst = qr2[t][:, 0:512].rearrange("p (g k d) -> p k g d", g=4, k=2, d=64)
                                kdst = qr2[t][:, 512:640].rearrange("p (k d) -> p k d", d=64)
                                return qdst, kdst
                            if j0 < NTL:
                                for half_, op_ in ((0, ALU.subtract), (1, ALU.add)):
                                    for (j, t) in pair:
                                        cosb = cs[:, j:j + 1, 0:32].broadcast_to([128, 10, 32])
                                        xa_ = qn2[t][:, :, 0:32] if half_ == 0 else qn2[t][:, :, 32:64]
                                        S.op("dve", lambda e: e.tensor_tensor(out=ta2[t][:], in0=xa_, in1=cosb, op=ALU.mult),
                                             reads=[("qn", t), "cs"], writes=[("ta", t)])
                                    for (j, t) in pair:
                                        sinb = cs[:, j:j + 1, 32:64].broadcast_to([128, 10, 32])
                                        xb_ = qn2[t][:, :, 32:64] if half_ == 0 else qn2[t][:, :, 0:32]
                                        S.op("dve", lambda e: e.tensor_tensor(out=tb2[t][:], in0=xb_, in1=sinb, op=ALU.mult),
                                             reads=[("qn", t), "cs"], writes=[("tb", t)])
                                    for (j, t) in pair:
                                        qdst, kdst = dsts(t)
                                        dsl = slice(0, 32) if half_ == 0 else slice(32, 64)
                                        S.op("dve", lambda e: e.tensor_tensor(out=qdst[:, :, :, dsl],
                                                                              in0=ta2[t][:, 0:8, :].rearrange("p (k g) d -> p k g d", k=2),
                                                                              in1=tb2[t][:, 0:8, :].rearrange("p (k g) d -> p k g d", k=2),
                                                                              op=op_),
                                             reads=[("ta", t), ("tb", t)], writes=[("qr", t, half_, 0)])
                                        S.op("dve", lambda e: e.tensor_tensor(out=kdst[:, :, dsl], in0=ta2[t][:, 8:10, :], in1=tb2[t][:, 8:10, :],
                                                                              op=op_),
                                             reads=[("ta", t), ("tb", t)], writes=[("qr", t, half_, 1)])
                            else:
                                for (j, t) in pair:
                                    qdst, kdst = dsts(t)
                                    S.op("dve", lambda e: e.tensor_copy(out=qdst, in_=qn2[t][:, 0:8, :].rearrange("p (k g) d -> p k g d", k=2)),
                                         reads=[("qn", t)], writes=[("qr", t, 0, 0), ("qr", t, 1, 0)])
                                    S.op("dve", lambda e: e.tensor_copy(out=kdst, in_=qn2[t][:, 8:10, :]),
                                         reads=[("qn", t)], writes=[("qr", t, 0, 1), ("qr", t, 1, 1)])
                            for (j, t) in pair:
                                for c in range(5):
                                    S.op("pe", lambda e: e.transpose(ptr2[t][:, 2 + c, :], qr2[t][:, c * 128:(c + 1) * 128], identb[:]),
                                         reads=[("qr", t, 0, 0), ("qr", t, 0, 1), ("qr", t, 1, 0), ("qr", t, 1, 1), "identb"],
                                         writes=[("ptr", t)])
                            for (j, t) in pair:
                                tok = slice(j * 128, (j + 1) * 128)
                                S.op("act", lambda e: e.activation(out=qT_all[:, :, tok], in_=ptr2[t][:, 2:6, :], func=AF.Copy),
                                     reads=[("ptr", t)], writes=[("qT", j)])
                                for kv_ in range(2):
                                    S.op("act", lambda e: e.activation(out=kT_all[kv_ * 64:(kv_ + 1) * 64, kv_, tok],
                                                                       in_=ptr2[t][kv_ * 64:(kv_ + 1) * 64, 6, :], func=AF.Copy),
                                         reads=[("ptr", t), "kT_zero"], writes=[("kT", j, kv_)])
                        S.barrier()
                        for j in range(NTL, NT):
                            s = 0 if j < NTL else 1
                            tok = slice(j * 128, (j + 1) * 128)
                            for cblk in range(3):
                                for k in range(8):
                                    S.op("pe", lambda e: e.matmul(pz[:, cblk * 512:(cblk + 1) * 512], lhsT=hT[:, k, :],
                                                                  rhs=winb[:, k, cblk * 512:(cblk + 1) * 512],
                                                                  start=(k == 0), stop=(k == 7)),
                                         reads=[("hT", k), "winb"], writes=[("pz", cblk)])
                            if j + 1 < NT:
                                front_B(j + 1)
                            S.op("act", lambda e: e.activation(out=p_all[:, j, :], in_=pz[:, 0:256], func=AF.Copy),
                                 reads=[("pz", 0)], writes=[("p_all", j)])
                            S.op("act", lambda e: e.activation(out=gu[:], in_=pz[:, 256:512], func=AF.Gelu_apprx_tanh),
                                 reads=[("pz", 0)], writes=["gu"])
                            S.op("act", lambda e: e.activation(out=gv[:], in_=pz[:, 512:768], func=AF.Gelu_apprx_tanh),
                                 reads=[("pz", 1)], writes=["gv"])
                            for h in range(4):
                                S.op("dve", lambda e: e.bn_stats(stv[:, h, :], gv[:, h * 64:(h + 1) * 64]),
                                     reads=["gv"], writes=[("stv", h)])
                                S.op("dve", lambda e: e.bn_aggr(mvv[:, h, :], stv[:, h, :]),
                                     reads=[("stv", h)], writes=["mvv"])
                            S.op("act", lambda e: e.activation(out=rsv[:], in_=mvv[:, :, 1], func=AF.Sqrt, bias=epsc[:], scale=1.0),
                                 reads=["mvv", "epsc"], writes=["rsv"])
                            S.op("dve", lambda e: e.reciprocal(rsv[:], rsv[:]), reads=["rsv"], writes=["rsv"])
                            for h in range(4):
                                S.op("dve", lambda e: e.tensor_scalar(out=vn[:, h * 64:(h + 1) * 64], in0=gv[:, h * 64:(h + 1) * 64],
                                                                      scalar1=mvv[:, h, 0:1], scalar2=rsv[:, h:h + 1],
                                                                      op0=ALU.subtract, op1=ALU.mult),
                                     reads=["gv", "mvv", "rsv"], writes=["vn"])
                            S.op("dve", lambda e: e.tensor_tensor(out=vh[:], in0=vn[:], in1=sgug[:], op=ALU.mult),
                                 reads=["vn", "sgug"], writes=["vh"])
                            for h in range(4):
                                S.op("pe", lambda e: e.matmul(psg[:, h * 64:(h + 1) * 64], lhsT=wsT[:, h, :],
                                                              rhs=vh[:, h * 64:(h + 1) * 64], start=True, stop=True),
                                     reads=["wsT", "vh"], writes=["psg"])
                            for h in range(4):
                                S.op("dve", lambda e: e.scalar_tensor_tensor(out=sgo[:, h * 64:(h + 1) * 64],
                                                                             in0=psg[:, h * 64:(h + 1) * 64],
                                                                             scalar=sbT[:, h:h + 1],
                                                                             in1=gu[:, h * 64:(h + 1) * 64],
                                                                             op0=ALU.add, op1=ALU.mult),
                                     reads=["psg", "sbT", "gu"], writes=["sgo"])
                            for c in range(2):
                                S.op("pe", lambda e: e.transpose(ptr[:, c, :], sgo[:, c * 128:(c + 1) * 128], identb[:]),
                                     reads=["sgo", "identb"], writes=[("ptr", 0)])
                            S.op("act", lambda e: e.activation(out=mixps[:, 2:4, tok], in_=ptr[:, 0:2, :], func=AF.Copy),
                                 reads=[("ptr", 0)], writes=[("mixps_s", j)])
                            S.op("act", lambda e: e.activation(out=sq[:], in_=pz[:, 768:1408], func=AF.Square),
                                 reads=[("pz", 1), ("pz", 2)], writes=["sq"])
                            S.op("dve", lambda e: e.tensor_reduce(out=ss[:], in_=sq[:].rearrange("p (h d) -> p h d", d=64),
                                                                  axis=AX.X, op=ALU.add),
                                 reads=["sq"], writes=["ss"])
                            S.op("act", lambda e: e.activation(out=ss[:], in_=ss[:], func=AF.Sqrt, bias=epsc[:], scale=1.0 / 64),
                                 reads=["ss", "epsc"], writes=["ss"])
                            S.op("dve", lambda e: e.reciprocal(ss[:], ss[:]), reads=["ss"], writes=["ss"])
                            S.op("dve", lambda e: e.tensor_tensor(out=qn[:], in0=pz[:, 768:1408].rearrange("p (h d) -> p h d", d=64),
                                                                  in1=ss[:].unsqueeze(2).broadcast_to([128, 10, 64]), op=ALU.mult),
                                 reads=[("pz", 1), ("pz", 2), "ss"], writes=["qn"])
                            S.op("dve", lambda e: e.tensor_tensor(out=qn[:], in0=qn[:],
                                                                  in1=qkg[:].rearrange("p (h d) -> p h d", d=64), op=ALU.mult),
                                 reads=["qn", "qkg"], writes=["qn"])
                            qdst = qr[:, 0:512].rearrange("p (g k d) -> p k g d", g=4, k=2, d=64)
                            kdst = qr[:, 512:640].rearrange("p (k d) -> p k d", d=64)
                            qsrc = qn[:, 0:8, :].rearrange("p (k g) d -> p k g d", k=2)
                            ksrc = qn[:, 8:10, :]
                            if j < NTL:
                                cosb = cs[:, j:j + 1, 0:32].broadcast_to([128, 10, 32])
                                sinb = cs[:, j:j + 1, 32:64].broadcast_to([128, 10, 32])
                                x1 = qn[:, :, 0:32]
                                x2 = qn[:, :, 32:64]
                                S.op("dve", lambda e: e.tensor_tensor(out=ta[:], in0=x1, in1=cosb, op=ALU.mult),
                                     reads=["qn", "cs"], writes=["ta"])
                                S.op("dve", lambda e: e.tensor_tensor(out=tb[:], in0=x2, in1=sinb, op=ALU.mult),
                                     reads=["qn", "cs"], writes=["tb"])
                                S.op("dve", lambda e: e.tensor_tensor(out=qdst[:, :, :, 0:32],
                                                                      in0=ta[:, 0:8, :].rearrange("p (k g) d -> p k g d", k=2),
                                                                      in1=tb[:, 0:8, :].rearrange("p (k g) d -> p k g d", k=2),
                                                                      op=ALU.subtract),
                                     reads=["ta", "tb"], writes=["qr_a"])
                                S.op("dve", lambda e: e.tensor_tensor(out=kdst[:, :, 0:32], in0=ta[:, 8:10, :], in1=tb[:, 8:10, :],
                                                                      op=ALU.subtract),
                                     reads=["ta", "tb"], writes=["qr_b"])
                                S.op("dve", lambda e: e.tensor_tensor(out=ta[:], in0=x2, in1=cosb, op=ALU.mult),
                                     reads=["qn", "cs"], writes=["ta"])
                                S.op("dve", lambda e: e.tensor_tensor(out=tb[:], in0=x1, in1=sinb, op=ALU.mult),
                                     reads=["qn", "cs"], writes=["tb"])
                                S.op("dve", lambda e: e.tensor_tensor(out=qdst[:, :, :, 32:64],
                                                                      in0=ta[:, 0:8, :].rearrange("p (k g) d -> p k g d", k=2),
                                                                      in1=tb[:, 0:8, :].rearrange("p (k g) d -> p k g d", k=2),
                                                                      op=ALU.add),
                                     reads=["ta", "tb"], writes=["qr_c"])
                                S.op("dve", lambda e: e.tensor_tensor(out=kdst[:, :, 32:64], in0=ta[:, 8:10, :], in1=tb[:, 8:10, :],
                                                                      op=ALU.add),
                                     reads=["ta", "tb"], writes=["qr_d"])
                            else:
                                S.op("dve", lambda e: e.tensor_copy(out=qdst, in_=qsrc), reads=["qn"], writes=["qr_a", "qr_c"])
                                S.op("dve", lambda e: e.tensor_copy(out=kdst, in_=ksrc), reads=["qn"], writes=["qr_b", "qr_d"])
                            for c in range(5):
                                S.op("pe", lambda e: e.transpose(ptr[:, 2 + c, :], qr[:, c * 128:(c + 1) * 128], identb[:]),
                                     reads=["qr_a", "qr_b", "qr_c", "qr_d", "identb"], writes=[("ptr", 1)])
                            S.op("act", lambda e: e.activation(out=qT_all[:, :, tok], in_=ptr[:, 2:6, :], func=AF.Copy),
                                 reads=[("ptr", 1)], writes=[("qT", j)])
                            for kv_ in range(2):
                                S.op("act", lambda e: e.activation(out=kT_all[kv_ * 64:(kv_ + 1) * 64, kv_, tok],
                                                                   in_=ptr[kv_ * 64:(kv_ + 1) * 64, 6, :], func=AF.Copy),
                                     reads=[("ptr", 1), "kT_zero"], writes=[("kT", j, kv_)])
                            S.op("dve", lambda e: e.tensor_copy(out=v_all[:, j, :, 0:64],
                                                                in_=pz[:, 1408:1536].rearrange("p (k d) -> p k d", d=64)),
                                 reads=[("pz", 2)], writes=[("v", j)])
                    S.barrier()
                    esC = ExitStack()
                    with esC:
                        bandf = SB(esC, "bandf", [128, 4, 5, 128], F32)
                        band = SB(esC, "band", [128, 4, 5, 128], BF16)
                        wbd = SB(esC, "wbd", [128, 2, 128], BF16)
                        pscT = SB(esC, "pscT", [128, 2], F32)
                        pooledT = SB(esC, "pooledT", [128, 2, 128], BF16)
                        ppool = PS(esC, "ppool", [128, 4, 128], F32)
                        pp2 = PS(esC, "pp2", [128, 2, 128], F32)
                        S.dma("sp", lambda e: e.dma_start(out=bandf[:], in_=band_in), writes=["bandf"])
                        S.op("dve", lambda e: e.tensor_copy(out=band[:], in_=bandf[:]), reads=["bandf"], writes=["band"])
                        S.dma("pool", lambda e: e.dma_start(out=wbd[:], in_=wbd_in[ll].rearrange("c p q -> p c q")), writes=["wbd"])
                        S.dma("sp", lambda e: e.dma_start(out=pscT[:], in_=pscT_in[ll]), writes=["pscT"])
                        for j in range(NT):
                            tok = slice(j * 128, (j + 1) * 128)
                            lo_t, hi_t = (0, NTL - 1) if j < NTL else (NTL, NT - 1)
                            rels = [r for r in (-1, 0, 1) if lo_t <= j + r <= hi_t]
                            for c in range(2):
                                for bi in range(2):
                                    g = 2 * c + bi
                                    for ri, r in enumerate(rels):
                                        if r == -1:
                                            v = 0
                                        elif r == 1:
                                            v = 4
                                        else:
                                            v = 2 if j == lo_t else (3 if j == hi_t else 1)
                                        S.op("pe", lambda e: e.matmul(ppool[:, c * 2 + bi, :], lhsT=p_all[:, j + r, c * 128:(c + 1) * 128],
                                                                      rhs=band[:, g, v, :], start=(ri == 0), stop=(ri == len(rels) - 1)),
                                             reads=["band"], writes=["ppool"])
                                S.op("act", lambda e: e.activation(out=pooledT[0:64, c, :], in_=ppool[0:64, c * 2, :], func=AF.Copy),
                                     reads=["ppool"], writes=[("pooledT", c, 0)])
                                S.op("act", lambda e: e.activation(out=pooledT[64:128, c, :], in_=ppool[64:128, c * 2 + 1, :], func=AF.Copy),
                                     reads=["ppool"], writes=[("pooledT", c, 1)])
                            for c in range(2):
                                S.op("pe", lambda e: e.matmul(pp2[:, c, :], lhsT=wbd[:, c, :], rhs=pooledT[:, c, :], start=True, stop=True),
                                     reads=["wbd", ("pooledT", c, 0), ("pooledT", c, 1)], writes=["pp2"])
                            for c in range(2):
                                S.op("act", lambda e: e.activation(out=mixps[:, c, tok], in_=pp2[:, c, :], func=AF.Copy,
                                                                   scale=pscT[:, c:c + 1]),
                                     reads=["pp2", "pscT"], writes=[("mixps_p", j)])
                    S.barrier()
                if debug and ll == 0:
                    S.dma("sp", lambda e: e.dma_start(out=dbg_mix, in_=mixps[:]), writes=["dbgmix"])
                esD = ExitStack()
                with esD:
                    woutb = SB(esD, "woutb", [128, 8, D], BF16)
                    lnr = SB(esD, "lnr", [128, 2, D], F32)
                    wr = SB(esD, "wr", [128, 8, NE], F32)
                    PT = [SB(esD, "PT%d" % i, [128, 512], BF16) for i in range(3)]
                    attn_tok = [SB(esD, "attn_tok%d" % i, [128, 4, 512], BF16) for i in range(2)]
                    rden = SB(esD, "rden", [128, 4], F32)
                    mixA = SB(esD, "mixA", [128, 4, 128], BF16)
                    bufA = SB(esD, "bufA", [128, D], F32)
                    bufB = SB(esD, "bufB", [128, D], F32)
                    xhb = SB(esD, "xhb", [128, D], BF16)
                    h2T = SB(esD, "h2T", [128, 8, 128], F32)
                    st = SB(esD, "stE", [128, 2, 6], F32)
                    mv = SB(esD, "mvE", [128, 2], F32)
                    rstd = SB(esD, "rstdE", [128, 1], F32)
                    mx = SB(esD, "mx", [128, 1], F32)
                    sm = SB(esD, "sm", [128, 1], F32)
                    ex = SB(esD, "ex", [128, NE], F32)
                    g1rows = build_gate_rows(esD, 16, "g1r")
                    pS = [PS(esD, "pS%d" % i, [128, 512], F32) for i in range(3)]
                    pO = [PS(esD, "pO%d" % i, [128, 4, 128], F32) for i in range(2)]
                    ptr = PS(esD, "ptrD", [128, 4, 128], BF16)
                    pbig = PS(esD, "pbig", [128, D], F32)
                    pr = pbig[:, 0:NE]
                    S.dma("pool", lambda e: e.dma_start(out=woutb[:], in_=w_out[ll].rearrange("(k p) c -> p k c", p=128)),
                          writes=["woutb"])
                    S.dma("sp", lambda e: e.dma_start(out=lnr[:], in_=ln_in[ll:ll + 1, 0:2, :].broadcast_to([128, 2, D])),
                          writes=["lnr"])
                    S.dma("sp", lambda e: e.dma_start(out=wr[:], in_=wr_in[ll]), writes=["wr"])

                    def rstd_act(tag):
                        S.op("act", lambda e: e.activation(out=rstd[:], in_=mv[:, 1:2], func=AF.Ln, bias=epsc[:], scale=1.0),
                             reads=[tag + "mv", "epsc"], writes=[tag + "rstd"])
                        S.op("act", lambda e: e.activation(out=rstd[:], in_=rstd[:], func=AF.Exp, scale=-0.5),
                             reads=[tag + "rstd"], writes=[tag + "rstd"])

                    def stats_dve(src, key_src, tag):
                        S.op("dve", lambda e: e.bn_stats(st[:, 0, :], src[:, 0:512]), reads=[key_src], writes=[tag + "st0"])
                        S.op("dve", lambda e: e.bn_stats(st[:, 1, :], src[:, 512:1024]), reads=[key_src], writes=[tag + "st1"])
                        S.op("dve", lambda e: e.bn_aggr(mv[:], st[:]), reads=[tag + "st0", tag + "st1"], writes=[tag + "mv"])

                    def make_ef_stages(j, qs, par):
                        s = 0 if j < NTL else 1
                        tok = slice(j * 128, (j + 1) * 128)
                        at = attn_tok[par]

                        def st0():
                            for cc in range(4):
                                S.op("pe", lambda e: e.transpose(ptr[:, cc, :], at[:, qs, cc * 128:(cc + 1) * 128], identb[:]),
                                     reads=[("attn_tok", par, qs), "identb"], writes=["ptrD"])
                            S.dma("sp", lambda e: e.dma_start(out=bufA[:], in_=src_tile(ll, j)), reads=[src_key(ll, j)], writes=["bufA"])

                        def st1():
                            S.op("act", lambda e: e.activation(out=mixA[:], in_=ptr[:], func=AF.Copy), reads=["ptrD"], writes=["mixA"])

                        def st2():
                            for half in range(2):
                                for c8 in range(8):
                                    lhs = mixps[:, c8, tok] if c8 < 4 else mixA[:, c8 - 4, :]
                                    S.op("pe", lambda e: e.matmul(pbig[:, half * 512:(half + 1) * 512], lhsT=lhs,
                                                                  rhs=woutb[:, c8, half * 512:(half + 1) * 512],
                                                                  start=(c8 == 0), stop=(c8 == 7)),
                                         reads=["mixA", "woutb"], writes=["pbig"])

                        def st3():
                            S.op("dve", lambda e: e.tensor_tensor(out=bufB[:], in0=pbig[:], in1=g1rows[s][:], op=ALU.mult),
                                 reads=["pbig", "g1r_%d" % s], writes=["bufB"])
                            S.op("dve", lambda e: e.scalar_tensor_tensor(out=bufA[:], in0=bufA[:], scalar=ALPHA, in1=bufB[:],
                                                                         op0=ALU.mult, op1=ALU.add),
                                 reads=["bufA", "bufB"], writes=["bufA"])
                            stats_dve(bufA, "bufA", "E")

                        def st4():
                            rstd_act("E")

                        def st5():
                            S.op("dve", lambda e: e.tensor_scalar(out=bufB[:], in0=bufA[:], scalar1=mv[:, 0:1], scalar2=rstd[:],
                                                                  op0=ALU.subtract, op1=ALU.mult),
                                 reads=["bufA", "Emv", "Erstd"], writes=["bufB"])
                            S.op("dve", lambda e: e.tensor_tensor(out=bufB[:], in0=bufB[:], in1=lnr[:, 0, :], op=ALU.mult),
                                 reads=["bufB", "lnr"], writes=["bufB"])
                            S.op("dve", lambda e: e.tensor_tensor(out=bufB[:], in0=bufB[:], in1=lnr[:, 1, :], op=ALU.add),
                                 reads=["bufB", "lnr"], writes=["bufB"])
                            S.op("dve", lambda e: e.tensor_scalar(out=bufA[:], in0=bufB[:], scalar1=ALPHA, scalar2=None, op0=ALU.mult),
                                 reads=["bufB"], writes=["bufA"])
                            S.dma("sp", lambda e: e.dma_start(out=macc[tok, :], in_=bufA[:]), reads=["bufA"], writes=[("macc", j)])
                            if debug and ll == 0:
                                S.dma("sp", lambda e: e.dma_start(out=dbg_x1[tok, :], in_=bufA[:]), reads=["bufA"], writes=[("dbgx1", j)])
                            stats_dve(bufB, "bufB", "E")

                        def st6():
                            rstd_act("E")

                        def st7():
                            S.op("dve", lambda e: e.tensor_scalar(out=bufB[:], in0=bufB[:], scalar1=mv[:, 0:1], scalar2=rstd[:],
                                                                  op0=ALU.subtract, op1=ALU.mult),
                                 reads=["bufB", "Emv", "Erstd"], writes=["bufB"])
                            S.op("dve", lambda e: e.tensor_copy(out=xhb[:], in_=bufB[:]), reads=["bufB"], writes=["xhb"])
                            S.dma("sp", lambda e: e.dma_start(out=xh2[tok, :], in_=xhb[:]), reads=["xhb"], writes=[("xh2", j)])

                        def st8():
                            for k in range(8):
                                S.op("pe", lambda e: e.transpose(pbig[:, k * 128:(k + 1) * 128], bufB[:, k * 128:(k + 1) * 128], identf[:]),
                                     reads=["bufB", "identf"], writes=["pbig"])

                        def st9():
                            for k in range(8):
                                S.op("dve", lambda e: e.tensor_scalar(out=h2T[:, k, :], in0=pbig[:, k * 128:(k + 1) * 128],
                                                                      scalar1=modT[:, 32 + k, s:s + 1], scalar2=modT[:, 24 + k, s:s + 1],
                                                                      op0=ALU.mult, op1=ALU.add),
                                     reads=["pbig", "modT"], writes=[("h2T", k)])

                        def st10():
                            for k in range(8):
                                S.op("pe", lambda e: e.matmul(pr, lhsT=h2T[:, k, :], rhs=wr[:, k, :], start=(k == 0), stop=(k == 7)),
                                     reads=[("h2T", k), "wr"], writes=["pbig"])

                        def st11():
                            S.op("dve", lambda e: e.reduce_max(out=mx[:], in_=pr, axis=AX.X), reads=["pbig"], writes=["mx"])
                            S.op("dve", lambda e: e.tensor_scalar(out=mx[:], in0=mx[:], scalar1=-1.0, scalar2=None, op0=ALU.mult),
                                 reads=["mx"], writes=["mx"])

                        def st12():
                            S.op("act", lambda e: e.activation(out=ex[:], in_=pr, func=AF.Exp, bias=mx[:], scale=1.0, accum_out=sm[:]),
                                 reads=["pbig", "mx"], writes=["ex", "sm"])

                        def st13():
                            S.op("dve", lambda e: e.reciprocal(sm[:], sm[:]), reads=["sm"], writes=["sm"])
                            S.op("dve", lambda e: e.tensor_scalar(out=aff_all[:, j, :], in0=ex[:], scalar1=sm[:], scalar2=None, op0=ALU.mult),
                                 reads=["ex", "sm"], writes=[("aff", j)])

                        costs = [1, 1, 4, 5, 1, 8, 1, 3, 2, 2, 2, 1, 1, 1]
                        return list(zip([st0, st1, st2, st3, st4, st5, st6, st7, st8, st9, st10, st11, st12, st13], costs))

                    blocks = [(NTL, 2)] + [(qb * 4, 4) for qb in range(8)]
                    nS = 0
                    nO = 0
                    pending = []
                    for bi, (t0, ntile) in enumerate(blocks):
                        par = bi % 2
                        N = ntile * 128
                        qtok = slice(t0 * 128, t0 * 128 + N)
                        kts = list(range(NT)) if t0 < NTL else [NTL, NTL + 1]
                        steps = [(c, kv, ki, kt) for c in range(4) for kv in range(2) for ki, kt in enumerate(kts)]
                        tot_cost = sum(c_ for (_, c_) in pending)
                        unit = (len(steps) - 6) / float(tot_cost) if tot_cost else 0.0
                        wait_ = 2.0

                        def emit_S(i):
                            c, kv, ki, kt = steps[i]
                            pSb = pS[(nS + i) % 3]
                            pSk = "pS%d" % ((nS + i) % 3)
                            PTb = PT[(nS + i) % 3]
                            PTk = "PT%d" % ((nS + i) % 3)
                            S.op("pe", lambda e: e.matmul(pSb[:, 0:N], lhsT=kT_all[:, kv, kt * 128:(kt + 1) * 128],
                                                          rhs=qT_all[:, c, qtok], start=True, stop=True),
                                 writes=[pSk])
                            S.op("act", lambda e: e.activation(out=PTb[:, 0:N], in_=pSb[:, 0:N], func=AF.Exp, scale=0.125),
                                 reads=[pSk], writes=[PTk])

                        emit_S(0)
                        emit_S(1)
                        for i, (c, kv, ki, kt) in enumerate(steps):
                            h = kv * 4 + c
                            if i + 2 < len(steps):
                                emit_S(i + 2)
                            wait_ -= 1.0
                            if pending and wait_ <= 0.0:
                                fn_, c_ = pending.pop(0)
                                fn_()
                                wait_ += c_ * unit
                            if ki == 0:
                                nO += 1
                                if nO == 1:
                                    S.op("dve", lambda e: e.memset(pO[nO % 2][:], 0.0), writes=["pO%d" % (nO % 2)])
                            if ki == 1:
                                S.op("dve", lambda e: e.memset(pO[(nO + 1) % 2][:], 0.0), writes=["pO%d" % ((nO + 1) % 2)])
                            pOb = pO[nO % 2]
                            pOk = "pO%d" % (nO % 2)
                            PTb = PT[(nS + i) % 3]
                            PTk = "PT%d" % ((nS + i) % 3)
                            for qs in range(ntile):
                                S.op("pe", lambda e: e.matmul(pOb[:, qs, 0:65], lhsT=PTb[:, qs * 128:(qs + 1) * 128],
                                                              rhs=v_all[:, kt, kv, :], start=False, stop=(ki == len(kts) - 1)),
                                     reads=[PTk], writes=[pOk])
                            if ki == len(kts) - 1:
                                S.op("dve", lambda e: e.reciprocal(rden[:, 0:ntile], pOb[:, 0:ntile, 64]),
                                     reads=[pOk], writes=["rden"])
                                for qs in range(ntile):
                                    S.op("dve", lambda e: e.tensor_scalar(out=attn_tok[par][:, qs, h * 64:(h + 1) * 64], in0=pOb[:, qs, 0:64],
                                                                          scalar1=rden[:, qs:qs + 1], scalar2=None, op0=ALU.mult),
                                         reads=[pOk, "rden"], writes=[("attn_tok", par, qs)])
                        nS += len(steps)
                        while pending:
                            pending.pop(0)[0]()
                        for qs in range(ntile):
                            pending.extend(make_ef_stages(t0 + qs, qs, par))
                    while pending:
                        pending.pop(0)[0]()
                S.barrier()
            if debug and ll == 0:
                S.dma("sp", lambda e: e.dma_start(out=dbg_aff, in_=aff_all[:]), writes=["dbgaff"])
            esGH = ExitStack()
            with esGH:
                idx_i = SB(esGH, "idx_i", [128, NE, 5], I32)
                gate_s = SB(esGH, "gate_s", [128, NE, 5], F32)
                esG = ExitStack()
                with esG:
                    lo = SB(esG, "lo", [128, 2, NE], F32)
                    hi = SB(esG, "hi", [128, 2, NE], F32)
                    mid = SB(esG, "mid", [128, 2, NE], F32)
                    capv = SB(esG, "capv", [128, 2, NE], F32)
                    cmp_ = SB(esG, "cmp", [128, NT, NE], F32)
                    cnt = SB(esG, "cnt", [128, 2, NE], F32)
                    ge = SB(esG, "ge", [128, 2, NE], F32)
                    d1 = SB(esG, "d1", [128, 2, NE], F32)
                    mask = SB(esG, "mask", [128, NT, NE], F32)
                    tg = SB(esG, "tg", [128, NT, NE, 5], BF16)
                    gr1 = SB(esG, "gr1", [128, NT, NE], F32)
                    gr2 = SB(esG, "gr2", [128, NT, NE], F32)
                    lst = SB(esG, "lst", [128, 5, 5], F32)
                    pos = SB(esG, "pos", [128, NT, NE], F32)
                    off = SB(esG, "off", [128, NT, NE], F32)
                    tot = SB(esG, "tot", [128, NT, NE], F32)
                    oh = [SB(esG, "oh%d" % i, [128, 512], BF16) for i in range(4)]
                    idx_f = SB(esG, "idx_f", [128, NE, 5], F32)
                    ptot = PS(esG, "ptot", [128, 2, NE], F32)
                    pwl = PS(esG, "pwl", [128, 512], F32)
                    pwc = PS(esG, "pwc", [128, 32], F32)
                    ptl = PS(esG, "ptl", [128, 512], F32)
                    ptc = PS(esG, "ptc", [128, 32], F32)
                    plist = [PS(esG, "plist0", [128, 5, 128], F32)]
                    S.op("pool", lambda e: e.memset(lo[:], 0.0), writes=["lo"])
                    S.op("pool", lambda e: e.memset(hi[:], 1.0), writes=["hi"])
                    S.op("pool", lambda e: e.memset(capv[:, 0, :], float(CAP_L)), writes=["capv0"])
                    S.op("pool", lambda e: e.memset(capv[:, 1, :], float(CAP_C)), writes=["capv1"])
                    aff_l = aff_all[:, 0:NTL, :]
                    aff_c = aff_all[:, NTL:NT, :]
                    for it in range(30):
                        half_ = 0.5 ** (it + 1)
                        S.op("dve", lambda e: e.tensor_scalar(out=mid[:], in0=lo[:], scalar1=half_, scalar2=None, op0=ALU.add),
                             reads=["lo"], writes=["mid"])
                        S.op("dve", lambda e: e.tensor_tensor(out=cmp_[:, 0:NTL, :], in0=aff_l,
                                                              in1=mid[:, 0:1, :].broadcast_to([128, NTL, NE]), op=ALU.is_ge),
                             reads=["mid"], writes=["cmp_l"])
                        S.op("dve", lambda e: e.tensor_tensor(out=cmp_[:, NTL:NT, :], in0=aff_c,
                                                              in1=mid[:, 1:2, :].broadcast_to([128, 2, NE]), op=ALU.is_ge),
                             reads=["mid"], writes=["cmp_c"])
                        S.op("dve", lambda e: e.tensor_reduce(out=cnt[:, 0, :], in_=cmp_[:, 0:NTL, :].rearrange("p j e -> p e j"),
                                                              axis=AX.X, op=ALU.add),
                             reads=["cmp_l"], writes=["cnt0"])
                        S.op("dve", lambda e: e.tensor_reduce(out=cnt[:, 1, :], in_=cmp_[:, NTL:NT, :].rearrange("p j e -> p e j"),
                                                              axis=AX.X, op=ALU.add),
                             reads=["cmp_c"], writes=["cnt1"])
                        S.op("pe", lambda e: e.matmul(ptot[:], lhsT=ones_f[:], rhs=cnt[:], start=True, stop=True),
                             reads=["cnt0", "cnt1", "ones_f"], writes=["ptot"])
                        S.op("dve", lambda e: e.tensor_tensor(out=ge[:], in0=ptot[:], in1=capv[:], op=ALU.is_ge),
                             reads=["ptot", "capv0", "capv1"], writes=["ge"])
                        S.op("dve", lambda e: e.scalar_tensor_tensor(out=lo[:], in0=ge[:], scalar=half_, in1=lo[:],
                                                                     op0=ALU.mult, op1=ALU.add),
                             reads=["ge", "lo"], writes=["lo"])
                    S.op("dve", lambda e: e.tensor_tensor(out=mask[:, 0:NTL, :], in0=aff_l,
                                                          in1=lo[:, 0:1, :].broadcast_to([128, NTL, NE]), op=ALU.is_ge),
                         reads=["lo"], writes=["mask_l"])
                    S.op("dve", lambda e: e.tensor_tensor(out=mask[:, NTL:NT, :], in0=aff_c,
                                                          in1=lo[:, 1:2, :].broadcast_to([128, 2, NE]), op=ALU.is_ge),
                         reads=["lo"], writes=["mask_c"])
                    S.op("dve", lambda e: e.tensor_copy(out=tg[:, :, :, 0], in_=tokA[:].unsqueeze(2).broadcast_to([128, NT, NE])),
                         reads=["tokA"], writes=["tg0"])
                    S.op("dve", lambda e: e.tensor_copy(out=tg[:, :, :, 1], in_=tokB[:].unsqueeze(2).broadcast_to([128, NT, NE])),
                         reads=["tokB"], writes=["tg0"])
                    S.op("dve", lambda e: e.tensor_copy(out=tg[:, :, :, 2], in_=aff_all[:]), writes=["tg1"])
                    S.op("dve", lambda e: e.tensor_tensor(out=gr1[:], in0=aff_all[:], in1=tg[:, :, :, 2], op=ALU.subtract),
                         reads=["tg1"], writes=["gr1"])
                    S.op("dve", lambda e: e.tensor_copy(out=tg[:, :, :, 3], in_=gr1[:]), reads=["gr1"], writes=["tg1"])
                    S.op("dve", lambda e: e.tensor_tensor(out=gr2[:], in0=gr1[:], in1=tg[:, :, :, 3], op=ALU.subtract),
                         reads=["gr1", "tg1"], writes=["gr2"])
                    S.op("dve", lambda e: e.tensor_copy(out=tg[:, :, :, 4], in_=gr2[:]), reads=["gr2"], writes=["tg1"])
                    mk2 = mask[:].rearrange("p j e -> p (j e)")
                    S.op("pe", lambda e: e.matmul(pwl[:], lhsT=lmat[:], rhs=mk2[:, 0:512], start=True, stop=True),
                         reads=["mask_l", "lmat"], writes=["pwl"])
                    S.op("pe", lambda e: e.matmul(pwc[:], lhsT=lmat[:], rhs=mk2[:, 512:544], start=True, stop=True),
                         reads=["mask_c", "lmat"], writes=["pwc"])
                    S.op("pe", lambda e: e.matmul(ptl[:], lhsT=ones_f[:], rhs=mk2[:, 0:512], start=True, stop=True),
                         reads=["mask_l", "ones_f"], writes=["ptl"])
                    S.op("pe", lambda e: e.matmul(ptc[:], lhsT=ones_f[:], rhs=mk2[:, 512:544], start=True, stop=True),
                         reads=["mask_c", "ones_f"], writes=["ptc"])
                    pos2 = pos[:].rearrange("p j e -> p (j e)")
                    tot2 = tot[:].rearrange("p j e -> p (j e)")
                    S.op("act", lambda e: e.activation(out=pos2[:, 0:512], in_=pwl[:], func=AF.Copy), reads=["pwl"], writes=["pos_l"])
                    S.op("act", lambda e: e.activation(out=pos2[:, 512:544], in_=pwc[:], func=AF.Copy), reads=["pwc"], writes=["pos_c"])
                    S.op("act", lambda e: e.activation(out=tot2[:, 0:512], in_=ptl[:], func=AF.Copy), reads=["ptl"], writes=["tot"])
                    S.op("act", lambda e: e.activation(out=tot2[:, 512:544], in_=ptc[:], func=AF.Copy), reads=["ptc"], writes=["tot"])
                    S.op("pool", lambda e: e.memset(off[:], 0.0), writes=["off"])
                    for j in range(1, NTL):
                        S.op("dve", lambda e: e.tensor_tensor(out=off[:, j, :], in0=off[:, j - 1, :], in1=tot[:, j - 1, :], op=ALU.add),
                             reads=["off", "tot"], writes=["off"])
                    S.op("dve", lambda e: e.tensor_copy(out=off[:, NTL + 1, :], in_=tot[:, NTL, :]), reads=["off", "tot"], writes=["off"])
                    S.op("dve", lambda e: e.tensor_tensor(out=pos[:], in0=pos[:], in1=off[:], op=ALU.add),
                         reads=["pos_l", "pos_c", "off"], writes=["pos_l", "pos_c"])
                    S.op("dve", lambda e: e.scalar_tensor_tensor(out=pos[:], in0=pos[:], scalar=-BIG, in1=mask[:], op0=ALU.add, op1=ALU.mult),
                         reads=["pos_l", "pos_c", "mask_l", "mask_c"], writes=["pos_l", "pos_c"])
                    S.op("dve", lambda e: e.tensor_scalar(out=pos[:], in0=pos[:], scalar1=BIG, scalar2=None, op0=ALU.add),
                         reads=["pos_l", "pos_c"], writes=["pos"])
                    n_oh = 0
                    for ex_ in range(NE):
                        pl = plist[0]
                        plk = "plist0"
                        S.op("dve", lambda e: e.memset(pl[:], 0.0), writes=[plk])
                        for j in range(NT):
                            ohb = oh[n_oh % 4]
                            ohk = "oh%d" % (n_oh % 4)
                            eng_ = "dve"
                            n_oh += 1
                            if j < NTL:
                                S.op(eng_, lambda e: e.tensor_scalar(out=ohb[:], in0=iota_row[:], scalar1=pos[:, j, ex_:ex_ + 1],
                                                                     scalar2=None, op0=ALU.is_equal),
                                     reads=["iota_row", "pos"], writes=[ohk])
                                for st_ in range(4):
                                    S.op("pe", lambda e: e.matmul(pl[:, st_, 0:5], lhsT=ohb[:, st_ * 128:(st_ + 1) * 128],
                                                                  rhs=tg[:, j, ex_, :], start=False, stop=(j == NTL - 1)),
                                         reads=[ohk, "tg0", "tg1"], writes=[plk])
                            else:
                                S.op(eng_, lambda e: e.tensor_scalar(out=ohb[:, 0:128], in0=iota_row[:, 0:128], scalar1=pos[:, j, ex_:ex_ + 1],
                                                                     scalar2=None, op0=ALU.is_equal),
                                     reads=["iota_row", "pos"], writes=[ohk])
                                S.op("pe", lambda e: e.matmul(pl[:, 4, 0:5], lhsT=ohb[:, 0:128], rhs=tg[:, j, ex_, :],
                                                              start=False, stop=(j == NT - 1)),
                                     reads=[ohk, "tg0", "tg1"], writes=[plk])
                        S.op("act", lambda e: e.activation(out=lst[:], in_=pl[:, :, 0:5], func=AF.Copy),
                             reads=[plk], writes=["lst"])
                        S.op("dve", lambda e: e.tensor_tensor(out=idx_f[:, ex_, :], in0=lst[:, :, 0], in1=lst[:, :, 1], op=ALU.add),
                             reads=["lst"], writes=["idx_f"])
                        S.op("dve", lambda e: e.tensor_tensor(out=gate_s[:, ex_, :], in0=lst[:, :, 2], in1=lst[:, :, 3], op=ALU.add),
                             reads=["lst"], writes=["gate_s"])
                        S.op("dve", lambda e: e.tensor_tensor(out=gate_s[:, ex_, :], in0=gate_s[:, ex_, :], in1=lst[:, :, 4], op=ALU.add),
                             reads=["lst", "gate_s"], writes=["gate_s"])
                    S.op("dve", lambda e: e.tensor_copy(out=idx_i[:], in_=idx_f[:]), reads=["idx_f"], writes=["idx_i"])
                    if debug and ll == 0:
                        S.dma("sp", lambda e: e.dma_start(out=dbg_idx, in_=idx_i[:]), reads=["idx_i"], writes=["dbgidx"])
                        S.dma("sp", lambda e: e.dma_start(out=dbg_gate, in_=gate_s[:]), reads=["gate_s"], writes=["dbggate"])
                        S.dma("sp", lambda e: e.dma_start(out=dbg_pos, in_=pos[:]), reads=["pos"], writes=["dbgpos"])
                S.barrier()
                esH = ExitStack()
                with esH:
                    g2rows = build_gate_rows(esH, 40, "g2r")
                    wb = [[SB(esH, "wb%d_%d" % (m, i), [128, 8, D], BF16) for m in range(3)] for i in range(2)]
                    xg = [[SB(esH, "xg%d_%d" % (st_, i), [128, D], BF16) for st_ in range(5)] for i in range(2)]
                    xgT = [SB(esH, "xgT%d" % i, [128, 8, 544], BF16) for i in range(2)]
                    hidT = SB(esH, "hidT", [128, 8, 544], BF16)
                    s1 = SB(esH, "s1", [128, 544], F32)
                    ysc = [SB(esH, "ysc%d" % i, [128, D], F32) for i in range(4)]
                    pxT = [PS(esH, "pxT%d" % i, [128, 8, 128], BF16) for i in range(1)]
                    ph1 = [PS(esH, "ph1_%d" % i, [128, 512], F32) for i in range(2)]
                    ph3 = [PS(esH, "ph3_%d" % i, [128, 512], F32) for i in range(2)]
                    phc = PS(esH, "phc", [128, 2, 32], F32)
                    py = [PS(esH, "pyH%d" % i, [128, 512], F32) for i in range(2)]
                    wsrc = (w1, w3, w2)

                    def issue_loads(ex_):
                        i = ex_ % 2
                        for st_ in range(5):
                            npart = 128 if st_ < 4 else CAP_C
                            S.dma("pool", lambda e: e.indirect_dma_start(
                                out=xg[i][st_][0:npart, :], out_offset=None, in_=xh2[:, :],
                                in_offset=bass.IndirectOffsetOnAxis(ap=idx_i[0:npart, ex_, st_:st_ + 1], axis=0),
                                bounds_check=bcreg, oob_is_err=False),
                                reads=["idx_i"] + [("xh2", j) for j in range(NT)], writes=["xg%d_%d" % (st_, i)])
                        for m in range(3):
                            S.dma("pool", lambda e: e.dma_start(out=wb[i][m][:], in_=wsrc[m][ll, ex_].rearrange("(k p) c -> p k c", p=128)),
                                  writes=["wb%d_%d" % (m, i)])

                    def prep_tile(ex_, st_):
                        i = ex_ % 2
                        npart = 128 if st_ < 4 else CAP_C
                        s = 0 if st_ < 4 else 1
                        c0 = st_ * 128
                        npx[0] += 1
                        pb = pxT[0]
                        pk = "pxT0"
                        for k in range(8):
                            S.op("pe", lambda e: e.transpose(pb[:, k, 0:npart], xg[i][st_][0:npart, k * 128:(k + 1) * 128],
                                                             identb[0:npart, 0:npart]),
                                 reads=["xg%d_%d" % (st_, i), "identb"], writes=[pk])
                        for k in range(8):
                            if k % 2 == 0:
                                S.op("act", lambda e: e.activation(out=xgT[i][:, k, c0:c0 + npart], in_=pb[:, k, 0:npart], func=AF.Identity,
                                                                   bias=modT[:, 24 + k, s:s + 1], scale=modT[:, 32 + k, s:s + 1]),
                                     reads=[pk, "modT"], writes=["xgT%d" % i])
                            else:
                                S.op("dve", lambda e: e.tensor_scalar(out=xgT[i][:, k, c0:c0 + npart], in0=pb[:, k, 0:npart],
                                                                      scalar1=modT[:, 32 + k, s:s + 1], scalar2=modT[:, 24 + k, s:s + 1],
                                                                      op0=ALU.mult, op1=ALU.add),
                                     reads=[pk, "modT"], writes=["xgT%d" % i])

                    npx = [0]
                    issue_loads(0)
                    for st_ in range(5):
                        prep_tile(0, st_)
                    nys = 0
                    npy = 0
                    for ex_ in range(NE):
                        i = ex_ % 2
                        xgTi = xgT[i]
                        xk_ = "xgT%d" % i
                        if ex_ + 1 < NE:
                            issue_loads(ex_ + 1)
                        for fc in range(8):
                            fb = fc % 2
                            for m, ph in ((0, ph1[fb]), (1, ph3[fb])):
                                for k in range(8):
                                    S.op("pe", lambda e: e.matmul(ph[:], lhsT=wb[i][m][:, k, fc * 128:(fc + 1) * 128], rhs=xgTi[:, k, 0:512],
                                                                  start=(k == 0), stop=(k == 7)),
                                         reads=[xk_, "wb%d_%d" % (m, i)], writes=["ph%d_%d" % (m, fb)])
                                for k in range(8):
                                    S.op("pe", lambda e: e.matmul(phc[:, m, :], lhsT=wb[i][m][:, k, fc * 128:(fc + 1) * 128], rhs=xgTi[:, k, 512:544],
                                                                  start=(k == 0), stop=(k == 7)),
                                         reads=[xk_, "wb%d_%d" % (m, i)], writes=["phc"])
                            S.op("act", lambda e: e.activation(out=s1[:, 512:544], in_=phc[:, 0, :], func=AF.Silu), reads=["phc"], writes=["s1b"])
                            S.op("dve", lambda e: e.tensor_tensor(out=hidT[:, fc, 512:544], in0=s1[:, 512:544], in1=phc[:, 1, :], op=ALU.mult),
                                 reads=["s1b", "phc"], writes=[("hidT", 1)])
                            S.op("act", lambda e: e.activation(out=s1[:, 0:512], in_=ph1[fb][:], func=AF.Silu), reads=["ph0_%d" % fb], writes=["s1a"])
                            S.op("dve", lambda e: e.tensor_tensor(out=hidT[:, fc, 0:512], in0=s1[:, 0:512], in1=ph3[fb][:], op=ALU.mult),
                                 reads=["s1a", "ph1_%d" % fb], writes=[("hidT", 0)])
                            if ex_ + 1 < NE and fc < 5:
                                prep_tile(ex_ + 1, fc)
                        for st_ in range(5):
                            npart = 128 if st_ < 4 else CAP_C
                            s = 0 if st_ < 4 else 1
                            c0 = st_ * 128
                            yb = ysc[nys % 4]
                            yk = "ysc%d" % (nys % 4)
                            nys += 1
                            for half in range(2):
                                pyb = py[npy % 2]
                                pyk = "pyH%d" % (npy % 2)
                                npy += 1
                                for fc in range(8):
                                    S.op("pe", lambda e: e.matmul(pyb[0:npart, :], lhsT=hidT[:, fc, c0:c0 + npart],
                                                                  rhs=wb[i][2][:, fc, half * 512:(half + 1) * 512],
                                                                  start=(fc == 0), stop=(fc == 7)),
                                         reads=[("hidT", 0), ("hidT", 1), "wb2_%d" % i], writes=[pyk])
                                S.op("dve", lambda e: e.scalar_tensor_tensor(out=yb[0:npart, half * 512:(half + 1) * 512], in0=pyb[0:npart, :],
                                                                             scalar=gate_s[0:npart, ex_, st_:st_ + 1],
                                                                             in1=g2rows[s][0:npart, half * 512:(half + 1) * 512],
                                                                             op0=ALU.mult, op1=ALU.mult),
                                     reads=[pyk, "gate_s", "g2r_%d" % s], writes=[(yk, half)])
                            S.dma("pool", lambda e: e.indirect_dma_start(
                                out=macc[:, :], out_offset=bass.IndirectOffsetOnAxis(ap=idx_i[0:npart, ex_, st_:st_ + 1], axis=0),
                                in_=yb[0:npart, :], in_offset=None, bounds_check=bcreg, oob_is_err=False,
                                compute_op=ALU.add),
                                reads=[(yk, 0), (yk, 1), "idx_i"] + [("msc", ex_ - 1, k) for k in range(5)],
                                writes=[("msc", ex_, st_)])
                S.barrier()
            if debug and ll == 0:
                S.dma("sp", lambda e: e.dma_start(out=dbg_macc, in_=macc), writes=["dbgmacc"])
                S.barrier()
            esI = ExitStack()
            with esI:
                finish_next_A = None
                if ll + 1 < L:
                    finish_next_A = emit_phase_A(ll + 1, esI, "pool")
                lnr2 = SB(esI, "lnr2", [128, 2, D], F32)
                mt = [SB(esI, "mt%d" % i, [128, D], F32) for i in range(4)]
                xo = [SB(esI, "xo%d" % i, [128, D], F32) for i in range(4)]
                st = SB(esI, "stI", [128, 2, 6], F32)
                mv = SB(esI, "mvI", [128, 2], F32)
                rstd = SB(esI, "rstdI", [128, 1], F32)
                nmr = SB(esI, "nmr", [128, 1], F32)
                S.dma("sp", lambda e: e.dma_start(out=lnr2[:], in_=ln_in[ll:ll + 1, 2:4, :].broadcast_to([128, 2, D])),
                      writes=["lnr2"])
                stI = [st, SB(esI, "stI_b", [128, 2, 6], F32)]
                mvI = [mv, SB(esI, "mvI_b", [128, 2], F32)]
                rsI = [rstd, SB(esI, "rstdI_b", [128, 1], F32)]
                nmI = [nmr, SB(esI, "nmr_b", [128, 1], F32)]
                for j0 in range(0, NT, 2):
                    pair = [(j0, 0), (j0 + 1, 1)]
                    bo = (j0 // 2 % 2) * 2
                    for (j, t) in pair:
                        tok = slice(j * 128, (j + 1) * 128)
                        S.dma("sp", lambda e: e.dma_start(out=mt[bo + t][:], in_=macc[tok, :]), reads=[("macc", j)], writes=[("mt", bo + t)])
                    for c_ in range(2):
                        for (j, t) in pair:
                            S.op("dve", lambda e: e.bn_stats(stI[t][:, c_, :], mt[bo + t][:, c_ * 512:(c_ + 1) * 512]),
                                 reads=[("mt", bo + t)], writes=[("Ist", t, c_)])
                    for (j, t) in pair:
                        S.op("dve", lambda e: e.bn_aggr(mvI[t][:], stI[t][:]), reads=[("Ist", t, 0), ("Ist", t, 1)], writes=[("Imv", t)])
                    for (j, t) in pair:
                        S.op("act", lambda e: e.activation(out=rsI[t][:], in_=mvI[t][:, 1:2], func=AF.Sqrt, bias=epsc[:], scale=1.0),
                             reads=[("Imv", t), "epsc"], writes=[("Irs", t)])
                    for (j, t) in pair:
                        S.op("dve", lambda e: e.reciprocal(rsI[t][:], rsI[t][:]), reads=[("Irs", t)], writes=[("Irs", t)])
                    for (j, t) in pair:
                        S.op("dve", lambda e: e.scalar_tensor_tensor(out=nmI[t][:], in0=mvI[t][:, 0:1], scalar=-1.0, in1=rsI[t][:],
                                                                     op0=ALU.mult, op1=ALU.mult),
                             reads=[("Imv", t), ("Irs", t)], writes=[("Inm", t)])
                    for (j, t) in pair:
                        S.op("act", lambda e: e.activation(out=xo[bo + t][:], in_=mt[bo + t][:], func=AF.Identity, bias=nmI[t][:], scale=rsI[t][:]),
                             reads=[("mt", bo + t), ("Inm", t), ("Irs", t)], writes=[("xo", bo + t)])
                    for (j, t) in pair:
                        S.op("dve", lambda e: e.tensor_tensor(out=xo[bo + t][:], in0=xo[bo + t][:], in1=lnr2[:, 0, :], op=ALU.mult),
                             reads=[("xo", bo + t), "lnr2"], writes=[("xo", bo + t)])
                    for (j, t) in pair:
                        S.op("dve", lambda e: e.tensor_tensor(out=xo[bo + t][:], in0=xo[bo + t][:], in1=lnr2[:, 1, :], op=ALU.add),
                             reads=[("xo", bo + t), "lnr2"], writes=[("xo", bo + t)])
                    for (j, t) in pair:
                        S.dma("sp", lambda e: e.dma_start(out=dst_tile(ll, j), in_=xo[bo + t][:]), reads=[("xo", bo + t)], writes=[dst_key(ll, j)])
                if finish_next_A is not None:
                    finish_next_A()
            S.barrier()
        print("instructions", S.n_inst, "waits", S.n_wait, flush=True)
    return nc


def _band_tables():
    wins = (2, 4, 8, 16)
    Ls = 384
    band = np.zeros((128, 4, 5, 128), np.float32)
    for g, w in enumerate(wins):
        A = np.zeros((Ls, Ls), np.float64)
        for t in range(Ls):
            lo = min(max(t - w // 2, 0), Ls)
            hi = min(max(t + w // 2, 0), Ls)
            A[t, lo:hi] = 1.0 / (hi - lo)
            A[t, t] -= 1.0
        AT = A.T
        band[:, g, 0, :] = AT[0:128, 128:256]
        band[:, g, 1, :] = AT[128:256, 128:256]
        band[:, g, 2, :] = AT[0:128, 0:128]
        band[:, g, 3, :] = AT[256:384, 256:384]
        band[:, g, 4, :] = AT[256:384, 128:256]
    return band


def _rope_tables():
    t = np.arange(SEQ)
    r = (t // 64).astype(np.float32)
    col = (t % 64).astype(np.float32)
    inv = (np.float32(10000.0) ** (-np.arange(16, dtype=np.float32) / np.float32(16))).astype(np.float32)
    ang = np.concatenate([r[:, None] * inv, col[:, None] * inv], axis=-1).astype(np.float32)
    cs = np.concatenate([np.cos(ang), np.sin(ang)], axis=-1).astype(np.float32)
    return np.ascontiguousarray(cs.reshape(NTL, 128, 64).transpose(1, 0, 2))


def _prep_common(inp, layers):
    L = len(layers)
    sl = lambda a: np.ascontiguousarray(np.asarray(a)[layers])
    pool_w = sl(inp["pool_w"])
    wbd = np.zeros((L, 2, 128, 128), np.float32)
    for c in range(2):
        for gi in range(2):
            wbd[:, c, gi * 64:(gi + 1) * 64, gi * 64:(gi + 1) * 64] = pool_w[:, 2 * c + gi]
    com = {
        "w_mod": sl(inp["w_mod"]),
        "b_modT": np.ascontiguousarray(sl(inp["b_mod"]).reshape(L, 48, 128).transpose(0, 2, 1)),
        "w_in": sl(inp["w_in"]),
        "wbd": wbd,
        "pscT": np.ascontiguousarray(sl(inp["pool_scale"]).reshape(L, 2, 128).transpose(0, 2, 1)),
        "band": _band_tables(),
        "sgu_g": np.ascontiguousarray(sl(inp["sgu_g"]).reshape(L, 256)),
        "sgu_wT": np.ascontiguousarray(sl(inp["sgu_w"]).transpose(0, 3, 1, 2)),
        "sgu_bT": np.ascontiguousarray(sl(inp["sgu_b"]).transpose(0, 2, 1)),
        "qkg": np.ascontiguousarray(np.concatenate([np.tile(sl(inp["q_g"]), (1, 8)), np.tile(sl(inp["k_g"]), (1, 2))], axis=1)),
        "w_out": sl(inp["w_out"]),
        "ln": np.ascontiguousarray(np.stack([sl(inp["ln1_g"]), sl(inp["ln1_b"]), sl(inp["ln2_g"]), sl(inp["ln2_b"])], axis=1)),
        "w_router": np.ascontiguousarray(sl(inp["w_router"]).reshape(L, 8, 128, NE).transpose(0, 2, 1, 3)),
        "w1": sl(inp["w1"]), "w3": sl(inp["w3"]), "w2": sl(inp["w2"]),
        "cs": _rope_tables(),
    }
    return com


_NC_CACHE = {}


def _run(inp, x, ctx, layers, n_cores=8):
    L = len(layers)
    if L not in _NC_CACHE:
        _NC_CACHE[L] = build(L)
    nc = _NC_CACHE[L]
    com = _prep_common(inp, layers)
    c = np.asarray(inp["c"], np.float32)
    c_ctx = np.asarray(inp["c_ctx"], np.float32)
    in_maps = []
    for b in range(n_cores):
        cond = np.stack([c[b].reshape(8, 128).T, c_ctx.reshape(8, 128).T], axis=-1)
        m = dict(com)
        m["x"] = np.ascontiguousarray(x[b])
        m["ctx"] = np.ascontiguousarray(ctx[b])
        m["cond"] = np.ascontiguousarray(cond.astype(np.float32))
        in_maps.append(m)
    res = run_bass_kernel_spmd(nc, in_maps, core_ids=list(range(n_cores)))
    xo = np.stack([np.asarray(r["out"]) for r in res.results], 0)
    co = np.stack([np.asarray(r["ctx_out"]) for r in res.results], 0)
    return xo, co


LAYERS_PER_LAUNCH = 4


def kernel(**inputs):
    inp = {k: np.asarray(v) for k, v in inputs.items()}
    x = np.asarray(inp["x"], np.float32)
    ctx = np.asarray(inp["ctx"], np.float32)
    for l0 in range(0, DEPTH, LAYERS_PER_LAUNCH):
        x, ctx = _run(inp, x, ctx, list(range(l0, l0 + LAYERS_PER_LAUNCH)))
    return x.astype(np.float32)
```
